# Optimizing a Trainium2 kernel written in Bass

```python
import math
import jax
import jax.numpy as jnp
from jax import lax
import numpy as np

D_MODEL = 1024
BATCH = 8
SEQ = 4096
DEPTH = 4

CTX_LEN = 256
GRID_W = 64
N_MIXERS = 3
N_ATTN = (DEPTH + 2) // N_MIXERS
N_FNET = (DEPTH + 1) // N_MIXERS
N_RWKV = DEPTH // N_MIXERS
NORM_EPS = 1e-6

DA_HEADS = 8
DA_HEAD_DIM = D_MODEL // DA_HEADS // 2
Q_BLOCK = 128
ROPE_BASE = 10000.0
SUBLN_EPS = 1e-5

FN_GROUPS = 8
FN_GROUP_DIM = D_MODEL // FN_GROUPS

RW_HEAD_DIM = 64
RW_HEADS = D_MODEL // RW_HEAD_DIM
RW_DECAY_RANK = 64
RW_ICLR_RANK = 64
RW_GN_EPS = 64e-5

kernel_name = 'hybrid_diffattn_fnet_rwkv7_dit'


def rms_norm(x, gain, eps=NORM_EPS):
    xf = x.astype(jnp.float32)
    y = xf * lax.rsqrt(jnp.mean(xf * xf, axis=-1, keepdims=True) + eps)
    return (y * gain.astype(jnp.float32)).astype(x.dtype)


def ada_modulation(cond, w, b):
    m = jax.nn.silu(cond) @ w + b
    return jnp.split(m, 3, axis=-1)


def axial_rope_tables(n):
    rows = n // GRID_W
    row = jnp.repeat(jnp.arange(rows), GRID_W).astype(jnp.float32)
    col = jnp.tile(jnp.arange(GRID_W), rows).astype(jnp.float32)
    axis_dim = DA_HEAD_DIM // 2
    inv_freq = ROPE_BASE ** (-jnp.arange(0, axis_dim, 2, dtype=jnp.float32) / axis_dim)
    ang = jnp.stack([row[:, None] * inv_freq, col[:, None] * inv_freq], axis=1)
    return jnp.cos(ang), jnp.sin(ang)


def apply_axial_rope(t, cos, sin):
    quarter = DA_HEAD_DIM // 4
    tf = t.astype(jnp.float32).reshape(t.shape[:-1] + (2, 2, quarter))
    t1, t2 = tf[..., 0, :], tf[..., 1, :]
    c = cos[:, None, None]
    s = sin[:, None, None]
    out = jnp.stack([t1 * c - t2 * s, t2 * c + t1 * s], axis=-2)
    return out.reshape(t.shape).astype(t.dtype)


def diff_attend(q, k, v, lam):
    s = jnp.einsum('bqhzd,bkhzd->bhzqk', q, k).astype(jnp.float32)
    p = jax.nn.softmax(s, axis=-1)
    a = p[:, :, 0] - lam * p[:, :, 1]
    return jnp.einsum('bhqk,bkhe->bqhe', a.astype(v.dtype), v)


def diff_attention(h_lat, h_ctx, cos, sin, w_in, lam_q, lam_k, subln_gain, w_out, lambda_init, need_ctx_out):
    B, N, D = h_lat.shape
    H, d = DA_HEADS, DA_HEAD_DIM
    scale = d ** -0.5

    def project(h):
        L = h.shape[1]
        q, k, v, z = jnp.split(h @ w_in, 4, axis=-1)
        return (q.reshape(B, L, H, 2, d) * scale, k.reshape(B, L, H, 2, d),
                v.reshape(B, L, H, 2 * d), z)

    q_l, k_l, v_l, z_l = project(h_lat)
    q_c, k_c, v_c, z_c = project(h_ctx)
    q_l = apply_axial_rope(q_l, cos, sin)
    k_l = apply_axial_rope(k_l, cos, sin)
    lq = lam_q.astype(jnp.float32)
    lk = lam_k.astype(jnp.float32)
    lam = jnp.exp(jnp.sum(lq[0] * lk[0])) - jnp.exp(jnp.sum(lq[1] * lk[1])) + lambda_init

    def finish(o, z):
        L = o.shape[1]
        of = o.astype(jnp.float32)
        of = of * lax.rsqrt(jnp.mean(of * of, axis=-1, keepdims=True) + SUBLN_EPS)
        of = of * subln_gain.astype(jnp.float32) * (1.0 - lambda_init)
        return (of.reshape(B, L, D).astype(z.dtype) * jax.nn.silu(z)) @ w_out

    k_all = jnp.concatenate([k_c, k_l], axis=1)
    v_all = jnp.concatenate([v_c, v_l], axis=1)
    n_blocks = N // Q_BLOCK
    q_blocks = q_l.reshape(B, n_blocks, Q_BLOCK, H, 2, d).swapaxes(0, 1)
    o_l = lax.map(lambda qb: diff_attend(qb, k_all, v_all, lam), q_blocks)
    o_l = o_l.swapaxes(0, 1).reshape(B, N, H, 2 * d)
    y_l = finish(o_l, z_l)
    y_c = finish(diff_attend(q_c, k_c, v_c, lam), z_c) if need_ctx_out else None
    return y_l, y_c


def fourier_mix(h, w_in, w_group, w_out):
    B, L, D = h.shape
    u, z = jnp.split(h @ w_in, 2, axis=-1)
    ug = u.astype(jnp.float32).reshape(B, L, FN_GROUPS, FN_GROUP_DIM)
    f = jnp.fft.fftn(ug, axes=(1, 3), norm='ortho').real.astype(h.dtype)
    y = jnp.einsum('blgc,gce->blge', f, w_group).reshape(B, L, D)
    return (y * jax.nn.silu(z)) @ w_out


def centred_shift(s):
    sp = jnp.pad(s, ((0, 0), (1, 1), (0, 0)))
    return 0.5 * (sp[:, :-2] + sp[:, 2:])


def rwkv7_features(h, w_in, mu, w0, w_up, a0, a_up, k_k, k_a):
    B, L, D = h.shape
    p = h @ w_in
    s, z = p[..., :-D], p[..., -D:]
    s = s + mu * (centred_shift(s) - s)
    r, k, v = s[..., :D], s[..., D:2 * D], s[..., 2 * D:3 * D]
    wd = s[..., 3 * D:3 * D + 2 * RW_DECAY_RANK].reshape(B, L, 2, RW_DECAY_RANK)
    ad = s[..., 3 * D + 2 * RW_DECAY_RANK:].reshape(B, L, 2, RW_ICLR_RANK)
    w_pre = (w0 + jnp.einsum('blzr,zrd->blzd', jnp.tanh(wd), w_up)).astype(jnp.float32)
    decay = jnp.exp(-jnp.exp(-jax.nn.softplus(-w_pre) - 0.5))
    a = jax.nn.sigmoid((a0 + jnp.einsum('blzr,zrd->blzd', ad, a_up)).astype(jnp.float32))
    kk = (k * k_k).astype(jnp.float32).reshape(B, L, RW_HEADS, RW_HEAD_DIM)
    kk = (kk * lax.rsqrt(jnp.sum(kk * kk, axis=-1, keepdims=True) + 1e-12)).reshape(B, L, D)
    k_dir = k.astype(jnp.float32)[:, :, None] * (1.0 + (a - 1.0) * k_a.astype(jnp.float32))
    b = kk[:, :, None] * a
    return r.astype(jnp.float32), v.astype(jnp.float32), z, decay, kk, k_dir, b


def to_scan_layout(t):
    B, L = t.shape[:2]
    t = jnp.stack([t[:, :, 0], jnp.flip(t[:, :, 1], axis=1)], axis=0)
    return t.transpose(2, 0, 1, 3).reshape(L, 2, B, RW_HEADS, RW_HEAD_DIM)


def wkv7_step(S, inp):
    r, w, k, v, a, b = inp
    sa = jnp.einsum('zbhij,zbhj->zbhi', S, a)
    S = S * w[..., None, :] + sa[..., :, None] * b[..., None, :] + v[..., :, None] * k[..., None, :]
    return S, jnp.einsum('zbhij,zbhj->zbhi', S, r)


def wkv7_bidirectional(S0, r, decay, k_dir, v, kk, b):
    both = lambda t: jnp.stack([t, t], axis=2)
    seqs = (to_scan_layout(both(r)), to_scan_layout(decay), to_scan_layout(k_dir),
            to_scan_layout(both(v)), to_scan_layout(both(-kk)), to_scan_layout(b))
    S, y = lax.scan(wkv7_step, S0, seqs)
    y = y[:, 0] + jnp.flip(y[:, 1], axis=0)
    return S, y.transpose(1, 0, 2, 3)


def rwkv7_output(y, r, k_dir, v, z, r_k, ln_w, ln_b, w_out):
    B, L = y.shape[:2]
    yc = y - jnp.mean(y, axis=-1, keepdims=True)
    yn = yc * lax.rsqrt(jnp.mean(yc * yc, axis=-1, keepdims=True) + RW_GN_EPS)
    yn = yn.reshape(B, L, D_MODEL) * ln_w.astype(jnp.float32) + ln_b.astype(jnp.float32)
    heads = lambda t: t.reshape(B, L, RW_HEADS, RW_HEAD_DIM)
    bonus = jnp.sum(heads(r * jnp.mean(k_dir, axis=2) * r_k.astype(jnp.float32)), axis=-1, keepdims=True) * heads(v)
    o = (yn + bonus.reshape(B, L, D_MODEL)).astype(z.dtype) * jax.nn.silu(z)
    return o @ w_out


def rwkv7_mix(h_lat, h_ctx, w_in, mu, w0, w_up, a0, a_up, k_k, k_a, r_k, ln_w, ln_b, w_out, need_ctx_out):
    B = h_lat.shape[0]
    r_c, v_c, z_c, dec_c, kk_c, kd_c, b_c = rwkv7_features(h_ctx, w_in, mu, w0, w_up, a0, a_up, k_k, k_a)
    r_l, v_l, z_l, dec_l, kk_l, kd_l, b_l = rwkv7_features(h_lat, w_in, mu, w0, w_up, a0, a_up, k_k, k_a)
    S0 = jnp.zeros((2, B, RW_HEADS, RW_HEAD_DIM, RW_HEAD_DIM), jnp.float32)
    S_c, y_c = wkv7_bidirectional(S0, r_c, dec_c, kd_c, v_c, kk_c, b_c)
    _, y_l = wkv7_bidirectional(S_c, r_l, dec_l, kd_l, v_l, kk_l, b_l)
    out_l = rwkv7_output(y_l, r_l, kd_l, v_l, z_l, r_k, ln_w, ln_b, w_out)
    out_c = rwkv7_output(y_c, r_c, kd_c, v_c, z_c, r_k, ln_w, ln_b, w_out) if need_ctx_out else None
    return out_l, out_c


def setup_inputs(seed: int = 0) -> dict:
    key = jax.random.key(seed)
    ks = iter(jax.random.split(key, 32))
    D = D_MODEL
    R_ALL = 2 * RW_DECAY_RANK + 2 * RW_ICLR_RANK

    def nrm(shape, scale):
        return jax.random.normal(next(ks), shape, jnp.float32) * scale

    inputs = {}
    inputs['x'] = nrm((BATCH, SEQ, D), 1.0)
    inputs['c'] = nrm((BATCH, D), 1.0)
    inputs['ctx'] = nrm((BATCH, CTX_LEN, D), 1.0)
    inputs['c_ctx'] = nrm((D,), 1.0)
    inputs['norm_gain'] = 1.0 + nrm((DEPTH, D), 0.02)
    inputs['ada_w'] = nrm((DEPTH, D, 3 * D), 0.3 * D ** -0.5)
    inputs['ada_b'] = nrm((DEPTH, 3 * D), 0.02)
    inputs['final_gain'] = 1.0 + nrm((D,), 0.02)
    inputs['da_w_in'] = nrm((N_ATTN, D, 4 * D), D ** -0.5)
    inputs['da_lam_q'] = nrm((N_ATTN, 2, DA_HEAD_DIM), 0.1)
    inputs['da_lam_k'] = nrm((N_ATTN, 2, DA_HEAD_DIM), 0.1)
    inputs['da_subln_gain'] = 1.0 + nrm((N_ATTN, 2 * DA_HEAD_DIM), 0.02)
    inputs['da_w_out'] = nrm((N_ATTN, D, D), D ** -0.5)
    inputs['fn_w_in'] = nrm((N_FNET, D, 2 * D), D ** -0.5)
    inputs['fn_w_group'] = nrm((N_FNET, FN_GROUPS, FN_GROUP_DIM, FN_GROUP_DIM), FN_GROUP_DIM ** -0.5)
    inputs['fn_w_out'] = nrm((N_FNET, D, D), D ** -0.5)
    inputs['rw_w_in'] = nrm((N_RWKV, D, 4 * D + R_ALL), D ** -0.5)
    inputs['rw_mu'] = jax.random.uniform(next(ks), (N_RWKV, 3 * D + R_ALL), jnp.float32, 0.0, 1.0)
    inputs['rw_w0'] = jax.random.uniform(next(ks), (N_RWKV, 2, D), jnp.float32, -6.5, -1.5)
    inputs['rw_w_up'] = nrm((N_RWKV, 2, RW_DECAY_RANK, D), 0.1)
    inputs['rw_a0'] = nrm((N_RWKV, 2, D), 0.1)
    inputs['rw_a_up'] = nrm((N_RWKV, 2, RW_ICLR_RANK, D), RW_ICLR_RANK ** -0.5)
    inputs['rw_k_k'] = 0.85 + nrm((N_RWKV, D), 0.02)
    inputs['rw_k_a'] = 1.0 + nrm((N_RWKV, D), 0.02)
    inputs['rw_r_k'] = nrm((N_RWKV, D), 0.1)
    inputs['rw_ln_w'] = 1.0 + nrm((N_RWKV, D), 0.02)
    inputs['rw_ln_b'] = nrm((N_RWKV, D), 0.02)
    inputs['rw_w_out'] = nrm((N_RWKV, D, D), D ** -0.5)
    return inputs


def reference(x, c, ctx, c_ctx, norm_gain, ada_w, ada_b, final_gain,
              da_w_in, da_lam_q, da_lam_k, da_subln_gain, da_w_out,
              fn_w_in, fn_w_group, fn_w_out,
              rw_w_in, rw_mu, rw_w0, rw_w_up, rw_a0, rw_a_up, rw_k_k, rw_k_a, rw_r_k,
              rw_ln_w, rw_ln_b, rw_w_out):
    cos, sin = axial_rope_tables(x.shape[1])
    x_lat, x_ctx = x, ctx
    for i in range(DEPTH):
        last = i == DEPTH - 1
        kind = i % N_MIXERS
        j = i // N_MIXERS
        sh, sc, gt = ada_modulation(c, ada_w[i], ada_b[i])
        sh_c, sc_c, gt_c = ada_modulation(c_ctx, ada_w[i], ada_b[i])
        h_lat = rms_norm(x_lat, norm_gain[i]) * (1.0 + sc[:, None, :]) + sh[:, None, :]
        h_ctx = rms_norm(x_ctx, norm_gain[i]) * (1.0 + sc_c) + sh_c
        if kind == 0:
            lambda_init = 0.8 - 0.6 * math.exp(-0.3 * i)
            y_lat, y_ctx = diff_attention(h_lat, h_ctx, cos, sin, da_w_in[j], da_lam_q[j], da_lam_k[j],
                                          da_subln_gain[j], da_w_out[j], lambda_init, not last)
        elif kind == 1:
            y_lat = fourier_mix(h_lat, fn_w_in[j], fn_w_group[j], fn_w_out[j])
            y_ctx = None if last else fourier_mix(h_ctx, fn_w_in[j], fn_w_group[j], fn_w_out[j])
        else:
            y_lat, y_ctx = rwkv7_mix(h_lat, h_ctx, rw_w_in[j], rw_mu[j], rw_w0[j], rw_w_up[j], rw_a0[j],
                                     rw_a_up[j], rw_k_k[j], rw_k_a[j], rw_r_k[j], rw_ln_w[j], rw_ln_b[j],
                                     rw_w_out[j], not last)
        x_lat = x_lat + gt[:, None, :] * y_lat
        if not last:
            x_ctx = x_ctx + gt_c * y_ctx
    return rms_norm(x_lat, final_gain)
```

```python
import math
from contextlib import ExitStack

import numpy as np
import concourse.bass as bass
import concourse.mybir as mybir
from concourse.bass_utils import run_bass_kernel_spmd

F32 = mybir.dt.float32
BF16 = mybir.dt.bfloat16
AF = mybir.ActivationFunctionType
ALU = mybir.AluOpType
AX = mybir.AxisListType

D = 1024
NL = 4096
NCX = 256
NT = NL + NCX
NTI = NT // 128
DEPTH = 4
NORM_EPS = 1e-6
SUBLN_EPS = 1e-5
GRID_W = 64


class Sem:
    def __init__(self, fw, name):
        self.name = name
        self.handle = fw.stack.enter_context(fw.nc.semaphore(name))
        self.issued = 0
        fw.sems.append(self)


class Res:
    __slots__ = ("name", "writers", "readers", "dsem")

    def __init__(self, name):
        self.name = name
        self.writers = []
        self.readers = []
        self.dsem = None


class Eng:
    def __init__(self, fw, key):
        self.key = key
        self.sem = Sem(fw, "e_" + key)
        self.ops = []
        self.waited = {}
        self.pend_r = []
        self.pend_w = []


class FW:
    def __init__(self, nc, stack):
        self.nc = nc
        self.stack = stack
        self.sems = []
        self.eng = {k: Eng(self, k) for k in ("tensor", "vector", "scalar", "gpsimd", "sync")}
        self.n_ins = 0
        self.free_sems = []

    def sb(self, name, shape, dtype, stack=None):
        self.uid = getattr(self, "uid", 0) + 1
        name = "%s_u%d" % (name, self.uid)
        t = (stack or self.stack).enter_context(self.nc.sbuf_tensor(name, list(shape), dtype))
        return t, self.res(name, stack)

    def res(self, name, stack=None):
        r = Res(name)
        if stack is not None:
            stack.callback(self._release, r)
        return r

    def _release(self, r):
        if r.dsem is not None:
            self.free_sems.append(r.dsem)
            r.dsem = None

    def _get_dsem(self, name):
        if self.free_sems:
            sem = self.free_sems.pop(0)
            for e in self.eng.values():
                assert e.waited.get(sem, 0) >= sem.issued, "semaphore reused before a barrier"
            return sem
        return Sem(self, "d%d" % len(self.sems))

    def ps(self, name, shape, dtype, stack=None):
        t = (stack or self.stack).enter_context(self.nc.psum_tensor(name, list(shape), dtype))
        return t, Res(name)

    def _waits_for(self, eng, reads, writes):
        need = {}

        def add(tok):
            sem, val = tok
            if val is None:
                val = sem.issued
            if need.get(sem, 0) < val:
                need[sem] = val
        for r in reads:
            for t in r.writers:
                add(t)
        for w in writes:
            for t in w.writers:
                add(t)
            for t in w.readers:
                add(t)
        out = []
        for sem, val in need.items():
            if eng.waited.get(sem, 0) >= val:
                continue
            eng.waited[sem] = val
            out.append((sem.handle, val))
        return out

    def _check_pending(self, eng, reads, writes):
        for e in self.eng.values():
            if e is eng or (not e.pend_w and not e.pend_r):
                continue
            for r in reads:
                assert r not in e.pend_w, ("unsignaled write pending", r.name, e.key)
            for w in writes:
                assert w not in e.pend_w and w not in e.pend_r, ("unsignaled access pending", w.name, e.key)

    def op(self, ek, fn, reads=(), writes=(), signal=True):
        eng = self.eng[ek]
        self._check_pending(eng, reads, writes)
        waits = self._waits_for(eng, reads, writes)
        self.n_ins += 1
        if signal:
            eng.sem.issued += 1
            tok = (eng.sem, eng.sem.issued)
            semh = eng.sem.handle

            def run(e, fn=fn, waits=waits, semh=semh):
                for (h, v) in waits:
                    e.wait_ge(h, v)
                fn(e).then_inc(semh, 1)
            rs = list(reads) + eng.pend_r
            ws = list(writes) + eng.pend_w
            eng.pend_r = []
            eng.pend_w = []
            for r in rs:
                r.readers.append(tok)
            for w in ws:
                w.writers = [tok]
                w.readers = []
        else:
            def run(e, fn=fn, waits=waits):
                for (h, v) in waits:
                    e.wait_ge(h, v)
                fn(e)
            eng.pend_r.extend(reads)
            eng.pend_w.extend(writes)
        eng.ops.append(run)

    def dma(self, out, in_, reads=(), writes=(), sem_res=None, ek="sync", **kw):
        eng = self.eng[ek]
        self._check_pending(eng, reads, writes)
        waits = self._waits_for(eng, reads, writes)
        if sem_res.dsem is None:
            sem_res.dsem = self._get_dsem(sem_res.name)
        sem = sem_res.dsem
        sem.issued += 16
        tok = (sem, None)
        semh = sem.handle
        self.n_ins += 1

        def run(e, waits=waits, semh=semh, out=out, in_=in_, kw=kw):
            for (h, v) in waits:
                e.wait_ge(h, v)
            e.dma_start(out=out, in_=in_, **kw).then_inc(semh, 16)
        eng.ops.append(run)
        for r in reads:
            r.readers.append(tok)
        for w in writes:
            w.writers = [tok]
            w.readers = []

    def barrier(self):
        for e in self.eng.values():
            assert not e.pend_r and not e.pend_w
        for e in self.eng.values():
            waits = []
            for s in self.sems:
                if s.issued > 0 and e.waited.get(s, 0) < s.issued:
                    e.waited[s] = s.issued
                    waits.append((s.handle, s.issued))

            def run(en, waits=waits):
                for (h, v) in waits:
                    en.wait_ge(h, v)
            e.ops.append(run)

    def finish(self):
        self.barrier()
        with self.nc.Block() as block:
            for key in ("sync", "tensor", "vector", "scalar", "gpsimd"):
                e = self.eng[key]

                def body(h, e=e):
                    for o in e.ops:
                        o(h)
                getattr(block, key)(body)


def rope_tables():
    n = np.arange(NL)
    row = (n // GRID_W).astype(np.float32)
    col = (n % GRID_W).astype(np.float32)
    inv = (10000.0 ** (-np.arange(0, 32, 2, dtype=np.float32) / 32.0)).astype(np.float32)
    C = np.ones((128, NT), np.float32)
    S = np.zeros((128, NT), np.float32)
    for f in range(128):
        axis = (f % 64) // 32
        j = f % 16
        pos = row if axis == 0 else col
        ang = (pos * inv[j]).astype(np.float32)
        C[f, NCX:] = np.cos(ang)
        S[f, NCX:] = np.sin(ang)
    return C, S


def host_consts():
    cs = {}
    cs["ident"] = np.eye(128, dtype=np.float32)
    C, S = rope_tables()
    cs["ropeC"] = C
    cs["ropeS"] = S
    import ml_dtypes
    bf = ml_dtypes.bfloat16
    for name, L in (("L", NL), ("X", NCX)):
        l = np.arange(L, dtype=np.int64)
        ang = (2.0 * np.pi / L) * ((l[:, None] * l[None, :]) % L).astype(np.float64)
        sc = 1.0 / math.sqrt(L)
        cs["dftC" + name] = (np.cos(ang) * sc).astype(np.float32).astype(bf)
        cs["dftS" + name] = (np.sin(ang) * sc).astype(np.float32).astype(bf)
    c = np.arange(128, dtype=np.int64)
    ang = (2.0 * np.pi / 128) * ((c[:, None] * c[None, :]) % 128).astype(np.float64)
    sc = 1.0 / math.sqrt(128.0)
    cs["dftCc"] = (np.cos(ang) * sc).astype(np.float32).astype(bf)
    cs["dftnSc"] = (-np.sin(ang) * sc).astype(np.float32).astype(bf)
    a = np.arange(128)
    same = (a[:, None] // 64) == (a[None, :] // 64)
    lt = (a[:, None] < a[None, :]) & same
    le = (a[:, None] <= a[None, :]) & same
    gt = (a[:, None] > a[None, :]) & same
    ge = (a[:, None] >= a[None, :]) & same
    cs["mN0"] = np.concatenate([lt, le], axis=1).astype(np.float32)
    cs["mL0"] = gt.astype(np.float32)
    cs["mN1"] = np.concatenate([gt, ge], axis=1).astype(np.float32)
    cs["mL1"] = lt.astype(np.float32)
    return cs


INPUT_SHAPES = {
    "x": [NL, D], "c": [1, D], "ctx": [NCX, D], "c_ctx": [1, D], "norm_gain": [4, D], "ada_w": [4, D, 3 * D],
    "ada_b": [4, 3 * D], "final_gain": [1, D], "da_w_in": [2, D, 4 * D], "da_lam_q": [2, 128], "da_lam_k": [2, 128],
    "da_subln_gain": [2, 128], "da_w_out": [2, D, D], "fn_w_in": [1, D, 2 * D], "fn_w_group": [1, 8, 128, 128],
    "fn_w_out": [1, D, D], "rw_w_in": [1, D, 4352], "rw_mu": [1, 3328], "rw_w0": [1, 2, D], "rw_w_up": [1, 2, 64, D],
    "rw_a0": [1, 2, D], "rw_a_up": [1, 2, 64, D], "rw_k_k": [1, D], "rw_k_a": [1, D], "rw_r_k": [1, D],
    "rw_ln_w": [1, D], "rw_ln_b": [1, D], "rw_w_out": [1, D, D],
}


class Prog:
    def __init__(self, n_layers=DEPTH, dbg=False):
        self.n_layers = n_layers
        self.dbg = dbg
        self.nc = bass.Bass("TRN2", target_bir_lowering=False)
        self.consts = host_consts()

    def din(self, name, shape, dt=F32):
        return self.nc.dram_tensor(name, list(shape), dt, kind="ExternalInput").ap()

    def build(self):
        nc = self.nc
        self.I = {k: self.din(k, s) for k, s in INPUT_SHAPES.items()}
        self.Cn = {k: self.din("k_" + k, v.shape, F32 if v.dtype == np.float32 else BF16) for k, v in self.consts.items()}
        self.out = nc.dram_tensor("out", [NL, D], F32, kind="ExternalOutput").ap()
        if self.dbg:
            self.dbg_ctx = nc.dram_tensor("dbg_ctx", [NCX, D], F32, kind="ExternalOutput").ap()
        self.xres = nc.dram_tensor("xres", [NT, D], F32, kind="Internal").ap()
        self.og = nc.dram_tensor("og", [8, 128, NT], BF16, kind="Internal").ap()
        self.zsd = nc.dram_tensor("zsd", [8, 128, NT], BF16, kind="Internal").ap()
        self.zsd_r = [[Res("zsd%d_%d" % (h, c)) for c in range(9)] for h in range(8)]
        self.xres_r = [Res("xres%d" % t) for t in range(NTI)]
        self.og_r = [[Res("og%d_%d" % (h, c)) for c in range(9)] for h in range(8)]
        self.out_r = [Res("out%d" % t) for t in range(NTI)]
        with ExitStack() as st:
            self.st = st
            fw = self.fw = FW(nc, st)
            self.setup_persistent()
            for i in range(self.n_layers):
                self.layer(i)
            fw.finish()
        return nc

    @staticmethod
    def chunk(ci):
        if ci == 0:
            return 0, NCX
        return NCX + (ci - 1) * 512, 512

    def xsrc(self, i, tt):
        if i == 0:
            if tt < 2:
                return self.I["ctx"][tt * 128:(tt + 1) * 128, :], []
            return self.I["x"][(tt - 2) * 128:(tt - 1) * 128, :], []
        return self.xres[tt * 128:(tt + 1) * 128, :], [self.xres_r[tt]]

    def setup_persistent(self):
        fw, st = self.fw, self.st
        self.ident, self.ident_r = fw.sb("ident", [128, 128], F32)
        fw.dma(self.ident[:], self.Cn["ident"][:, :], writes=[self.ident_r], sem_res=self.ident_r)
        self.identb, self.identb_r = fw.sb("identb", [128, 128], BF16)
        fw.op("vector", lambda e: e.tensor_copy(out=self.identb[:], in_=self.ident[:]), reads=[self.ident_r], writes=[self.identb_r])
        self.ones_b, self.ones_b_r = fw.sb("ones_b", [128, 128], BF16)
        fw.op("vector", lambda e: e.memset(self.ones_b[:], 1.0), writes=[self.ones_b_r])
        self.mean_b, self.mean_b_r = fw.sb("mean_b", [128, 128], BF16)
        fw.op("vector", lambda e: e.memset(self.mean_b[:], 1.0 / 128.0), writes=[self.mean_b_r])
        self.epsn, self.epsn_r = fw.sb("epsn", [128, 1], F32)
        fw.op("vector", lambda e: e.memset(self.epsn[:], NORM_EPS), writes=[self.epsn_r])
        self.epss, self.epss_r = fw.sb("epss", [128, 1], F32)
        fw.op("vector", lambda e: e.memset(self.epss[:], SUBLN_EPS), writes=[self.epss_r])
        self.gsc, self.gsc_r = fw.sb("gsc", [128, 8, 2], F32)
        self.shf, self.shf_r = fw.sb("shf", [128, 8, 2], F32)
        self.gate, self.gate_r = fw.sb("gate", [128, 2, D], F32)
        self.s_fm, self.s_fm_r = fw.sb("s_fm", [128, 8, 2], F32)
        self.s_rep, self.s_rep_r = fw.sb("s_rep", [128, 8, 2, 128], F32)
        self.pb = []
        for b in range(8):
            t, r = fw.ps("pb%d" % b, [128, 512], F32)
            self.pb.append((t, r))
        with ExitStack() as ts:
            cc, cc_r = fw.sb("cc", [2, D], F32, ts)
            fw.dma(cc[0:1, :], self.I["c"][:, :], writes=[cc_r], sem_res=cc_r)
            fw.dma(cc[1:2, :], self.I["c_ctx"][:, :], writes=[cc_r], sem_res=cc_r)
            fw.op("scalar", lambda e: e.activation(out=cc[:], in_=cc[:], func=AF.Silu), reads=[cc_r], writes=[cc_r])
            pt, pr = self.pb[0]
            for kc in range(8):
                fw.op("tensor", lambda e, kc=kc: e.transpose(pt[:, kc * 2:kc * 2 + 2], cc[:, kc * 128:(kc + 1) * 128], self.ident[0:2, 0:2]),
                      reads=[cc_r, self.ident_r], writes=[pr], signal=(kc == 7))
            fw.op("vector", lambda e: e.tensor_copy(out=self.s_fm[:].rearrange("p k w -> p (k w)"), in_=pt[:, 0:16]), reads=[pr], writes=[self.s_fm_r])
            for kc in range(8):
                for w in range(2):
                    fw.op("gpsimd", lambda e, kc=kc, w=w: e.tensor_copy(out=self.s_rep[:, kc, w, :], in_=self.s_fm[:, kc, w:w + 1].to_broadcast([128, 128])),
                          reads=[self.s_fm_r], writes=[self.s_rep_r])
            fw.barrier()

    def load_fm(self, dst, dst_r, src2d, n, ts):
        fw = self.fw
        tmp, tmp_r = fw.sb("lfm_tmp%d" % fw.n_ins, [n, 128], F32, ts)
        fw.dma(tmp[:], src2d, writes=[tmp_r], sem_res=tmp_r)
        pt, pr = self.pb[1]
        fw.op("tensor", lambda e: e.transpose(pt[:, 0:n], tmp[:], self.ident[0:n, 0:n]), reads=[tmp_r, self.ident_r], writes=[pr])
        fw.op("vector", lambda e: e.tensor_copy(out=dst, in_=pt[:, 0:n]), reads=[pr], writes=[dst_r])

    def phase_mod(self, i):
        fw = self.fw
        W = self.I["ada_w"][i]
        with ExitStack() as ts:
            bfm, bfm_r = fw.sb("bfm", [128, 16], F32, ts)
            gfm, gfm_r = fw.sb("gfm", [128, 8], F32, ts)
            self.load_fm(bfm[:], bfm_r, self.I["ada_b"][i, 0:2048].rearrange("(n p) -> n p", p=128), 16, ts)
            self.load_fm(gfm[:], gfm_r, self.I["norm_gain"][i, :].rearrange("(n p) -> n p", p=128), 8, ts)
            bg, bg_r = fw.sb("bg", [128, D], F32, ts)
            fw.dma(bg[:], self.I["ada_b"][i, 2048:3072].partition_broadcast(128), writes=[bg_r], sem_res=bg_r)
            wt = [fw.sb("adaw%d" % k, [128, 8, 512], F32, ts) for k in range(2)]
            pt, pr = self.pb[2]
            for cc in range(6):
                w_t, w_r = wt[cc % 2]
                for kc in range(8):
                    fw.dma(w_t[:, kc, :], W[kc * 128:(kc + 1) * 128, cc * 512:(cc + 1) * 512], writes=[w_r], sem_res=w_r)
                if cc < 4:
                    for fl in range(4):
                        fc = cc * 4 + fl
                        for kc in range(8):
                            fw.op("tensor", lambda e, kc=kc, fl=fl, fc=fc, w_t=w_t: e.matmul(pt[:, fc * 2:fc * 2 + 2], lhsT=w_t[:, kc, fl * 128:(fl + 1) * 128], rhs=self.s_fm[:, kc, :], start=(kc == 0), stop=(kc == 7)),
                                  reads=[w_r, self.s_fm_r], writes=[pr], signal=(kc == 7 and fl == 3))
                    if cc == 3:
                        ps3 = pt[:, 0:32].rearrange("p (f w) -> p f w", w=2)
                        for w in range(2):
                            fw.op("vector", lambda e, w=w: e.tensor_tensor(out=self.shf[:, :, w], in0=ps3[:, 0:8, w], in1=bfm[:, 0:8], op=ALU.add),
                                  reads=[pr, bfm_r], writes=[self.shf_r])
                            fw.op("vector", lambda e, w=w: e.tensor_tensor(out=self.gsc[:, :, w], in0=ps3[:, 8:16, w], in1=bfm[:, 8:16], op=ALU.add),
                                  reads=[pr, bfm_r], writes=[self.gsc_r])
                            fw.op("vector", lambda e, w=w: e.scalar_tensor_tensor(out=self.gsc[:, :, w], in0=self.gsc[:, :, w], scalar=1.0, in1=gfm[:], op0=ALU.add, op1=ALU.mult),
                                  reads=[self.gsc_r, gfm_r], writes=[self.gsc_r])
                else:
                    cg = cc - 4
                    for w in range(2):
                        gp, gr = self.pb[3 + w]
                        for kc in range(8):
                            fw.op("tensor", lambda e, kc=kc, w=w, gp=gp, w_t=w_t: e.matmul(gp[:, :], lhsT=self.s_rep[:, kc, w, :], rhs=w_t[:, kc, :], start=(kc == 0), stop=(kc == 7)),
                                  reads=[w_r, self.s_rep_r], writes=[gr], signal=(kc == 7))
                        fw.op("vector", lambda e, w=w, gp=gp, cg=cg: e.tensor_tensor(out=self.gate[:, w, cg * 512:(cg + 1) * 512], in0=gp[:, :], in1=bg[:, cg * 512:(cg + 1) * 512], op=ALU.add),
                              reads=[gr, bg_r], writes=[self.gate_r])
            fw.barrier()

    def phase_norm(self, i, hT, hT_r, ts):
        fw = self.fw
        xt = [fw.sb("nx%d" % k, [128, D], F32, ts) for k in range(3)]
        sq, sq_r = fw.sb("nsq", [128, D], BF16, ts)
        ss = [fw.sb("nss%d" % k, [128, 1], F32, ts) for k in range(2)]
        dg = [fw.sb("ndg%d" % k, [128, 128], F32, ts) for k in range(2)]
        for tt in range(NTI):
            x_t, x_r = xt[tt % 3]
            s_t, s_r = ss[tt % 2]
            d_t, d_r = dg[tt % 2]
            w = 1 if tt < 2 else 0
            src, src_r = self.xsrc(i, tt)
            fw.dma(x_t[:], src, reads=src_r, writes=[x_r], sem_res=x_r)
            fw.op("scalar", lambda e, x_t=x_t, s_t=s_t: e.activation(out=sq[:], in_=x_t[:], func=AF.Square, accum_out=s_t[:]), reads=[x_r], writes=[sq_r, s_r])
            fw.op("scalar", lambda e, s_t=s_t: e.activation(out=s_t[:], in_=s_t[:], func=AF.Sqrt, bias=self.epsn[:], scale=1.0 / D), reads=[s_r, self.epsn_r], writes=[s_r])
            fw.op("vector", lambda e, s_t=s_t: e.reciprocal(out=s_t[:], in_=s_t[:]), reads=[s_r], writes=[s_r])
            fw.op("vector", lambda e, s_t=s_t, d_t=d_t: e.tensor_scalar(out=d_t[:], in0=self.ident[:], scalar1=s_t[:, 0:1], scalar2=None, op0=ALU.mult), reads=[s_r, self.ident_r], writes=[d_r])
            for half in range(2):
                pt, pr = self.pb[(tt * 2 + half) % 4]
                for k4 in range(4):
                    kc = half * 4 + k4
                    fw.op("tensor", lambda e, kc=kc, k4=k4, pt=pt, x_t=x_t, d_t=d_t: e.matmul(pt[:, k4 * 128:(k4 + 1) * 128], lhsT=x_t[:, kc * 128:(kc + 1) * 128], rhs=d_t[:], start=True, stop=True),
                          reads=[x_r, d_r], writes=[pr], signal=(k4 == 3))
                for k4 in range(4):
                    kc = half * 4 + k4
                    ek = "scalar" if k4 % 2 == 0 else "vector"
                    if ek == "scalar":
                        fw.op("scalar", lambda e, kc=kc, k4=k4, pt=pt, tt=tt, w=w: e.activation(out=hT[:, kc, tt * 128:(tt + 1) * 128], in_=pt[:, k4 * 128:(k4 + 1) * 128], func=AF.Identity, bias=self.shf[:, kc, w:w + 1], scale=self.gsc[:, kc, w:w + 1]),
                              reads=[pr, self.shf_r, self.gsc_r], writes=[hT_r[tt]])
                    else:
                        fw.op("vector", lambda e, kc=kc, k4=k4, pt=pt, tt=tt, w=w: e.tensor_scalar(out=hT[:, kc, tt * 128:(tt + 1) * 128], in0=pt[:, k4 * 128:(k4 + 1) * 128], scalar1=self.gsc[:, kc, w:w + 1], scalar2=self.shf[:, kc, w:w + 1], op0=ALU.mult, op1=ALU.add),
                              reads=[pr, self.shf_r, self.gsc_r], writes=[hT_r[tt]])

    def phase_out(self, i, w_out, ts):
        fw = self.fw
        last = (i == DEPTH - 1)
        wst = [fw.sb("wo_st%d" % k, [128, D], F32, ts) for k in range(2)]
        wb, wb_r = fw.sb("wo_b", [128, 8, D], BF16, ts)
        for kc in range(8):
            s_t, s_r = wst[kc % 2]
            fw.dma(s_t[:], w_out[kc * 128:(kc + 1) * 128, :], writes=[s_r], sem_res=s_r)
            fw.op("gpsimd", lambda e, kc=kc, s_t=s_t: e.tensor_copy(out=wb[:, kc, :], in_=s_t[:]), reads=[s_r], writes=[wb_r])
        ogt = [fw.sb("po_og%d" % k, [128, 8, 512], BF16, ts) for k in range(2)]
        xt = [fw.sb("po_x%d" % k, [128, D], F32, ts) for k in range(2)]
        yt = [fw.sb("po_y%d" % k, [128, D], F32, ts) for k in range(2)]
        if last:
            fg, fg_r = fw.sb("po_fg", [128, D], F32, ts)
            fw.dma(fg[:], self.I["final_gain"][0, :].partition_broadcast(128), writes=[fg_r], sem_res=fg_r)
            sq, sq_r = fw.sb("po_sq", [128, D], BF16, ts)
            ss = [fw.sb("po_ss%d" % k, [128, 1], F32, ts) for k in range(2)]
        cis = range(1, 9) if last else range(9)
        n = 0
        for ci in cis:
            t0, tn = self.chunk(ci)
            o_t, o_r = ogt[ci % 2]
            for hc in range(8):
                fw.dma(o_t[:, hc, 0:tn], self.og[hc, :, t0:t0 + tn], reads=[self.og_r[hc][ci]], writes=[o_r], sem_res=o_r)
            for tl in range(tn // 128):
                tt = t0 // 128 + tl
                w = 1 if tt < 2 else 0
                x_t, x_r = xt[n % 2]
                y_t, y_r = yt[n % 2]
                src, src_r = self.xsrc(i, tt)
                fw.dma(x_t[:], src, reads=src_r, writes=[x_r], sem_res=x_r)
                for half in range(2):
                    pt, pr = self.pb[(n * 2 + half) % 4]
                    for hc in range(8):
                        fw.op("tensor", lambda e, hc=hc, half=half, pt=pt, o_t=o_t, tl=tl: e.matmul(pt[:, :], lhsT=o_t[:, hc, tl * 128:(tl + 1) * 128], rhs=wb[:, hc, half * 512:(half + 1) * 512], start=(hc == 0), stop=(hc == 7)),
                              reads=[o_r, wb_r], writes=[pr], signal=(hc == 7))
                    fw.op("vector", lambda e, half=half, pt=pt, y_t=y_t, w=w: e.tensor_tensor(out=y_t[:, half * 512:(half + 1) * 512], in0=pt[:, :], in1=self.gate[:, w, half * 512:(half + 1) * 512], op=ALU.mult),
                          reads=[pr, self.gate_r], writes=[y_r])
                fw.op("gpsimd", lambda e, y_t=y_t, x_t=x_t: e.tensor_tensor(out=y_t[:], in0=y_t[:], in1=x_t[:], op=ALU.add), reads=[y_r, x_r], writes=[y_r])
                if not last:
                    fw.dma(self.xres[tt * 128:(tt + 1) * 128, :], y_t[:], reads=[y_r], writes=[self.xres_r[tt]], sem_res=y_r)
                    if self.dbg and i == self.n_layers - 1:
                        if tt < 2:
                            fw.dma(self.dbg_ctx[tt * 128:(tt + 1) * 128, :], y_t[:], reads=[y_r], writes=[self.out_r[tt]], sem_res=y_r)
                        else:
                            fw.dma(self.out[(tt - 2) * 128:(tt - 1) * 128, :], y_t[:], reads=[y_r], writes=[self.out_r[tt]], sem_res=y_r)
                else:
                    s_t, s_r = ss[n % 2]
                    fw.op("scalar", lambda e, y_t=y_t, s_t=s_t: e.activation(out=sq[:], in_=y_t[:], func=AF.Square, accum_out=s_t[:]), reads=[y_r], writes=[sq_r, s_r])
                    fw.op("scalar", lambda e, s_t=s_t: e.activation(out=s_t[:], in_=s_t[:], func=AF.Sqrt, bias=self.epsn[:], scale=1.0 / D), reads=[s_r, self.epsn_r], writes=[s_r])
                    fw.op("vector", lambda e, s_t=s_t: e.reciprocal(out=s_t[:], in_=s_t[:]), reads=[s_r], writes=[s_r])
                    fw.op("vector", lambda e, y_t=y_t, s_t=s_t: e.scalar_tensor_tensor(out=y_t[:], in0=y_t[:], scalar=s_t[:, 0:1], in1=fg[:], op0=ALU.mult, op1=ALU.mult), reads=[y_r, s_r, fg_r], writes=[y_r])
                    fw.dma(self.out[(tt - 2) * 128:(tt - 1) * 128, :], y_t[:], reads=[y_r], writes=[self.out_r[tt]], sem_res=y_r)
                n += 1

    def layer(self, i):
        fw = self.fw
        kind = i % 3
        j = i // 3
        self.phase_mod(i)
        with ExitStack() as ts0:
            pre = None
            if kind == 1:
                pre = self.fnet_prealloc(ts0)
            elif kind == 2:
                pre = self.rwkv_prealloc(ts0)
            with ExitStack() as ts:
                hT, _ = fw.sb("hT", [128, 8, NT], BF16, ts)
                hT_r = [Res("hT%d" % t) for t in range(NTI)]
                with ExitStack() as ts2:
                    self.phase_norm(i, hT, hT_r, ts2)
                    fw.barrier()
                with ExitStack() as ts2:
                    if kind == 0:
                        self.mixer_attn(i, j, hT, hT_r, ts2)
                    elif kind == 1:
                        self.fnet_part1(i, j, hT, hT_r, pre, ts2)
                    else:
                        self.rwkv_part1(i, j, hT, hT_r, pre, ts2)
                    fw.barrier()
            if kind == 1:
                with ExitStack() as ts2:
                    self.fnet_part2(i, j, pre, ts2)
                    fw.barrier()
            elif kind == 2:
                self.rwkv_part2(i, j, pre, ts0)
        with ExitStack() as ts:
            w_out = {0: self.I["da_w_out"], 1: self.I["fn_w_out"], 2: self.I["rw_w_out"]}[kind][j]
            self.phase_out(i, w_out, ts)
            fw.barrier()

    def mixer_attn(self, i, j, hT, hT_r, ts):
        fw = self.fw
        last = (i == DEPTH - 1)
        lambda_init = 0.8 - 0.6 * math.exp(-0.3 * i)
        Win = self.I["da_w_in"][j]
        rC, rC_r = fw.sb("ropeC", [128, NT], F32, ts)
        rS, rS_r = fw.sb("ropeS", [128, NT], F32, ts)
        fw.dma(rC[:], self.Cn["ropeC"][:, :], writes=[rC_r], sem_res=rC_r)
        fw.dma(rS[:], self.Cn["ropeS"][:, :], writes=[rS_r], sem_res=rS_r)
        nlam, nlam_r = fw.sb("nlam", [128, 1], F32, ts)
        lq, lq_r = fw.sb("lq", [128, 128], F32, ts)
        lk, lk_r = fw.sb("lk", [128, 128], F32, ts)
        l2, l2_r = fw.sb("l2", [128, 2], F32, ts)
        fw.dma(lq[:], self.I["da_lam_q"][j, :].partition_broadcast(128), writes=[lq_r], sem_res=lq_r)
        fw.dma(lk[:], self.I["da_lam_k"][j, :].partition_broadcast(128), writes=[lk_r], sem_res=lk_r)
        fw.op("vector", lambda e: e.tensor_tensor(out=lq[:], in0=lq[:], in1=lk[:], op=ALU.mult), reads=[lq_r, lk_r], writes=[lq_r])
        fw.op("vector", lambda e: e.tensor_reduce(out=l2[:], in_=lq[:].rearrange("p (z d) -> p z d", z=2), axis=AX.X, op=ALU.add), reads=[lq_r], writes=[l2_r])
        fw.op("scalar", lambda e: e.activation(out=l2[:], in_=l2[:], func=AF.Exp), reads=[l2_r], writes=[l2_r])
        fw.op("vector", lambda e: e.tensor_tensor(out=nlam[:], in0=l2[:, 1:2], in1=l2[:, 0:1], op=ALU.subtract), reads=[l2_r], writes=[nlam_r])
        fw.op("vector", lambda e: e.tensor_scalar(out=nlam[:], in0=nlam[:], scalar1=-lambda_init, scalar2=None, op0=ALU.add), reads=[nlam_r], writes=[nlam_r])
        sg, sg_r = fw.sb("sg", [128, 1], F32, ts)
        self.load_fm(sg[:], sg_r, self.I["da_subln_gain"][j:j + 1, :], 1, ts)
        fw.op("vector", lambda e: e.tensor_scalar(out=sg[:], in0=sg[:], scalar1=1.0 - lambda_init, scalar2=None, op0=ALU.mult), reads=[sg_r], writes=[sg_r])

        wst = [fw.sb("aw_st%d" % k, [128, 8, 128], F32, ts) for k in range(2)]
        wq, wq_r = fw.sb("wq", [128, 8, 128], BF16, ts)
        wq2, wq2_r = fw.sb("wq2", [128, 8, 128], BF16, ts)
        wk, wk_r = fw.sb("wk", [128, 8, 128], BF16, ts)
        wk2, wk2_r = fw.sb("wk2", [128, 8, 128], BF16, ts)
        wv, wv_r = fw.sb("wv", [128, 8, 128], BF16, ts)
        wz, wz_r = fw.sb("wz", [128, 8, 128], BF16, ts)
        qT, qT_r = fw.sb("qT", [128, NT], BF16, ts)
        kT, kT_r = None, None
        kTz = [fw.sb("kTz%d" % z, [128, NT], BF16, ts) for z in range(2)]
        for z in range(2):
            fw.op("gpsimd", lambda e, z=z: e.memset(kTz[z][0][:], 0.0), writes=[kTz[z][1]])
        zs, zs_r = fw.sb("zs", [128, NT], BF16, ts)
        vt, vt_r = fw.sb("vt", [128, NTI, 128], BF16, ts)
        t1 = [fw.sb("rp1_%d" % k, [128, 512], F32, ts) for k in range(2)]
        t2 = [fw.sb("rp2_%d" % k, [128, 512], F32, ts) for k in range(2)]
        Eb = [fw.sb("E%d" % k, [128, 512], BF16, ts) for k in range(6)]
        r1, r1_r = fw.sb("ep_r1", [128, 512], F32, ts)
        a1, a1_r = fw.sb("ep_a1", [128, 512], F32, ts)
        r2, r2_r = r1, r1_r
        a2, a2_r = fw.sb("ep_a2", [128, 512], F32, ts)
        sqb, sqb_r = fw.sb("ep_sq", [128, 512], BF16, ts)
        rs, rs_r = r1, r1_r
        ogt = [fw.sb("ep_og%d" % k, [128, 512], BF16, ts) for k in range(2)]

        def rot_cast(dst, dst2, dst_r, dst2_r, s_t, s_r, scale):
            fw.op("gpsimd", lambda e: e.tensor_scalar(out=dst[:], in0=s_t[:], scalar1=scale, scalar2=None, op0=ALU.mult), reads=[s_r], writes=[dst_r])
            sv = s_t[:].rearrange("p k (b h j) -> p k b h j", b=4, h=2)
            dv = dst2[:].rearrange("p k (b h j) -> p k b h j", b=4, h=2)
            for kc in range(8):
                fw.op("gpsimd", lambda e, kc=kc: e.tensor_scalar(out=dv[:, kc, :, 0, :], in0=sv[:, kc, :, 1, :], scalar1=-scale, scalar2=None, op0=ALU.mult), reads=[s_r], writes=[dst2_r])
                fw.op("gpsimd", lambda e, kc=kc: e.tensor_scalar(out=dv[:, kc, :, 1, :], in0=sv[:, kc, :, 0, :], scalar1=scale, scalar2=None, op0=ALU.mult), reads=[s_r], writes=[dst2_r])

        nst = 0
        ncnt = 0
        for hd in range(8):
            for which in range(4):
                s_t, s_r = wst[nst % 2]
                nst += 1
                col0 = which * D + hd * 128
                fw.dma(s_t[:], Win[:, col0:col0 + 128].rearrange("(k p) c -> p k c", p=128), writes=[s_r], sem_res=s_r)
                if which == 0:
                    rot_cast(wq, wq2, wq_r, wq2_r, s_t, s_r, 0.125)
                elif which == 1:
                    rot_cast(wk, wk2, wk_r, wk2_r, s_t, s_r, 1.0)
                elif which == 2:
                    fw.op("gpsimd", lambda e, s_t=s_t: e.tensor_copy(out=wv[:], in_=s_t[:]), reads=[s_r], writes=[wv_r])
                else:
                    fw.op("gpsimd", lambda e, s_t=s_t: e.tensor_copy(out=wz[:], in_=s_t[:]), reads=[s_r], writes=[wz_r])
            for ci in range(9):
                t0, tn = self.chunk(ci)
                tts = list(range(t0 // 128, (t0 + tn) // 128))
                hrs = [hT_r[t] for t in tts]
                banks = [self.pb[(ci * 6 + b) % 8] for b in range(6)]
                for b, (wt_, wr_) in enumerate([(wq, wq_r), (wq2, wq2_r), (wk, wk_r), (wk2, wk2_r), (wz, wz_r)]):
                    pt, pr = banks[b]
                    for kc in range(8):
                        fw.op("tensor", lambda e, kc=kc, pt=pt, wt_=wt_, t0=t0, tn=tn: e.matmul(pt[:, 0:tn], lhsT=wt_[:, kc, :], rhs=hT[:, kc, t0:t0 + tn], start=(kc == 0), stop=(kc == 7)),
                              reads=[wr_] + hrs, writes=[pr], signal=(kc == 7))
                pt, pr = banks[5]
                for tl, tt in enumerate(tts):
                    for kc in range(8):
                        fw.op("tensor", lambda e, kc=kc, pt=pt, tl=tl, tt=tt: e.matmul(pt[:, tl * 128:(tl + 1) * 128], lhsT=hT[:, kc, tt * 128:(tt + 1) * 128], rhs=wv[:, kc, :], start=(kc == 0), stop=(kc == 7)),
                              reads=[wv_r, hT_r[tt]], writes=[pr], signal=(kc == 7 and tl == len(tts) - 1))
                for b0, dst, dst_r in ((0, qT, qT_r), (2, kT, kT_r)):
                    p1, p1r = banks[b0]
                    p2, p2r = banks[b0 + 1]
                    ta, ta_r = t1[ncnt % 2]
                    tb, tb_r = t2[ncnt % 2]
                    ncnt += 1
                    fw.op("vector", lambda e, p1=p1, ta=ta, t0=t0, tn=tn: e.tensor_tensor(out=ta[:, 0:tn], in0=p1[:, 0:tn], in1=rC[:, t0:t0 + tn], op=ALU.mult), reads=[p1r, rC_r], writes=[ta_r])
                    fw.op("vector", lambda e, p2=p2, tb=tb, t0=t0, tn=tn: e.tensor_tensor(out=tb[:, 0:tn], in0=p2[:, 0:tn], in1=rS[:, t0:t0 + tn], op=ALU.mult), reads=[p2r, rS_r], writes=[tb_r])
                    if b0 == 0:
                        fw.op("gpsimd", lambda e, ta=ta, tb=tb, dst=dst, t0=t0, tn=tn: e.tensor_tensor(out=dst[:, t0:t0 + tn], in0=ta[:, 0:tn], in1=tb[:, 0:tn], op=ALU.add), reads=[ta_r, tb_r], writes=[dst_r])
                    else:
                        for z in range(2):
                            zr = slice(z * 64, (z + 1) * 64)
                            fw.op("gpsimd", lambda e, ta=ta, tb=tb, z=z, zr=zr, t0=t0, tn=tn: e.tensor_tensor(out=kTz[z][0][zr, t0:t0 + tn], in0=ta[zr, 0:tn], in1=tb[zr, 0:tn], op=ALU.add), reads=[ta_r, tb_r], writes=[kTz[z][1]])
                p4, p4r = banks[4]
                fw.op("scalar", lambda e, p4=p4, t0=t0, tn=tn: e.activation(out=zs[:, t0:t0 + tn], in_=p4[:, 0:tn], func=AF.Silu), reads=[p4r], writes=[zs_r])
                p5, p5r = banks[5]
                fw.op("vector", lambda e, p5=p5, t0=t0, tn=tn: e.tensor_copy(out=vt[:, t0 // 128:(t0 + tn) // 128, :].rearrange("p t e -> p (t e)"), in_=p5[:, 0:tn]), reads=[p5r], writes=[vt_r])
            groups = ([] if last else [(0, list(range(2)))]) + [(ci, list(range(NTI))) for ci in range(1, 9)]
            items = []
            for gi, (ci, kts) in enumerate(groups):
                for z in range(2):
                    for idx, kt in enumerate(kts):
                        items.append((gi, ci, z, idx, kt, len(kts)))
            LA = 2
            Ebuf = {}

            def emit_front(n):
                gi, ci, z, idx, kt, nk = items[n]
                q0, qn = self.chunk(ci)
                sp, spr = self.pb[n % 3]
                E_t, E_r = Eb[n % len(Eb)]
                Ebuf[n] = (E_t, E_r)
                fw.op("tensor", lambda e, sp=sp, kt=kt, z=z, q0=q0, qn=qn: e.matmul(sp[:, 0:qn], lhsT=kTz[z][0][:, kt * 128:(kt + 1) * 128], rhs=qT[:, q0:q0 + qn], start=True, stop=True),
                      reads=[kTz[z][1], qT_r], writes=[spr])
                fw.op("scalar", lambda e, sp=sp, E_t=E_t, qn=qn: e.activation(out=E_t[:, 0:qn], in_=sp[:, 0:qn], func=AF.Exp), reads=[spr], writes=[E_r])

            def emit_back(n):
                gi, ci, z, idx, kt, nk = items[n]
                q0, qn = self.chunk(ci)
                E_t, E_r = Ebuf.pop(n)
                po, por = self.pb[3 + z * 2]
                psm, psr = self.pb[4 + z * 2]
                fw.op("tensor", lambda e, po=po, E_t=E_t, kt=kt, qn=qn, idx=idx, nk=nk: e.matmul(po[:, 0:qn], lhsT=vt[:, kt, :], rhs=E_t[:, 0:qn], start=(idx == 0), stop=(idx == nk - 1)),
                      reads=[vt_r, E_r], writes=[por], signal=False)
                fw.op("tensor", lambda e, psm=psm, E_t=E_t, qn=qn, idx=idx, nk=nk: e.matmul(psm[:, 0:qn], lhsT=self.ones_b[:], rhs=E_t[:, 0:qn], start=(idx == 0), stop=(idx == nk - 1)),
                      reads=[self.ones_b_r, E_r], writes=[psr])
                if z == 1 and idx == nk - 1:
                    epilogue(ci)

            def epilogue(ci):
                q0, qn = self.chunk(ci)
                po1, por1 = self.pb[3]
                ps1, psr1 = self.pb[4]
                po2, por2 = self.pb[5]
                ps2, psr2 = self.pb[6]
                fw.op("vector", lambda e, qn=qn: e.reciprocal(out=r1[:, 0:qn], in_=ps1[:, 0:qn]), reads=[psr1], writes=[r1_r])
                fw.op("vector", lambda e, qn=qn: e.tensor_tensor(out=a1[:, 0:qn], in0=po1[:, 0:qn], in1=r1[:, 0:qn], op=ALU.mult), reads=[por1, r1_r], writes=[a1_r])
                fw.op("vector", lambda e, qn=qn: e.reciprocal(out=r2[:, 0:qn], in_=ps2[:, 0:qn]), reads=[psr2], writes=[r2_r])
                fw.op("vector", lambda e, qn=qn: e.tensor_tensor(out=a2[:, 0:qn], in0=po2[:, 0:qn], in1=r2[:, 0:qn], op=ALU.mult), reads=[por2, r2_r], writes=[a2_r])
                fw.op("vector", lambda e, qn=qn: e.scalar_tensor_tensor(out=a1[:, 0:qn], in0=a2[:, 0:qn], scalar=nlam[:, 0:1], in1=a1[:, 0:qn], op0=ALU.mult, op1=ALU.add), reads=[a1_r, a2_r, nlam_r], writes=[a1_r])
                fw.op("gpsimd", lambda e, qn=qn: e.tensor_tensor(out=sqb[:, 0:qn], in0=a1[:, 0:qn], in1=a1[:, 0:qn], op=ALU.mult), reads=[a1_r], writes=[sqb_r])
                mp, mpr = self.pb[7]
                fw.op("tensor", lambda e, qn=qn: e.matmul(mp[:, 0:qn], lhsT=self.mean_b[:], rhs=sqb[:, 0:qn], start=True, stop=True), reads=[self.mean_b_r, sqb_r], writes=[mpr])
                fw.op("scalar", lambda e, qn=qn: e.activation(out=rs[:, 0:qn], in_=mp[:, 0:qn], func=AF.Ln, bias=self.epss[:], scale=1.0), reads=[mpr, self.epss_r], writes=[rs_r])
                fw.op("scalar", lambda e, qn=qn: e.activation(out=rs[:, 0:qn], in_=rs[:, 0:qn], func=AF.Exp, scale=-0.5), reads=[rs_r], writes=[rs_r])
                fw.op("vector", lambda e, qn=qn: e.tensor_tensor(out=a1[:, 0:qn], in0=a1[:, 0:qn], in1=rs[:, 0:qn], op=ALU.mult), reads=[a1_r, rs_r], writes=[a1_r])
                og_t, og_r_ = ogt[ci % 2]
                fw.op("vector", lambda e, og_t=og_t, q0=q0, qn=qn: e.scalar_tensor_tensor(out=og_t[:, 0:qn], in0=a1[:, 0:qn], scalar=sg[:, 0:1], in1=zs[:, q0:q0 + qn], op0=ALU.mult, op1=ALU.mult), reads=[a1_r, sg_r, zs_r], writes=[og_r_])
                fw.dma(self.og[hd, :, q0:q0 + qn], og_t[:, 0:qn], reads=[og_r_], writes=[self.og_r[hd][ci]], sem_res=og_r_)

            for n in range(len(items) + LA):
                if n < len(items):
                    emit_front(n)
                if n - LA >= 0:
                    emit_back(n - LA)

    def load_w_bf16(self, dst, dst_r, w2d, ncols, ts, name):
        fw = self.fw
        wst = [fw.sb("%s_st%d" % (name, k), [128, ncols], F32, ts) for k in range(2)]
        for kc in range(8):
            s_t, s_r = wst[kc % 2]
            fw.dma(s_t[:], w2d[kc * 128:(kc + 1) * 128, :], writes=[s_r], sem_res=s_r)
            fw.op("gpsimd", lambda e, kc=kc, s_t=s_t: e.tensor_copy(out=dst[:, kc, :], in_=s_t[:]), reads=[s_r], writes=[dst_r])

    def gate_proj(self, wz, wz_r, hT, hT_r, ts):
        fw = self.fw
        zt = [fw.sb("gz%d" % k, [128, 512], BF16, ts) for k in range(3)]
        n = 0
        for ci in range(9):
            t0, tn = self.chunk(ci)
            hrs = [hT_r[t] for t in range(t0 // 128, (t0 + tn) // 128)]
            for fc in range(8):
                pt, pr = self.pb[n % 4]
                z_t, z_r = zt[n % 3]
                n += 1
                for kc in range(8):
                    fw.op("tensor", lambda e, kc=kc, pt=pt, fc=fc, t0=t0, tn=tn: e.matmul(pt[:, 0:tn], lhsT=wz[:, kc, fc * 128:(fc + 1) * 128], rhs=hT[:, kc, t0:t0 + tn], start=(kc == 0), stop=(kc == 7)),
                          reads=[wz_r] + hrs, writes=[pr], signal=(kc == 7))
                fw.op("scalar", lambda e, pt=pt, z_t=z_t, tn=tn: e.activation(out=z_t[:, 0:tn], in_=pt[:, 0:tn], func=AF.Silu), reads=[pr], writes=[z_r])
                fw.dma(self.zsd[fc, :, t0:t0 + tn], z_t[:, 0:tn], reads=[z_r], writes=[self.zsd_r[fc][ci]], sem_res=z_r)

    def fnet_prealloc(self, ts):
        fw = self.fw
        Utm, _ = fw.sb("Utm", [128, NTI, D], BF16, ts)
        Utm_r = [Res("Utm%d" % t) for t in range(NTI)]
        return Utm, Utm_r

    def fnet_part1(self, i, j, hT, hT_r, pre, ts):
        fw = self.fw
        Utm, Utm_r = pre
        Win = self.I["fn_w_in"][j]
        wu, wu_r = fw.sb("fwu", [128, 8, D], BF16, ts)
        wz, wz_r = fw.sb("fwz", [128, 8, D], BF16, ts)
        self.load_w_bf16(wu, wu_r, Win[:, 0:D], D, ts, "fwu")
        self.load_w_bf16(wz, wz_r, Win[:, D:2 * D], D, ts, "fwz")
        n = 0
        for tt in range(NTI):
            for half in range(2):
                pt, pr = self.pb[4 + n % 4]
                for kc in range(8):
                    fw.op("tensor", lambda e, kc=kc, pt=pt, tt=tt, half=half: e.matmul(pt[:, :], lhsT=hT[:, kc, tt * 128:(tt + 1) * 128], rhs=wu[:, kc, half * 512:(half + 1) * 512], start=(kc == 0), stop=(kc == 7)),
                          reads=[wu_r, hT_r[tt]], writes=[pr], signal=(kc == 7))
                if n % 2 == 0:
                    fw.op("vector", lambda e, pt=pt, tt=tt, half=half: e.tensor_copy(out=Utm[:, tt, half * 512:(half + 1) * 512], in_=pt[:, :]), reads=[pr], writes=[Utm_r[tt]])
                else:
                    fw.op("scalar", lambda e, pt=pt, tt=tt, half=half: e.copy(out=Utm[:, tt, half * 512:(half + 1) * 512], in_=pt[:, :]), reads=[pr], writes=[Utm_r[tt]])
                n += 1
        self.gate_proj(wz, wz_r, hT, hT_r, ts)

    def fnet_part2(self, i, j, pre, ts):
        fw = self.fw
        Utm, Utm_r = pre
        cc_, cc_r = fw.sb("fCc", [128, 128], BF16, ts)
        sc_, sc_r = fw.sb("fSc", [128, 128], BF16, ts)
        fw.dma(cc_[:], self.Cn["dftCc"][:, :], writes=[cc_r], sem_res=cc_r)
        fw.dma(sc_[:], self.Cn["dftnSc"][:, :], writes=[sc_r], sem_res=sc_r)
        wgs, wgs_r = fw.sb("fwg_st", [128, 8, 128], F32, ts)
        wg, wg_r = fw.sb("fwg", [128, 8, 128], BF16, ts)
        fw.dma(wgs[:], self.I["fn_w_group"][j].rearrange("g c e -> c g e"), writes=[wgs_r], sem_res=wgs_r)
        fw.op("gpsimd", lambda e: e.tensor_copy(out=wg[:], in_=wgs[:]), reads=[wgs_r], writes=[wg_r])
        CB, _ = fw.sb("fCB", [128, 32, 512], BF16, ts)
        SB, _ = fw.sb("fSB", [128, 32, 512], BF16, ts)
        CB_r = [fw.res("fCB%d" % k, ts) for k in range(4)]
        SB_r = [fw.res("fSB%d" % k, ts) for k in range(4)]
        Pb = [fw.sb("fPb%d" % k, [128, 512], BF16, ts) for k in range(2)]
        Qb = [fw.sb("fQb%d" % k, [128, 512], BF16, ts) for k in range(2)]
        Fb = [fw.sb("fFb%d" % k, [128, 512], BF16, ts) for k in range(2)]
        zt = [fw.sb("fzt%d" % k, [128, 512], BF16, ts) for k in range(2)]
        ot = [fw.sb("fot%d" % k, [128, 512], BF16, ts) for k in range(2)]
        last = (i == DEPTH - 1)
        n = 0
        for ci in (range(1, 9) if last else range(9)):
            t0, tn = self.chunk(ci)
            if ci == 0:
                nlt, lt0 = 2, 0
                for k in range(2):
                    fw.dma(CB[:, k, 0:256], self.Cn["dftCX"][k * 128:(k + 1) * 128, :], writes=[CB_r[0]], sem_res=CB_r[0])
                    fw.dma(SB[:, k, 0:256], self.Cn["dftSX"][k * 128:(k + 1) * 128, :], writes=[SB_r[0]], sem_res=SB_r[0])
            else:
                nlt, lt0 = 32, 2
                c0 = (ci - 1) * 512
                for k in range(4):
                    fw.dma(CB[:, k * 8:(k + 1) * 8, :], self.Cn["dftCL"][k * 1024:(k + 1) * 1024, c0:c0 + 512].rearrange("(t p) c -> p t c", p=128), writes=[CB_r[k]], sem_res=CB_r[k])
                    fw.dma(SB[:, k * 8:(k + 1) * 8, :], self.Cn["dftSL"][k * 1024:(k + 1) * 1024, c0:c0 + 512].rearrange("(t p) c -> p t c", p=128), writes=[SB_r[k]], sem_res=SB_r[k])
            for g in range(8):
                pp, ppr = self.pb[(n % 2) * 2]
                qp, qpr = self.pb[(n % 2) * 2 + 1]
                fp, fpr = self.pb[4 + n % 2]
                yp, ypr = self.pb[6 + n % 2]
                P_t, P_r = Pb[n % 2]
                Q_t, Q_r = Qb[n % 2]
                F_t, F_r = Fb[n % 2]
                z_t, z_r = zt[n % 2]
                o_t, o_r = ot[n % 2]
                n += 1
                fw.dma(z_t[:, 0:tn], self.zsd[g, :, t0:t0 + tn], reads=[self.zsd_r[g][ci]], writes=[z_r], sem_res=z_r)
                for (acc, accr, buf, bufr) in ((pp, ppr, CB, CB_r), (qp, qpr, SB, SB_r)):
                    for lt in range(nlt):
                        fw.op("tensor", lambda e, acc=acc, buf=buf, lt=lt, g=g, tn=tn, nlt=nlt, lt0=lt0: e.matmul(acc[:, 0:tn], lhsT=Utm[:, lt0 + lt, g * 128:(g + 1) * 128], rhs=buf[:, lt, 0:tn], start=(lt == 0), stop=(lt == nlt - 1)),
                              reads=[Utm_r[lt0 + lt], bufr[lt // 8]], writes=[accr], signal=(lt == nlt - 1 or lt % 8 == 7))
                fw.op("scalar", lambda e, pp=pp, P_t=P_t, tn=tn: e.copy(out=P_t[:, 0:tn], in_=pp[:, 0:tn]), reads=[ppr], writes=[P_r])
                fw.op("vector", lambda e, qp=qp, Q_t=Q_t, tn=tn: e.tensor_copy(out=Q_t[:, 0:tn], in_=qp[:, 0:tn]), reads=[qpr], writes=[Q_r])
                fw.op("tensor", lambda e, fp=fp, P_t=P_t, tn=tn: e.matmul(fp[:, 0:tn], lhsT=cc_[:], rhs=P_t[:, 0:tn], start=True, stop=False), reads=[cc_r, P_r], writes=[fpr], signal=False)
                fw.op("tensor", lambda e, fp=fp, Q_t=Q_t, tn=tn: e.matmul(fp[:, 0:tn], lhsT=sc_[:], rhs=Q_t[:, 0:tn], start=False, stop=True), reads=[sc_r, Q_r], writes=[fpr])
                fw.op("vector", lambda e, fp=fp, F_t=F_t, tn=tn: e.tensor_copy(out=F_t[:, 0:tn], in_=fp[:, 0:tn]), reads=[fpr], writes=[F_r])
                fw.op("tensor", lambda e, yp=yp, F_t=F_t, g=g, tn=tn: e.matmul(yp[:, 0:tn], lhsT=wg[:, g, :], rhs=F_t[:, 0:tn], start=True, stop=True), reads=[wg_r, F_r], writes=[ypr])
                fw.op("vector", lambda e, yp=yp, z_t=z_t, o_t=o_t, tn=tn: e.tensor_tensor(out=o_t[:, 0:tn], in0=yp[:, 0:tn], in1=z_t[:, 0:tn], op=ALU.mult), reads=[ypr, z_r], writes=[o_r])
                fw.dma(self.og[g, :, t0:t0 + tn], o_t[:, 0:tn], reads=[o_r], writes=[self.og_r[g][ci]], sem_res=o_r)

    def rwkv_prealloc(self, ts):
        fw = self.fw
        nc = self.nc
        R = {}
        R["Dall"] = [fw.sb("Dall%d" % z, [128, 8, 68], F32, ts) for z in range(2)]
        R["stack"] = ts.enter_context(ExitStack())
        R["twd"] = fw.sb("twd", [128, NT], F32, R["stack"])
        R["adT"] = fw.sb("adT", [128, NT], F32, R["stack"])
        if not hasattr(self, "sTd"):
            dk = "ExternalOutput" if self.dbg else "Internal"
            self.sTd = nc.dram_tensor("sTd", [24, 128, NT], F32, kind=dk).ap()
            self.FMd = [nc.dram_tensor("FMd%d" % z, [NTI, 128, 8, 4, 128], BF16, kind="Internal").ap() for z in range(2)]
            self.TMd = [nc.dram_tensor("TMd%d" % z, [NTI, 128, 2, 8, 128], BF16, kind="Internal").ap() for z in range(2)]
            self.Vd = nc.dram_tensor("Vd", [NTI, 128, 8, 128], BF16, kind="Internal").ap()
            self.bonusd = nc.dram_tensor("bonusd", [8, 128, NT], F32, kind=dk).ap()
            self.yd = [nc.dram_tensor("yd%d" % z, [NT, D], F32, kind=dk).ap() for z in range(2)]
        return R

    def rwkv_part1(self, i, j, hT, hT_r, R, ts):
        fw = self.fw
        W = self.I["rw_w_in"][j]
        twd, twd_r = R["twd"]
        adT, adT_r = R["adT"]
        wz, wz_r = fw.sb("rwz", [128, 8, D], BF16, ts)
        self.load_w_bf16(wz, wz_r, W[:, 3328:4352], D, ts, "rwz")
        self.gate_proj(wz, wz_r, hT, hT_r, ts)
        mu_fm, mu_r = fw.sb("mu_fm", [128, 26], F32, ts)
        self.load_fm(mu_fm[:], mu_r, self.I["rw_mu"][j, :].rearrange("(n p) -> n p", p=128), 26, ts)
        ca, ca_r = fw.sb("mix_a", [128, 26], F32, ts)
        cb, cb_r = fw.sb("mix_b", [128, 26], F32, ts)
        fw.op("vector", lambda e: e.tensor_scalar(out=ca[:], in0=mu_fm[:], scalar1=-1.0, scalar2=1.0, op0=ALU.mult, op1=ALU.add), reads=[mu_r], writes=[ca_r])
        fw.op("vector", lambda e: e.tensor_scalar(out=cb[:], in0=mu_fm[:], scalar1=0.5, scalar2=None, op0=ALU.mult), reads=[mu_r], writes=[cb_r])
        NP = NT + 3
        sraw = [fw.sb("sraw%d" % k, [128, NP], F32, ts) for k in range(2)]
        for k in range(2):
            fw.op("gpsimd", lambda e, k=k: e.memset(sraw[k][0][:], 0.0), writes=[sraw[k][1]])
        wst = [fw.sb("rw_st%d" % k, [128, 8, 128], F32, ts) for k in range(2)]
        wbs = [fw.sb("rw_wb%d" % k, [128, 8, 128], BF16, ts) for k in range(2)]
        PW = 512
        tmpb = [fw.sb("mixt%d" % k, [128, PW], F32, ts) for k in range(2)]
        smxb = [fw.sb("mixs%d" % k, [128, PW], F32, ts) for k in range(2)]
        npc = 0
        nev = 0
        for fc in range(26):
            s_t, s_r = wst[fc % 2]
            w_t, w_r = wbs[fc % 2]
            sr_t, sr_r = sraw[fc % 2]
            fw.dma(s_t[:], W[:, fc * 128:(fc + 1) * 128].rearrange("(k p) c -> p k c", p=128), writes=[s_r], sem_res=s_r)
            fw.op("gpsimd", lambda e, s_t=s_t, w_t=w_t: e.tensor_copy(out=w_t[:], in_=s_t[:]), reads=[s_r], writes=[w_r])
            for ci in range(9):
                t0, tn = self.chunk(ci)
                off = t0 + (1 if ci == 0 else 2)
                hrs = [hT_r[t] for t in range(t0 // 128, (t0 + tn) // 128)]
                pt, pr = self.pb[4 + nev % 4]
                for kc in range(8):
                    fw.op("tensor", lambda e, kc=kc, pt=pt, w_t=w_t, t0=t0, tn=tn: e.matmul(pt[:, 0:tn], lhsT=w_t[:, kc, :], rhs=hT[:, kc, t0:t0 + tn], start=(kc == 0), stop=(kc == 7)),
                          reads=[w_r] + hrs, writes=[pr], signal=(kc == 7))
                if nev % 2 == 0:
                    fw.op("scalar", lambda e, pt=pt, sr_t=sr_t, off=off, tn=tn: e.copy(out=sr_t[:, off:off + tn], in_=pt[:, 0:tn]), reads=[pr], writes=[sr_r])
                else:
                    fw.op("vector", lambda e, pt=pt, sr_t=sr_t, off=off, tn=tn: e.tensor_copy(out=sr_t[:, off:off + tn], in_=pt[:, 0:tn]), reads=[pr], writes=[sr_r])
                nev += 1
            for (c0, cn, tok0) in [(1, 256, 0)] + [(258 + q * 512, 512, 256 + q * 512) for q in range(8)]:
                tm_t, tm_r = tmpb[npc % 2]
                sm_t, sm_r = smxb[npc % 2]
                npc += 1
                fw.op("gpsimd", lambda e, tm_t=tm_t, sr_t=sr_t, c0=c0, cn=cn: e.tensor_tensor(out=tm_t[:, 0:cn], in0=sr_t[:, c0 - 1:c0 - 1 + cn], in1=sr_t[:, c0 + 1:c0 + 1 + cn], op=ALU.add), reads=[sr_r], writes=[tm_r])
                fw.op("vector", lambda e, tm_t=tm_t, fc=fc, cn=cn: e.tensor_scalar(out=tm_t[:, 0:cn], in0=tm_t[:, 0:cn], scalar1=cb[:, fc:fc + 1], scalar2=None, op0=ALU.mult), reads=[tm_r, cb_r], writes=[tm_r])
                if fc < 24:
                    fw.op("vector", lambda e, tm_t=tm_t, sm_t=sm_t, sr_t=sr_t, fc=fc, c0=c0, cn=cn: e.scalar_tensor_tensor(out=sm_t[:, 0:cn], in0=sr_t[:, c0:c0 + cn], scalar=ca[:, fc:fc + 1], in1=tm_t[:, 0:cn], op0=ALU.mult, op1=ALU.add),
                          reads=[sr_r, tm_r, ca_r], writes=[sm_r])
                    fw.dma(self.sTd[fc, :, tok0:tok0 + cn], sm_t[:, 0:cn], reads=[sm_r], sem_res=sm_r)
                else:
                    dst, dst_r = (twd, twd_r) if fc == 24 else (adT, adT_r)
                    fw.op("vector", lambda e, tm_t=tm_t, dst=dst, sr_t=sr_t, fc=fc, c0=c0, cn=cn, tok0=tok0: e.scalar_tensor_tensor(out=dst[:, tok0:tok0 + cn], in0=sr_t[:, c0:c0 + cn], scalar=ca[:, fc:fc + 1], in1=tm_t[:, 0:cn], op0=ALU.mult, op1=ALU.add),
                          reads=[sr_r, tm_r, ca_r], writes=[dst_r])
                    if fc == 24:
                        fw.op("scalar", lambda e, tok0=tok0, cn=cn: e.activation(out=twd[:, tok0:tok0 + cn], in_=twd[:, tok0:tok0 + cn], func=AF.Tanh), reads=[twd_r], writes=[twd_r])

    def rwkv_part2(self, i, j, R, ts0):
        fw = self.fw
        stop = getattr(self, "stop_at", 99)
        fw.barrier()
        if stop < 2:
            return
        with ExitStack() as ts:
            self.rwkv_F2(i, j, R, ts)
            fw.barrier()
        R["stack"].close()
        if stop < 3:
            return
        with ExitStack() as ts:
            self.rwkv_S(i, j, R, ts)
            fw.barrier()
        if stop < 4:
            return
        with ExitStack() as ts:
            self.rwkv_O(i, j, R, ts)
            fw.barrier()

    def rwkv_F2(self, i, j, R, ts):
        fw = self.fw
        I = self.I
        twd, twd_r = R["twd"]
        adT, adT_r = R["adT"]
        def fm(name, src, n):
            t, r = fw.sb(name, [128, n], F32, ts)
            self.load_fm(t[:], r, src, n, ts)
            return t, r
        kk_p, kk_pr = fm("p_kk", I["rw_k_k"][j, :].rearrange("(n p) -> n p", p=128), 8)
        ka_p, ka_pr = fm("p_ka", I["rw_k_a"][j, :].rearrange("(n p) -> n p", p=128), 8)
        rk_p, rk_pr = fm("p_rk", I["rw_r_k"][j, :].rearrange("(n p) -> n p", p=128), 8)
        w0_p, w0_pr = fm("p_w0", I["rw_w0"][j].rearrange("z (n p) -> (z n) p", p=128), 16)
        a0_p, a0_pr = fm("p_a0", I["rw_a0"][j].rearrange("z (n p) -> (z n) p", p=128), 16)
        oka, oka_r = fw.sb("p_oka", [128, 8], F32, ts)
        fw.op("vector", lambda e: e.tensor_scalar(out=oka[:], in0=ka_p[:], scalar1=-1.0, scalar2=1.0, op0=ALU.mult, op1=ALU.add), reads=[ka_pr], writes=[oka_r])
        wup, wup_r = fw.sb("wup", [128, D], F32, ts)
        aup, aup_r = fw.sb("aup", [128, D], F32, ts)
        fw.dma(wup[:], I["rw_w_up"][j].rearrange("z r f -> (z r) f"), writes=[wup_r], sem_res=wup_r)
        fw.dma(aup[:], I["rw_a_up"][j].rearrange("z r f -> (z r) f"), writes=[aup_r], sem_res=aup_r)
        bones, bones_r = fw.sb("bones", [128, 128], F32, ts)
        fw.op("vector", lambda e: e.memset(bones[:], 0.0), writes=[bones_r])
        fw.op("vector", lambda e: e.memset(bones[0:64, 0:64], 1.0), writes=[bones_r])
        fw.op("vector", lambda e: e.memset(bones[64:128, 64:128], 1.0), writes=[bones_r])
        eps12, eps12_r = fw.sb("eps12", [128, 1], F32, ts)
        fw.op("vector", lambda e: e.memset(eps12[:], 1e-12), writes=[eps12_r])
        cmask, cmask_r = fw.sb("cmask", [128, 512], F32, ts)
        fw.op("vector", lambda e: e.memset(cmask[:], 1.0), writes=[cmask_r])
        fw.op("vector", lambda e: e.memset(cmask[:].rearrange("p (c t) -> p c t", t=64)[:, :, 0:1], 0.0), writes=[cmask_r])

        cnt = [0]

        def T(name, dt=F32, n=2, w=512):
            return [fw.sb("%s_%d" % (name, k), [128, w], dt, ts) for k in range(n)]
        rTb, kTb, vTb = T("f_r"), T("f_k"), T("f_v")
        lwb = [T("f_lw0"), T("f_lw1")]
        azb = [T("f_az0"), T("f_az1")]
        kdb = [T("f_kd0"), T("f_kd1")]
        bzb = [T("f_b0"), T("f_b1")]
        kkrb, sqb_, rnb, kkb, tb1, tb2, tb3 = T("f_kkr"), T("f_sq"), T("f_rn"), T("f_kk"), T("f_t1"), T("f_t2"), T("f_t3")
        cwb, e1b, e2b, e3b, e4b = T("f_cw"), T("f_e1"), T("f_e2"), T("f_e3"), T("f_e4")
        fmo = [T("f_o%d" % k, dt=BF16) for k in range(4)]
        hato = [T("f_h%d" % k) for k in range(2)]
        trs = T("f_trs", dt=BF16, n=3)
        bonb = T("f_bon")
        NEG = -math.exp(-0.5)
        nb = 0
        ntr = [0]

        def transpose_store(src_t, src_r, dst_ap_fn, ntile):
            tp, tpr = self.pb[4 + ntr[0] % 4]
            o_t, o_r = trs[ntr[0] % 3]
            ntr[0] += 1
            for tl in range(ntile):
                fw.op("tensor", lambda e, tl=tl, tp=tp: e.transpose(tp[:, tl * 128:(tl + 1) * 128], src_t[:, tl * 128:(tl + 1) * 128], self.ident[:]),
                      reads=[src_r, self.ident_r], writes=[tpr], signal=(tl == ntile - 1))
            fw.op("scalar", lambda e, tp=tp, o_t=o_t, ntile=ntile: e.copy(out=o_t[:, 0:ntile * 128], in_=tp[:, 0:ntile * 128]), reads=[tpr], writes=[o_r])
            for tl in range(ntile):
                fw.dma(dst_ap_fn(tl), o_t[:, tl * 128:(tl + 1) * 128], reads=[o_r], sem_res=o_r)

        for hp in range(8):
            for ci in range(9):
                t0, N = self.chunk(ci)
                tile0 = t0 // 128
                ntile = N // 128
                nch = N // 64
                k = nb % 2
                nb += 1
                (rT, rT_r), (kT, kT_r), (vT, vT_r) = rTb[k], kTb[k], vTb[k]
                fw.dma(rT[:, 0:N], self.sTd[hp, :, t0:t0 + N], writes=[rT_r], sem_res=rT_r)
                fw.dma(kT[:, 0:N], self.sTd[8 + hp, :, t0:t0 + N], writes=[kT_r], sem_res=kT_r)
                fw.dma(vT[:, 0:N], self.sTd[16 + hp, :, t0:t0 + N], writes=[vT_r], sem_res=vT_r)
                transpose_store(vT, vT_r, lambda tl, tile0=tile0, hp=hp: self.Vd[tile0 + tl, :, hp, :], ntile)
                for z in range(2):
                    rows = slice(z * 64, (z + 1) * 64)
                    pw, pwr = self.pb[z]
                    pa, par = self.pb[2 + z]
                    lw, lw_r = lwb[z][k]
                    az, az_r = azb[z][k]
                    fw.op("tensor", lambda e, pw=pw, rows=rows, hp=hp, t0=t0, N=N: e.matmul(pw[:, 0:N], lhsT=wup[rows, hp * 128:(hp + 1) * 128], rhs=twd[rows, t0:t0 + N], start=True, stop=True), reads=[wup_r, twd_r], writes=[pwr])
                    fw.op("tensor", lambda e, pa=pa, rows=rows, hp=hp, t0=t0, N=N: e.matmul(pa[:, 0:N], lhsT=aup[rows, hp * 128:(hp + 1) * 128], rhs=adT[rows, t0:t0 + N], start=True, stop=True), reads=[aup_r, adT_r], writes=[par])
                    fw.op("scalar", lambda e, pw=pw, lw=lw, z=z, hp=hp, N=N: e.activation(out=lw[:, 0:N], in_=pw[:, 0:N], func=AF.Sigmoid, bias=w0_p[:, z * 8 + hp:z * 8 + hp + 1], scale=1.0), reads=[pwr, w0_pr], writes=[lw_r])
                    fw.op("vector", lambda e, lw=lw, N=N: e.tensor_scalar(out=lw[:, 0:N], in0=lw[:, 0:N], scalar1=NEG, scalar2=None, op0=ALU.mult), reads=[lw_r], writes=[lw_r])
                    fw.op("scalar", lambda e, pa=pa, az=az, z=z, hp=hp, N=N: e.activation(out=az[:, 0:N], in_=pa[:, 0:N], func=AF.Sigmoid, bias=a0_p[:, z * 8 + hp:z * 8 + hp + 1], scale=1.0), reads=[par, a0_pr], writes=[az_r])
                (kkr, kkr_r), (sq, sq_r), (rn, rn_r), (kk, kk_r) = kkrb[k], sqb_[k], rnb[k], kkb[k]
                fw.op("vector", lambda e, kkr=kkr, kT=kT, hp=hp, N=N: e.tensor_scalar(out=kkr[:, 0:N], in0=kT[:, 0:N], scalar1=kk_p[:, hp:hp + 1], scalar2=None, op0=ALU.mult), reads=[kT_r, kk_pr], writes=[kkr_r])
                fw.op("gpsimd", lambda e, kkr=kkr, sq=sq, N=N: e.tensor_tensor(out=sq[:, 0:N], in0=kkr[:, 0:N], in1=kkr[:, 0:N], op=ALU.mult), reads=[kkr_r], writes=[sq_r])
                pss, pssr = self.pb[4 + ntr[0] % 4]
                ntr[0] += 1
                fw.op("tensor", lambda e, pss=pss, sq=sq, N=N: e.matmul(pss[:, 0:N], lhsT=bones[:], rhs=sq[:, 0:N], start=True, stop=True), reads=[bones_r, sq_r], writes=[pssr])
                fw.op("scalar", lambda e, pss=pss, rn=rn, N=N: e.activation(out=rn[:, 0:N], in_=pss[:, 0:N], func=AF.Sqrt, bias=eps12[:], scale=1.0), reads=[pssr, eps12_r], writes=[rn_r])
                fw.op("vector", lambda e, rn=rn, N=N: e.reciprocal(out=rn[:, 0:N], in_=rn[:, 0:N]), reads=[rn_r], writes=[rn_r])
                fw.op("gpsimd", lambda e, kk=kk, kkr=kkr, rn=rn, N=N: e.tensor_tensor(out=kk[:, 0:N], in0=kkr[:, 0:N], in1=rn[:, 0:N], op=ALU.mult), reads=[kkr_r, rn_r], writes=[kk_r])
                for z in range(2):
                    az, az_r = azb[z][k]
                    kd, kd_r = kdb[z][k]
                    bz, bz_r = bzb[z][k]
                    fw.op("vector", lambda e, kd=kd, az=az, hp=hp, N=N: e.tensor_scalar(out=kd[:, 0:N], in0=az[:, 0:N], scalar1=ka_p[:, hp:hp + 1], scalar2=oka[:, hp:hp + 1], op0=ALU.mult, op1=ALU.add), reads=[az_r, ka_pr, oka_r], writes=[kd_r])
                    fw.op("vector", lambda e, kd=kd, kT=kT, N=N: e.tensor_tensor(out=kd[:, 0:N], in0=kd[:, 0:N], in1=kT[:, 0:N], op=ALU.mult), reads=[kd_r, kT_r], writes=[kd_r])
                    fw.op("gpsimd", lambda e, bz=bz, kk=kk, az=az, N=N: e.tensor_tensor(out=bz[:, 0:N], in0=kk[:, 0:N], in1=az[:, 0:N], op=ALU.mult), reads=[kk_r, az_r], writes=[bz_r])
                (x1, x1_r), (x2, x2_r) = tb1[k], tb2[k]
                fw.op("vector", lambda e, x1=x1, N=N, k=k: e.tensor_tensor(out=x1[:, 0:N], in0=kdb[0][k][0][:, 0:N], in1=kdb[1][k][0][:, 0:N], op=ALU.add), reads=[kdb[0][k][1], kdb[1][k][1]], writes=[x1_r])
                fw.op("vector", lambda e, x2=x2, rT=rT, hp=hp, N=N: e.tensor_scalar(out=x2[:, 0:N], in0=rT[:, 0:N], scalar1=rk_p[:, hp:hp + 1], scalar2=0.5, op0=ALU.mult, op1=ALU.mult), reads=[rT_r, rk_pr], writes=[x2_r])
                fw.op("vector", lambda e, x1=x1, x2=x2, N=N: e.tensor_tensor(out=x1[:, 0:N], in0=x1[:, 0:N], in1=x2[:, 0:N], op=ALU.mult), reads=[x1_r, x2_r], writes=[x1_r])
                psb, psbr = self.pb[4 + ntr[0] % 4]
                ntr[0] += 1
                fw.op("tensor", lambda e, psb=psb, x1=x1, N=N: e.matmul(psb[:, 0:N], lhsT=bones[:], rhs=x1[:, 0:N], start=True, stop=True), reads=[bones_r, x1_r], writes=[psbr])
                bo, bo_r = bonb[k]
                fw.op("vector", lambda e, bo=bo, psb=psb, vT=vT, N=N: e.tensor_tensor(out=bo[:, 0:N], in0=psb[:, 0:N], in1=vT[:, 0:N], op=ALU.mult), reads=[psbr, vT_r], writes=[bo_r])
                fw.dma(self.bonusd[hp, :, t0:t0 + N], bo[:, 0:N], reads=[bo_r], sem_res=bo_r)
                for z in range(2):
                    lw, lw_r = lwb[z][k]
                    kd, kd_r = kdb[z][k]
                    bz, bz_r = bzb[z][k]
                    cw, cw_r = cwb[z]
                    (e1, e1_r), (e2, e2_r), (e3, e3_r), (e4, e4_r) = e1b[z], e2b[z], e3b[z], e4b[z]
                    (y1, y1_r) = tb3[z]
                    Dall, Dall_r = R["Dall"][z]
                    fw.op("vector", lambda e, cw=cw, lw=lw, N=N: e.tensor_tensor_scan(out=cw[:, 0:N], data0=cmask[:, 0:N], data1=lw[:, 0:N], initial=0.0, op0=ALU.mult, op1=ALU.add), reads=[cmask_r, lw_r], writes=[cw_r])
                    cw3 = cw[:, 0:N].rearrange("p (c t) -> p c t", t=64)
                    totb = cw3[:, :, 63:64].to_broadcast([128, nch, 64])
                    ch0 = t0 // 64
                    fw.op("scalar", lambda e, Dall=Dall, cw3=cw3, hp=hp, ch0=ch0, nch=nch: e.activation(out=Dall[:, hp, ch0:ch0 + nch], in_=cw3[:, :, 63], func=AF.Exp), reads=[cw_r], writes=[Dall_r])
                    y13 = y1[:, 0:N].rearrange("p (c t) -> p c t", t=64)
                    if z == 0:
                        fw.op("gpsimd", lambda e, y13=y13, totb=totb, cw3=cw3: e.tensor_tensor(out=y13, in0=totb, in1=cw3, op=ALU.subtract), reads=[cw_r], writes=[y1_r])
                        cwz, cwz_r = cw, cw_r
                    else:
                        fw.op("gpsimd", lambda e, y1=y1, cw=cw, lw=lw, N=N: e.tensor_tensor(out=y1[:, 0:N], in0=cw[:, 0:N], in1=lw[:, 0:N], op=ALU.subtract), reads=[cw_r, lw_r], writes=[y1_r])
                        cwz, cwz_r = e4, e4_r
                        e43 = e4[:, 0:N].rearrange("p (c t) -> p c t", t=64)
                        fw.op("gpsimd", lambda e, e43=e43, totb=totb, y13=y13: e.tensor_tensor(out=e43, in0=totb, in1=y13, op=ALU.subtract), reads=[cw_r, y1_r], writes=[e4_r])
                    fw.op("scalar", lambda e, e1=e1, cwz=cwz, N=N: e.activation(out=e1[:, 0:N], in_=cwz[:, 0:N], func=AF.Exp), reads=[cwz_r], writes=[e1_r])
                    fw.op("scalar", lambda e, e2=e2, cwz=cwz, N=N: e.activation(out=e2[:, 0:N], in_=cwz[:, 0:N], func=AF.Exp, scale=-1.0), reads=[cwz_r], writes=[e2_r])
                    fw.op("vector", lambda e, e3=e3, cwz=cwz, lw=lw, N=N: e.tensor_tensor(out=e3[:, 0:N], in0=cwz[:, 0:N], in1=lw[:, 0:N], op=ALU.subtract), reads=[cwz_r, lw_r], writes=[e3_r])
                    fw.op("scalar", lambda e, e3=e3, N=N: e.activation(out=e3[:, 0:N], in_=e3[:, 0:N], func=AF.Exp), reads=[e3_r], writes=[e3_r])
                    fw.op("scalar", lambda e, y1=y1, N=N: e.activation(out=y1[:, 0:N], in_=y1[:, 0:N], func=AF.Exp), reads=[y1_r], writes=[y1_r])
                    outs = [fmo[q][z] for q in range(4)]
                    fw.op("vector", lambda e, o=outs[0][0], kk=kk, e3=e3, N=N: e.scalar_tensor_tensor(out=o[:, 0:N], in0=kk[:, 0:N], scalar=-1.0, in1=e3[:, 0:N], op0=ALU.mult, op1=ALU.mult), reads=[kk_r, e3_r], writes=[outs[0][1]])
                    fw.op("gpsimd", lambda e, o=outs[1][0], rT=rT, e1=e1, N=N: e.tensor_tensor(out=o[:, 0:N], in0=rT[:, 0:N], in1=e1[:, 0:N], op=ALU.mult), reads=[rT_r, e1_r], writes=[outs[1][1]])
                    fw.op("vector", lambda e, o=outs[2][0], bz=bz, e2=e2, N=N: e.tensor_tensor(out=o[:, 0:N], in0=bz[:, 0:N], in1=e2[:, 0:N], op=ALU.mult), reads=[bz_r, e2_r], writes=[outs[2][1]])
                    fw.op("vector", lambda e, o=outs[3][0], kd=kd, e2=e2, N=N: e.tensor_tensor(out=o[:, 0:N], in0=kd[:, 0:N], in1=e2[:, 0:N], op=ALU.mult), reads=[kd_r, e2_r], writes=[outs[3][1]])
                    for q in range(4):
                        fw.dma(self.FMd[z][tile0:tile0 + ntile, :, hp, q, :].rearrange("n p t -> p n t"), outs[q][0][:, 0:N].rearrange("p (n t) -> p n t", t=128), reads=[outs[q][1]], sem_res=outs[q][1])
                    (h0, h0_r), (h1, h1_r) = hato[0][z], hato[1][z]
                    fw.op("vector", lambda e, h0=h0, bz=bz, y1=y1, N=N: e.tensor_tensor(out=h0[:, 0:N], in0=bz[:, 0:N], in1=y1[:, 0:N], op=ALU.mult), reads=[bz_r, y1_r], writes=[h0_r])
                    fw.op("gpsimd", lambda e, h1=h1, kd=kd, y1=y1, N=N: e.tensor_tensor(out=h1[:, 0:N], in0=kd[:, 0:N], in1=y1[:, 0:N], op=ALU.mult), reads=[kd_r, y1_r], writes=[h1_r])
                    transpose_store(h0, h0_r, lambda tl, z=z, tile0=tile0, hp=hp: self.TMd[z][tile0 + tl, :, 0, hp, :], ntile)
                    transpose_store(h1, h1_r, lambda tl, z=z, tile0=tile0, hp=hp: self.TMd[z][tile0 + tl, :, 1, hp, :], ntile)

    def rwkv_S(self, i, j, R, ts):
        fw = self.fw
        mk = {}
        for z in range(2):
            mNf, mNf_r = fw.sb("mNf%d" % z, [128, 256], F32, ts)
            mLf, mLf_r = fw.sb("mLf%d" % z, [128, 128], F32, ts)
            mN, mN_r = fw.sb("mN%d" % z, [128, 2, 256], BF16, ts)
            mL, mL_r = fw.sb("mL%d" % z, [128, 4, 128], BF16, ts)
            fw.dma(mNf[:], self.Cn["mN%d" % z][:, :], writes=[mNf_r], sem_res=mNf_r)
            fw.dma(mLf[:], self.Cn["mL%d" % z][:, :], writes=[mLf_r], sem_res=mLf_r)
            for q in range(2):
                fw.op("vector", lambda e, mN=mN, mNf=mNf, q=q: e.tensor_copy(out=mN[:, q, :], in_=mNf[:]), reads=[mNf_r], writes=[mN_r])
            for q in range(4):
                fw.op("vector", lambda e, mL=mL, mLf=mLf, q=q: e.tensor_copy(out=mL[:, q, :], in_=mLf[:]), reads=[mLf_r], writes=[mL_r])
            mk[z] = (mN, mN_r, mL, mL_r)
        identb4, identb4_r = fw.sb("ident4", [128, 4, 128], BF16, ts)
        for q in range(4):
            fw.op("vector", lambda e, q=q: e.tensor_copy(out=identb4[:, q, :], in_=self.ident[:]), reads=[self.ident_r], writes=[identb4_r])
        Mst = []
        for z in range(2):
            row = []
            for hf in range(2):
                t, r = fw.sb("M%d_%d" % (z, hf), [128, 8, 64], F32, ts)
                fw.op("vector", lambda e, t=t: e.memset(t[:], 0.0), writes=[r])
                tb_, rb_ = fw.sb("Mb%d_%d" % (z, hf), [128, 8, 64], BF16, ts)
                fw.op("vector", lambda e, tb_=tb_: e.memset(tb_[:], 0.0), writes=[rb_])
                row.append((t, r, tb_, rb_))
            Mst.append(row)
        FMb = [fw.sb("sFM%d" % k, [128, 8, 4, 128], BF16, ts) for k in range(2)]
        TMb = [fw.sb("sTM%d" % k, [128, 2, 8, 128], BF16, ts) for k in range(2)]
        Vb = [fw.sb("sV%d" % k, [128, 8, 128], BF16, ts) for k in range(2)]
        G13, _ = fw.sb("sG13", [128, 16, 256], BF16, ts)
        G24, _ = fw.sb("sG24", [128, 16, 256], BF16, ts)
        G13_r = [Res("sG13_%d" % g) for g in range(8)]
        G24_r = [Res("sG24_%d" % g) for g in range(8)]
        Lb_ = [fw.sb("sL%d" % k, [128, 16, 128], BF16, ts)[0] for k in range(2)]
        Nb_ = [fw.sb("sN%d" % k, [128, 16, 128], BF16, ts)[0] for k in range(2)]
        Pm, _ = fw.sb("sP", [128, 16, 128], BF16, ts)
        L_r = [[Res("sL%d_%d" % (k, g)) for g in range(4)] for k in range(2)]
        N_r = [[Res("sN%d_%d" % (k, g)) for g in range(4)] for k in range(2)]
        P_r = [Res("sP_%d" % g) for g in range(4)]
        Z2s = [fw.sb("sZ2s%d" % k, [128, 512], F32, ts) for k in range(2)]
        Zs = [fw.sb("sZs%d" % k, [128, 512], BF16, ts) for k in range(2)]
        Us = [fw.sb("sUs%d" % k, [128, 512], BF16, ts) for k in range(2)]
        Ys = [fw.sb("sYs%d" % k, [128, 512], F32, ts) for k in range(2)]
        npb = [0]

        def prep_bank():
            b = self.pb[4 + npb[0] % 4]
            npb[0] += 1
            return b

        order = [list(range(NTI)), [1, 0] + list(range(NTI - 1, 1, -1))]
        nstep = 0
        for si in range(NTI):
            for z in getattr(self, "zorder", (0, 1)):
                tile = order[z][si]
                k = nstep % 2
                nstep += 1
                FM, FM_r = FMb[k]
                TM, TM_r = TMb[k]
                V, V_r = Vb[k]
                mN, mN_r, mL, mL_r = mk[z]
                fw.dma(FM[:], self.FMd[z][tile], writes=[FM_r], sem_res=FM_r)
                fw.dma(TM[:], self.TMd[z][tile], writes=[TM_r], sem_res=TM_r)
                fw.dma(V[:], self.Vd[tile], writes=[V_r], sem_res=V_r)
                for g2 in range(8):
                    pB, pBr = prep_bank()
                    pK, pKr = prep_bank()
                    pL, pLr = prep_bank()
                    for hl in range(2):
                        h = g2 * 2 + hl
                        hp, hh = h % 8, h // 8
                        rows = slice(hh * 64, (hh + 1) * 64)
                        ar = FM[rows, hp, 0:2, :].rearrange("p a t -> p (a t)")
                        fw.op("tensor", lambda e, FM=FM, TM=TM, V=V, pB=pB, rows=rows, hp=hp, hl=hl, ar=ar: e.matmul(pB[:, hl * 256:(hl + 1) * 256], lhsT=FM[rows, hp, 2, :], rhs=ar, start=True, stop=True), reads=[FM_r], writes=[pBr], signal=(hl == 1))
                        fw.op("tensor", lambda e, FM=FM, TM=TM, V=V, pK=pK, rows=rows, hp=hp, hl=hl, ar=ar: e.matmul(pK[:, hl * 256:(hl + 1) * 256], lhsT=FM[rows, hp, 3, :], rhs=ar, start=True, stop=True), reads=[FM_r], writes=[pKr], signal=(hl == 1))
                        fw.op("tensor", lambda e, FM=FM, TM=TM, V=V, pL=pL, rows=rows, hp=hp, hl=hl: e.matmul(pL[:, hl * 128:(hl + 1) * 128], lhsT=FM[rows, hp, 0, :], rhs=FM[rows, hp, 2, :], start=True, stop=True), reads=[FM_r], writes=[pLr], signal=(hl == 1))
                    h0 = g2 * 2
                    fw.op("vector", lambda e, FM=FM, TM=TM, V=V, pB=pB, h0=h0, mN=mN: e.tensor_tensor(out=G13[:, h0:h0 + 2, :], in0=pB[:, :].rearrange("p (h c) -> p h c", h=2), in1=mN[:], op=ALU.mult), reads=[pBr, mN_r], writes=[G13_r[g2]])
                    fw.op("vector", lambda e, FM=FM, TM=TM, V=V, pK=pK, h0=h0, mN=mN: e.tensor_tensor(out=G24[:, h0:h0 + 2, :], in0=pK[:, :].rearrange("p (h c) -> p h c", h=2), in1=mN[:], op=ALU.mult), reads=[pKr, mN_r], writes=[G24_r[g2]])
                    fw.op("vector", lambda e, FM=FM, TM=TM, V=V, pL=pL, h0=h0, mL=mL: e.tensor_tensor(out=Lb_[0][:, h0:h0 + 2, :], in0=pL[:, 0:256].rearrange("p (h c) -> p h c", h=2), in1=mL[:, 0:2, :], op=ALU.mult), reads=[pLr, mL_r], writes=[L_r[0][g2 // 2]])
                sstop = getattr(self, "s_stop", 99)
                if sstop < 1:
                    continue
                for g4 in range(4):
                    hs = slice(g4 * 4, g4 * 4 + 4)
                    fw.op("gpsimd", lambda e, FM=FM, TM=TM, V=V, hs=hs: e.tensor_copy(out=Nb_[0][:, hs, :], in_=G13[:, hs, 0:128]), reads=[G13_r[g4 * 2], G13_r[g4 * 2 + 1]], writes=[N_r[0][g4]])
                    fw.op("gpsimd", lambda e, FM=FM, TM=TM, V=V, hs=hs: e.tensor_tensor(out=Pm[:, hs, :], in0=G13[:, hs, 0:128], in1=identb4[:], op=ALU.add), reads=[G13_r[g4 * 2], G13_r[g4 * 2 + 1], identb4_r], writes=[P_r[g4]])
                for lev in range(1, 6):
                    a, b = (lev - 1) % 2, lev % 2
                    for g4 in range(4):
                        hs = slice(g4 * 4, g4 * 4 + 4)
                        pl, plr = prep_bank()
                        for hl in range(4):
                            h = g4 * 4 + hl
                            fw.op("tensor", lambda e, FM=FM, TM=TM, V=V, pl=pl, h=h, hl=hl, a=a: e.matmul(pl[:, hl * 128:(hl + 1) * 128], lhsT=Nb_[a][:, h, :], rhs=Lb_[a][:, h, :], start=True, stop=True), reads=[N_r[a][g4], L_r[a][g4]], writes=[plr], signal=(hl == 3))
                        fw.op("scalar", lambda e, FM=FM, TM=TM, V=V, pl=pl, hs=hs, b=b: e.copy(out=Lb_[b][:, hs, :], in_=pl[:, :].rearrange("p (h c) -> p h c", h=4)), reads=[plr], writes=[L_r[b][g4]])
                        if lev < 5:
                            pn, pnr = prep_bank()
                            for hl in range(4):
                                h = g4 * 4 + hl
                                fw.op("tensor", lambda e, FM=FM, TM=TM, V=V, pn=pn, h=h, hl=hl, a=a: e.matmul(pn[:, hl * 128:(hl + 1) * 128], lhsT=Lb_[a][:, h, :], rhs=Nb_[a][:, h, :], start=True, stop=True), reads=[N_r[a][g4], L_r[a][g4]], writes=[pnr], signal=(hl == 3))
                            fw.op("vector", lambda e, FM=FM, TM=TM, V=V, pn=pn, hs=hs, b=b: e.tensor_copy(out=Nb_[b][:, hs, :], in_=pn[:, :].rearrange("p (h c) -> p h c", h=4)), reads=[pnr], writes=[N_r[b][g4]])
                        pq, pqr = prep_bank()
                        for hl in range(4):
                            h = g4 * 4 + hl
                            fw.op("tensor", lambda e, FM=FM, TM=TM, V=V, pq=pq, h=h, hl=hl, b=b: e.matmul(pq[:, hl * 128:(hl + 1) * 128], lhsT=Lb_[b][:, h, :], rhs=Pm[:, h, :], start=True, stop=True), reads=[L_r[b][g4], P_r[g4]], writes=[pqr], signal=(hl == 3))
                        fw.op("vector", lambda e, FM=FM, TM=TM, V=V, pq=pq, hs=hs: e.tensor_tensor(out=Pm[:, hs, :], in0=pq[:, :].rearrange("p (h c) -> p h c", h=4), in1=Pm[:, hs, :], op=ALU.add), reads=[pqr, P_r[g4]], writes=[P_r[g4]])
                if self.dbg and nstep == 1:
                    self.dG13 = self.nc.dram_tensor("dG13", [128, 16, 256], F32, kind="ExternalOutput").ap()
                    self.dG24 = self.nc.dram_tensor("dG24", [128, 16, 256], F32, kind="ExternalOutput").ap()
                    self.dP = self.nc.dram_tensor("dP", [128, 16, 128], F32, kind="ExternalOutput").ap()
                    self.dFM = self.nc.dram_tensor("dFM", [128, 8, 4, 128], F32, kind="ExternalOutput").ap()
                    self.dTM = self.nc.dram_tensor("dTM", [128, 2, 8, 128], F32, kind="ExternalOutput").ap()
                    self.dV = self.nc.dram_tensor("dV", [128, 8, 128], F32, kind="ExternalOutput").ap()
                    dr = fw.res("dbgS", ts)
                    fw.dma(self.dFM, FM[:], reads=[FM_r], sem_res=dr, ek="gpsimd")
                    fw.dma(self.dTM, TM[:], reads=[TM_r], sem_res=dr, ek="gpsimd")
                    fw.dma(self.dV, V[:], reads=[V_r], sem_res=dr, ek="gpsimd")
                    fw.dma(self.dG13, G13[:], reads=list(G13_r), sem_res=dr, ek="gpsimd")
                    fw.dma(self.dG24, G24[:], reads=list(G24_r), sem_res=dr, ek="gpsimd")
                    fw.dma(self.dP, Pm[:], reads=list(P_r), sem_res=dr, ek="gpsimd")
                if sstop < 2:
                    continue
                cks = (0, 1) if z == 0 else (1, 0)
                for hf in range(2):
                    hr = slice(hf * 64, (hf + 1) * 64)
                    M, M_r, Mb, Mb_r = Mst[z][hf]
                    Dall, Dall_r = R["Dall"][z]
                    tA, tAr = self.pb[0]
                    tB, tBr = self.pb[1]
                    tC, tCr = self.pb[2]
                    tD, tDr = self.pb[3]
                    z2, z2_r = Z2s[hf]
                    zs_, zs_r = Zs[hf]
                    us, us_r = Us[hf]
                    ys, ys_r = Ys[hf]
                    g13r = list(G13_r)
                    g24r = list(G24_r)
                    pr_ = list(P_r)
                    for ck in range(2):
                        rows = slice(ck * 64, (ck + 1) * 64)
                        for hp in range(8):
                            h = hf * 8 + hp
                            fw.op("tensor", lambda e, FM=FM, TM=TM, V=V, rows=rows, h=h, hp=hp, hr=hr, ck=ck: e.matmul(tA[rows, hp * 64:(hp + 1) * 64], lhsT=G24[rows, h, ck * 64:(ck + 1) * 64], rhs=V[rows, hp, hr], start=True, stop=True),
                                  reads=g24r + [V_r], writes=[tAr], signal=(ck == 1 and hp == 7))
                    fw.op("scalar", lambda e, FM=FM, TM=TM, V=V, z2=z2: e.copy(out=z2[:], in_=tA[:, :]), reads=[tAr], writes=[z2_r])
                    for ck in (cks if sstop >= 3 else ()):
                        rows = slice(ck * 64, (ck + 1) * 64)
                        cols = slice(ck * 64, (ck + 1) * 64)
                        for hp in range(8):
                            fw.op("tensor", lambda e, FM=FM, TM=TM, V=V, rows=rows, cols=cols, hp=hp, hr=hr, Mb=Mb: e.matmul(tB[rows, hp * 64:(hp + 1) * 64], lhsT=FM[hr, hp, 0, cols], rhs=Mb[hr, hp, :], start=True, stop=True),
                                  reads=[FM_r, Mb_r], writes=[tBr], signal=(hp == 7))
                        for hp in range(8):
                            fw.op("tensor", lambda e, FM=FM, TM=TM, V=V, rows=rows, cols=cols, hp=hp, hr=hr, Mb=Mb: e.matmul(tC[rows, hp * 64:(hp + 1) * 64], lhsT=FM[hr, hp, 1, cols], rhs=Mb[hr, hp, :], start=True, stop=True),
                                  reads=[FM_r, Mb_r], writes=[tCr], signal=(hp == 7))
                        fw.op("vector", lambda e, FM=FM, TM=TM, V=V, rows=rows, zs_=zs_, z2=z2: e.tensor_tensor(out=zs_[rows, :], in0=tB[rows, :], in1=z2[rows, :], op=ALU.add), reads=[tBr, z2_r], writes=[zs_r])
                        if sstop < 4:
                            continue
                        for hp in range(8):
                            h = hf * 8 + hp
                            fw.op("tensor", lambda e, FM=FM, TM=TM, V=V, rows=rows, cols=cols, hp=hp, h=h, zs_=zs_: e.matmul(tB[rows, hp * 64:(hp + 1) * 64], lhsT=Pm[rows, h, cols], rhs=zs_[rows, hp * 64:(hp + 1) * 64], start=True, stop=True),
                                  reads=pr_ + [zs_r], writes=[tBr], signal=(hp == 7))
                        fw.op("scalar", lambda e, FM=FM, TM=TM, V=V, rows=rows, us=us: e.copy(out=us[rows, :], in_=tB[rows, :]), reads=[tBr], writes=[us_r])
                        if sstop < 5:
                            continue
                        for hp in range(8):
                            h = hf * 8 + hp
                            fw.op("tensor", lambda e, FM=FM, TM=TM, V=V, rows=rows, ck=ck, hp=hp, h=h, us=us: e.matmul(tA[rows, hp * 64:(hp + 1) * 64], lhsT=G13[rows, h, 128 + ck * 64:128 + (ck + 1) * 64], rhs=us[rows, hp * 64:(hp + 1) * 64], start=True, stop=False),
                                  reads=g13r + [us_r], writes=[tAr], signal=False)
                            fw.op("tensor", lambda e, FM=FM, TM=TM, V=V, rows=rows, ck=ck, hp=hp, h=h, hr=hr: e.matmul(tA[rows, hp * 64:(hp + 1) * 64], lhsT=G24[rows, h, 128 + ck * 64:128 + (ck + 1) * 64], rhs=V[rows, hp, hr], start=False, stop=True),
                                  reads=g24r + [V_r], writes=[tAr], signal=(hp == 7))
                        if sstop < 6:
                            continue
                        for hp in range(8):
                            fw.op("tensor", lambda e, FM=FM, TM=TM, V=V, rows=rows, hp=hp, hr=hr, us=us: e.matmul(tD[hr, hp * 64:(hp + 1) * 64], lhsT=TM[rows, 0, hp, hr], rhs=us[rows, hp * 64:(hp + 1) * 64], start=True, stop=False),
                                  reads=[TM_r, us_r], writes=[tDr], signal=False)
                            fw.op("tensor", lambda e, FM=FM, TM=TM, V=V, rows=rows, hp=hp, hr=hr: e.matmul(tD[hr, hp * 64:(hp + 1) * 64], lhsT=TM[rows, 1, hp, hr], rhs=V[rows, hp, hr], start=False, stop=True),
                                  reads=[TM_r, V_r], writes=[tDr], signal=(hp == 7))
                        c = tile * 2 + ck
                        dbc = Dall[hr, :, c:c + 1].to_broadcast([64, 8, 64])
                        fw.op("vector", lambda e, FM=FM, TM=TM, V=V, M=M, dbc=dbc, hr=hr: e.tensor_tensor(out=M[hr, :, :], in0=M[hr, :, :], in1=dbc, op=ALU.mult), reads=[M_r, Dall_r], writes=[M_r])
                        fw.op("vector", lambda e, FM=FM, TM=TM, V=V, M=M, hr=hr: e.tensor_tensor(out=M[hr, :, :], in0=tD[hr, :].rearrange("p (h i) -> p h i", h=8), in1=M[hr, :, :], op=ALU.add), reads=[M_r, tDr], writes=[M_r])
                        fw.op("scalar", lambda e, FM=FM, TM=TM, V=V, M=M, Mb=Mb, hr=hr: e.copy(out=Mb[hr, :, :], in_=M[hr, :, :]), reads=[M_r], writes=[Mb_r])
                    if sstop < 7:
                        continue
                    fw.op("scalar", lambda e, FM=FM, TM=TM, V=V, ys=ys: e.copy(out=ys[:], in_=tC[:, :]), reads=[tCr], writes=[ys_r])
                    fw.op("vector", lambda e, FM=FM, TM=TM, V=V, ys=ys: e.tensor_tensor(out=ys[:], in0=tA[:, :], in1=ys[:], op=ALU.add), reads=[tAr, ys_r], writes=[ys_r])
                    ydst = self.yd[z][tile * 128:(tile + 1) * 128, :].rearrange("t (hp hh i) -> t hp hh i", hh=2, i=64)[:, :, hf, :]
                    fw.dma(ydst, ys[:].rearrange("p (hp i) -> p hp i", i=64), reads=[ys_r], sem_res=ys_r)

    def rwkv_O(self, i, j, R, ts):
        fw = self.fw
        I = self.I
        lnw, lnw_r = fw.sb("o_lnw", [128, D], F32, ts)
        lnb, lnb_r = fw.sb("o_lnb", [128, D], F32, ts)
        fw.dma(lnw[:], I["rw_ln_w"][j, :].partition_broadcast(128), writes=[lnw_r], sem_res=lnw_r)
        fw.dma(lnb[:], I["rw_ln_b"][j, :].partition_broadcast(128), writes=[lnb_r], sem_res=lnb_r)
        epsg, epsg_r = fw.sb("o_eps", [128, 1], F32, ts)
        fw.op("vector", lambda e: e.memset(epsg[:], 64e-5), writes=[epsg_r])
        y0b = [fw.sb("o_y0%d" % k, [128, D], F32, ts) for k in range(2)]
        y1b = [fw.sb("o_y1%d" % k, [128, D], F32, ts) for k in range(2)]
        sqb = [fw.sb("o_sq%d" % k, [128, D], F32, ts) for k in range(2)]
        stb = [fw.sb("o_st%d" % k, [128, 2, 16], F32, ts) for k in range(2)]
        bnb = [fw.sb("o_bn%d" % k, [128, 8, 128], F32, ts) for k in range(2)]
        zb = [fw.sb("o_z%d" % k, [128, 8, 128], BF16, ts) for k in range(2)]
        ob = [fw.sb("o_o%d" % k, [128, 8, 128], F32, ts) for k in range(2)]
        ogb = [fw.sb("o_og%d" % k, [128, 8, 128], BF16, ts) for k in range(2)]
        for tt in range(NTI):
            k = tt % 2
            (y0, y0_r), (y1, y1_r), (sq, sq_r), (st, st_r) = y0b[k], y1b[k], sqb[k], stb[k]
            (bn, bn_r), (zt, zt_r), (o_, o_r), (og_, og_r_) = bnb[k], zb[k], ob[k], ogb[k]
            rs = slice(tt * 128, (tt + 1) * 128)
            fw.dma(y0[:], self.yd[0][rs, :], writes=[y0_r], sem_res=y0_r)
            fw.dma(y1[:], self.yd[1][rs, :], writes=[y1_r], sem_res=y1_r)
            fw.dma(bn[:], self.bonusd[:, :, rs].rearrange("h p t -> p h t"), writes=[bn_r], sem_res=bn_r)
            fw.dma(zt[:], self.zsd[:, :, rs].rearrange("h p t -> p h t"), writes=[zt_r], sem_res=zt_r)
            fw.op("gpsimd", lambda e, y0=y0, y1=y1: e.tensor_tensor(out=y0[:], in0=y0[:], in1=y1[:], op=ALU.add), reads=[y0_r, y1_r], writes=[y0_r])
            y3 = y0[:].rearrange("p (h d) -> p h d", d=64)
            s3 = sq[:].rearrange("p (h d) -> p h d", d=64)
            fw.op("vector", lambda e, st=st, y3=y3: e.tensor_reduce(out=st[:, 0, :], in_=y3, axis=AX.X, op=ALU.add), reads=[y0_r], writes=[st_r])
            fw.op("vector", lambda e, st=st: e.tensor_scalar(out=st[:, 0, :], in0=st[:, 0, :], scalar1=1.0 / 64.0, scalar2=None, op0=ALU.mult), reads=[st_r], writes=[st_r])
            mbc = st[:, 0, :].unsqueeze(2).to_broadcast([128, 16, 64])
            fw.op("vector", lambda e, y3=y3, mbc=mbc: e.tensor_tensor(out=y3, in0=y3, in1=mbc, op=ALU.subtract), reads=[y0_r, st_r], writes=[y0_r])
            fw.op("gpsimd", lambda e, sq=sq, y0=y0: e.tensor_tensor(out=sq[:], in0=y0[:], in1=y0[:], op=ALU.mult), reads=[y0_r], writes=[sq_r])
            fw.op("vector", lambda e, st=st, s3=s3: e.tensor_reduce(out=st[:, 1, :], in_=s3, axis=AX.X, op=ALU.add), reads=[sq_r], writes=[st_r])
            fw.op("scalar", lambda e, st=st: e.activation(out=st[:, 1, :], in_=st[:, 1, :], func=AF.Sqrt, bias=epsg[:], scale=1.0 / 64.0), reads=[st_r, epsg_r], writes=[st_r])
            fw.op("vector", lambda e, st=st: e.reciprocal(out=st[:, 1, :], in_=st[:, 1, :]), reads=[st_r], writes=[st_r])
            rbc = st[:, 1, :].unsqueeze(2).to_broadcast([128, 16, 64])
            fw.op("vector", lambda e, y3=y3, rbc=rbc: e.tensor_tensor(out=y3, in0=y3, in1=rbc, op=ALU.mult), reads=[y0_r, st_r], writes=[y0_r])
            fw.op("gpsimd", lambda e, y0=y0: e.tensor_tensor(out=y0[:], in0=y0[:], in1=lnw[:], op=ALU.mult), reads=[y0_r, lnw_r], writes=[y0_r])
            fw.op("gpsimd", lambda e, y0=y0: e.tensor_tensor(out=y0[:], in0=y0[:], in1=lnb[:], op=ALU.add), reads=[y0_r, lnb_r], writes=[y0_r])
            for half in range(2):
                tp, tpr = self.pb[(tt * 2 + half) % 8]
                for q in range(4):
                    hc = half * 4 + q
                    fw.op("tensor", lambda e, tp=tp, q=q, hc=hc, y0=y0: e.transpose(tp[:, q * 128:(q + 1) * 128], y0[:, hc * 128:(hc + 1) * 128], self.ident[:]), reads=[y0_r, self.ident_r], writes=[tpr], signal=(q == 3))
                fw.op("vector", lambda e, tp=tp, half=half, o_=o_, bn=bn: e.tensor_tensor(out=o_[:, half * 4:half * 4 + 4, :], in0=tp[:, :].rearrange("p (h t) -> p h t", h=4), in1=bn[:, half * 4:half * 4 + 4, :], op=ALU.add), reads=[tpr, bn_r], writes=[o_r])
            fw.op("gpsimd", lambda e, og_=og_, o_=o_, zt=zt: e.tensor_tensor(out=og_[:], in0=o_[:], in1=zt[:], op=ALU.mult), reads=[o_r, zt_r], writes=[og_r_])
            fw.dma(self.og[:, :, rs].rearrange("h p t -> p h t"), og_[:], reads=[og_r_], sem_res=og_r_)


_CACHE = {}


def get_prog(n_layers=DEPTH, dbg=False):
    key = (n_layers, dbg)
    if key not in _CACHE:
        p = Prog(n_layers, dbg)
        p.build()
        _CACHE[key] = p
    return _CACHE[key]


def make_in_maps(inputs, consts, cores):
    maps = []
    for b in cores:
        m = {}
        for k, shp in INPUT_SHAPES.items():
            a = np.asarray(inputs[k])
            if k in ("x", "ctx"):
                a = a[b]
            elif k == "c":
                a = a[b:b + 1]
            a = np.ascontiguousarray(a, dtype=np.float32).reshape(shp)
            m[k] = a
        for k, v in consts.items():
            m["k_" + k] = v
        maps.append(m)
    return maps


def kernel(**inputs):
    p = get_prog()
    cores = list(range(8))
    in_maps = make_in_maps(inputs, p.consts, cores)
    res = run_bass_kernel_spmd(p.nc, in_maps, core_ids=cores)
    out = np.stack([np.asarray(r["out"]) for r in res.results], axis=0)
    return out.astype(np.float32)
```

```python
import math
from contextlib import ExitStack

import numpy as np
import concourse.bass as bass
import concourse.mybir as mybir
from concourse.bass_utils import run_bass_kernel_spmd

F32 = mybir.dt.float32
BF16 = mybir.dt.bfloat16
AF = mybir.ActivationFunctionType
ALU = mybir.AluOpType
AX = mybir.AxisListType

D = 1024
NL = 4096
NCX = 256
NT = NL + NCX
NTI = NT // 128
DEPTH = 4
NORM_EPS = 1e-6
SUBLN_EPS = 1e-5
GRID_W = 64


class Sem:
    def __init__(self, fw, name):
        self.name = name
        self.handle = fw.stack.enter_context(fw.nc.semaphore(name))
        self.issued = 0
        fw.sems.append(self)


class Res:
    __slots__ = ("name", "writers", "readers", "dsem")

    def __init__(self, name):
        self.name = name
        self.writers = []
        self.readers = []
        self.dsem = None


class Eng:
    def __init__(self, fw, key):
        self.key = key
        self.sem = Sem(fw, "e_" + key)
        self.ops = []
        self.waited = {}
        self.pend_r = []
        self.pend_w = []


class FW:
    def __init__(self, nc, stack):
        self.nc = nc
        self.stack = stack
        self.sems = []
        self.eng = {k: Eng(self, k) for k in ("tensor", "vector", "scalar", "gpsimd", "sync")}
        self.n_ins = 0
        self.free_sems = []

    def sb(self, name, shape, dtype, stack=None):
        self.uid = getattr(self, "uid", 0) + 1
        name = "%s_u%d" % (name, self.uid)
        t = (stack or self.stack).enter_context(self.nc.sbuf_tensor(name, list(shape), dtype))
        return t, self.res(name, stack)

    def res(self, name, stack=None):
        r = Res(name)
        if stack is not None:
            stack.callback(self._release, r)
        return r

    def _release(self, r):
        if r.dsem is not None:
            self.free_sems.append(r.dsem)
            r.dsem = None

    def _get_dsem(self, name):
        if self.free_sems:
            sem = self.free_sems.pop(0)
            for e in self.eng.values():
                assert e.waited.get(sem, 0) >= sem.issued, "semaphore reused before a barrier"
            return sem
        return Sem(self, "d%d" % len(self.sems))

    def ps(self, name, shape, dtype, stack=None):
        t = (stack or self.stack).enter_context(self.nc.psum_tensor(name, list(shape), dtype))
        return t, Res(name)

    def _waits_for(self, eng, reads, writes):
        need = {}

        def add(tok):
            sem, val = tok
            if val is None:
                val = sem.issued
            if need.get(sem, 0) < val:
                need[sem] = val
        for r in reads:
            for t in r.writers:
                add(t)
        for w in writes:
            for t in w.writers:
                add(t)
            for t in w.readers:
                add(t)
        out = []
        for sem, val in need.items():
            if eng.waited.get(sem, 0) >= val:
                continue
            eng.waited[sem] = val
            out.append((sem.handle, val))
        return out

    def _check_pending(self, eng, reads, writes):
        for e in self.eng.values():
            if e is eng or (not e.pend_w and not e.pend_r):
                continue
            for r in reads:
                assert r not in e.pend_w, ("unsignaled write pending", r.name, e.key)
            for w in writes:
                assert w not in e.pend_w and w not in e.pend_r, ("unsignaled access pending", w.name, e.key)

    def op(self, ek, fn, reads=(), writes=(), signal=True):
        eng = self.eng[ek]
        self._check_pending(eng, reads, writes)
        waits = self._waits_for(eng, reads, writes)
        self.n_ins += 1
        if signal:
            eng.sem.issued += 1
            tok = (eng.sem, eng.sem.issued)
            semh = eng.sem.handle

            def run(e, fn=fn, waits=waits, semh=semh):
                for (h, v) in waits:
                    e.wait_ge(h, v)
                fn(e).then_inc(semh, 1)
            rs = list(reads) + eng.pend_r
            ws = list(writes) + eng.pend_w
            eng.pend_r = []
            eng.pend_w = []
            for r in rs:
                r.readers.append(tok)
            for w in ws:
                w.writers = [tok]
                w.readers = []
        else:
            def run(e, fn=fn, waits=waits):
                for (h, v) in waits:
                    e.wait_ge(h, v)
                fn(e)
            eng.pend_r.extend(reads)
            eng.pend_w.extend(writes)
        eng.ops.append(run)

    def dma(self, out, in_, reads=(), writes=(), sem_res=None, ek="sync", **kw):
        eng = self.eng[ek]
        self._check_pending(eng, reads, writes)
        waits = self._waits_for(eng, reads, writes)
        if sem_res.dsem is None:
            sem_res.dsem = self._get_dsem(sem_res.name)
        sem = sem_res.dsem
        sem.issued += 16
        tok = (sem, None)
        semh = sem.handle
        self.n_ins += 1

        def run(e, waits=waits, semh=semh, out=out, in_=in_, kw=kw):
            for (h, v) in waits:
                e.wait_ge(h, v)
            e.dma_start(out=out, in_=in_, **kw).then_inc(semh, 16)
        eng.ops.append(run)
        for r in reads:
            r.readers.append(tok)
        for w in writes:
            w.writers = [tok]
            w.readers = []

    def barrier(self):
        for e in self.eng.values():
            assert not e.pend_r and not e.pend_w
        for e in self.eng.values():
            waits = []
            for s in self.sems:
                if s.issued > 0 and e.waited.get(s, 0) < s.issued:
                    e.waited[s] = s.issued
                    waits.append((s.handle, s.issued))

            def run(en, waits=waits):
                for (h, v) in waits:
                    en.wait_ge(h, v)
            e.ops.append(run)

    def finish(self):
        self.barrier()
        with self.nc.Block() as block:
            for key in ("sync", "tensor", "vector", "scalar", "gpsimd"):
                e = self.eng[key]

                def body(h, e=e):
                    for o in e.ops:
                        o(h)
                getattr(block, key)(body)


def rope_tables():
    n = np.arange(NL)
    row = (n // GRID_W).astype(np.float32)
    col = (n % GRID_W).astype(np.float32)
    inv = (10000.0 ** (-np.arange(0, 32, 2, dtype=np.float32) / 32.0)).astype(np.float32)
    C = np.ones((128, NT), np.float32)
    S = np.zeros((128, NT), np.float32)
    for f in range(128):
        axis = (f % 64) // 32
        j = f % 16
        pos = row if axis == 0 else col
        ang = (pos * inv[j]).astype(np.float32)
        C[f, NCX:] = np.cos(ang)
        S[f, NCX:] = np.sin(ang)
    return C, S


def host_consts():
    cs = {}
    cs["ident"] = np.eye(128, dtype=np.float32)
    C, S = rope_tables()
    cs["ropeC"] = C
    cs["ropeS"] = S
    import ml_dtypes
    bf = ml_dtypes.bfloat16
    for name, L in (("L", NL), ("X", NCX)):
        l = np.arange(L, dtype=np.int64)
        ang = (2.0 * np.pi / L) * ((l[:, None] * l[None, :]) % L).astype(np.float64)
        sc = 1.0 / math.sqrt(L)
        cs["dftC" + name] = (np.cos(ang) * sc).astype(np.float32).astype(bf)
        cs["dftS" + name] = (np.sin(ang) * sc).astype(np.float32).astype(bf)
    c = np.arange(128, dtype=np.int64)
    ang = (2.0 * np.pi / 128) * ((c[:, None] * c[None, :]) % 128).astype(np.float64)
    sc = 1.0 / math.sqrt(128.0)
    cs["dftCc"] = (np.cos(ang) * sc).astype(np.float32).astype(bf)
    cs["dftnSc"] = (-np.sin(ang) * sc).astype(np.float32).astype(bf)
    a = np.arange(128)
    same = (a[:, None] // 64) == (a[None, :] // 64)
    lt = (a[:, None] < a[None, :]) & same
    le = (a[:, None] <= a[None, :]) & same
    gt = (a[:, None] > a[None, :]) & same
    ge = (a[:, None] >= a[None, :]) & same
    cs["mN0"] = np.concatenate([lt, le], axis=1).astype(np.float32)
    cs["mL0"] = gt.astype(np.float32)
    cs["mN1"] = np.concatenate([gt, ge], axis=1).astype(np.float32)
    cs["mL1"] = lt.astype(np.float32)
    return cs


INPUT_SHAPES = {
    "x": [NL, D], "c": [1, D], "ctx": [NCX, D], "c_ctx": [1, D], "norm_gain": [4, D], "ada_w": [4, D, 3 * D],
    "ada_b": [4, 3 * D], "final_gain": [1, D], "da_w_in": [2, D, 4 * D], "da_lam_q": [2, 128], "da_lam_k": [2, 128],
    "da_subln_gain": [2, 128], "da_w_out": [2, D, D], "fn_w_in": [1, D, 2 * D], "fn_w_group": [1, 8, 128, 128],
    "fn_w_out": [1, D, D], "rw_w_in": [1, D, 4352], "rw_mu": [1, 3328], "rw_w0": [1, 2, D], "rw_w_up": [1, 2, 64, D],
    "rw_a0": [1, 2, D], "rw_a_up": [1, 2, 64, D], "rw_k_k": [1, D], "rw_k_a": [1, D], "rw_r_k": [1, D],
    "rw_ln_w": [1, D], "rw_ln_b": [1, D], "rw_w_out": [1, D, D],
}


class Prog:
    def __init__(self, n_layers=DEPTH, dbg=False):
        self.n_layers = n_layers
        self.dbg = dbg
        self.nc = bass.Bass("TRN2", target_bir_lowering=False)
        self.consts = host_consts()

    def din(self, name, shape, dt=F32):
        return self.nc.dram_tensor(name, list(shape), dt, kind="ExternalInput").ap()

    def build(self):
        nc = self.nc
        self.I = {k: self.din(k, s) for k, s in INPUT_SHAPES.items()}
        self.Cn = {k: self.din("k_" + k, v.shape, F32 if v.dtype == np.float32 else BF16) for k, v in self.consts.items()}
        self.out = nc.dram_tensor("out", [NL, D], F32, kind="ExternalOutput").ap()
        if self.dbg:
            self.dbg_ctx = nc.dram_tensor("dbg_ctx", [NCX, D], F32, kind="ExternalOutput").ap()
        self.xres = nc.dram_tensor("xres", [NT, D], F32, kind="Internal").ap()
        self.og = nc.dram_tensor("og", [8, 128, NT], BF16, kind="Internal").ap()
        self.zsd = nc.dram_tensor("zsd", [8, 128, NT], BF16, kind="Internal").ap()
        self.zsd_r = [[Res("zsd%d_%d" % (h, c)) for c in range(9)] for h in range(8)]
        self.xres_r = [Res("xres%d" % t) for t in range(NTI)]
        self.og_r = [[Res("og%d_%d" % (h, c)) for c in range(9)] for h in range(8)]
        self.out_r = [Res("out%d" % t) for t in range(NTI)]
        with ExitStack() as st:
            self.st = st
            fw = self.fw = FW(nc, st)
            self.setup_persistent()
            for i in range(self.n_layers):
                self.layer(i)
            fw.finish()
        return nc

    @staticmethod
    def chunk(ci):
        if ci == 0:
            return 0, NCX
        return NCX + (ci - 1) * 512, 512

    def xsrc(self, i, tt):
        if i == 0:
            if tt < 2:
                return self.I["ctx"][tt * 128:(tt + 1) * 128, :], []
            return self.I["x"][(tt - 2) * 128:(tt - 1) * 128, :], []
        return self.xres[tt * 128:(tt + 1) * 128, :], [self.xres_r[tt]]

    def setup_persistent(self):
        fw, st = self.fw, self.st
        self.ident, self.ident_r = fw.sb("ident", [128, 128], F32)
        fw.dma(self.ident[:], self.Cn["ident"][:, :], writes=[self.ident_r], sem_res=self.ident_r)
        self.identb, self.identb_r = fw.sb("identb", [128, 128], BF16)
        fw.op("vector", lambda e: e.tensor_copy(out=self.identb[:], in_=self.ident[:]), reads=[self.ident_r], writes=[self.identb_r])
        self.ones_b, self.ones_b_r = fw.sb("ones_b", [128, 128], BF16)
        fw.op("vector", lambda e: e.memset(self.ones_b[:], 1.0), writes=[self.ones_b_r])
        self.mean_b, self.mean_b_r = fw.sb("mean_b", [128, 128], BF16)
        fw.op("vector", lambda e: e.memset(self.mean_b[:], 1.0 / 128.0), writes=[self.mean_b_r])
        self.epsn, self.epsn_r = fw.sb("epsn", [128, 1], F32)
        fw.op("vector", lambda e: e.memset(self.epsn[:], NORM_EPS), writes=[self.epsn_r])
        self.epss, self.epss_r = fw.sb("epss", [128, 1], F32)
        fw.op("vector", lambda e: e.memset(self.epss[:], SUBLN_EPS), writes=[self.epss_r])
        self.gsc, self.gsc_r = fw.sb("gsc", [128, 8, 2], F32)
        self.shf, self.shf_r = fw.sb("shf", [128, 8, 2], F32)
        self.gate, self.gate_r = fw.sb("gate", [128, 2, D], F32)
        self.s_fm, self.s_fm_r = fw.sb("s_fm", [128, 8, 2], F32)
        self.pb = []
        for b in range(8):
            t, r = fw.ps("pb%d" % b, [128, 512], F32)
            self.pb.append((t, r))
        with ExitStack() as ts:
            cc, cc_r = fw.sb("cc", [2, D], F32, ts)
            fw.dma(cc[0:1, :], self.I["c"][:, :], writes=[cc_r], sem_res=cc_r)
            fw.dma(cc[1:2, :], self.I["c_ctx"][:, :], writes=[cc_r], sem_res=cc_r)
            fw.op("scalar", lambda e: e.activation(out=cc[:], in_=cc[:], func=AF.Silu), reads=[cc_r], writes=[cc_r])
            pt, pr = self.pb[0]
            for kc in range(8):
                fw.op("tensor", lambda e, kc=kc: e.transpose(pt[:, kc * 2:kc * 2 + 2], cc[:, kc * 128:(kc + 1) * 128], self.ident[0:2, 0:2]),
                      reads=[cc_r, self.ident_r], writes=[pr], signal=(kc == 7))
            fw.op("vector", lambda e: e.tensor_copy(out=self.s_fm[:].rearrange("p k w -> p (k w)"), in_=pt[:, 0:16]), reads=[pr], writes=[self.s_fm_r])
            fw.barrier()

    def load_fm(self, dst, dst_r, src2d, n, ts):
        fw = self.fw
        tmp, tmp_r = fw.sb("lfm_tmp%d" % fw.n_ins, [n, 128], F32, ts)
        fw.dma(tmp[:], src2d, writes=[tmp_r], sem_res=tmp_r)
        pt, pr = self.pb[1]
        fw.op("tensor", lambda e: e.transpose(pt[:, 0:n], tmp[:], self.ident[0:n, 0:n]), reads=[tmp_r, self.ident_r], writes=[pr])
        fw.op("vector", lambda e: e.tensor_copy(out=dst, in_=pt[:, 0:n]), reads=[pr], writes=[dst_r])

    def phase_mod(self, i):
        fw = self.fw
        W = self.I["ada_w"][i]
        with ExitStack() as ts:
            self.s_rep, self.s_rep_r = fw.sb("s_rep", [128, 8, 2, 128], F32, ts)
            for kc in range(8):
                for w in range(2):
                    fw.op("gpsimd", lambda e, kc=kc, w=w: e.tensor_copy(out=self.s_rep[:, kc, w, :], in_=self.s_fm[:, kc, w:w + 1].to_broadcast([128, 128])),
                          reads=[self.s_fm_r], writes=[self.s_rep_r])
            bfm, bfm_r = fw.sb("bfm", [128, 16], F32, ts)
            gfm, gfm_r = fw.sb("gfm", [128, 8], F32, ts)
            self.load_fm(bfm[:], bfm_r, self.I["ada_b"][i, 0:2048].rearrange("(n p) -> n p", p=128), 16, ts)
            self.load_fm(gfm[:], gfm_r, self.I["norm_gain"][i, :].rearrange("(n p) -> n p", p=128), 8, ts)
            bg, bg_r = fw.sb("bg", [128, D], F32, ts)
            fw.dma(bg[:], self.I["ada_b"][i, 2048:3072].partition_broadcast(128), writes=[bg_r], sem_res=bg_r)
            wt = [fw.sb("adaw%d" % k, [128, 8, 512], F32, ts) for k in range(2)]
            pt, pr = self.pb[2]
            for cc in range(6):
                w_t, w_r = wt[cc % 2]
                for kc in range(8):
                    fw.dma(w_t[:, kc, :], W[kc * 128:(kc + 1) * 128, cc * 512:(cc + 1) * 512], writes=[w_r], sem_res=w_r)
                if cc < 4:
                    for fl in range(4):
                        fc = cc * 4 + fl
                        for kc in range(8):
                            fw.op("tensor", lambda e, kc=kc, fl=fl, fc=fc, w_t=w_t: e.matmul(pt[:, fc * 2:fc * 2 + 2], lhsT=w_t[:, kc, fl * 128:(fl + 1) * 128], rhs=self.s_fm[:, kc, :], start=(kc == 0), stop=(kc == 7)),
                                  reads=[w_r, self.s_fm_r], writes=[pr], signal=(kc == 7 and fl == 3))
                    if cc == 3:
                        ps3 = pt[:, 0:32].rearrange("p (f w) -> p f w", w=2)
                        for w in range(2):
                            fw.op("vector", lambda e, w=w: e.tensor_tensor(out=self.shf[:, :, w], in0=ps3[:, 0:8, w], in1=bfm[:, 0:8], op=ALU.add),
                                  reads=[pr, bfm_r], writes=[self.shf_r])
                            fw.op("vector", lambda e, w=w: e.tensor_tensor(out=self.gsc[:, :, w], in0=ps3[:, 8:16, w], in1=bfm[:, 8:16], op=ALU.add),
                                  reads=[pr, bfm_r], writes=[self.gsc_r])
                            fw.op("vector", lambda e, w=w: e.scalar_tensor_tensor(out=self.gsc[:, :, w], in0=self.gsc[:, :, w], scalar=1.0, in1=gfm[:], op0=ALU.add, op1=ALU.mult),
                                  reads=[self.gsc_r, gfm_r], writes=[self.gsc_r])
                else:
                    cg = cc - 4
                    for w in range(2):
                        gp, gr = self.pb[3 + w]
                        for kc in range(8):
                            fw.op("tensor", lambda e, kc=kc, w=w, gp=gp, w_t=w_t: e.matmul(gp[:, :], lhsT=self.s_rep[:, kc, w, :], rhs=w_t[:, kc, :], start=(kc == 0), stop=(kc == 7)),
                                  reads=[w_r, self.s_rep_r], writes=[gr], signal=(kc == 7))
                        fw.op("vector", lambda e, w=w, gp=gp, cg=cg: e.tensor_tensor(out=self.gate[:, w, cg * 512:(cg + 1) * 512], in0=gp[:, :], in1=bg[:, cg * 512:(cg + 1) * 512], op=ALU.add),
                              reads=[gr, bg_r], writes=[self.gate_r])
            fw.barrier()

    def phase_norm(self, i, hT, hT_r, ts):
        fw = self.fw
        xt = [fw.sb("nx%d" % k, [128, D], F32, ts) for k in range(3)]
        sq, sq_r = fw.sb("nsq", [128, D], BF16, ts)
        ss = [fw.sb("nss%d" % k, [128, 1], F32, ts) for k in range(2)]
        dg = [fw.sb("ndg%d" % k, [128, 128], F32, ts) for k in range(2)]
        for tt in range(NTI):
            x_t, x_r = xt[tt % 3]
            s_t, s_r = ss[tt % 2]
            d_t, d_r = dg[tt % 2]
            w = 1 if tt < 2 else 0
            src, src_r = self.xsrc(i, tt)
            fw.dma(x_t[:], src, reads=src_r, writes=[x_r], sem_res=x_r)
            fw.op("scalar", lambda e, x_t=x_t, s_t=s_t: e.activation(out=sq[:], in_=x_t[:], func=AF.Square, accum_out=s_t[:]), reads=[x_r], writes=[sq_r, s_r])
            fw.op("scalar", lambda e, s_t=s_t: e.activation(out=s_t[:], in_=s_t[:], func=AF.Sqrt, bias=self.epsn[:], scale=1.0 / D), reads=[s_r, self.epsn_r], writes=[s_r])
            fw.op("vector", lambda e, s_t=s_t: e.reciprocal(out=s_t[:], in_=s_t[:]), reads=[s_r], writes=[s_r])
            fw.op("vector", lambda e, s_t=s_t, d_t=d_t: e.tensor_scalar(out=d_t[:], in0=self.ident[:], scalar1=s_t[:, 0:1], scalar2=None, op0=ALU.mult), reads=[s_r, self.ident_r], writes=[d_r])
            for half in range(2):
                pt, pr = self.pb[(tt * 2 + half) % 4]
                for k4 in range(4):
                    kc = half * 4 + k4
                    fw.op("tensor", lambda e, kc=kc, k4=k4, pt=pt, x_t=x_t, d_t=d_t: e.matmul(pt[:, k4 * 128:(k4 + 1) * 128], lhsT=x_t[:, kc * 128:(kc + 1) * 128], rhs=d_t[:], start=True, stop=True),
                          reads=[x_r, d_r], writes=[pr], signal=(k4 == 3))
                for k4 in range(4):
                    kc = half * 4 + k4
                    ek = "scalar" if k4 % 2 == 0 else "vector"
                    if ek == "scalar":
                        fw.op("scalar", lambda e, kc=kc, k4=k4, pt=pt, tt=tt, w=w: e.activation(out=hT[:, kc, tt * 128:(tt + 1) * 128], in_=pt[:, k4 * 128:(k4 + 1) * 128], func=AF.Identity, bias=self.shf[:, kc, w:w + 1], scale=self.gsc[:, kc, w:w + 1]),
                              reads=[pr, self.shf_r, self.gsc_r], writes=[hT_r[tt]])
                    else:
                        fw.op("vector", lambda e, kc=kc, k4=k4, pt=pt, tt=tt, w=w: e.tensor_scalar(out=hT[:, kc, tt * 128:(tt + 1) * 128], in0=pt[:, k4 * 128:(k4 + 1) * 128], scalar1=self.gsc[:, kc, w:w + 1], scalar2=self.shf[:, kc, w:w + 1], op0=ALU.mult, op1=ALU.add),
                              reads=[pr, self.shf_r, self.gsc_r], writes=[hT_r[tt]])

    def phase_out(self, i, w_out, ts):
        fw = self.fw
        last = (i == DEPTH - 1)
        wst = [fw.sb("wo_st%d" % k, [128, D], F32, ts) for k in range(2)]
        wb, wb_r = fw.sb("wo_b", [128, 8, D], BF16, ts)
        for kc in range(8):
            s_t, s_r = wst[kc % 2]
            fw.dma(s_t[:], w_out[kc * 128:(kc + 1) * 128, :], writes=[s_r], sem_res=s_r)
            fw.op("gpsimd", lambda e, kc=kc, s_t=s_t: e.tensor_copy(out=wb[:, kc, :], in_=s_t[:]), reads=[s_r], writes=[wb_r])
        ogt = [fw.sb("po_og%d" % k, [128, 8, 512], BF16, ts) for k in range(2)]
        xt = [fw.sb("po_x%d" % k, [128, D], F32, ts) for k in range(2)]
        yt = [fw.sb("po_y%d" % k, [128, D], F32, ts) for k in range(2)]
        if last:
            fg, fg_r = fw.sb("po_fg", [128, D], F32, ts)
            fw.dma(fg[:], self.I["final_gain"][0, :].partition_broadcast(128), writes=[fg_r], sem_res=fg_r)
            sq, sq_r = fw.sb("po_sq", [128, D], BF16, ts)
            ss = [fw.sb("po_ss%d" % k, [128, 1], F32, ts) for k in range(2)]
        cis = range(1, 9) if last else range(9)
        n = 0
        for ci in cis:
            t0, tn = self.chunk(ci)
            o_t, o_r = ogt[ci % 2]
            for hc in range(8):
                fw.dma(o_t[:, hc, 0:tn], self.og[hc, :, t0:t0 + tn], reads=[self.og_r[hc][ci]], writes=[o_r], sem_res=o_r)
            for tl in range(tn // 128):
                tt = t0 // 128 + tl
                w = 1 if tt < 2 else 0
                x_t, x_r = xt[n % 2]
                y_t, y_r = yt[n % 2]
                src, src_r = self.xsrc(i, tt)
                fw.dma(x_t[:], src, reads=src_r, writes=[x_r], sem_res=x_r)
                for half in range(2):
                    pt, pr = self.pb[(n * 2 + half) % 4]
                    for hc in range(8):
                        fw.op("tensor", lambda e, hc=hc, half=half, pt=pt, o_t=o_t, tl=tl: e.matmul(pt[:, :], lhsT=o_t[:, hc, tl * 128:(tl + 1) * 128], rhs=wb[:, hc, half * 512:(half + 1) * 512], start=(hc == 0), stop=(hc == 7)),
                              reads=[o_r, wb_r], writes=[pr], signal=(hc == 7))
                    fw.op("vector", lambda e, half=half, pt=pt, y_t=y_t, w=w: e.tensor_tensor(out=y_t[:, half * 512:(half + 1) * 512], in0=pt[:, :], in1=self.gate[:, w, half * 512:(half + 1) * 512], op=ALU.mult),
                          reads=[pr, self.gate_r], writes=[y_r])
                fw.op("gpsimd", lambda e, y_t=y_t, x_t=x_t: e.tensor_tensor(out=y_t[:], in0=y_t[:], in1=x_t[:], op=ALU.add), reads=[y_r, x_r], writes=[y_r])
                if not last:
                    fw.dma(self.xres[tt * 128:(tt + 1) * 128, :], y_t[:], reads=[y_r], writes=[self.xres_r[tt]], sem_res=y_r)
                    if self.dbg and i == self.n_layers - 1:
                        if tt < 2:
                            fw.dma(self.dbg_ctx[tt * 128:(tt + 1) * 128, :], y_t[:], reads=[y_r], writes=[self.out_r[tt]], sem_res=y_r)
                        else:
                            fw.dma(self.out[(tt - 2) * 128:(tt - 1) * 128, :], y_t[:], reads=[y_r], writes=[self.out_r[tt]], sem_res=y_r)
                else:
                    s_t, s_r = ss[n % 2]
                    fw.op("scalar", lambda e, y_t=y_t, s_t=s_t: e.activation(out=sq[:], in_=y_t[:], func=AF.Square, accum_out=s_t[:]), reads=[y_r], writes=[sq_r, s_r])
                    fw.op("scalar", lambda e, s_t=s_t: e.activation(out=s_t[:], in_=s_t[:], func=AF.Sqrt, bias=self.epsn[:], scale=1.0 / D), reads=[s_r, self.epsn_r], writes=[s_r])
                    fw.op("vector", lambda e, s_t=s_t: e.reciprocal(out=s_t[:], in_=s_t[:]), reads=[s_r], writes=[s_r])
                    fw.op("vector", lambda e, y_t=y_t, s_t=s_t: e.scalar_tensor_tensor(out=y_t[:], in0=y_t[:], scalar=s_t[:, 0:1], in1=fg[:], op0=ALU.mult, op1=ALU.mult), reads=[y_r, s_r, fg_r], writes=[y_r])
                    fw.dma(self.out[(tt - 2) * 128:(tt - 1) * 128, :], y_t[:], reads=[y_r], writes=[self.out_r[tt]], sem_res=y_r)
                n += 1

    def layer(self, i):
        fw = self.fw
        kind = i % 3
        j = i // 3
        self.phase_mod(i)
        with ExitStack() as ts0:
            pre = None
            if kind == 1:
                pre = self.fnet_prealloc(ts0)
            elif kind == 2:
                pre = self.rwkv_prealloc(ts0)
            with ExitStack() as ts:
                hT, _ = fw.sb("hT", [128, 8, NT], BF16, ts)
                hT_r = [Res("hT%d" % t) for t in range(NTI)]
                with ExitStack() as ts2:
                    self.phase_norm(i, hT, hT_r, ts2)
                    fw.barrier()
                with ExitStack() as ts2:
                    if kind == 0:
                        self.mixer_attn(i, j, hT, hT_r, ts2)
                    elif kind == 1:
                        self.fnet_part1(i, j, hT, hT_r, pre, ts2)
                    else:
                        self.rwkv_part1(i, j, hT, hT_r, pre, ts2)
                    fw.barrier()
            if kind == 1:
                with ExitStack() as ts2:
                    self.fnet_part2(i, j, pre, ts2)
                    fw.barrier()
            elif kind == 2:
                self.rwkv_part2(i, j, pre, ts0)
        with ExitStack() as ts:
            w_out = {0: self.I["da_w_out"], 1: self.I["fn_w_out"], 2: self.I["rw_w_out"]}[kind][j]
            self.phase_out(i, w_out, ts)
            fw.barrier()

    def mixer_attn(self, i, j, hT, hT_r, ts):
        fw = self.fw
        last = (i == DEPTH - 1)
        lambda_init = 0.8 - 0.6 * math.exp(-0.3 * i)
        Win = self.I["da_w_in"][j]
        rC, rC_r = fw.sb("ropeC", [128, NT], F32, ts)
        rS, rS_r = fw.sb("ropeS", [128, NT], F32, ts)
        fw.dma(rC[:], self.Cn["ropeC"][:, :], writes=[rC_r], sem_res=rC_r)
        fw.dma(rS[:], self.Cn["ropeS"][:, :], writes=[rS_r], sem_res=rS_r)
        nlam, nlam_r = fw.sb("nlam", [128, 1], F32, ts)
        lq, lq_r = fw.sb("lq", [128, 128], F32, ts)
        lk, lk_r = fw.sb("lk", [128, 128], F32, ts)
        l2, l2_r = fw.sb("l2", [128, 2], F32, ts)
        fw.dma(lq[:], self.I["da_lam_q"][j, :].partition_broadcast(128), writes=[lq_r], sem_res=lq_r)
        fw.dma(lk[:], self.I["da_lam_k"][j, :].partition_broadcast(128), writes=[lk_r], sem_res=lk_r)
        fw.op("vector", lambda e: e.tensor_tensor(out=lq[:], in0=lq[:], in1=lk[:], op=ALU.mult), reads=[lq_r, lk_r], writes=[lq_r])
        fw.op("vector", lambda e: e.tensor_reduce(out=l2[:], in_=lq[:].rearrange("p (z d) -> p z d", z=2), axis=AX.X, op=ALU.add), reads=[lq_r], writes=[l2_r])
        fw.op("scalar", lambda e: e.activation(out=l2[:], in_=l2[:], func=AF.Exp), reads=[l2_r], writes=[l2_r])
        fw.op("vector", lambda e: e.tensor_tensor(out=nlam[:], in0=l2[:, 1:2], in1=l2[:, 0:1], op=ALU.subtract), reads=[l2_r], writes=[nlam_r])
        fw.op("vector", lambda e: e.tensor_scalar(out=nlam[:], in0=nlam[:], scalar1=-lambda_init, scalar2=None, op0=ALU.add), reads=[nlam_r], writes=[nlam_r])
        sg, sg_r = fw.sb("sg", [128, 1], F32, ts)
        self.load_fm(sg[:], sg_r, self.I["da_subln_gain"][j:j + 1, :], 1, ts)
        fw.op("vector", lambda e: e.tensor_scalar(out=sg[:], in0=sg[:], scalar1=1.0 - lambda_init, scalar2=None, op0=ALU.mult), reads=[sg_r], writes=[sg_r])

        wst = [fw.sb("aw_st%d" % k, [128, 8, 128], F32, ts) for k in range(2)]
        WS = [{nm: fw.sb("%s_%d" % (nm, k), [128, 8, 128], BF16, ts) for nm in ("wq", "wq2", "wk", "wk2", "wv", "wz")} for k in range(2)]
        qT, qT_r = fw.sb("qT", [128, NT], BF16, ts)
        kT, kT_r = None, None
        kTz = [fw.sb("kTz%d" % z, [128, NT], BF16, ts) for z in range(2)]
        for z in range(2):
            fw.op("gpsimd", lambda e, z=z: e.memset(kTz[z][0][:], 0.0), writes=[kTz[z][1]])
        zs, zs_r = fw.sb("zs", [128, NT], BF16, ts)
        vt, vt_r = fw.sb("vt", [128, NTI, 128], BF16, ts)
        t1 = [fw.sb("rp1_%d" % k, [128, 512], F32, ts) for k in range(1)] * 2
        t2 = [fw.sb("rp2_%d" % k, [128, 512], F32, ts) for k in range(1)] * 2
        Eb = [fw.sb("E%d" % k, [128, 512], BF16, ts) for k in range(4)]
        r1, r1_r = fw.sb("ep_r1", [128, 512], F32, ts)
        a1, a1_r = fw.sb("ep_a1", [128, 512], F32, ts)
        r2, r2_r = r1, r1_r
        a2, a2_r = fw.sb("ep_a2", [128, 512], F32, ts)
        sqb, sqb_r = fw.sb("ep_sq", [128, 512], BF16, ts)
        rs, rs_r = r1, r1_r
        ogt = [fw.sb("ep_og%d" % k, [128, 512], BF16, ts) for k in range(2)]

        def rot_cast(dst, dst2, dst_r, dst2_r, s_t, s_r, scale):
            fw.op("gpsimd", lambda e: e.tensor_scalar(out=dst[:], in0=s_t[:], scalar1=scale, scalar2=None, op0=ALU.mult), reads=[s_r], writes=[dst_r])
            sv = s_t[:].rearrange("p k (b h j) -> p k b h j", b=4, h=2)
            dv = dst2[:].rearrange("p k (b h j) -> p k b h j", b=4, h=2)
            for kc in range(0, 8, 4):
                fw.op("gpsimd", lambda e, kc=kc: e.tensor_scalar(out=dv[:, kc:kc + 4, :, 0, :], in0=sv[:, kc:kc + 4, :, 1, :], scalar1=-scale, scalar2=None, op0=ALU.mult), reads=[s_r], writes=[dst2_r])
                fw.op("gpsimd", lambda e, kc=kc: e.tensor_scalar(out=dv[:, kc:kc + 4, :, 1, :], in0=sv[:, kc:kc + 4, :, 0, :], scalar1=scale, scalar2=None, op0=ALU.mult), reads=[s_r], writes=[dst2_r])

        nst = [0]

        def load_weights(hd):
            Wd = WS[hd % 2]
            for which in range(4):
                s_t, s_r = wst[nst[0] % 2]
                nst[0] += 1
                col0 = which * D + hd * 128
                fw.dma(s_t[:], Win[:, col0:col0 + 128].rearrange("(k p) c -> p k c", p=128), writes=[s_r], sem_res=s_r)
                if which == 0:
                    rot_cast(Wd["wq"][0], Wd["wq2"][0], Wd["wq"][1], Wd["wq2"][1], s_t, s_r, 0.125)
                elif which == 1:
                    rot_cast(Wd["wk"][0], Wd["wk2"][0], Wd["wk"][1], Wd["wk2"][1], s_t, s_r, 1.0)
                elif which == 2:
                    fw.op("gpsimd", lambda e, s_t=s_t, d=Wd["wv"][0]: e.tensor_copy(out=d[:], in_=s_t[:]), reads=[s_r], writes=[Wd["wv"][1]])
                else:
                    fw.op("gpsimd", lambda e, s_t=s_t, d=Wd["wz"][0]: e.tensor_copy(out=d[:], in_=s_t[:]), reads=[s_r], writes=[Wd["wz"][1]])

        ncnt = 0
        load_weights(0)
        for hd in range(8):
            Wd = WS[hd % 2]
            (wq, wq_r), (wq2, wq2_r), (wk, wk_r), (wk2, wk2_r), (wv, wv_r), (wz, wz_r) = [Wd[nm] for nm in ("wq", "wq2", "wk", "wk2", "wv", "wz")]
            for ci in range(9):
                t0, tn = self.chunk(ci)
                tts = list(range(t0 // 128, (t0 + tn) // 128))
                hrs = [hT_r[t] for t in tts]
                banks = [self.pb[(ci * 6 + b) % 8] for b in range(6)]
                for b, (wt_, wr_) in enumerate([(wq, wq_r), (wq2, wq2_r), (wk, wk_r), (wk2, wk2_r), (wz, wz_r)]):
                    pt, pr = banks[b]
                    for kc in range(8):
                        fw.op("tensor", lambda e, kc=kc, pt=pt, wt_=wt_, t0=t0, tn=tn: e.matmul(pt[:, 0:tn], lhsT=wt_[:, kc, :], rhs=hT[:, kc, t0:t0 + tn], start=(kc == 0), stop=(kc == 7)),
                              reads=[wr_] + hrs, writes=[pr], signal=(kc == 7))
                pt, pr = banks[5]
                for tl, tt in enumerate(tts):
                    for kc in range(8):
                        fw.op("tensor", lambda e, kc=kc, pt=pt, tl=tl, tt=tt, wv=wv: e.matmul(pt[:, tl * 128:(tl + 1) * 128], lhsT=hT[:, kc, tt * 128:(tt + 1) * 128], rhs=wv[:, kc, :], start=(kc == 0), stop=(kc == 7)),
                              reads=[wv_r, hT_r[tt]], writes=[pr], signal=(kc == 7 and tl == len(tts) - 1))
                for b0, dst, dst_r in ((0, qT, qT_r), (2, kT, kT_r)):
                    p1, p1r = banks[b0]
                    p2, p2r = banks[b0 + 1]
                    ta, ta_r = t1[ncnt % 2]
                    tb, tb_r = t2[ncnt % 2]
                    ncnt += 1
                    fw.op("vector", lambda e, p1=p1, ta=ta, t0=t0, tn=tn: e.tensor_tensor(out=ta[:, 0:tn], in0=p1[:, 0:tn], in1=rC[:, t0:t0 + tn], op=ALU.mult), reads=[p1r, rC_r], writes=[ta_r])
                    fw.op("vector", lambda e, p2=p2, tb=tb, t0=t0, tn=tn: e.tensor_tensor(out=tb[:, 0:tn], in0=p2[:, 0:tn], in1=rS[:, t0:t0 + tn], op=ALU.mult), reads=[p2r, rS_r], writes=[tb_r])
                    if b0 == 0:
                        fw.op("gpsimd", lambda e, ta=ta, tb=tb, dst=dst, t0=t0, tn=tn: e.tensor_tensor(out=dst[:, t0:t0 + tn], in0=ta[:, 0:tn], in1=tb[:, 0:tn], op=ALU.add), reads=[ta_r, tb_r], writes=[dst_r])
                    else:
                        for z in range(2):
                            zr = slice(z * 64, (z + 1) * 64)
                            fw.op("gpsimd", lambda e, ta=ta, tb=tb, z=z, zr=zr, t0=t0, tn=tn: e.tensor_tensor(out=kTz[z][0][zr, t0:t0 + tn], in0=ta[zr, 0:tn], in1=tb[zr, 0:tn], op=ALU.add), reads=[ta_r, tb_r], writes=[kTz[z][1]])
                p4, p4r = banks[4]
                fw.op("scalar", lambda e, p4=p4, t0=t0, tn=tn: e.activation(out=zs[:, t0:t0 + tn], in_=p4[:, 0:tn], func=AF.Silu), reads=[p4r], writes=[zs_r])
                p5, p5r = banks[5]
                fw.op("vector", lambda e, p5=p5, t0=t0, tn=tn: e.tensor_copy(out=vt[:, t0 // 128:(t0 + tn) // 128, :].rearrange("p t e -> p (t e)"), in_=p5[:, 0:tn]), reads=[p5r], writes=[vt_r])
            if hd + 1 < 8:
                load_weights(hd + 1)
            groups = ([] if last else [(0, list(range(2)))]) + [(ci, list(range(NTI))) for ci in range(1, 9)]
            items = []
            for gi, (ci, kts) in enumerate(groups):
                for z in range(2):
                    for idx, kt in enumerate(kts):
                        items.append((gi, ci, z, idx, kt, len(kts)))
            LA = 2
            Ebuf = {}

            def emit_front(n):
                gi, ci, z, idx, kt, nk = items[n]
                q0, qn = self.chunk(ci)
                sp, spr = self.pb[n % 3]
                E_t, E_r = Eb[n % len(Eb)]
                Ebuf[n] = (E_t, E_r)
                fw.op("tensor", lambda e, sp=sp, kt=kt, z=z, q0=q0, qn=qn: e.matmul(sp[:, 0:qn], lhsT=kTz[z][0][:, kt * 128:(kt + 1) * 128], rhs=qT[:, q0:q0 + qn], start=True, stop=True),
                      reads=[kTz[z][1], qT_r], writes=[spr])
                fw.op("scalar", lambda e, sp=sp, E_t=E_t, qn=qn: e.activation(out=E_t[:, 0:qn], in_=sp[:, 0:qn], func=AF.Exp), reads=[spr], writes=[E_r])

            def emit_back(n):
                gi, ci, z, idx, kt, nk = items[n]
                q0, qn = self.chunk(ci)
                E_t, E_r = Ebuf.pop(n)
                po, por = self.pb[3 + z * 2]
                psm, psr = self.pb[4 + z * 2]
                fw.op("tensor", lambda e, po=po, E_t=E_t, kt=kt, qn=qn, idx=idx, nk=nk: e.matmul(po[:, 0:qn], lhsT=vt[:, kt, :], rhs=E_t[:, 0:qn], start=(idx == 0), stop=(idx == nk - 1)),
                      reads=[vt_r, E_r], writes=[por], signal=False)
                fw.op("tensor", lambda e, psm=psm, E_t=E_t, qn=qn, idx=idx, nk=nk: e.matmul(psm[:, 0:qn], lhsT=self.ones_b[:], rhs=E_t[:, 0:qn], start=(idx == 0), stop=(idx == nk - 1)),
                      reads=[self.ones_b_r, E_r], writes=[psr])
                if z == 1 and idx == nk - 1:
                    epilogue(ci)

            def epilogue(ci):
                q0, qn = self.chunk(ci)
                po1, por1 = self.pb[3]
                ps1, psr1 = self.pb[4]
                po2, por2 = self.pb[5]
                ps2, psr2 = self.pb[6]
                fw.op("vector", lambda e, qn=qn: e.reciprocal(out=r1[:, 0:qn], in_=ps1[:, 0:qn]), reads=[psr1], writes=[r1_r])
                fw.op("vector", lambda e, qn=qn: e.tensor_tensor(out=a1[:, 0:qn], in0=po1[:, 0:qn], in1=r1[:, 0:qn], op=ALU.mult), reads=[por1, r1_r], writes=[a1_r])
                fw.op("vector", lambda e, qn=qn: e.reciprocal(out=r2[:, 0:qn], in_=ps2[:, 0:qn]), reads=[psr2], writes=[r2_r])
                fw.op("vector", lambda e, qn=qn: e.tensor_tensor(out=a2[:, 0:qn], in0=po2[:, 0:qn], in1=r2[:, 0:qn], op=ALU.mult), reads=[por2, r2_r], writes=[a2_r])
                fw.op("vector", lambda e, qn=qn: e.scalar_tensor_tensor(out=a1[:, 0:qn], in0=a2[:, 0:qn], scalar=nlam[:, 0:1], in1=a1[:, 0:qn], op0=ALU.mult, op1=ALU.add), reads=[a1_r, a2_r, nlam_r], writes=[a1_r])
                fw.op("vector", lambda e, qn=qn: e.tensor_tensor(out=sqb[:, 0:qn], in0=a1[:, 0:qn], in1=a1[:, 0:qn], op=ALU.mult), reads=[a1_r], writes=[sqb_r])
                mp, mpr = self.pb[7]
                fw.op("tensor", lambda e, qn=qn: e.matmul(mp[:, 0:qn], lhsT=self.mean_b[:], rhs=sqb[:, 0:qn], start=True, stop=True), reads=[self.mean_b_r, sqb_r], writes=[mpr])
                fw.op("scalar", lambda e, qn=qn: e.activation(out=rs[:, 0:qn], in_=mp[:, 0:qn], func=AF.Ln, bias=self.epss[:], scale=1.0), reads=[mpr, self.epss_r], writes=[rs_r])
                fw.op("scalar", lambda e, qn=qn: e.activation(out=rs[:, 0:qn], in_=rs[:, 0:qn], func=AF.Exp, scale=-0.5), reads=[rs_r], writes=[rs_r])
                fw.op("vector", lambda e, qn=qn: e.tensor_tensor(out=a1[:, 0:qn], in0=a1[:, 0:qn], in1=rs[:, 0:qn], op=ALU.mult), reads=[a1_r, rs_r], writes=[a1_r])
                og_t, og_r_ = ogt[ci % 2]
                fw.op("vector", lambda e, og_t=og_t, q0=q0, qn=qn: e.scalar_tensor_tensor(out=og_t[:, 0:qn], in0=a1[:, 0:qn], scalar=sg[:, 0:1], in1=zs[:, q0:q0 + qn], op0=ALU.mult, op1=ALU.mult), reads=[a1_r, sg_r, zs_r], writes=[og_r_])
                fw.dma(self.og[hd, :, q0:q0 + qn], og_t[:, 0:qn], reads=[og_r_], writes=[self.og_r[hd][ci]], sem_res=og_r_)

            for n in range(len(items) + LA):
                if n < len(items):
                    emit_front(n)
                if n - LA >= 0:
                    emit_back(n - LA)

    def load_w_bf16(self, dst, dst_r, w2d, ncols, ts, name):
        fw = self.fw
        wst = [fw.sb("%s_st%d" % (name, k), [128, ncols], F32, ts) for k in range(2)]
        for kc in range(8):
            s_t, s_r = wst[kc % 2]
            fw.dma(s_t[:], w2d[kc * 128:(kc + 1) * 128, :], writes=[s_r], sem_res=s_r)
            fw.op("gpsimd", lambda e, kc=kc, s_t=s_t: e.tensor_copy(out=dst[:, kc, :], in_=s_t[:]), reads=[s_r], writes=[dst_r])

    def gate_proj(self, wz, wz_r, hT, hT_r, ts):
        fw = self.fw
        zt = [fw.sb("gz%d" % k, [128, 512], BF16, ts) for k in range(3)]
        n = 0
        for ci in range(9):
            t0, tn = self.chunk(ci)
            hrs = [hT_r[t] for t in range(t0 // 128, (t0 + tn) // 128)]
            for fc in range(8):
                pt, pr = self.pb[n % 4]
                z_t, z_r = zt[n % 3]
                n += 1
                for kc in range(8):
                    fw.op("tensor", lambda e, kc=kc, pt=pt, fc=fc, t0=t0, tn=tn: e.matmul(pt[:, 0:tn], lhsT=wz[:, kc, fc * 128:(fc + 1) * 128], rhs=hT[:, kc, t0:t0 + tn], start=(kc == 0), stop=(kc == 7)),
                          reads=[wz_r] + hrs, writes=[pr], signal=(kc == 7))
                fw.op("scalar", lambda e, pt=pt, z_t=z_t, tn=tn: e.activation(out=z_t[:, 0:tn], in_=pt[:, 0:tn], func=AF.Silu), reads=[pr], writes=[z_r])
                fw.dma(self.zsd[fc, :, t0:t0 + tn], z_t[:, 0:tn], reads=[z_r], writes=[self.zsd_r[fc][ci]], sem_res=z_r)

    def fnet_prealloc(self, ts):
        fw = self.fw
        Utm, _ = fw.sb("Utm", [128, NTI, D], BF16, ts)
        Utm_r = [Res("Utm%d" % t) for t in range(NTI)]
        return Utm, Utm_r

    def fnet_part1(self, i, j, hT, hT_r, pre, ts):
        fw = self.fw
        Utm, Utm_r = pre
        Win = self.I["fn_w_in"][j]
        wu, wu_r = fw.sb("fwu", [128, 8, D], BF16, ts)
        wz, wz_r = fw.sb("fwz", [128, 8, D], BF16, ts)
        self.load_w_bf16(wu, wu_r, Win[:, 0:D], D, ts, "fwu")
        self.load_w_bf16(wz, wz_r, Win[:, D:2 * D], D, ts, "fwz")
        n = 0
        for tt in range(NTI):
            for half in range(2):
                pt, pr = self.pb[4 + n % 4]
                for kc in range(8):
                    fw.op("tensor", lambda e, kc=kc, pt=pt, tt=tt, half=half: e.matmul(pt[:, :], lhsT=hT[:, kc, tt * 128:(tt + 1) * 128], rhs=wu[:, kc, half * 512:(half + 1) * 512], start=(kc == 0), stop=(kc == 7)),
                          reads=[wu_r, hT_r[tt]], writes=[pr], signal=(kc == 7))
                if n % 2 == 0:
                    fw.op("vector", lambda e, pt=pt, tt=tt, half=half: e.tensor_copy(out=Utm[:, tt, half * 512:(half + 1) * 512], in_=pt[:, :]), reads=[pr], writes=[Utm_r[tt]])
                else:
                    fw.op("scalar", lambda e, pt=pt, tt=tt, half=half: e.copy(out=Utm[:, tt, half * 512:(half + 1) * 512], in_=pt[:, :]), reads=[pr], writes=[Utm_r[tt]])
                n += 1
        self.gate_proj(wz, wz_r, hT, hT_r, ts)

    def fnet_part2(self, i, j, pre, ts):
        fw = self.fw
        Utm, Utm_r = pre
        cc_, cc_r = fw.sb("fCc", [128, 128], BF16, ts)
        sc_, sc_r = fw.sb("fSc", [128, 128], BF16, ts)
        fw.dma(cc_[:], self.Cn["dftCc"][:, :], writes=[cc_r], sem_res=cc_r)
        fw.dma(sc_[:], self.Cn["dftnSc"][:, :], writes=[sc_r], sem_res=sc_r)
        wgs, wgs_r = fw.sb("fwg_st", [128, 8, 128], F32, ts)
        wg, wg_r = fw.sb("fwg", [128, 8, 128], BF16, ts)
        fw.dma(wgs[:], self.I["fn_w_group"][j].rearrange("g c e -> c g e"), writes=[wgs_r], sem_res=wgs_r)
        fw.op("gpsimd", lambda e: e.tensor_copy(out=wg[:], in_=wgs[:]), reads=[wgs_r], writes=[wg_r])
        CB, _ = fw.sb("fCB", [128, 32, 512], BF16, ts)
        SB, _ = fw.sb("fSB", [128, 32, 512], BF16, ts)
        CB_r = [fw.res("fCB%d" % k, ts) for k in range(4)]
        SB_r = [fw.res("fSB%d" % k, ts) for k in range(4)]
        Pb = [fw.sb("fPb%d" % k, [128, 512], BF16, ts) for k in range(2)]
        Qb = [fw.sb("fQb%d" % k, [128, 512], BF16, ts) for k in range(2)]
        Fb = [fw.sb("fFb%d" % k, [128, 512], BF16, ts) for k in range(2)]
        zt = [fw.sb("fzt%d" % k, [128, 512], BF16, ts) for k in range(2)]
        ot = [fw.sb("fot%d" % k, [128, 512], BF16, ts) for k in range(2)]
        last = (i == DEPTH - 1)
        n = 0
        for ci in (range(1, 9) if last else range(9)):
            t0, tn = self.chunk(ci)
            if ci == 0:
                nlt, lt0 = 2, 0
                for k in range(2):
                    fw.dma(CB[:, k, 0:256], self.Cn["dftCX"][k * 128:(k + 1) * 128, :], writes=[CB_r[0]], sem_res=CB_r[0])
                    fw.dma(SB[:, k, 0:256], self.Cn["dftSX"][k * 128:(k + 1) * 128, :], writes=[SB_r[0]], sem_res=SB_r[0])
            else:
                nlt, lt0 = 32, 2
                c0 = (ci - 1) * 512
                for k in range(4):
                    fw.dma(CB[:, k * 8:(k + 1) * 8, :], self.Cn["dftCL"][k * 1024:(k + 1) * 1024, c0:c0 + 512].rearrange("(t p) c -> p t c", p=128), writes=[CB_r[k]], sem_res=CB_r[k])
                    fw.dma(SB[:, k * 8:(k + 1) * 8, :], self.Cn["dftSL"][k * 1024:(k + 1) * 1024, c0:c0 + 512].rearrange("(t p) c -> p t c", p=128), writes=[SB_r[k]], sem_res=SB_r[k])
            for g in range(8):
                pp, ppr = self.pb[(n % 2) * 2]
                qp, qpr = self.pb[(n % 2) * 2 + 1]
                fp, fpr = self.pb[4 + n % 2]
                yp, ypr = self.pb[6 + n % 2]
                P_t, P_r = Pb[n % 2]
                Q_t, Q_r = Qb[n % 2]
                F_t, F_r = Fb[n % 2]
                z_t, z_r = zt[n % 2]
                o_t, o_r = ot[n % 2]
                n += 1
                fw.dma(z_t[:, 0:tn], self.zsd[g, :, t0:t0 + tn], reads=[self.zsd_r[g][ci]], writes=[z_r], sem_res=z_r)
                for (acc, accr, buf, bufr) in ((pp, ppr, CB, CB_r), (qp, qpr, SB, SB_r)):
                    for lt in range(nlt):
                        fw.op("tensor", lambda e, acc=acc, buf=buf, lt=lt, g=g, tn=tn, nlt=nlt, lt0=lt0: e.matmul(acc[:, 0:tn], lhsT=Utm[:, lt0 + lt, g * 128:(g + 1) * 128], rhs=buf[:, lt, 0:tn], start=(lt == 0), stop=(lt == nlt - 1)),
                              reads=[Utm_r[lt0 + lt], bufr[lt // 8]], writes=[accr], signal=(lt == nlt - 1 or lt % 8 == 7))
                fw.op("scalar", lambda e, pp=pp, P_t=P_t, tn=tn: e.copy(out=P_t[:, 0:tn], in_=pp[:, 0:tn]), reads=[ppr], writes=[P_r])
                fw.op("vector", lambda e, qp=qp, Q_t=Q_t, tn=tn: e.tensor_copy(out=Q_t[:, 0:tn], in_=qp[:, 0:tn]), reads=[qpr], writes=[Q_r])
                fw.op("tensor", lambda e, fp=fp, P_t=P_t, tn=tn: e.matmul(fp[:, 0:tn], lhsT=cc_[:], rhs=P_t[:, 0:tn], start=True, stop=False), reads=[cc_r, P_r], writes=[fpr], signal=False)
                fw.op("tensor", lambda e, fp=fp, Q_t=Q_t, tn=tn: e.matmul(fp[:, 0:tn], lhsT=sc_[:], rhs=Q_t[:, 0:tn], start=False, stop=True), reads=[sc_r, Q_r], writes=[fpr])
                fw.op("vector", lambda e, fp=fp, F_t=F_t, tn=tn: e.tensor_copy(out=F_t[:, 0:tn], in_=fp[:, 0:tn]), reads=[fpr], writes=[F_r])
                fw.op("tensor", lambda e, yp=yp, F_t=F_t, g=g, tn=tn: e.matmul(yp[:, 0:tn], lhsT=wg[:, g, :], rhs=F_t[:, 0:tn], start=True, stop=True), reads=[wg_r, F_r], writes=[ypr])
                fw.op("vector", lambda e, yp=yp, z_t=z_t, o_t=o_t, tn=tn: e.tensor_tensor(out=o_t[:, 0:tn], in0=yp[:, 0:tn], in1=z_t[:, 0:tn], op=ALU.mult), reads=[ypr, z_r], writes=[o_r])
                fw.dma(self.og[g, :, t0:t0 + tn], o_t[:, 0:tn], reads=[o_r], writes=[self.og_r[g][ci]], sem_res=o_r)

    def rwkv_prealloc(self, ts):
        fw = self.fw
        nc = self.nc
        R = {}
        R["Dall"] = [fw.sb("Dall%d" % z, [128, 8, 68], F32, ts) for z in range(2)]
        R["stack"] = ts.enter_context(ExitStack())
        R["twd"] = fw.sb("twd", [128, NT], F32, R["stack"])
        R["adT"] = fw.sb("adT", [128, NT], F32, R["stack"])
        if not hasattr(self, "sTd"):
            dk = "ExternalOutput" if self.dbg else "Internal"
            self.sTd = nc.dram_tensor("sTd", [24, 128, NT], F32, kind=dk).ap()
            self.FMd = [nc.dram_tensor("FMd%d" % z, [NTI, 128, 8, 4, 128], BF16, kind="Internal").ap() for z in range(2)]
            self.TMd = [nc.dram_tensor("TMd%d" % z, [NTI, 128, 2, 8, 128], BF16, kind="Internal").ap() for z in range(2)]
            self.Vd = nc.dram_tensor("Vd", [NTI, 128, 8, 128], BF16, kind="Internal").ap()
            self.bonusd = nc.dram_tensor("bonusd", [8, 128, NT], F32, kind=dk).ap()
            self.yd = [nc.dram_tensor("yd%d" % z, [NT, D], F32, kind=dk).ap() for z in range(2)]
        return R

    def rwkv_part1(self, i, j, hT, hT_r, R, ts):
        fw = self.fw
        W = self.I["rw_w_in"][j]
        twd, twd_r = R["twd"]
        adT, adT_r = R["adT"]
        wz, wz_r = fw.sb("rwz", [128, 8, D], BF16, ts)
        self.load_w_bf16(wz, wz_r, W[:, 3328:4352], D, ts, "rwz")
        self.gate_proj(wz, wz_r, hT, hT_r, ts)
        mu_fm, mu_r = fw.sb("mu_fm", [128, 26], F32, ts)
        self.load_fm(mu_fm[:], mu_r, self.I["rw_mu"][j, :].rearrange("(n p) -> n p", p=128), 26, ts)
        ca, ca_r = fw.sb("mix_a", [128, 26], F32, ts)
        cb, cb_r = fw.sb("mix_b", [128, 26], F32, ts)
        fw.op("vector", lambda e: e.tensor_scalar(out=ca[:], in0=mu_fm[:], scalar1=-1.0, scalar2=1.0, op0=ALU.mult, op1=ALU.add), reads=[mu_r], writes=[ca_r])
        fw.op("vector", lambda e: e.tensor_scalar(out=cb[:], in0=mu_fm[:], scalar1=0.5, scalar2=None, op0=ALU.mult), reads=[mu_r], writes=[cb_r])
        NP = NT + 3
        sraw = [fw.sb("sraw%d" % k, [128, NP], F32, ts) for k in range(2)]
        for k in range(2):
            fw.op("gpsimd", lambda e, k=k: e.memset(sraw[k][0][:], 0.0), writes=[sraw[k][1]])
        wst = [fw.sb("rw_st%d" % k, [128, 8, 128], F32, ts) for k in range(2)]
        wbs = [fw.sb("rw_wb%d" % k, [128, 8, 128], BF16, ts) for k in range(2)]
        PW = 512
        tmpb = [fw.sb("mixt%d" % k, [128, PW], F32, ts) for k in range(2)]
        smxb = [fw.sb("mixs%d" % k, [128, PW], F32, ts) for k in range(2)]
        npc = 0
        nev = 0
        for fc in range(26):
            s_t, s_r = wst[fc % 2]
            w_t, w_r = wbs[fc % 2]
            sr_t, sr_r = sraw[fc % 2]
            fw.dma(s_t[:], W[:, fc * 128:(fc + 1) * 128].rearrange("(k p) c -> p k c", p=128), writes=[s_r], sem_res=s_r)
            fw.op("gpsimd", lambda e, s_t=s_t, w_t=w_t: e.tensor_copy(out=w_t[:], in_=s_t[:]), reads=[s_r], writes=[w_r])
            for ci in range(9):
                t0, tn = self.chunk(ci)
                off = t0 + (1 if ci == 0 else 2)
                hrs = [hT_r[t] for t in range(t0 // 128, (t0 + tn) // 128)]
                pt, pr = self.pb[4 + nev % 4]
                for kc in range(8):
                    fw.op("tensor", lambda e, kc=kc, pt=pt, w_t=w_t, t0=t0, tn=tn: e.matmul(pt[:, 0:tn], lhsT=w_t[:, kc, :], rhs=hT[:, kc, t0:t0 + tn], start=(kc == 0), stop=(kc == 7)),
                          reads=[w_r] + hrs, writes=[pr], signal=(kc == 7))
                if nev % 2 == 0:
                    fw.op("scalar", lambda e, pt=pt, sr_t=sr_t, off=off, tn=tn: e.copy(out=sr_t[:, off:off + tn], in_=pt[:, 0:tn]), reads=[pr], writes=[sr_r])
                else:
                    fw.op("vector", lambda e, pt=pt, sr_t=sr_t, off=off, tn=tn: e.tensor_copy(out=sr_t[:, off:off + tn], in_=pt[:, 0:tn]), reads=[pr], writes=[sr_r])
                nev += 1
            for (c0, cn, tok0) in [(1, 256, 0)] + [(258 + q * 512, 512, 256 + q * 512) for q in range(8)]:
                tm_t, tm_r = tmpb[npc % 2]
                sm_t, sm_r = smxb[npc % 2]
                npc += 1
                fw.op("gpsimd", lambda e, tm_t=tm_t, sr_t=sr_t, c0=c0, cn=cn: e.tensor_tensor(out=tm_t[:, 0:cn], in0=sr_t[:, c0 - 1:c0 - 1 + cn], in1=sr_t[:, c0 + 1:c0 + 1 + cn], op=ALU.add), reads=[sr_r], writes=[tm_r])
                fw.op("vector", lambda e, tm_t=tm_t, fc=fc, cn=cn: e.tensor_scalar(out=tm_t[:, 0:cn], in0=tm_t[:, 0:cn], scalar1=cb[:, fc:fc + 1], scalar2=None, op0=ALU.mult), reads=[tm_r, cb_r], writes=[tm_r])
                if fc < 24:
                    fw.op("vector", lambda e, tm_t=tm_t, sm_t=sm_t, sr_t=sr_t, fc=fc, c0=c0, cn=cn: e.scalar_tensor_tensor(out=sm_t[:, 0:cn], in0=sr_t[:, c0:c0 + cn], scalar=ca[:, fc:fc + 1], in1=tm_t[:, 0:cn], op0=ALU.mult, op1=ALU.add),
                          reads=[sr_r, tm_r, ca_r], writes=[sm_r])
                    fw.dma(self.sTd[fc, :, tok0:tok0 + cn], sm_t[:, 0:cn], reads=[sm_r], sem_res=sm_r)
                else:
                    dst, dst_r = (twd, twd_r) if fc == 24 else (adT, adT_r)
                    fw.op("vector", lambda e, tm_t=tm_t, dst=dst, sr_t=sr_t, fc=fc, c0=c0, cn=cn, tok0=tok0: e.scalar_tensor_tensor(out=dst[:, tok0:tok0 + cn], in0=sr_t[:, c0:c0 + cn], scalar=ca[:, fc:fc + 1], in1=tm_t[:, 0:cn], op0=ALU.mult, op1=ALU.add),
                          reads=[sr_r, tm_r, ca_r], writes=[dst_r])
                    if fc == 24:
                        fw.op("scalar", lambda e, tok0=tok0, cn=cn: e.activation(out=twd[:, tok0:tok0 + cn], in_=twd[:, tok0:tok0 + cn], func=AF.Tanh), reads=[twd_r], writes=[twd_r])

    def rwkv_part2(self, i, j, R, ts0):
        fw = self.fw
        stop = getattr(self, "stop_at", 99)
        fw.barrier()
        if stop < 2:
            return
        with ExitStack() as ts:
            self.rwkv_F2(i, j, R, ts)
            fw.barrier()
        R["stack"].close()
        if stop < 3:
            return
        with ExitStack() as ts:
            self.rwkv_S(i, j, R, ts)
            fw.barrier()
        if stop < 4:
            return
        with ExitStack() as ts:
            self.rwkv_O(i, j, R, ts)
            fw.barrier()

    def rwkv_F2(self, i, j, R, ts):
        fw = self.fw
        I = self.I
        twd, twd_r = R["twd"]
        adT, adT_r = R["adT"]
        def fm(name, src, n):
            t, r = fw.sb(name, [128, n], F32, ts)
            self.load_fm(t[:], r, src, n, ts)
            return t, r
        kk_p, kk_pr = fm("p_kk", I["rw_k_k"][j, :].rearrange("(n p) -> n p", p=128), 8)
        ka_p, ka_pr = fm("p_ka", I["rw_k_a"][j, :].rearrange("(n p) -> n p", p=128), 8)
        rk_p, rk_pr = fm("p_rk", I["rw_r_k"][j, :].rearrange("(n p) -> n p", p=128), 8)
        w0_p, w0_pr = fm("p_w0", I["rw_w0"][j].rearrange("z (n p) -> (z n) p", p=128), 16)
        a0_p, a0_pr = fm("p_a0", I["rw_a0"][j].rearrange("z (n p) -> (z n) p", p=128), 16)
        oka, oka_r = fw.sb("p_oka", [128, 8], F32, ts)
        fw.op("vector", lambda e: e.tensor_scalar(out=oka[:], in0=ka_p[:], scalar1=-1.0, scalar2=1.0, op0=ALU.mult, op1=ALU.add), reads=[ka_pr], writes=[oka_r])
        wup, wup_r = fw.sb("wup", [128, D], F32, ts)
        aup, aup_r = fw.sb("aup", [128, D], F32, ts)
        fw.dma(wup[:], I["rw_w_up"][j].rearrange("z r f -> (z r) f"), writes=[wup_r], sem_res=wup_r)
        fw.dma(aup[:], I["rw_a_up"][j].rearrange("z r f -> (z r) f"), writes=[aup_r], sem_res=aup_r)
        bones, bones_r = fw.sb("bones", [128, 128], F32, ts)
        fw.op("vector", lambda e: e.memset(bones[:], 0.0), writes=[bones_r])
        fw.op("vector", lambda e: e.memset(bones[0:64, 0:64], 1.0), writes=[bones_r])
        fw.op("vector", lambda e: e.memset(bones[64:128, 64:128], 1.0), writes=[bones_r])
        eps12, eps12_r = fw.sb("eps12", [128, 1], F32, ts)
        fw.op("vector", lambda e: e.memset(eps12[:], 1e-12), writes=[eps12_r])
        cmask, cmask_r = fw.sb("cmask", [128, 512], F32, ts)
        fw.op("vector", lambda e: e.memset(cmask[:], 1.0), writes=[cmask_r])
        fw.op("vector", lambda e: e.memset(cmask[:].rearrange("p (c t) -> p c t", t=64)[:, :, 0:1], 0.0), writes=[cmask_r])

        cnt = [0]

        def T(name, dt=F32, n=2, w=512):
            return [fw.sb("%s_%d" % (name, k), [128, w], dt, ts) for k in range(n)]
        rTb, kTb, vTb = T("f_r"), T("f_k"), T("f_v")
        lwb = [T("f_lw0"), T("f_lw1")]
        azb = [T("f_az0"), T("f_az1")]
        kdb = [T("f_kd0"), T("f_kd1")]
        bzb = [T("f_b0"), T("f_b1")]
        kkrb, sqb_, rnb, kkb, tb1, tb2, tb3 = T("f_kkr"), T("f_sq"), T("f_rn"), T("f_kk"), T("f_t1"), T("f_t2"), T("f_t3")
        cwb, e1b, e2b, e3b, e4b = T("f_cw"), T("f_e1"), T("f_e2"), T("f_e3"), T("f_e4")
        fmo = [T("f_o%d" % k, dt=BF16) for k in range(4)]
        hato = [T("f_h%d" % k) for k in range(2)]
        trs = T("f_trs", dt=BF16, n=3)
        bonb = T("f_bon")
        NEG = -math.exp(-0.5)
        nb = 0
        ntr = [0]

        def transpose_store(src_t, src_r, dst_ap_fn, ntile):
            tp, tpr = self.pb[4 + ntr[0] % 4]
            o_t, o_r = trs[ntr[0] % 3]
            ntr[0] += 1
            for tl in range(ntile):
                fw.op("tensor", lambda e, tl=tl, tp=tp: e.transpose(tp[:, tl * 128:(tl + 1) * 128], src_t[:, tl * 128:(tl + 1) * 128], self.ident[:]),
                      reads=[src_r, self.ident_r], writes=[tpr], signal=(tl == ntile - 1))
            fw.op("scalar", lambda e, tp=tp, o_t=o_t, ntile=ntile: e.copy(out=o_t[:, 0:ntile * 128], in_=tp[:, 0:ntile * 128]), reads=[tpr], writes=[o_r])
            for tl in range(ntile):
                fw.dma(dst_ap_fn(tl), o_t[:, tl * 128:(tl + 1) * 128], reads=[o_r], sem_res=o_r)

        for hp in range(8):
            for ci in range(9):
                t0, N = self.chunk(ci)
                tile0 = t0 // 128
                ntile = N // 128
                nch = N // 64
                k = nb % 2
                nb += 1
                (rT, rT_r), (kT, kT_r), (vT, vT_r) = rTb[k], kTb[k], vTb[k]
                fw.dma(rT[:, 0:N], self.sTd[hp, :, t0:t0 + N], writes=[rT_r], sem_res=rT_r)
                fw.dma(kT[:, 0:N], self.sTd[8 + hp, :, t0:t0 + N], writes=[kT_r], sem_res=kT_r)
                fw.dma(vT[:, 0:N], self.sTd[16 + hp, :, t0:t0 + N], writes=[vT_r], sem_res=vT_r)
                transpose_store(vT, vT_r, lambda tl, tile0=tile0, hp=hp: self.Vd[tile0 + tl, :, hp, :], ntile)
                for z in range(2):
                    rows = slice(z * 64, (z + 1) * 64)
                    pw, pwr = self.pb[z]
                    pa, par = self.pb[2 + z]
                    lw, lw_r = lwb[z][k]
                    az, az_r = azb[z][k]
                    fw.op("tensor", lambda e, pw=pw, rows=rows, hp=hp, t0=t0, N=N: e.matmul(pw[:, 0:N], lhsT=wup[rows, hp * 128:(hp + 1) * 128], rhs=twd[rows, t0:t0 + N], start=True, stop=True), reads=[wup_r, twd_r], writes=[pwr])
                    fw.op("tensor", lambda e, pa=pa, rows=rows, hp=hp, t0=t0, N=N: e.matmul(pa[:, 0:N], lhsT=aup[rows, hp * 128:(hp + 1) * 128], rhs=adT[rows, t0:t0 + N], start=True, stop=True), reads=[aup_r, adT_r], writes=[par])
                    fw.op("scalar", lambda e, pw=pw, lw=lw, z=z, hp=hp, N=N: e.activation(out=lw[:, 0:N], in_=pw[:, 0:N], func=AF.Sigmoid, bias=w0_p[:, z * 8 + hp:z * 8 + hp + 1], scale=1.0), reads=[pwr, w0_pr], writes=[lw_r])
                    fw.op("vector", lambda e, lw=lw, N=N: e.tensor_scalar(out=lw[:, 0:N], in0=lw[:, 0:N], scalar1=NEG, scalar2=None, op0=ALU.mult), reads=[lw_r], writes=[lw_r])
                    fw.op("scalar", lambda e, pa=pa, az=az, z=z, hp=hp, N=N: e.activation(out=az[:, 0:N], in_=pa[:, 0:N], func=AF.Sigmoid, bias=a0_p[:, z * 8 + hp:z * 8 + hp + 1], scale=1.0), reads=[par, a0_pr], writes=[az_r])
                (kkr, kkr_r), (sq, sq_r), (rn, rn_r), (kk, kk_r) = kkrb[k], sqb_[k], rnb[k], kkb[k]
                fw.op("vector", lambda e, kkr=kkr, kT=kT, hp=hp, N=N: e.tensor_scalar(out=kkr[:, 0:N], in0=kT[:, 0:N], scalar1=kk_p[:, hp:hp + 1], scalar2=None, op0=ALU.mult), reads=[kT_r, kk_pr], writes=[kkr_r])
                fw.op("gpsimd", lambda e, kkr=kkr, sq=sq, N=N: e.tensor_tensor(out=sq[:, 0:N], in0=kkr[:, 0:N], in1=kkr[:, 0:N], op=ALU.mult), reads=[kkr_r], writes=[sq_r])
                pss, pssr = self.pb[4 + ntr[0] % 4]
                ntr[0] += 1
                fw.op("tensor", lambda e, pss=pss, sq=sq, N=N: e.matmul(pss[:, 0:N], lhsT=bones[:], rhs=sq[:, 0:N], start=True, stop=True), reads=[bones_r, sq_r], writes=[pssr])
                fw.op("scalar", lambda e, pss=pss, rn=rn, N=N: e.activation(out=rn[:, 0:N], in_=pss[:, 0:N], func=AF.Sqrt, bias=eps12[:], scale=1.0), reads=[pssr, eps12_r], writes=[rn_r])
                fw.op("vector", lambda e, rn=rn, N=N: e.reciprocal(out=rn[:, 0:N], in_=rn[:, 0:N]), reads=[rn_r], writes=[rn_r])
                fw.op("gpsimd", lambda e, kk=kk, kkr=kkr, rn=rn, N=N: e.tensor_tensor(out=kk[:, 0:N], in0=kkr[:, 0:N], in1=rn[:, 0:N], op=ALU.mult), reads=[kkr_r, rn_r], writes=[kk_r])
                for z in range(2):
                    az, az_r = azb[z][k]
                    kd, kd_r = kdb[z][k]
                    bz, bz_r = bzb[z][k]
                    fw.op("vector", lambda e, kd=kd, az=az, hp=hp, N=N: e.tensor_scalar(out=kd[:, 0:N], in0=az[:, 0:N], scalar1=ka_p[:, hp:hp + 1], scalar2=oka[:, hp:hp + 1], op0=ALU.mult, op1=ALU.add), reads=[az_r, ka_pr, oka_r], writes=[kd_r])
                    fw.op("vector", lambda e, kd=kd, kT=kT, N=N: e.tensor_tensor(out=kd[:, 0:N], in0=kd[:, 0:N], in1=kT[:, 0:N], op=ALU.mult), reads=[kd_r, kT_r], writes=[kd_r])
                    fw.op("gpsimd", lambda e, bz=bz, kk=kk, az=az, N=N: e.tensor_tensor(out=bz[:, 0:N], in0=kk[:, 0:N], in1=az[:, 0:N], op=ALU.mult), reads=[kk_r, az_r], writes=[bz_r])
                (x1, x1_r), (x2, x2_r) = tb1[k], tb2[k]
                fw.op("vector", lambda e, x1=x1, N=N, k=k: e.tensor_tensor(out=x1[:, 0:N], in0=kdb[0][k][0][:, 0:N], in1=kdb[1][k][0][:, 0:N], op=ALU.add), reads=[kdb[0][k][1], kdb[1][k][1]], writes=[x1_r])
                fw.op("vector", lambda e, x2=x2, rT=rT, hp=hp, N=N: e.tensor_scalar(out=x2[:, 0:N], in0=rT[:, 0:N], scalar1=rk_p[:, hp:hp + 1], scalar2=0.5, op0=ALU.mult, op1=ALU.mult), reads=[rT_r, rk_pr], writes=[x2_r])
                fw.op("vector", lambda e, x1=x1, x2=x2, N=N: e.tensor_tensor(out=x1[:, 0:N], in0=x1[:, 0:N], in1=x2[:, 0:N], op=ALU.mult), reads=[x1_r, x2_r], writes=[x1_r])
                psb, psbr = self.pb[4 + ntr[0] % 4]
                ntr[0] += 1
                fw.op("tensor", lambda e, psb=psb, x1=x1, N=N: e.matmul(psb[:, 0:N], lhsT=bones[:], rhs=x1[:, 0:N], start=True, stop=True), reads=[bones_r, x1_r], writes=[psbr])
                bo, bo_r = bonb[k]
                fw.op("vector", lambda e, bo=bo, psb=psb, vT=vT, N=N: e.tensor_tensor(out=bo[:, 0:N], in0=psb[:, 0:N], in1=vT[:, 0:N], op=ALU.mult), reads=[psbr, vT_r], writes=[bo_r])
                fw.dma(self.bonusd[hp, :, t0:t0 + N], bo[:, 0:N], reads=[bo_r], sem_res=bo_r)
                for z in range(2):
                    lw, lw_r = lwb[z][k]
                    kd, kd_r = kdb[z][k]
                    bz, bz_r = bzb[z][k]
                    cw, cw_r = cwb[z]
                    (e1, e1_r), (e2, e2_r), (e3, e3_r), (e4, e4_r) = e1b[z], e2b[z], e3b[z], e4b[z]
                    (y1, y1_r) = tb3[z]
                    Dall, Dall_r = R["Dall"][z]
                    fw.op("vector", lambda e, cw=cw, lw=lw, N=N: e.tensor_tensor_scan(out=cw[:, 0:N], data0=cmask[:, 0:N], data1=lw[:, 0:N], initial=0.0, op0=ALU.mult, op1=ALU.add), reads=[cmask_r, lw_r], writes=[cw_r])
                    cw3 = cw[:, 0:N].rearrange("p (c t) -> p c t", t=64)
                    totb = cw3[:, :, 63:64].to_broadcast([128, nch, 64])
                    ch0 = t0 // 64
                    fw.op("scalar", lambda e, Dall=Dall, cw3=cw3, hp=hp, ch0=ch0, nch=nch: e.activation(out=Dall[:, hp, ch0:ch0 + nch], in_=cw3[:, :, 63], func=AF.Exp), reads=[cw_r], writes=[Dall_r])
                    y13 = y1[:, 0:N].rearrange("p (c t) -> p c t", t=64)
                    if z == 0:
                        fw.op("gpsimd", lambda e, y13=y13, totb=totb, cw3=cw3: e.tensor_tensor(out=y13, in0=totb, in1=cw3, op=ALU.subtract), reads=[cw_r], writes=[y1_r])
                        cwz, cwz_r = cw, cw_r
                    else:
                        fw.op("gpsimd", lambda e, y1=y1, cw=cw, lw=lw, N=N: e.tensor_tensor(out=y1[:, 0:N], in0=cw[:, 0:N], in1=lw[:, 0:N], op=ALU.subtract), reads=[cw_r, lw_r], writes=[y1_r])
                        cwz, cwz_r = e4, e4_r
                        e43 = e4[:, 0:N].rearrange("p (c t) -> p c t", t=64)
                        fw.op("gpsimd", lambda e, e43=e43, totb=totb, y13=y13: e.tensor_tensor(out=e43, in0=totb, in1=y13, op=ALU.subtract), reads=[cw_r, y1_r], writes=[e4_r])
                    fw.op("scalar", lambda e, e1=e1, cwz=cwz, N=N: e.activation(out=e1[:, 0:N], in_=cwz[:, 0:N], func=AF.Exp), reads=[cwz_r], writes=[e1_r])
                    fw.op("scalar", lambda e, e2=e2, cwz=cwz, N=N: e.activation(out=e2[:, 0:N], in_=cwz[:, 0:N], func=AF.Exp, scale=-1.0), reads=[cwz_r], writes=[e2_r])
                    fw.op("vector", lambda e, e3=e3, cwz=cwz, lw=lw, N=N: e.tensor_tensor(out=e3[:, 0:N], in0=cwz[:, 0:N], in1=lw[:, 0:N], op=ALU.subtract), reads=[cwz_r, lw_r], writes=[e3_r])
                    fw.op("scalar", lambda e, e3=e3, N=N: e.activation(out=e3[:, 0:N], in_=e3[:, 0:N], func=AF.Exp), reads=[e3_r], writes=[e3_r])
                    fw.op("scalar", lambda e, y1=y1, N=N: e.activation(out=y1[:, 0:N], in_=y1[:, 0:N], func=AF.Exp), reads=[y1_r], writes=[y1_r])
                    outs = [fmo[q][z] for q in range(4)]
                    fw.op("vector", lambda e, o=outs[0][0], kk=kk, e3=e3, N=N: e.scalar_tensor_tensor(out=o[:, 0:N], in0=kk[:, 0:N], scalar=-1.0, in1=e3[:, 0:N], op0=ALU.mult, op1=ALU.mult), reads=[kk_r, e3_r], writes=[outs[0][1]])
                    fw.op("gpsimd", lambda e, o=outs[1][0], rT=rT, e1=e1, N=N: e.tensor_tensor(out=o[:, 0:N], in0=rT[:, 0:N], in1=e1[:, 0:N], op=ALU.mult), reads=[rT_r, e1_r], writes=[outs[1][1]])
                    fw.op("vector", lambda e, o=outs[2][0], bz=bz, e2=e2, N=N: e.tensor_tensor(out=o[:, 0:N], in0=bz[:, 0:N], in1=e2[:, 0:N], op=ALU.mult), reads=[bz_r, e2_r], writes=[outs[2][1]])
                    fw.op("vector", lambda e, o=outs[3][0], kd=kd, e2=e2, N=N: e.tensor_tensor(out=o[:, 0:N], in0=kd[:, 0:N], in1=e2[:, 0:N], op=ALU.mult), reads=[kd_r, e2_r], writes=[outs[3][1]])
                    for q in range(4):
                        fw.dma(self.FMd[z][tile0:tile0 + ntile, :, hp, q, :].rearrange("n p t -> p n t"), outs[q][0][:, 0:N].rearrange("p (n t) -> p n t", t=128), reads=[outs[q][1]], sem_res=outs[q][1])
                    (h0, h0_r), (h1, h1_r) = hato[0][z], hato[1][z]
                    fw.op("vector", lambda e, h0=h0, bz=bz, y1=y1, N=N: e.tensor_tensor(out=h0[:, 0:N], in0=bz[:, 0:N], in1=y1[:, 0:N], op=ALU.mult), reads=[bz_r, y1_r], writes=[h0_r])
                    fw.op("gpsimd", lambda e, h1=h1, kd=kd, y1=y1, N=N: e.tensor_tensor(out=h1[:, 0:N], in0=kd[:, 0:N], in1=y1[:, 0:N], op=ALU.mult), reads=[kd_r, y1_r], writes=[h1_r])
                    transpose_store(h0, h0_r, lambda tl, z=z, tile0=tile0, hp=hp: self.TMd[z][tile0 + tl, :, 0, hp, :], ntile)
                    transpose_store(h1, h1_r, lambda tl, z=z, tile0=tile0, hp=hp: self.TMd[z][tile0 + tl, :, 1, hp, :], ntile)

    def rwkv_S(self, i, j, R, ts):
        fw = self.fw
        mk = {}
        for z in range(2):
            mNf, mNf_r = fw.sb("mNf%d" % z, [128, 256], F32, ts)
            mLf, mLf_r = fw.sb("mLf%d" % z, [128, 128], F32, ts)
            mN, mN_r = fw.sb("mN%d" % z, [128, 2, 256], BF16, ts)
            mL, mL_r = fw.sb("mL%d" % z, [128, 4, 128], BF16, ts)
            fw.dma(mNf[:], self.Cn["mN%d" % z][:, :], writes=[mNf_r], sem_res=mNf_r)
            fw.dma(mLf[:], self.Cn["mL%d" % z][:, :], writes=[mLf_r], sem_res=mLf_r)
            for q in range(2):
                fw.op("vector", lambda e, mN=mN, mNf=mNf, q=q: e.tensor_copy(out=mN[:, q, :], in_=mNf[:]), reads=[mNf_r], writes=[mN_r])
            for q in range(4):
                fw.op("vector", lambda e, mL=mL, mLf=mLf, q=q: e.tensor_copy(out=mL[:, q, :], in_=mLf[:]), reads=[mLf_r], writes=[mL_r])
            mk[z] = (mN, mN_r, mL, mL_r)
        identb4, identb4_r = fw.sb("ident4", [128, 4, 128], BF16, ts)
        for q in range(4):
            fw.op("vector", lambda e, q=q: e.tensor_copy(out=identb4[:, q, :], in_=self.ident[:]), reads=[self.ident_r], writes=[identb4_r])
        Mst = []
        for z in range(2):
            row = []
            for hf in range(2):
                t, r = fw.sb("M%d_%d" % (z, hf), [128, 8, 64], F32, ts)
                fw.op("vector", lambda e, t=t: e.memset(t[:], 0.0), writes=[r])
                tb_, rb_ = fw.sb("Mb%d_%d" % (z, hf), [128, 8, 64], BF16, ts)
                fw.op("vector", lambda e, tb_=tb_: e.memset(tb_[:], 0.0), writes=[rb_])
                row.append((t, r, tb_, rb_))
            Mst.append(row)
        FMb = [fw.sb("sFM%d" % k, [128, 8, 4, 128], BF16, ts) for k in range(2)]
        TMb = [fw.sb("sTM%d" % k, [128, 2, 8, 128], BF16, ts) for k in range(2)]
        Vb = [fw.sb("sV%d" % k, [128, 8, 128], BF16, ts) for k in range(2)]
        G13, _ = fw.sb("sG13", [128, 16, 256], BF16, ts)
        G24, _ = fw.sb("sG24", [128, 16, 256], BF16, ts)
        G13_r = [Res("sG13_%d" % g) for g in range(8)]
        G24_r = [Res("sG24_%d" % g) for g in range(8)]
        Lb_ = [fw.sb("sL%d" % k, [128, 16, 128], BF16, ts)[0] for k in range(2)]
        Nb_ = [fw.sb("sN%d" % k, [128, 16, 128], BF16, ts)[0] for k in range(2)]
        Pm, _ = fw.sb("sP", [128, 16, 128], BF16, ts)
        L_r = [[Res("sL%d_%d" % (k, g)) for g in range(4)] for k in range(2)]
        N_r = [[Res("sN%d_%d" % (k, g)) for g in range(4)] for k in range(2)]
        P_r = [Res("sP_%d" % g) for g in range(4)]
        Z2s = [fw.sb("sZ2s%d" % k, [128, 512], F32, ts) for k in range(2)]
        Zs = [fw.sb("sZs%d" % k, [128, 512], BF16, ts) for k in range(2)]
        Us = [fw.sb("sUs%d" % k, [128, 512], BF16, ts) for k in range(2)]
        Ys = [fw.sb("sYs%d" % k, [128, 512], F32, ts) for k in range(2)]
        npb = [0]

        def prep_bank():
            b = self.pb[4 + npb[0] % 4]
            npb[0] += 1
            return b

        order = [list(range(NTI)), [1, 0] + list(range(NTI - 1, 1, -1))]
        nstep = 0
        for si in range(NTI):
            for z in getattr(self, "zorder", (0, 1)):
                tile = order[z][si]
                k = nstep % 2
                nstep += 1
                FM, FM_r = FMb[k]
                TM, TM_r = TMb[k]
                V, V_r = Vb[k]
                mN, mN_r, mL, mL_r = mk[z]
                fw.dma(FM[:], self.FMd[z][tile], writes=[FM_r], sem_res=FM_r)
                fw.dma(TM[:], self.TMd[z][tile], writes=[TM_r], sem_res=TM_r)
                fw.dma(V[:], self.Vd[tile], writes=[V_r], sem_res=V_r)
                for g2 in range(8):
                    pB, pBr = prep_bank()
                    pK, pKr = prep_bank()
                    pL, pLr = prep_bank()
                    for hl in range(2):
                        h = g2 * 2 + hl
                        hp, hh = h % 8, h // 8
                        rows = slice(hh * 64, (hh + 1) * 64)
                        ar = FM[rows, hp, 0:2, :].rearrange("p a t -> p (a t)")
                        fw.op("tensor", lambda e, FM=FM, TM=TM, V=V, pB=pB, rows=rows, hp=hp, hl=hl, ar=ar: e.matmul(pB[:, hl * 256:(hl + 1) * 256], lhsT=FM[rows, hp, 2, :], rhs=ar, start=True, stop=True), reads=[FM_r], writes=[pBr], signal=(hl == 1))
                        fw.op("tensor", lambda e, FM=FM, TM=TM, V=V, pK=pK, rows=rows, hp=hp, hl=hl, ar=ar: e.matmul(pK[:, hl * 256:(hl + 1) * 256], lhsT=FM[rows, hp, 3, :], rhs=ar, start=True, stop=True), reads=[FM_r], writes=[pKr], signal=(hl == 1))
                        fw.op("tensor", lambda e, FM=FM, TM=TM, V=V, pL=pL, rows=rows, hp=hp, hl=hl: e.matmul(pL[:, hl * 128:(hl + 1) * 128], lhsT=FM[rows, hp, 0, :], rhs=FM[rows, hp, 2, :], start=True, stop=True), reads=[FM_r], writes=[pLr], signal=(hl == 1))
                    h0 = g2 * 2
                    fw.op("vector", lambda e, FM=FM, TM=TM, V=V, pB=pB, h0=h0, mN=mN: e.tensor_tensor(out=G13[:, h0:h0 + 2, :], in0=pB[:, :].rearrange("p (h c) -> p h c", h=2), in1=mN[:], op=ALU.mult), reads=[pBr, mN_r], writes=[G13_r[g2]])
                    fw.op("vector", lambda e, FM=FM, TM=TM, V=V, pK=pK, h0=h0, mN=mN: e.tensor_tensor(out=G24[:, h0:h0 + 2, :], in0=pK[:, :].rearrange("p (h c) -> p h c", h=2), in1=mN[:], op=ALU.mult), reads=[pKr, mN_r], writes=[G24_r[g2]])
                    fw.op("vector", lambda e, FM=FM, TM=TM, V=V, pL=pL, h0=h0, mL=mL: e.tensor_tensor(out=Lb_[0][:, h0:h0 + 2, :], in0=pL[:, 0:256].rearrange("p (h c) -> p h c", h=2), in1=mL[:, 0:2, :], op=ALU.mult), reads=[pLr, mL_r], writes=[L_r[0][g2 // 2]])
                sstop = getattr(self, "s_stop", 99)
                if sstop < 1:
                    continue
                for g4 in range(4):
                    hs = slice(g4 * 4, g4 * 4 + 4)
                    fw.op("gpsimd", lambda e, FM=FM, TM=TM, V=V, hs=hs: e.tensor_copy(out=Nb_[0][:, hs, :], in_=G13[:, hs, 0:128]), reads=[G13_r[g4 * 2], G13_r[g4 * 2 + 1]], writes=[N_r[0][g4]])
                    fw.op("gpsimd", lambda e, FM=FM, TM=TM, V=V, hs=hs: e.tensor_tensor(out=Pm[:, hs, :], in0=G13[:, hs, 0:128], in1=identb4[:], op=ALU.add), reads=[G13_r[g4 * 2], G13_r[g4 * 2 + 1], identb4_r], writes=[P_r[g4]])
                for lev in range(1, 6):
                    a, b = (lev - 1) % 2, lev % 2
                    for g4 in range(4):
                        hs = slice(g4 * 4, g4 * 4 + 4)
                        pl, plr = prep_bank()
                        for hl in range(4):
                            h = g4 * 4 + hl
                            fw.op("tensor", lambda e, FM=FM, TM=TM, V=V, pl=pl, h=h, hl=hl, a=a: e.matmul(pl[:, hl * 128:(hl + 1) * 128], lhsT=Nb_[a][:, h, :], rhs=Lb_[a][:, h, :], start=True, stop=True), reads=[N_r[a][g4], L_r[a][g4]], writes=[plr], signal=(hl == 3))
                        fw.op("scalar", lambda e, FM=FM, TM=TM, V=V, pl=pl, hs=hs, b=b: e.copy(out=Lb_[b][:, hs, :], in_=pl[:, :].rearrange("p (h c) -> p h c", h=4)), reads=[plr], writes=[L_r[b][g4]])
                        if lev < 5:
                            pn, pnr = prep_bank()
                            for hl in range(4):
                                h = g4 * 4 + hl
                                fw.op("tensor", lambda e, FM=FM, TM=TM, V=V, pn=pn, h=h, hl=hl, a=a: e.matmul(pn[:, hl * 128:(hl + 1) * 128], lhsT=Lb_[a][:, h, :], rhs=Nb_[a][:, h, :], start=True, stop=True), reads=[N_r[a][g4], L_r[a][g4]], writes=[pnr], signal=(hl == 3))
                            fw.op("vector", lambda e, FM=FM, TM=TM, V=V, pn=pn, hs=hs, b=b: e.tensor_copy(out=Nb_[b][:, hs, :], in_=pn[:, :].rearrange("p (h c) -> p h c", h=4)), reads=[pnr], writes=[N_r[b][g4]])
                        pq, pqr = prep_bank()
                        for hl in range(4):
                            h = g4 * 4 + hl
                            fw.op("tensor", lambda e, FM=FM, TM=TM, V=V, pq=pq, h=h, hl=hl, b=b: e.matmul(pq[:, hl * 128:(hl + 1) * 128], lhsT=Lb_[b][:, h, :], rhs=Pm[:, h, :], start=True, stop=True), reads=[L_r[b][g4], P_r[g4]], writes=[pqr], signal=(hl == 3))
                        fw.op("vector", lambda e, FM=FM, TM=TM, V=V, pq=pq, hs=hs: e.tensor_tensor(out=Pm[:, hs, :], in0=pq[:, :].rearrange("p (h c) -> p h c", h=4), in1=Pm[:, hs, :], op=ALU.add), reads=[pqr, P_r[g4]], writes=[P_r[g4]])
                if self.dbg and nstep == 1:
                    self.dG13 = self.nc.dram_tensor("dG13", [128, 16, 256], F32, kind="ExternalOutput").ap()
                    self.dG24 = self.nc.dram_tensor("dG24", [128, 16, 256], F32, kind="ExternalOutput").ap()
                    self.dP = self.nc.dram_tensor("dP", [128, 16, 128], F32, kind="ExternalOutput").ap()
                    self.dFM = self.nc.dram_tensor("dFM", [128, 8, 4, 128], F32, kind="ExternalOutput").ap()
                    self.dTM = self.nc.dram_tensor("dTM", [128, 2, 8, 128], F32, kind="ExternalOutput").ap()
                    self.dV = self.nc.dram_tensor("dV", [128, 8, 128], F32, kind="ExternalOutput").ap()
                    dr = fw.res("dbgS", ts)
                    fw.dma(self.dFM, FM[:], reads=[FM_r], sem_res=dr, ek="gpsimd")
                    fw.dma(self.dTM, TM[:], reads=[TM_r], sem_res=dr, ek="gpsimd")
                    fw.dma(self.dV, V[:], reads=[V_r], sem_res=dr, ek="gpsimd")
                    fw.dma(self.dG13, G13[:], reads=list(G13_r), sem_res=dr, ek="gpsimd")
                    fw.dma(self.dG24, G24[:], reads=list(G24_r), sem_res=dr, ek="gpsimd")
                    fw.dma(self.dP, Pm[:], reads=list(P_r), sem_res=dr, ek="gpsimd")
                if sstop < 2:
                    continue
                cks = (0, 1) if z == 0 else (1, 0)
                for hf in range(2):
                    hr = slice(hf * 64, (hf + 1) * 64)
                    M, M_r, Mb, Mb_r = Mst[z][hf]
                    Dall, Dall_r = R["Dall"][z]
                    tA, tAr = self.pb[0]
                    tB, tBr = self.pb[1]
                    tC, tCr = self.pb[2]
                    tD, tDr = self.pb[3]
                    z2, z2_r = Z2s[hf]
                    zs_, zs_r = Zs[hf]
                    us, us_r = Us[hf]
                    ys, ys_r = Ys[hf]
                    g13r = list(G13_r)
                    g24r = list(G24_r)
                    pr_ = list(P_r)
                    for ck in range(2):
                        rows = slice(ck * 64, (ck + 1) * 64)
                        for hp in range(8):
                            h = hf * 8 + hp
                            fw.op("tensor", lambda e, FM=FM, TM=TM, V=V, rows=rows, h=h, hp=hp, hr=hr, ck=ck: e.matmul(tA[rows, hp * 64:(hp + 1) * 64], lhsT=G24[rows, h, ck * 64:(ck + 1) * 64], rhs=V[rows, hp, hr], start=True, stop=True),
                                  reads=g24r + [V_r], writes=[tAr], signal=(ck == 1 and hp == 7))
                    fw.op("scalar", lambda e, FM=FM, TM=TM, V=V, z2=z2: e.copy(out=z2[:], in_=tA[:, :]), reads=[tAr], writes=[z2_r])
                    for ck in (cks if sstop >= 3 else ()):
                        rows = slice(ck * 64, (ck + 1) * 64)
                        cols = slice(ck * 64, (ck + 1) * 64)
                        for hp in range(8):
                            fw.op("tensor", lambda e, FM=FM, TM=TM, V=V, rows=rows, cols=cols, hp=hp, hr=hr, Mb=Mb: e.matmul(tB[rows, hp * 64:(hp + 1) * 64], lhsT=FM[hr, hp, 0, cols], rhs=Mb[hr, hp, :], start=True, stop=True),
                                  reads=[FM_r, Mb_r], writes=[tBr], signal=(hp == 7))
                        for hp in range(8):
                            fw.op("tensor", lambda e, FM=FM, TM=TM, V=V, rows=rows, cols=cols, hp=hp, hr=hr, Mb=Mb: e.matmul(tC[rows, hp * 64:(hp + 1) * 64], lhsT=FM[hr, hp, 1, cols], rhs=Mb[hr, hp, :], start=True, stop=True),
                                  reads=[FM_r, Mb_r], writes=[tCr], signal=(hp == 7))
                        fw.op("vector", lambda e, FM=FM, TM=TM, V=V, rows=rows, zs_=zs_, z2=z2: e.tensor_tensor(out=zs_[rows, :], in0=tB[rows, :], in1=z2[rows, :], op=ALU.add), reads=[tBr, z2_r], writes=[zs_r])
                        if sstop < 4:
                            continue
                        for hp in range(8):
                            h = hf * 8 + hp
                            fw.op("tensor", lambda e, FM=FM, TM=TM, V=V, rows=rows, cols=cols, hp=hp, h=h, zs_=zs_: e.matmul(tB[rows, hp * 64:(hp + 1) * 64], lhsT=Pm[rows, h, cols], rhs=zs_[rows, hp * 64:(hp + 1) * 64], start=True, stop=True),
                                  reads=pr_ + [zs_r], writes=[tBr], signal=(hp == 7))
                        fw.op("scalar", lambda e, FM=FM, TM=TM, V=V, rows=rows, us=us: e.copy(out=us[rows, :], in_=tB[rows, :]), reads=[tBr], writes=[us_r])
                        if sstop < 5:
                            continue
                        for hp in range(8):
                            h = hf * 8 + hp
                            fw.op("tensor", lambda e, FM=FM, TM=TM, V=V, rows=rows, ck=ck, hp=hp, h=h, us=us: e.matmul(tA[rows, hp * 64:(hp + 1) * 64], lhsT=G13[rows, h, 128 + ck * 64:128 + (ck + 1) * 64], rhs=us[rows, hp * 64:(hp + 1) * 64], start=True, stop=False),
                                  reads=g13r + [us_r], writes=[tAr], signal=False)
                            fw.op("tensor", lambda e, FM=FM, TM=TM, V=V, rows=rows, ck=ck, hp=hp, h=h, hr=hr: e.matmul(tA[rows, hp * 64:(hp + 1) * 64], lhsT=G24[rows, h, 128 + ck * 64:128 + (ck + 1) * 64], rhs=V[rows, hp, hr], start=False, stop=True),
                                  reads=g24r + [V_r], writes=[tAr], signal=(hp == 7))
                        if sstop < 6:
                            continue
                        for hp in range(8):
                            fw.op("tensor", lambda e, FM=FM, TM=TM, V=V, rows=rows, hp=hp, hr=hr, us=us: e.matmul(tD[hr, hp * 64:(hp + 1) * 64], lhsT=TM[rows, 0, hp, hr], rhs=us[rows, hp * 64:(hp + 1) * 64], start=True, stop=False),
                                  reads=[TM_r, us_r], writes=[tDr], signal=False)
                            fw.op("tensor", lambda e, FM=FM, TM=TM, V=V, rows=rows, hp=hp, hr=hr: e.matmul(tD[hr, hp * 64:(hp + 1) * 64], lhsT=TM[rows, 1, hp, hr], rhs=V[rows, hp, hr], start=False, stop=True),
                                  reads=[TM_r, V_r], writes=[tDr], signal=(hp == 7))
                        c = tile * 2 + ck
                        dbc = Dall[hr, :, c:c + 1].to_broadcast([64, 8, 64])
                        fw.op("vector", lambda e, FM=FM, TM=TM, V=V, M=M, dbc=dbc, hr=hr: e.tensor_tensor(out=M[hr, :, :], in0=M[hr, :, :], in1=dbc, op=ALU.mult), reads=[M_r, Dall_r], writes=[M_r])
                        fw.op("vector", lambda e, FM=FM, TM=TM, V=V, M=M, hr=hr: e.tensor_tensor(out=M[hr, :, :], in0=tD[hr, :].rearrange("p (h i) -> p h i", h=8), in1=M[hr, :, :], op=ALU.add), reads=[M_r, tDr], writes=[M_r])
                        fw.op("scalar", lambda e, FM=FM, TM=TM, V=V, M=M, Mb=Mb, hr=hr: e.copy(out=Mb[hr, :, :], in_=M[hr, :, :]), reads=[M_r], writes=[Mb_r])
                    if sstop < 7:
                        continue
                    fw.op("scalar", lambda e, FM=FM, TM=TM, V=V, ys=ys: e.copy(out=ys[:], in_=tC[:, :]), reads=[tCr], writes=[ys_r])
                    fw.op("vector", lambda e, FM=FM, TM=TM, V=V, ys=ys: e.tensor_tensor(out=ys[:], in0=tA[:, :], in1=ys[:], op=ALU.add), reads=[tAr, ys_r], writes=[ys_r])
                    ydst = self.yd[z][tile * 128:(tile + 1) * 128, :].rearrange("t (hp hh i) -> t hp hh i", hh=2, i=64)[:, :, hf, :]
                    fw.dma(ydst, ys[:].rearrange("p (hp i) -> p hp i", i=64), reads=[ys_r], sem_res=ys_r)

    def rwkv_O(self, i, j, R, ts):
        fw = self.fw
        I = self.I
        lnw, lnw_r = fw.sb("o_lnw", [128, D], F32, ts)
        lnb, lnb_r = fw.sb("o_lnb", [128, D], F32, ts)
        fw.dma(lnw[:], I["rw_ln_w"][j, :].partition_broadcast(128), writes=[lnw_r], sem_res=lnw_r)
        fw.dma(lnb[:], I["rw_ln_b"][j, :].partition_broadcast(128), writes=[lnb_r], sem_res=lnb_r)
        epsg, epsg_r = fw.sb("o_eps", [128, 1], F32, ts)
        fw.op("vector", lambda e: e.memset(epsg[:], 64e-5), writes=[epsg_r])
        y0b = [fw.sb("o_y0%d" % k, [128, D], F32, ts) for k in range(2)]
        y1b = [fw.sb("o_y1%d" % k, [128, D], F32, ts) for k in range(2)]
        sqb = [fw.sb("o_sq%d" % k, [128, D], F32, ts) for k in range(2)]
        stb = [fw.sb("o_st%d" % k, [128, 2, 16], F32, ts) for k in range(2)]
        bnb = [fw.sb("o_bn%d" % k, [128, 8, 128], F32, ts) for k in range(2)]
        zb = [fw.sb("o_z%d" % k, [128, 8, 128], BF16, ts) for k in range(2)]
        ob = [fw.sb("o_o%d" % k, [128, 8, 128], F32, ts) for k in range(2)]
        ogb = [fw.sb("o_og%d" % k, [128, 8, 128], BF16, ts) for k in range(2)]
        for tt in range(NTI):
            k = tt % 2
            (y0, y0_r), (y1, y1_r), (sq, sq_r), (st, st_r) = y0b[k], y1b[k], sqb[k], stb[k]
            (bn, bn_r), (zt, zt_r), (o_, o_r), (og_, og_r_) = bnb[k], zb[k], ob[k], ogb[k]
            rs = slice(tt * 128, (tt + 1) * 128)
            fw.dma(y0[:], self.yd[0][rs, :], writes=[y0_r], sem_res=y0_r)
            fw.dma(y1[:], self.yd[1][rs, :], writes=[y1_r], sem_res=y1_r)
            fw.dma(bn[:], self.bonusd[:, :, rs].rearrange("h p t -> p h t"), writes=[bn_r], sem_res=bn_r)
            fw.dma(zt[:], self.zsd[:, :, rs].rearrange("h p t -> p h t"), writes=[zt_r], sem_res=zt_r)
            fw.op("gpsimd", lambda e, y0=y0, y1=y1: e.tensor_tensor(out=y0[:], in0=y0[:], in1=y1[:], op=ALU.add), reads=[y0_r, y1_r], writes=[y0_r])
            y3 = y0[:].rearrange("p (h d) -> p h d", d=64)
            s3 = sq[:].rearrange("p (h d) -> p h d", d=64)
            fw.op("vector", lambda e, st=st, y3=y3: e.tensor_reduce(out=st[:, 0, :], in_=y3, axis=AX.X, op=ALU.add), reads=[y0_r], writes=[st_r])
            fw.op("vector", lambda e, st=st: e.tensor_scalar(out=st[:, 0, :], in0=st[:, 0, :], scalar1=1.0 / 64.0, scalar2=None, op0=ALU.mult), reads=[st_r], writes=[st_r])
            mbc = st[:, 0, :].unsqueeze(2).to_broadcast([128, 16, 64])
            fw.op("vector", lambda e, y3=y3, mbc=mbc: e.tensor_tensor(out=y3, in0=y3, in1=mbc, op=ALU.subtract), reads=[y0_r, st_r], writes=[y0_r])
            fw.op("gpsimd", lambda e, sq=sq, y0=y0: e.tensor_tensor(out=sq[:], in0=y0[:], in1=y0[:], op=ALU.mult), reads=[y0_r], writes=[sq_r])
            fw.op("vector", lambda e, st=st, s3=s3: e.tensor_reduce(out=st[:, 1, :], in_=s3, axis=AX.X, op=ALU.add), reads=[sq_r], writes=[st_r])
            fw.op("scalar", lambda e, st=st: e.activation(out=st[:, 1, :], in_=st[:, 1, :], func=AF.Sqrt, bias=epsg[:], scale=1.0 / 64.0), reads=[st_r, epsg_r], writes=[st_r])
            fw.op("vector", lambda e, st=st: e.reciprocal(out=st[:, 1, :], in_=st[:, 1, :]), reads=[st_r], writes=[st_r])
            rbc = st[:, 1, :].unsqueeze(2).to_broadcast([128, 16, 64])
            fw.op("vector", lambda e, y3=y3, rbc=rbc: e.tensor_tensor(out=y3, in0=y3, in1=rbc, op=ALU.mult), reads=[y0_r, st_r], writes=[y0_r])
            fw.op("gpsimd", lambda e, y0=y0: e.tensor_tensor(out=y0[:], in0=y0[:], in1=lnw[:], op=ALU.mult), reads=[y0_r, lnw_r], writes=[y0_r])
            fw.op("gpsimd", lambda e, y0=y0: e.tensor_tensor(out=y0[:], in0=y0[:], in1=lnb[:], op=ALU.add), reads=[y0_r, lnb_r], writes=[y0_r])
            for half in range(2):
                tp, tpr = self.pb[(tt * 2 + half) % 8]
                for q in range(4):
                    hc = half * 4 + q
                    fw.op("tensor", lambda e, tp=tp, q=q, hc=hc, y0=y0: e.transpose(tp[:, q * 128:(q + 1) * 128], y0[:, hc * 128:(hc + 1) * 128], self.ident[:]), reads=[y0_r, self.ident_r], writes=[tpr], signal=(q == 3))
                fw.op("vector", lambda e, tp=tp, half=half, o_=o_, bn=bn: e.tensor_tensor(out=o_[:, half * 4:half * 4 + 4, :], in0=tp[:, :].rearrange("p (h t) -> p h t", h=4), in1=bn[:, half * 4:half * 4 + 4, :], op=ALU.add), reads=[tpr, bn_r], writes=[o_r])
            fw.op("gpsimd", lambda e, og_=og_, o_=o_, zt=zt: e.tensor_tensor(out=og_[:], in0=o_[:], in1=zt[:], op=ALU.mult), reads=[o_r, zt_r], writes=[og_r_])
            fw.dma(self.og[:, :, rs].rearrange("h p t -> p h t"), og_[:], reads=[og_r_], sem_res=og_r_)


_CACHE = {}


def get_prog(n_layers=DEPTH, dbg=False):
    key = (n_layers, dbg)
    if key not in _CACHE:
        p = Prog(n_layers, dbg)
        p.build()
        _CACHE[key] = p
    return _CACHE[key]


def make_in_maps(inputs, consts, cores):
    maps = []
    for b in cores:
        m = {}
        for k, shp in INPUT_SHAPES.items():
            a = np.asarray(inputs[k])
            if k in ("x", "ctx"):
                a = a[b]
            elif k == "c":
                a = a[b:b + 1]
            a = np.ascontiguousarray(a, dtype=np.float32).reshape(shp)
            m[k] = a
        for k, v in consts.items():
            m["k_" + k] = v
        maps.append(m)
    return maps


def kernel(**inputs):
    p = get_prog()
    cores = list(range(8))
    in_maps = make_in_maps(inputs, p.consts, cores)
    res = run_bass_kernel_spmd(p.nc, in_maps, core_ids=cores)
    out = np.stack([np.asarray(r["out"]) for r in res.results], axis=0)
    return out.astype(np.float32)
```

```python
import math
from contextlib import ExitStack

import numpy as np
import concourse.bass as bass
import concourse.mybir as mybir
from concourse.bass_utils import run_bass_kernel_spmd

F32 = mybir.dt.float32
BF16 = mybir.dt.bfloat16
AF = mybir.ActivationFunctionType
ALU = mybir.AluOpType
AX = mybir.AxisListType

D = 1024
NL = 4096
NCX = 256
NT = NL + NCX
NTI = NT // 128
DEPTH = 4
NORM_EPS = 1e-6
SUBLN_EPS = 1e-5
GRID_W = 64


class Sem:
    def __init__(self, fw, name):
        self.name = name
        self.handle = fw.stack.enter_context(fw.nc.semaphore(name))
        self.issued = 0
        fw.sems.append(self)


class Res:
    __slots__ = ("name", "writers", "readers", "dsem")

    def __init__(self, name):
        self.name = name
        self.writers = []
        self.readers = []
        self.dsem = None


class Eng:
    def __init__(self, fw, key):
        self.key = key
        self.sem = Sem(fw, "e_" + key)
        self.ops = []
        self.waited = {}
        self.pend_r = []
        self.pend_w = []


class FW:
    def __init__(self, nc, stack):
        self.nc = nc
        self.stack = stack
        self.sems = []
        self.eng = {k: Eng(self, k) for k in ("tensor", "vector", "scalar", "gpsimd", "sync")}
        self.n_ins = 0
        self.free_sems = []

    def sb(self, name, shape, dtype, stack=None):
        self.uid = getattr(self, "uid", 0) + 1
        name = "%s_u%d" % (name, self.uid)
        t = (stack or self.stack).enter_context(self.nc.sbuf_tensor(name, list(shape), dtype))
        return t, self.res(name, stack)

    def res(self, name, stack=None):
        r = Res(name)
        if stack is not None:
            stack.callback(self._release, r)
        return r

    def _release(self, r):
        if r.dsem is not None:
            self.free_sems.append(r.dsem)
            r.dsem = None

    def _get_dsem(self, name):
        if self.free_sems:
            sem = self.free_sems.pop(0)
            for e in self.eng.values():
                assert e.waited.get(sem, 0) >= sem.issued, "semaphore reused before a barrier"
            return sem
        return Sem(self, "d%d" % len(self.sems))

    def ps(self, name, shape, dtype, stack=None):
        t = (stack or self.stack).enter_context(self.nc.psum_tensor(name, list(shape), dtype))
        return t, Res(name)

    def _waits_for(self, eng, reads, writes):
        need = {}

        def add(tok):
            sem, val = tok
            if val is None:
                val = sem.issued
            if need.get(sem, 0) < val:
                need[sem] = val
        for r in reads:
            for t in r.writers:
                add(t)
        for w in writes:
            for t in w.writers:
                add(t)
            for t in w.readers:
                add(t)
        out = []
        for sem, val in need.items():
            if eng.waited.get(sem, 0) >= val:
                continue
            eng.waited[sem] = val
            out.append((sem.handle, val))
        return out

    def _check_pending(self, eng, reads, writes):
        for e in self.eng.values():
            if e is eng or (not e.pend_w and not e.pend_r):
                continue
            for r in reads:
                assert r not in e.pend_w, ("unsignaled write pending", r.name, e.key)
            for w in writes:
                assert w not in e.pend_w and w not in e.pend_r, ("unsignaled access pending", w.name, e.key)

    def op(self, ek, fn, reads=(), writes=(), signal=True):
        eng = self.eng[ek]
        self._check_pending(eng, reads, writes)
        waits = self._waits_for(eng, reads, writes)
        self.n_ins += 1
        if signal:
            eng.sem.issued += 1
            tok = (eng.sem, eng.sem.issued)
            semh = eng.sem.handle

            def run(e, fn=fn, waits=waits, semh=semh):
                for (h, v) in waits:
                    e.wait_ge(h, v)
                fn(e).then_inc(semh, 1)
            rs = list(reads) + eng.pend_r
            ws = list(writes) + eng.pend_w
            eng.pend_r = []
            eng.pend_w = []
            for r in rs:
                r.readers.append(tok)
            for w in ws:
                w.writers = [tok]
                w.readers = []
        else:
            def run(e, fn=fn, waits=waits):
                for (h, v) in waits:
                    e.wait_ge(h, v)
                fn(e)
            eng.pend_r.extend(reads)
            eng.pend_w.extend(writes)
        eng.ops.append(run)

    def dma(self, out, in_, reads=(), writes=(), sem_res=None, ek="sync", **kw):
        eng = self.eng[ek]
        self._check_pending(eng, reads, writes)
        waits = self._waits_for(eng, reads, writes)
        if sem_res.dsem is None:
            sem_res.dsem = self._get_dsem(sem_res.name)
        sem = sem_res.dsem
        sem.issued += 16
        tok = (sem, None)
        semh = sem.handle
        self.n_ins += 1

        def run(e, waits=waits, semh=semh, out=out, in_=in_, kw=kw):
            for (h, v) in waits:
                e.wait_ge(h, v)
            e.dma_start(out=out, in_=in_, **kw).then_inc(semh, 16)
        eng.ops.append(run)
        for r in reads:
            r.readers.append(tok)
        for w in writes:
            w.writers = [tok]
            w.readers = []

    def barrier(self):
        for e in self.eng.values():
            assert not e.pend_r and not e.pend_w
        for e in self.eng.values():
            waits = []
            for s in self.sems:
                if s.issued > 0 and e.waited.get(s, 0) < s.issued:
                    e.waited[s] = s.issued
                    waits.append((s.handle, s.issued))

            def run(en, waits=waits):
                for (h, v) in waits:
                    en.wait_ge(h, v)
            e.ops.append(run)

    def finish(self):
        self.barrier()
        with self.nc.Block() as block:
            for key in ("sync", "tensor", "vector", "scalar", "gpsimd"):
                e = self.eng[key]

                def body(h, e=e):
                    for o in e.ops:
                        o(h)
                getattr(block, key)(body)


class Rec:
    def __init__(self):
        self.items = []

    def op(self, *a, **k):
        self.items.append((0, a, k))

    def dma(self, *a, **k):
        self.items.append((1, a, k))


def interleave(fw, recs):
    idx = [0] * len(recs)
    n = [len(r.items) for r in recs]
    while True:
        best, bf = -1, 2.0
        for i in range(len(recs)):
            if idx[i] < n[i]:
                f = idx[i] / n[i]
                if f < bf:
                    best, bf = i, f
        if best < 0:
            break
        kind, a, k = recs[best].items[idx[best]]
        idx[best] += 1
        (fw.dma if kind else fw.op)(*a, **k)


def rope_tables():
    n = np.arange(NL)
    row = (n // GRID_W).astype(np.float32)
    col = (n % GRID_W).astype(np.float32)
    inv = (10000.0 ** (-np.arange(0, 32, 2, dtype=np.float32) / 32.0)).astype(np.float32)
    C = np.ones((128, NT), np.float32)
    S = np.zeros((128, NT), np.float32)
    for f in range(128):
        axis = (f % 64) // 32
        j = f % 16
        pos = row if axis == 0 else col
        ang = (pos * inv[j]).astype(np.float32)
        C[f, NCX:] = np.cos(ang)
        S[f, NCX:] = np.sin(ang)
    return C, S


def host_consts():
    cs = {}
    cs["ident"] = np.eye(128, dtype=np.float32)
    C, S = rope_tables()
    cs["ropeC"] = C
    cs["ropeS"] = S
    import ml_dtypes
    bf = ml_dtypes.bfloat16
    for name, L in (("L", NL), ("X", NCX)):
        l = np.arange(L, dtype=np.int64)
        ang = (2.0 * np.pi / L) * ((l[:, None] * l[None, :]) % L).astype(np.float64)
        sc = 1.0 / math.sqrt(L)
        cs["dftC" + name] = (np.cos(ang) * sc).astype(np.float32).astype(bf)
        cs["dftS" + name] = (np.sin(ang) * sc).astype(np.float32).astype(bf)
    c = np.arange(128, dtype=np.int64)
    ang = (2.0 * np.pi / 128) * ((c[:, None] * c[None, :]) % 128).astype(np.float64)
    sc = 1.0 / math.sqrt(128.0)
    cs["dftCc"] = (np.cos(ang) * sc).astype(np.float32).astype(bf)
    cs["dftnSc"] = (-np.sin(ang) * sc).astype(np.float32).astype(bf)
    a = np.arange(128)
    same = (a[:, None] // 64) == (a[None, :] // 64)
    lt = (a[:, None] < a[None, :]) & same
    le = (a[:, None] <= a[None, :]) & same
    gt = (a[:, None] > a[None, :]) & same
    ge = (a[:, None] >= a[None, :]) & same
    cs["mN0"] = np.concatenate([lt, le], axis=1).astype(np.float32)
    cs["mL0"] = gt.astype(np.float32)
    cs["mN1"] = np.concatenate([gt, ge], axis=1).astype(np.float32)
    cs["mL1"] = lt.astype(np.float32)
    return cs


INPUT_SHAPES = {
    "x": [NL, D], "c": [1, D], "ctx": [NCX, D], "c_ctx": [1, D], "norm_gain": [4, D], "ada_w": [4, D, 3 * D],
    "ada_b": [4, 3 * D], "final_gain": [1, D], "da_w_in": [2, D, 4 * D], "da_lam_q": [2, 128], "da_lam_k": [2, 128],
    "da_subln_gain": [2, 128], "da_w_out": [2, D, D], "fn_w_in": [1, D, 2 * D], "fn_w_group": [1, 8, 128, 128],
    "fn_w_out": [1, D, D], "rw_w_in": [1, D, 4352], "rw_mu": [1, 3328], "rw_w0": [1, 2, D], "rw_w_up": [1, 2, 64, D],
    "rw_a0": [1, 2, D], "rw_a_up": [1, 2, 64, D], "rw_k_k": [1, D], "rw_k_a": [1, D], "rw_r_k": [1, D],
    "rw_ln_w": [1, D], "rw_ln_b": [1, D], "rw_w_out": [1, D, D],
}


class Prog:
    def __init__(self, n_layers=DEPTH, dbg=False):
        self.n_layers = n_layers
        self.dbg = dbg
        self.nc = bass.Bass("TRN2", target_bir_lowering=False)
        self.consts = host_consts()

    def din(self, name, shape, dt=F32):
        return self.nc.dram_tensor(name, list(shape), dt, kind="ExternalInput").ap()

    def build(self):
        nc = self.nc
        self.I = {k: self.din(k, s) for k, s in INPUT_SHAPES.items()}
        self.Cn = {k: self.din("k_" + k, v.shape, F32 if v.dtype == np.float32 else BF16) for k, v in self.consts.items()}
        self.out = nc.dram_tensor("out", [NL, D], F32, kind="ExternalOutput").ap()
        if self.dbg:
            self.dbg_ctx = nc.dram_tensor("dbg_ctx", [NCX, D], F32, kind="ExternalOutput").ap()
        self.xres = nc.dram_tensor("xres", [NT, D], F32, kind="Internal").ap()
        self.og = nc.dram_tensor("og", [8, 128, NT], BF16, kind="Internal").ap()
        self.zsd = nc.dram_tensor("zsd", [8, 128, NT], BF16, kind="Internal").ap()
        self.zsd_r = [[Res("zsd%d_%d" % (h, c)) for c in range(9)] for h in range(8)]
        self.xres_r = [Res("xres%d" % t) for t in range(NTI)]
        self.og_r = [[Res("og%d_%d" % (h, c)) for c in range(9)] for h in range(8)]
        self.out_r = [Res("out%d" % t) for t in range(NTI)]
        with ExitStack() as st:
            self.st = st
            fw = self.fw = FW(nc, st)
            self.setup_persistent()
            for i in range(self.n_layers):
                self.layer(i)
            fw.finish()
        return nc

    @staticmethod
    def chunk(ci):
        if ci == 0:
            return 0, NCX
        return NCX + (ci - 1) * 512, 512

    def xsrc(self, i, tt):
        if i == 0:
            if tt < 2:
                return self.I["ctx"][tt * 128:(tt + 1) * 128, :], []
            return self.I["x"][(tt - 2) * 128:(tt - 1) * 128, :], []
        return self.xres[tt * 128:(tt + 1) * 128, :], [self.xres_r[tt]]

    def setup_persistent(self):
        fw, st = self.fw, self.st
        self.ident, self.ident_r = fw.sb("ident", [128, 128], F32)
        fw.dma(self.ident[:], self.Cn["ident"][:, :], writes=[self.ident_r], sem_res=self.ident_r)
        self.identb, self.identb_r = fw.sb("identb", [128, 128], BF16)
        fw.op("vector", lambda e: e.tensor_copy(out=self.identb[:], in_=self.ident[:]), reads=[self.ident_r], writes=[self.identb_r])
        self.ones_b, self.ones_b_r = fw.sb("ones_b", [128, 128], BF16)
        fw.op("vector", lambda e: e.memset(self.ones_b[:], 1.0), writes=[self.ones_b_r])
        self.mean_b, self.mean_b_r = fw.sb("mean_b", [128, 128], BF16)
        fw.op("vector", lambda e: e.memset(self.mean_b[:], 1.0 / 128.0), writes=[self.mean_b_r])
        self.epsn, self.epsn_r = fw.sb("epsn", [128, 1], F32)
        fw.op("vector", lambda e: e.memset(self.epsn[:], NORM_EPS), writes=[self.epsn_r])
        self.epss, self.epss_r = fw.sb("epss", [128, 1], F32)
        fw.op("vector", lambda e: e.memset(self.epss[:], SUBLN_EPS), writes=[self.epss_r])
        self.gsc, self.gsc_r = fw.sb("gsc", [128, 8, 2], F32)
        self.shf, self.shf_r = fw.sb("shf", [128, 8, 2], F32)
        self.gate, self.gate_r = fw.sb("gate", [128, 2, D], F32)
        self.s_fm, self.s_fm_r = fw.sb("s_fm", [128, 8, 2], F32)
        self.pb = []
        for b in range(8):
            t, r = fw.ps("pb%d" % b, [128, 512], F32)
            self.pb.append((t, r))
        with ExitStack() as ts:
            cc, cc_r = fw.sb("cc", [2, D], F32, ts)
            fw.dma(cc[0:1, :], self.I["c"][:, :], writes=[cc_r], sem_res=cc_r)
            fw.dma(cc[1:2, :], self.I["c_ctx"][:, :], writes=[cc_r], sem_res=cc_r)
            fw.op("scalar", lambda e: e.activation(out=cc[:], in_=cc[:], func=AF.Silu), reads=[cc_r], writes=[cc_r])
            pt, pr = self.pb[0]
            for kc in range(8):
                fw.op("tensor", lambda e, kc=kc: e.transpose(pt[:, kc * 2:kc * 2 + 2], cc[:, kc * 128:(kc + 1) * 128], self.ident[0:2, 0:2]),
                      reads=[cc_r, self.ident_r], writes=[pr], signal=(kc == 7))
            fw.op("vector", lambda e: e.tensor_copy(out=self.s_fm[:].rearrange("p k w -> p (k w)"), in_=pt[:, 0:16]), reads=[pr], writes=[self.s_fm_r])
            fw.barrier()

    def load_fm(self, dst, dst_r, src2d, n, ts):
        fw = self.fw
        tmp, tmp_r = fw.sb("lfm_tmp%d" % fw.n_ins, [n, 128], F32, ts)
        fw.dma(tmp[:], src2d, writes=[tmp_r], sem_res=tmp_r)
        pt, pr = self.pb[1]
        fw.op("tensor", lambda e: e.transpose(pt[:, 0:n], tmp[:], self.ident[0:n, 0:n]), reads=[tmp_r, self.ident_r], writes=[pr])
        fw.op("vector", lambda e: e.tensor_copy(out=dst, in_=pt[:, 0:n]), reads=[pr], writes=[dst_r])

    def phase_mod(self, i):
        fw = self.fw
        W = self.I["ada_w"][i]
        with ExitStack() as ts:
            self.s_rep, self.s_rep_r = fw.sb("s_rep", [128, 8, 2, 128], F32, ts)
            for kc in range(8):
                for w in range(2):
                    fw.op("gpsimd", lambda e, kc=kc, w=w: e.tensor_copy(out=self.s_rep[:, kc, w, :], in_=self.s_fm[:, kc, w:w + 1].to_broadcast([128, 128])),
                          reads=[self.s_fm_r], writes=[self.s_rep_r])
            bfm, bfm_r = fw.sb("bfm", [128, 16], F32, ts)
            gfm, gfm_r = fw.sb("gfm", [128, 8], F32, ts)
            self.load_fm(bfm[:], bfm_r, self.I["ada_b"][i, 0:2048].rearrange("(n p) -> n p", p=128), 16, ts)
            self.load_fm(gfm[:], gfm_r, self.I["norm_gain"][i, :].rearrange("(n p) -> n p", p=128), 8, ts)
            bg, bg_r = fw.sb("bg", [128, D], F32, ts)
            fw.dma(bg[:], self.I["ada_b"][i, 2048:3072].partition_broadcast(128), writes=[bg_r], sem_res=bg_r)
            wt = [fw.sb("adaw%d" % k, [128, 8, 512], F32, ts) for k in range(2)]
            pt, pr = self.pb[2]
            for cc in range(6):
                w_t, w_r = wt[cc % 2]
                for kc in range(8):
                    fw.dma(w_t[:, kc, :], W[kc * 128:(kc + 1) * 128, cc * 512:(cc + 1) * 512], writes=[w_r], sem_res=w_r)
                if cc < 4:
                    for fl in range(4):
                        fc = cc * 4 + fl
                        for kc in range(8):
                            fw.op("tensor", lambda e, kc=kc, fl=fl, fc=fc, w_t=w_t: e.matmul(pt[:, fc * 2:fc * 2 + 2], lhsT=w_t[:, kc, fl * 128:(fl + 1) * 128], rhs=self.s_fm[:, kc, :], start=(kc == 0), stop=(kc == 7)),
                                  reads=[w_r, self.s_fm_r], writes=[pr], signal=(kc == 7 and fl == 3))
                    if cc == 3:
                        ps3 = pt[:, 0:32].rearrange("p (f w) -> p f w", w=2)
                        for w in range(2):
                            fw.op("vector", lambda e, w=w: e.tensor_tensor(out=self.shf[:, :, w], in0=ps3[:, 0:8, w], in1=bfm[:, 0:8], op=ALU.add),
                                  reads=[pr, bfm_r], writes=[self.shf_r])
                            fw.op("vector", lambda e, w=w: e.tensor_tensor(out=self.gsc[:, :, w], in0=ps3[:, 8:16, w], in1=bfm[:, 8:16], op=ALU.add),
                                  reads=[pr, bfm_r], writes=[self.gsc_r])
                            fw.op("vector", lambda e, w=w: e.scalar_tensor_tensor(out=self.gsc[:, :, w], in0=self.gsc[:, :, w], scalar=1.0, in1=gfm[:], op0=ALU.add, op1=ALU.mult),
                                  reads=[self.gsc_r, gfm_r], writes=[self.gsc_r])
                else:
                    cg = cc - 4
                    for w in range(2):
                        gp, gr = self.pb[3 + w]
                        for kc in range(8):
                            fw.op("tensor", lambda e, kc=kc, w=w, gp=gp, w_t=w_t: e.matmul(gp[:, :], lhsT=self.s_rep[:, kc, w, :], rhs=w_t[:, kc, :], start=(kc == 0), stop=(kc == 7)),
                                  reads=[w_r, self.s_rep_r], writes=[gr], signal=(kc == 7))
                        fw.op("vector", lambda e, w=w, gp=gp, cg=cg: e.tensor_tensor(out=self.gate[:, w, cg * 512:(cg + 1) * 512], in0=gp[:, :], in1=bg[:, cg * 512:(cg + 1) * 512], op=ALU.add),
                              reads=[gr, bg_r], writes=[self.gate_r])
            fw.barrier()

    def phase_norm(self, i, hT, hT_r, ts):
        fw = self.fw
        xt = [fw.sb("nx%d" % k, [128, D], F32, ts) for k in range(3)]
        sq, sq_r = fw.sb("nsq", [128, D], BF16, ts)
        ss = [fw.sb("nss%d" % k, [128, 1], F32, ts) for k in range(2)]
        dg = [fw.sb("ndg%d" % k, [128, 128], F32, ts) for k in range(2)]
        for tt in range(NTI):
            x_t, x_r = xt[tt % 3]
            s_t, s_r = ss[tt % 2]
            d_t, d_r = dg[tt % 2]
            w = 1 if tt < 2 else 0
            src, src_r = self.xsrc(i, tt)
            fw.dma(x_t[:], src, reads=src_r, writes=[x_r], sem_res=x_r)
            fw.op("scalar", lambda e, x_t=x_t, s_t=s_t: e.activation(out=sq[:], in_=x_t[:], func=AF.Square, accum_out=s_t[:]), reads=[x_r], writes=[sq_r, s_r])
            fw.op("scalar", lambda e, s_t=s_t: e.activation(out=s_t[:], in_=s_t[:], func=AF.Sqrt, bias=self.epsn[:], scale=1.0 / D), reads=[s_r, self.epsn_r], writes=[s_r])
            fw.op("vector", lambda e, s_t=s_t: e.reciprocal(out=s_t[:], in_=s_t[:]), reads=[s_r], writes=[s_r])
            fw.op("vector", lambda e, s_t=s_t, d_t=d_t: e.tensor_scalar(out=d_t[:], in0=self.ident[:], scalar1=s_t[:, 0:1], scalar2=None, op0=ALU.mult), reads=[s_r, self.ident_r], writes=[d_r])
            for half in range(2):
                pt, pr = self.pb[(tt * 2 + half) % 4]
                for k4 in range(4):
                    kc = half * 4 + k4
                    fw.op("tensor", lambda e, kc=kc, k4=k4, pt=pt, x_t=x_t, d_t=d_t: e.matmul(pt[:, k4 * 128:(k4 + 1) * 128], lhsT=x_t[:, kc * 128:(kc + 1) * 128], rhs=d_t[:], start=True, stop=True),
                          reads=[x_r, d_r], writes=[pr], signal=(k4 == 3))
                for k4 in range(4):
                    kc = half * 4 + k4
                    ek = "scalar" if k4 % 2 == 0 else "vector"
                    if ek == "scalar":
                        fw.op("scalar", lambda e, kc=kc, k4=k4, pt=pt, tt=tt, w=w: e.activation(out=hT[:, kc, tt * 128:(tt + 1) * 128], in_=pt[:, k4 * 128:(k4 + 1) * 128], func=AF.Identity, bias=self.shf[:, kc, w:w + 1], scale=self.gsc[:, kc, w:w + 1]),
                              reads=[pr, self.shf_r, self.gsc_r], writes=[hT_r[tt]])
                    else:
                        fw.op("vector", lambda e, kc=kc, k4=k4, pt=pt, tt=tt, w=w: e.tensor_scalar(out=hT[:, kc, tt * 128:(tt + 1) * 128], in0=pt[:, k4 * 128:(k4 + 1) * 128], scalar1=self.gsc[:, kc, w:w + 1], scalar2=self.shf[:, kc, w:w + 1], op0=ALU.mult, op1=ALU.add),
                              reads=[pr, self.shf_r, self.gsc_r], writes=[hT_r[tt]])

    def phase_out(self, i, w_out, ts):
        fw = self.fw
        last = (i == DEPTH - 1)
        wst = [fw.sb("wo_st%d" % k, [128, D], F32, ts) for k in range(2)]
        wb, wb_r = fw.sb("wo_b", [128, 8, D], BF16, ts)
        for kc in range(8):
            s_t, s_r = wst[kc % 2]
            fw.dma(s_t[:], w_out[kc * 128:(kc + 1) * 128, :], writes=[s_r], sem_res=s_r)
            fw.op("gpsimd", lambda e, kc=kc, s_t=s_t: e.tensor_copy(out=wb[:, kc, :], in_=s_t[:]), reads=[s_r], writes=[wb_r])
        ogt = [fw.sb("po_og%d" % k, [128, 8, 512], BF16, ts) for k in range(2)]
        xt = [fw.sb("po_x%d" % k, [128, D], F32, ts) for k in range(2)]
        yt = [fw.sb("po_y%d" % k, [128, D], F32, ts) for k in range(2)]
        if last:
            fg, fg_r = fw.sb("po_fg", [128, D], F32, ts)
            fw.dma(fg[:], self.I["final_gain"][0, :].partition_broadcast(128), writes=[fg_r], sem_res=fg_r)
            sq, sq_r = fw.sb("po_sq", [128, D], BF16, ts)
            ss = [fw.sb("po_ss%d" % k, [128, 1], F32, ts) for k in range(2)]
        cis = range(1, 9) if last else range(9)
        n = 0
        for ci in cis:
            t0, tn = self.chunk(ci)
            o_t, o_r = ogt[ci % 2]
            for hc in range(8):
                fw.dma(o_t[:, hc, 0:tn], self.og[hc, :, t0:t0 + tn], reads=[self.og_r[hc][ci]], writes=[o_r], sem_res=o_r)
            for tl in range(tn // 128):
                tt = t0 // 128 + tl
                w = 1 if tt < 2 else 0
                x_t, x_r = xt[n % 2]
                y_t, y_r = yt[n % 2]
                src, src_r = self.xsrc(i, tt)
                fw.dma(x_t[:], src, reads=src_r, writes=[x_r], sem_res=x_r)
                for half in range(2):
                    pt, pr = self.pb[(n * 2 + half) % 4]
                    for hc in range(8):
                        fw.op("tensor", lambda e, hc=hc, half=half, pt=pt, o_t=o_t, tl=tl: e.matmul(pt[:, :], lhsT=o_t[:, hc, tl * 128:(tl + 1) * 128], rhs=wb[:, hc, half * 512:(half + 1) * 512], start=(hc == 0), stop=(hc == 7)),
                              reads=[o_r, wb_r], writes=[pr], signal=(hc == 7))
                    fw.op("vector", lambda e, half=half, pt=pt, y_t=y_t, w=w: e.tensor_tensor(out=y_t[:, half * 512:(half + 1) * 512], in0=pt[:, :], in1=self.gate[:, w, half * 512:(half + 1) * 512], op=ALU.mult),
                          reads=[pr, self.gate_r], writes=[y_r])
                fw.op("gpsimd", lambda e, y_t=y_t, x_t=x_t: e.tensor_tensor(out=y_t[:], in0=y_t[:], in1=x_t[:], op=ALU.add), reads=[y_r, x_r], writes=[y_r])
                if not last:
                    fw.dma(self.xres[tt * 128:(tt + 1) * 128, :], y_t[:], reads=[y_r], writes=[self.xres_r[tt]], sem_res=y_r)
                    if self.dbg and i == self.n_layers - 1:
                        if tt < 2:
                            fw.dma(self.dbg_ctx[tt * 128:(tt + 1) * 128, :], y_t[:], reads=[y_r], writes=[self.out_r[tt]], sem_res=y_r)
                        else:
                            fw.dma(self.out[(tt - 2) * 128:(tt - 1) * 128, :], y_t[:], reads=[y_r], writes=[self.out_r[tt]], sem_res=y_r)
                else:
                    s_t, s_r = ss[n % 2]
                    fw.op("scalar", lambda e, y_t=y_t, s_t=s_t: e.activation(out=sq[:], in_=y_t[:], func=AF.Square, accum_out=s_t[:]), reads=[y_r], writes=[sq_r, s_r])
                    fw.op("scalar", lambda e, s_t=s_t: e.activation(out=s_t[:], in_=s_t[:], func=AF.Sqrt, bias=self.epsn[:], scale=1.0 / D), reads=[s_r, self.epsn_r], writes=[s_r])
                    fw.op("vector", lambda e, s_t=s_t: e.reciprocal(out=s_t[:], in_=s_t[:]), reads=[s_r], writes=[s_r])
                    fw.op("vector", lambda e, y_t=y_t, s_t=s_t: e.scalar_tensor_tensor(out=y_t[:], in0=y_t[:], scalar=s_t[:, 0:1], in1=fg[:], op0=ALU.mult, op1=ALU.mult), reads=[y_r, s_r, fg_r], writes=[y_r])
                    fw.dma(self.out[(tt - 2) * 128:(tt - 1) * 128, :], y_t[:], reads=[y_r], writes=[self.out_r[tt]], sem_res=y_r)
                n += 1

    def layer(self, i):
        fw = self.fw
        kind = i % 3
        j = i // 3
        self.phase_mod(i)
        with ExitStack() as ts0:
            pre = None
            if kind == 1:
                pre = self.fnet_prealloc(ts0)
            elif kind == 2:
                pre = self.rwkv_prealloc(ts0)
            with ExitStack() as ts:
                hT, _ = fw.sb("hT", [128, 8, NT], BF16, ts)
                hT_r = [Res("hT%d" % t) for t in range(NTI)]
                with ExitStack() as ts2:
                    self.phase_norm(i, hT, hT_r, ts2)
                    fw.barrier()
                with ExitStack() as ts2:
                    if kind == 0:
                        self.mixer_attn(i, j, hT, hT_r, ts2)
                    elif kind == 1:
                        self.fnet_part1(i, j, hT, hT_r, pre, ts2)
                    else:
                        self.rwkv_part1(i, j, hT, hT_r, pre, ts2)
                    fw.barrier()
            if kind == 1:
                with ExitStack() as ts2:
                    self.fnet_part2(i, j, pre, ts2)
                    fw.barrier()
            elif kind == 2:
                self.rwkv_part2(i, j, pre, ts0)
        with ExitStack() as ts:
            w_out = {0: self.I["da_w_out"], 1: self.I["fn_w_out"], 2: self.I["rw_w_out"]}[kind][j]
            self.phase_out(i, w_out, ts)
            fw.barrier()

    def mixer_attn(self, i, j, hT, hT_r, ts):
        fw = self.fw
        last = (i == DEPTH - 1)
        lambda_init = 0.8 - 0.6 * math.exp(-0.3 * i)
        Win = self.I["da_w_in"][j]
        rC, rC_r = fw.sb("ropeC", [128, NT], F32, ts)
        rS, rS_r = fw.sb("ropeS", [128, NT], F32, ts)
        fw.dma(rC[:], self.Cn["ropeC"][:, :], writes=[rC_r], sem_res=rC_r)
        fw.dma(rS[:], self.Cn["ropeS"][:, :], writes=[rS_r], sem_res=rS_r)
        nlam, nlam_r = fw.sb("nlam", [128, 1], F32, ts)
        lq, lq_r = fw.sb("lq", [128, 128], F32, ts)
        lk, lk_r = fw.sb("lk", [128, 128], F32, ts)
        l2, l2_r = fw.sb("l2", [128, 2], F32, ts)
        fw.dma(lq[:], self.I["da_lam_q"][j, :].partition_broadcast(128), writes=[lq_r], sem_res=lq_r)
        fw.dma(lk[:], self.I["da_lam_k"][j, :].partition_broadcast(128), writes=[lk_r], sem_res=lk_r)
        fw.op("vector", lambda e: e.tensor_tensor(out=lq[:], in0=lq[:], in1=lk[:], op=ALU.mult), reads=[lq_r, lk_r], writes=[lq_r])
        fw.op("vector", lambda e: e.tensor_reduce(out=l2[:], in_=lq[:].rearrange("p (z d) -> p z d", z=2), axis=AX.X, op=ALU.add), reads=[lq_r], writes=[l2_r])
        fw.op("scalar", lambda e: e.activation(out=l2[:], in_=l2[:], func=AF.Exp), reads=[l2_r], writes=[l2_r])
        fw.op("vector", lambda e: e.tensor_tensor(out=nlam[:], in0=l2[:, 1:2], in1=l2[:, 0:1], op=ALU.subtract), reads=[l2_r], writes=[nlam_r])
        fw.op("vector", lambda e: e.tensor_scalar(out=nlam[:], in0=nlam[:], scalar1=-lambda_init, scalar2=None, op0=ALU.add), reads=[nlam_r], writes=[nlam_r])
        sg, sg_r = fw.sb("sg", [128, 1], F32, ts)
        self.load_fm(sg[:], sg_r, self.I["da_subln_gain"][j:j + 1, :], 1, ts)
        fw.op("vector", lambda e: e.tensor_scalar(out=sg[:], in0=sg[:], scalar1=1.0 - lambda_init, scalar2=None, op0=ALU.mult), reads=[sg_r], writes=[sg_r])

        wst = [fw.sb("aw_st%d" % k, [128, 8, 128], F32, ts) for k in range(2)]
        WS = [{nm: fw.sb("%s_%d" % (nm, k), [128, 8, 128], BF16, ts) for nm in ("wq", "wq2", "wk", "wk2", "wv", "wz")} for k in range(2)]
        qT, qT_r = fw.sb("qT", [128, NT], BF16, ts)
        kT, kT_r = None, None
        kTz = [fw.sb("kTz%d" % z, [128, NT], BF16, ts) for z in range(2)]
        for z in range(2):
            fw.op("gpsimd", lambda e, z=z: e.memset(kTz[z][0][:], 0.0), writes=[kTz[z][1]])
        zs, zs_r = fw.sb("zs", [128, NT], BF16, ts)
        vt, vt_r = fw.sb("vt", [128, NTI, 128], BF16, ts)
        t1 = [fw.sb("rp1_%d" % k, [128, 512], F32, ts) for k in range(1)] * 2
        t2 = [fw.sb("rp2_%d" % k, [128, 512], F32, ts) for k in range(1)] * 2
        Eb = [fw.sb("E%d" % k, [128, 512], BF16, ts) for k in range(4)]
        r1, r1_r = fw.sb("ep_r1", [128, 512], F32, ts)
        a1, a1_r = fw.sb("ep_a1", [128, 512], F32, ts)
        r2, r2_r = r1, r1_r
        a2, a2_r = fw.sb("ep_a2", [128, 512], F32, ts)
        sqb, sqb_r = fw.sb("ep_sq", [128, 512], BF16, ts)
        rs, rs_r = r1, r1_r
        ogt = [fw.sb("ep_og%d" % k, [128, 512], BF16, ts) for k in range(2)]

        def rot_cast(dst, dst2, dst_r, dst2_r, s_t, s_r, scale):
            fw.op("gpsimd", lambda e: e.tensor_scalar(out=dst[:], in0=s_t[:], scalar1=scale, scalar2=None, op0=ALU.mult), reads=[s_r], writes=[dst_r])
            sv = s_t[:].rearrange("p k (b h j) -> p k b h j", b=4, h=2)
            dv = dst2[:].rearrange("p k (b h j) -> p k b h j", b=4, h=2)
            for kc in range(0, 8, 4):
                fw.op("gpsimd", lambda e, kc=kc: e.tensor_scalar(out=dv[:, kc:kc + 4, :, 0, :], in0=sv[:, kc:kc + 4, :, 1, :], scalar1=-scale, scalar2=None, op0=ALU.mult), reads=[s_r], writes=[dst2_r])
                fw.op("gpsimd", lambda e, kc=kc: e.tensor_scalar(out=dv[:, kc:kc + 4, :, 1, :], in0=sv[:, kc:kc + 4, :, 0, :], scalar1=scale, scalar2=None, op0=ALU.mult), reads=[s_r], writes=[dst2_r])

        nst = [0]

        def load_weights(hd):
            Wd = WS[hd % 2]
            for which in range(4):
                s_t, s_r = wst[nst[0] % 2]
                nst[0] += 1
                col0 = which * D + hd * 128
                fw.dma(s_t[:], Win[:, col0:col0 + 128].rearrange("(k p) c -> p k c", p=128), writes=[s_r], sem_res=s_r)
                if which == 0:
                    rot_cast(Wd["wq"][0], Wd["wq2"][0], Wd["wq"][1], Wd["wq2"][1], s_t, s_r, 0.125)
                elif which == 1:
                    rot_cast(Wd["wk"][0], Wd["wk2"][0], Wd["wk"][1], Wd["wk2"][1], s_t, s_r, 1.0)
                elif which == 2:
                    fw.op("gpsimd", lambda e, s_t=s_t, d=Wd["wv"][0]: e.tensor_copy(out=d[:], in_=s_t[:]), reads=[s_r], writes=[Wd["wv"][1]])
                else:
                    fw.op("gpsimd", lambda e, s_t=s_t, d=Wd["wz"][0]: e.tensor_copy(out=d[:], in_=s_t[:]), reads=[s_r], writes=[Wd["wz"][1]])

        ncnt = 0
        load_weights(0)
        for hd in range(8):
            Wd = WS[hd % 2]
            (wq, wq_r), (wq2, wq2_r), (wk, wk_r), (wk2, wk2_r), (wv, wv_r), (wz, wz_r) = [Wd[nm] for nm in ("wq", "wq2", "wk", "wk2", "wv", "wz")]
            for ci in range(9):
                t0, tn = self.chunk(ci)
                tts = list(range(t0 // 128, (t0 + tn) // 128))
                hrs = [hT_r[t] for t in tts]
                banks = [self.pb[(ci * 6 + b) % 8] for b in range(6)]
                for b, (wt_, wr_) in enumerate([(wq, wq_r), (wq2, wq2_r), (wk, wk_r), (wk2, wk2_r), (wz, wz_r)]):
                    pt, pr = banks[b]
                    for kc in range(8):
                        fw.op("tensor", lambda e, kc=kc, pt=pt, wt_=wt_, t0=t0, tn=tn: e.matmul(pt[:, 0:tn], lhsT=wt_[:, kc, :], rhs=hT[:, kc, t0:t0 + tn], start=(kc == 0), stop=(kc == 7)),
                              reads=[wr_] + hrs, writes=[pr], signal=(kc == 7))
                pt, pr = banks[5]
                for tl, tt in enumerate(tts):
                    for kc in range(8):
                        fw.op("tensor", lambda e, kc=kc, pt=pt, tl=tl, tt=tt, wv=wv: e.matmul(pt[:, tl * 128:(tl + 1) * 128], lhsT=hT[:, kc, tt * 128:(tt + 1) * 128], rhs=wv[:, kc, :], start=(kc == 0), stop=(kc == 7)),
                              reads=[wv_r, hT_r[tt]], writes=[pr], signal=(kc == 7 and tl == len(tts) - 1))
                for b0, dst, dst_r in ((0, qT, qT_r), (2, kT, kT_r)):
                    p1, p1r = banks[b0]
                    p2, p2r = banks[b0 + 1]
                    ta, ta_r = t1[ncnt % 2]
                    tb, tb_r = t2[ncnt % 2]
                    ncnt += 1
                    fw.op("vector", lambda e, p1=p1, ta=ta, t0=t0, tn=tn: e.tensor_tensor(out=ta[:, 0:tn], in0=p1[:, 0:tn], in1=rC[:, t0:t0 + tn], op=ALU.mult), reads=[p1r, rC_r], writes=[ta_r])
                    fw.op("vector", lambda e, p2=p2, tb=tb, t0=t0, tn=tn: e.tensor_tensor(out=tb[:, 0:tn], in0=p2[:, 0:tn], in1=rS[:, t0:t0 + tn], op=ALU.mult), reads=[p2r, rS_r], writes=[tb_r])
                    if b0 == 0:
                        fw.op("gpsimd", lambda e, ta=ta, tb=tb, dst=dst, t0=t0, tn=tn: e.tensor_tensor(out=dst[:, t0:t0 + tn], in0=ta[:, 0:tn], in1=tb[:, 0:tn], op=ALU.add), reads=[ta_r, tb_r], writes=[dst_r])
                    else:
                        for z in range(2):
                            zr = slice(z * 64, (z + 1) * 64)
                            fw.op("gpsimd", lambda e, ta=ta, tb=tb, z=z, zr=zr, t0=t0, tn=tn: e.tensor_tensor(out=kTz[z][0][zr, t0:t0 + tn], in0=ta[zr, 0:tn], in1=tb[zr, 0:tn], op=ALU.add), reads=[ta_r, tb_r], writes=[kTz[z][1]])
                p4, p4r = banks[4]
                fw.op("scalar", lambda e, p4=p4, t0=t0, tn=tn: e.activation(out=zs[:, t0:t0 + tn], in_=p4[:, 0:tn], func=AF.Silu), reads=[p4r], writes=[zs_r])
                p5, p5r = banks[5]
                fw.op("vector", lambda e, p5=p5, t0=t0, tn=tn: e.tensor_copy(out=vt[:, t0 // 128:(t0 + tn) // 128, :].rearrange("p t e -> p (t e)"), in_=p5[:, 0:tn]), reads=[p5r], writes=[vt_r])
            if hd + 1 < 8:
                load_weights(hd + 1)
            groups = ([] if last else [(0, list(range(2)))]) + [(ci, list(range(NTI))) for ci in range(1, 9)]
            items = []
            for gi, (ci, kts) in enumerate(groups):
                for z in range(2):
                    for idx, kt in enumerate(kts):
                        items.append((gi, ci, z, idx, kt, len(kts)))
            LA = 2
            Ebuf = {}

            def emit_front(n):
                gi, ci, z, idx, kt, nk = items[n]
                q0, qn = self.chunk(ci)
                sp, spr = self.pb[n % 3]
                E_t, E_r = Eb[n % len(Eb)]
                Ebuf[n] = (E_t, E_r)
                fw.op("tensor", lambda e, sp=sp, kt=kt, z=z, q0=q0, qn=qn: e.matmul(sp[:, 0:qn], lhsT=kTz[z][0][:, kt * 128:(kt + 1) * 128], rhs=qT[:, q0:q0 + qn], start=True, stop=True),
                      reads=[kTz[z][1], qT_r], writes=[spr])
                fw.op("scalar", lambda e, sp=sp, E_t=E_t, qn=qn: e.activation(out=E_t[:, 0:qn], in_=sp[:, 0:qn], func=AF.Exp), reads=[spr], writes=[E_r])

            def emit_back(n):
                gi, ci, z, idx, kt, nk = items[n]
                q0, qn = self.chunk(ci)
                E_t, E_r = Ebuf.pop(n)
                po, por = self.pb[3 + z * 2]
                psm, psr = self.pb[4 + z * 2]
                fw.op("tensor", lambda e, po=po, E_t=E_t, kt=kt, qn=qn, idx=idx, nk=nk: e.matmul(po[:, 0:qn], lhsT=vt[:, kt, :], rhs=E_t[:, 0:qn], start=(idx == 0), stop=(idx == nk - 1)),
                      reads=[vt_r, E_r], writes=[por], signal=False)
                fw.op("tensor", lambda e, psm=psm, E_t=E_t, qn=qn, idx=idx, nk=nk: e.matmul(psm[:, 0:qn], lhsT=self.ones_b[:], rhs=E_t[:, 0:qn], start=(idx == 0), stop=(idx == nk - 1)),
                      reads=[self.ones_b_r, E_r], writes=[psr])
                if z == 1 and idx == nk - 1:
                    epilogue(ci)

            def epilogue(ci):
                q0, qn = self.chunk(ci)
                po1, por1 = self.pb[3]
                ps1, psr1 = self.pb[4]
                po2, por2 = self.pb[5]
                ps2, psr2 = self.pb[6]
                fw.op("vector", lambda e, qn=qn: e.reciprocal(out=r1[:, 0:qn], in_=ps1[:, 0:qn]), reads=[psr1], writes=[r1_r])
                fw.op("vector", lambda e, qn=qn: e.tensor_tensor(out=a1[:, 0:qn], in0=po1[:, 0:qn], in1=r1[:, 0:qn], op=ALU.mult), reads=[por1, r1_r], writes=[a1_r])
                fw.op("vector", lambda e, qn=qn: e.reciprocal(out=r2[:, 0:qn], in_=ps2[:, 0:qn]), reads=[psr2], writes=[r2_r])
                fw.op("vector", lambda e, qn=qn: e.tensor_tensor(out=a2[:, 0:qn], in0=po2[:, 0:qn], in1=r2[:, 0:qn], op=ALU.mult), reads=[por2, r2_r], writes=[a2_r])
                fw.op("vector", lambda e, qn=qn: e.scalar_tensor_tensor(out=a1[:, 0:qn], in0=a2[:, 0:qn], scalar=nlam[:, 0:1], in1=a1[:, 0:qn], op0=ALU.mult, op1=ALU.add), reads=[a1_r, a2_r, nlam_r], writes=[a1_r])
                fw.op("vector", lambda e, qn=qn: e.tensor_tensor(out=sqb[:, 0:qn], in0=a1[:, 0:qn], in1=a1[:, 0:qn], op=ALU.mult), reads=[a1_r], writes=[sqb_r])
                mp, mpr = self.pb[7]
                fw.op("tensor", lambda e, qn=qn: e.matmul(mp[:, 0:qn], lhsT=self.mean_b[:], rhs=sqb[:, 0:qn], start=True, stop=True), reads=[self.mean_b_r, sqb_r], writes=[mpr])
                fw.op("scalar", lambda e, qn=qn: e.activation(out=rs[:, 0:qn], in_=mp[:, 0:qn], func=AF.Ln, bias=self.epss[:], scale=1.0), reads=[mpr, self.epss_r], writes=[rs_r])
                fw.op("scalar", lambda e, qn=qn: e.activation(out=rs[:, 0:qn], in_=rs[:, 0:qn], func=AF.Exp, scale=-0.5), reads=[rs_r], writes=[rs_r])
                fw.op("vector", lambda e, qn=qn: e.tensor_tensor(out=a1[:, 0:qn], in0=a1[:, 0:qn], in1=rs[:, 0:qn], op=ALU.mult), reads=[a1_r, rs_r], writes=[a1_r])
                og_t, og_r_ = ogt[ci % 2]
                fw.op("vector", lambda e, og_t=og_t, q0=q0, qn=qn: e.scalar_tensor_tensor(out=og_t[:, 0:qn], in0=a1[:, 0:qn], scalar=sg[:, 0:1], in1=zs[:, q0:q0 + qn], op0=ALU.mult, op1=ALU.mult), reads=[a1_r, sg_r, zs_r], writes=[og_r_])
                fw.dma(self.og[hd, :, q0:q0 + qn], og_t[:, 0:qn], reads=[og_r_], writes=[self.og_r[hd][ci]], sem_res=og_r_)

            for n in range(len(items) + LA):
                if n < len(items):
                    emit_front(n)
                if n - LA >= 0:
                    emit_back(n - LA)

    def load_w_bf16(self, dst, dst_r, w2d, ncols, ts, name):
        fw = self.fw
        wst = [fw.sb("%s_st%d" % (name, k), [128, ncols], F32, ts) for k in range(2)]
        for kc in range(8):
            s_t, s_r = wst[kc % 2]
            fw.dma(s_t[:], w2d[kc * 128:(kc + 1) * 128, :], writes=[s_r], sem_res=s_r)
            fw.op("gpsimd", lambda e, kc=kc, s_t=s_t: e.tensor_copy(out=dst[:, kc, :], in_=s_t[:]), reads=[s_r], writes=[dst_r])

    def gate_proj(self, wz, wz_r, hT, hT_r, ts):
        fw = self.fw
        zt = [fw.sb("gz%d" % k, [128, 512], BF16, ts) for k in range(3)]
        n = 0
        for ci in range(9):
            t0, tn = self.chunk(ci)
            hrs = [hT_r[t] for t in range(t0 // 128, (t0 + tn) // 128)]
            for fc in range(8):
                pt, pr = self.pb[n % 4]
                z_t, z_r = zt[n % 3]
                n += 1
                for kc in range(8):
                    fw.op("tensor", lambda e, kc=kc, pt=pt, fc=fc, t0=t0, tn=tn: e.matmul(pt[:, 0:tn], lhsT=wz[:, kc, fc * 128:(fc + 1) * 128], rhs=hT[:, kc, t0:t0 + tn], start=(kc == 0), stop=(kc == 7)),
                          reads=[wz_r] + hrs, writes=[pr], signal=(kc == 7))
                fw.op("scalar", lambda e, pt=pt, z_t=z_t, tn=tn: e.activation(out=z_t[:, 0:tn], in_=pt[:, 0:tn], func=AF.Silu), reads=[pr], writes=[z_r])
                fw.dma(self.zsd[fc, :, t0:t0 + tn], z_t[:, 0:tn], reads=[z_r], writes=[self.zsd_r[fc][ci]], sem_res=z_r)

    def fnet_prealloc(self, ts):
        fw = self.fw
        Utm, _ = fw.sb("Utm", [128, NTI, D], BF16, ts)
        Utm_r = [Res("Utm%d" % t) for t in range(NTI)]
        return Utm, Utm_r

    def fnet_part1(self, i, j, hT, hT_r, pre, ts):
        fw = self.fw
        Utm, Utm_r = pre
        Win = self.I["fn_w_in"][j]
        wu, wu_r = fw.sb("fwu", [128, 8, D], BF16, ts)
        wz, wz_r = fw.sb("fwz", [128, 8, D], BF16, ts)
        self.load_w_bf16(wu, wu_r, Win[:, 0:D], D, ts, "fwu")
        self.load_w_bf16(wz, wz_r, Win[:, D:2 * D], D, ts, "fwz")
        n = 0
        for tt in range(NTI):
            for half in range(2):
                pt, pr = self.pb[4 + n % 4]
                for kc in range(8):
                    fw.op("tensor", lambda e, kc=kc, pt=pt, tt=tt, half=half: e.matmul(pt[:, :], lhsT=hT[:, kc, tt * 128:(tt + 1) * 128], rhs=wu[:, kc, half * 512:(half + 1) * 512], start=(kc == 0), stop=(kc == 7)),
                          reads=[wu_r, hT_r[tt]], writes=[pr], signal=(kc == 7))
                if n % 2 == 0:
                    fw.op("vector", lambda e, pt=pt, tt=tt, half=half: e.tensor_copy(out=Utm[:, tt, half * 512:(half + 1) * 512], in_=pt[:, :]), reads=[pr], writes=[Utm_r[tt]])
                else:
                    fw.op("scalar", lambda e, pt=pt, tt=tt, half=half: e.copy(out=Utm[:, tt, half * 512:(half + 1) * 512], in_=pt[:, :]), reads=[pr], writes=[Utm_r[tt]])
                n += 1
        self.gate_proj(wz, wz_r, hT, hT_r, ts)

    def fnet_part2(self, i, j, pre, ts):
        fw = self.fw
        Utm, Utm_r = pre
        cc_, cc_r = fw.sb("fCc", [128, 128], BF16, ts)
        sc_, sc_r = fw.sb("fSc", [128, 128], BF16, ts)
        fw.dma(cc_[:], self.Cn["dftCc"][:, :], writes=[cc_r], sem_res=cc_r)
        fw.dma(sc_[:], self.Cn["dftnSc"][:, :], writes=[sc_r], sem_res=sc_r)
        wgs, wgs_r = fw.sb("fwg_st", [128, 8, 128], F32, ts)
        wg, wg_r = fw.sb("fwg", [128, 8, 128], BF16, ts)
        fw.dma(wgs[:], self.I["fn_w_group"][j].rearrange("g c e -> c g e"), writes=[wgs_r], sem_res=wgs_r)
        fw.op("gpsimd", lambda e: e.tensor_copy(out=wg[:], in_=wgs[:]), reads=[wgs_r], writes=[wg_r])
        CB, _ = fw.sb("fCB", [128, 32, 512], BF16, ts)
        SB, _ = fw.sb("fSB", [128, 32, 512], BF16, ts)
        CB_r = [fw.res("fCB%d" % k, ts) for k in range(4)]
        SB_r = [fw.res("fSB%d" % k, ts) for k in range(4)]
        Pb = [fw.sb("fPb%d" % k, [128, 512], BF16, ts) for k in range(2)]
        Qb = [fw.sb("fQb%d" % k, [128, 512], BF16, ts) for k in range(2)]
        Fb = [fw.sb("fFb%d" % k, [128, 512], BF16, ts) for k in range(2)]
        zt = [fw.sb("fzt%d" % k, [128, 512], BF16, ts) for k in range(2)]
        ot = [fw.sb("fot%d" % k, [128, 512], BF16, ts) for k in range(2)]
        last = (i == DEPTH - 1)
        n = 0
        for ci in (range(1, 9) if last else range(9)):
            t0, tn = self.chunk(ci)
            if ci == 0:
                nlt, lt0 = 2, 0
                for k in range(2):
                    fw.dma(CB[:, k, 0:256], self.Cn["dftCX"][k * 128:(k + 1) * 128, :], writes=[CB_r[0]], sem_res=CB_r[0])
                    fw.dma(SB[:, k, 0:256], self.Cn["dftSX"][k * 128:(k + 1) * 128, :], writes=[SB_r[0]], sem_res=SB_r[0])
            else:
                nlt, lt0 = 32, 2
                c0 = (ci - 1) * 512
                for k in range(4):
                    fw.dma(CB[:, k * 8:(k + 1) * 8, :], self.Cn["dftCL"][k * 1024:(k + 1) * 1024, c0:c0 + 512].rearrange("(t p) c -> p t c", p=128), writes=[CB_r[k]], sem_res=CB_r[k])
                    fw.dma(SB[:, k * 8:(k + 1) * 8, :], self.Cn["dftSL"][k * 1024:(k + 1) * 1024, c0:c0 + 512].rearrange("(t p) c -> p t c", p=128), writes=[SB_r[k]], sem_res=SB_r[k])
            for g in range(8):
                pp, ppr = self.pb[(n % 2) * 2]
                qp, qpr = self.pb[(n % 2) * 2 + 1]
                fp, fpr = self.pb[4 + n % 2]
                yp, ypr = self.pb[6 + n % 2]
                P_t, P_r = Pb[n % 2]
                Q_t, Q_r = Qb[n % 2]
                F_t, F_r = Fb[n % 2]
                z_t, z_r = zt[n % 2]
                o_t, o_r = ot[n % 2]
                n += 1
                fw.dma(z_t[:, 0:tn], self.zsd[g, :, t0:t0 + tn], reads=[self.zsd_r[g][ci]], writes=[z_r], sem_res=z_r)
                for (acc, accr, buf, bufr) in ((pp, ppr, CB, CB_r), (qp, qpr, SB, SB_r)):
                    for lt in range(nlt):
                        fw.op("tensor", lambda e, acc=acc, buf=buf, lt=lt, g=g, tn=tn, nlt=nlt, lt0=lt0: e.matmul(acc[:, 0:tn], lhsT=Utm[:, lt0 + lt, g * 128:(g + 1) * 128], rhs=buf[:, lt, 0:tn], start=(lt == 0), stop=(lt == nlt - 1)),
                              reads=[Utm_r[lt0 + lt], bufr[lt // 8]], writes=[accr], signal=(lt == nlt - 1 or lt % 8 == 7))
                fw.op("scalar", lambda e, pp=pp, P_t=P_t, tn=tn: e.copy(out=P_t[:, 0:tn], in_=pp[:, 0:tn]), reads=[ppr], writes=[P_r])
                fw.op("vector", lambda e, qp=qp, Q_t=Q_t, tn=tn: e.tensor_copy(out=Q_t[:, 0:tn], in_=qp[:, 0:tn]), reads=[qpr], writes=[Q_r])
                fw.op("tensor", lambda e, fp=fp, P_t=P_t, tn=tn: e.matmul(fp[:, 0:tn], lhsT=cc_[:], rhs=P_t[:, 0:tn], start=True, stop=False), reads=[cc_r, P_r], writes=[fpr], signal=False)
                fw.op("tensor", lambda e, fp=fp, Q_t=Q_t, tn=tn: e.matmul(fp[:, 0:tn], lhsT=sc_[:], rhs=Q_t[:, 0:tn], start=False, stop=True), reads=[sc_r, Q_r], writes=[fpr])
                fw.op("vector", lambda e, fp=fp, F_t=F_t, tn=tn: e.tensor_copy(out=F_t[:, 0:tn], in_=fp[:, 0:tn]), reads=[fpr], writes=[F_r])
                fw.op("tensor", lambda e, yp=yp, F_t=F_t, g=g, tn=tn: e.matmul(yp[:, 0:tn], lhsT=wg[:, g, :], rhs=F_t[:, 0:tn], start=True, stop=True), reads=[wg_r, F_r], writes=[ypr])
                fw.op("vector", lambda e, yp=yp, z_t=z_t, o_t=o_t, tn=tn: e.tensor_tensor(out=o_t[:, 0:tn], in0=yp[:, 0:tn], in1=z_t[:, 0:tn], op=ALU.mult), reads=[ypr, z_r], writes=[o_r])
                fw.dma(self.og[g, :, t0:t0 + tn], o_t[:, 0:tn], reads=[o_r], writes=[self.og_r[g][ci]], sem_res=o_r)

    def rwkv_prealloc(self, ts):
        fw = self.fw
        nc = self.nc
        R = {}
        R["Dall"] = [fw.sb("Dall%d" % z, [128, 8, 68], F32, ts) for z in range(2)]
        R["stack"] = ts.enter_context(ExitStack())
        R["twd"] = fw.sb("twd", [128, NT], F32, R["stack"])
        R["adT"] = fw.sb("adT", [128, NT], F32, R["stack"])
        if not hasattr(self, "sTd"):
            dk = "ExternalOutput" if self.dbg else "Internal"
            self.sTd = nc.dram_tensor("sTd", [24, 128, NT], F32, kind=dk).ap()
            self.FMd = [nc.dram_tensor("FMd%d" % z, [NTI, 128, 8, 4, 128], BF16, kind="Internal").ap() for z in range(2)]
            self.TMd = [nc.dram_tensor("TMd%d" % z, [NTI, 128, 2, 8, 128], BF16, kind="Internal").ap() for z in range(2)]
            self.Vd = nc.dram_tensor("Vd", [NTI, 128, 8, 128], BF16, kind="Internal").ap()
            self.bonusd = nc.dram_tensor("bonusd", [8, 128, NT], F32, kind=dk).ap()
            self.yd = [nc.dram_tensor("yd%d" % z, [NT, D], F32, kind=dk).ap() for z in range(2)]
        return R

    def rwkv_part1(self, i, j, hT, hT_r, R, ts):
        fw = self.fw
        W = self.I["rw_w_in"][j]
        twd, twd_r = R["twd"]
        adT, adT_r = R["adT"]
        wz, wz_r = fw.sb("rwz", [128, 8, D], BF16, ts)
        self.load_w_bf16(wz, wz_r, W[:, 3328:4352], D, ts, "rwz")
        self.gate_proj(wz, wz_r, hT, hT_r, ts)
        mu_fm, mu_r = fw.sb("mu_fm", [128, 26], F32, ts)
        self.load_fm(mu_fm[:], mu_r, self.I["rw_mu"][j, :].rearrange("(n p) -> n p", p=128), 26, ts)
        ca, ca_r = fw.sb("mix_a", [128, 26], F32, ts)
        cb, cb_r = fw.sb("mix_b", [128, 26], F32, ts)
        fw.op("vector", lambda e: e.tensor_scalar(out=ca[:], in0=mu_fm[:], scalar1=-1.0, scalar2=1.0, op0=ALU.mult, op1=ALU.add), reads=[mu_r], writes=[ca_r])
        fw.op("vector", lambda e: e.tensor_scalar(out=cb[:], in0=mu_fm[:], scalar1=0.5, scalar2=None, op0=ALU.mult), reads=[mu_r], writes=[cb_r])
        NP = NT + 3
        sraw = [fw.sb("sraw%d" % k, [128, NP], F32, ts) for k in range(2)]
        for k in range(2):
            fw.op("gpsimd", lambda e, k=k: e.memset(sraw[k][0][:], 0.0), writes=[sraw[k][1]])
        wst = [fw.sb("rw_st%d" % k, [128, 8, 128], F32, ts) for k in range(2)]
        wbs = [fw.sb("rw_wb%d" % k, [128, 8, 128], BF16, ts) for k in range(2)]
        PW = 512
        tmpb = [fw.sb("mixt%d" % k, [128, PW], F32, ts) for k in range(2)]
        smxb = [fw.sb("mixs%d" % k, [128, PW], F32, ts) for k in range(2)]
        npc = 0
        nev = 0
        for fc in range(26):
            s_t, s_r = wst[fc % 2]
            w_t, w_r = wbs[fc % 2]
            sr_t, sr_r = sraw[fc % 2]
            fw.dma(s_t[:], W[:, fc * 128:(fc + 1) * 128].rearrange("(k p) c -> p k c", p=128), writes=[s_r], sem_res=s_r)
            fw.op("gpsimd", lambda e, s_t=s_t, w_t=w_t: e.tensor_copy(out=w_t[:], in_=s_t[:]), reads=[s_r], writes=[w_r])
            for ci in range(9):
                t0, tn = self.chunk(ci)
                off = t0 + (1 if ci == 0 else 2)
                hrs = [hT_r[t] for t in range(t0 // 128, (t0 + tn) // 128)]
                pt, pr = self.pb[4 + nev % 4]
                for kc in range(8):
                    fw.op("tensor", lambda e, kc=kc, pt=pt, w_t=w_t, t0=t0, tn=tn: e.matmul(pt[:, 0:tn], lhsT=w_t[:, kc, :], rhs=hT[:, kc, t0:t0 + tn], start=(kc == 0), stop=(kc == 7)),
                          reads=[w_r] + hrs, writes=[pr], signal=(kc == 7))
                if nev % 2 == 0:
                    fw.op("scalar", lambda e, pt=pt, sr_t=sr_t, off=off, tn=tn: e.copy(out=sr_t[:, off:off + tn], in_=pt[:, 0:tn]), reads=[pr], writes=[sr_r])
                else:
                    fw.op("vector", lambda e, pt=pt, sr_t=sr_t, off=off, tn=tn: e.tensor_copy(out=sr_t[:, off:off + tn], in_=pt[:, 0:tn]), reads=[pr], writes=[sr_r])
                nev += 1
            for (c0, cn, tok0) in [(1, 256, 0)] + [(258 + q * 512, 512, 256 + q * 512) for q in range(8)]:
                tm_t, tm_r = tmpb[npc % 2]
                sm_t, sm_r = smxb[npc % 2]
                npc += 1
                fw.op("gpsimd", lambda e, tm_t=tm_t, sr_t=sr_t, c0=c0, cn=cn: e.tensor_tensor(out=tm_t[:, 0:cn], in0=sr_t[:, c0 - 1:c0 - 1 + cn], in1=sr_t[:, c0 + 1:c0 + 1 + cn], op=ALU.add), reads=[sr_r], writes=[tm_r])
                fw.op("vector", lambda e, tm_t=tm_t, fc=fc, cn=cn: e.tensor_scalar(out=tm_t[:, 0:cn], in0=tm_t[:, 0:cn], scalar1=cb[:, fc:fc + 1], scalar2=None, op0=ALU.mult), reads=[tm_r, cb_r], writes=[tm_r])
                if fc < 24:
                    fw.op("vector", lambda e, tm_t=tm_t, sm_t=sm_t, sr_t=sr_t, fc=fc, c0=c0, cn=cn: e.scalar_tensor_tensor(out=sm_t[:, 0:cn], in0=sr_t[:, c0:c0 + cn], scalar=ca[:, fc:fc + 1], in1=tm_t[:, 0:cn], op0=ALU.mult, op1=ALU.add),
                          reads=[sr_r, tm_r, ca_r], writes=[sm_r])
                    fw.dma(self.sTd[fc, :, tok0:tok0 + cn], sm_t[:, 0:cn], reads=[sm_r], sem_res=sm_r)
                else:
                    dst, dst_r = (twd, twd_r) if fc == 24 else (adT, adT_r)
                    fw.op("vector", lambda e, tm_t=tm_t, dst=dst, sr_t=sr_t, fc=fc, c0=c0, cn=cn, tok0=tok0: e.scalar_tensor_tensor(out=dst[:, tok0:tok0 + cn], in0=sr_t[:, c0:c0 + cn], scalar=ca[:, fc:fc + 1], in1=tm_t[:, 0:cn], op0=ALU.mult, op1=ALU.add),
                          reads=[sr_r, tm_r, ca_r], writes=[dst_r])
                    if fc == 24:
                        fw.op("scalar", lambda e, tok0=tok0, cn=cn: e.activation(out=twd[:, tok0:tok0 + cn], in_=twd[:, tok0:tok0 + cn], func=AF.Tanh), reads=[twd_r], writes=[twd_r])

    def rwkv_part2(self, i, j, R, ts0):
        fw = self.fw
        stop = getattr(self, "stop_at", 99)
        fw.barrier()
        if stop < 2:
            return
        with ExitStack() as ts:
            self.rwkv_F2(i, j, R, ts)
            fw.barrier()
        R["stack"].close()
        if stop < 3:
            return
        with ExitStack() as ts:
            self.rwkv_S(i, j, R, ts)
            fw.barrier()
        if stop < 4:
            return
        with ExitStack() as ts:
            self.rwkv_O(i, j, R, ts)
            fw.barrier()

    def rwkv_F2(self, i, j, R, ts):
        fw = self.fw
        I = self.I
        twd, twd_r = R["twd"]
        adT, adT_r = R["adT"]
        def fm(name, src, n):
            t, r = fw.sb(name, [128, n], F32, ts)
            self.load_fm(t[:], r, src, n, ts)
            return t, r
        kk_p, kk_pr = fm("p_kk", I["rw_k_k"][j, :].rearrange("(n p) -> n p", p=128), 8)
        ka_p, ka_pr = fm("p_ka", I["rw_k_a"][j, :].rearrange("(n p) -> n p", p=128), 8)
        rk_p, rk_pr = fm("p_rk", I["rw_r_k"][j, :].rearrange("(n p) -> n p", p=128), 8)
        w0_p, w0_pr = fm("p_w0", I["rw_w0"][j].rearrange("z (n p) -> (z n) p", p=128), 16)
        a0_p, a0_pr = fm("p_a0", I["rw_a0"][j].rearrange("z (n p) -> (z n) p", p=128), 16)
        oka, oka_r = fw.sb("p_oka", [128, 8], F32, ts)
        fw.op("vector", lambda e: e.tensor_scalar(out=oka[:], in0=ka_p[:], scalar1=-1.0, scalar2=1.0, op0=ALU.mult, op1=ALU.add), reads=[ka_pr], writes=[oka_r])
        wup, wup_r = fw.sb("wup", [128, D], F32, ts)
        aup, aup_r = fw.sb("aup", [128, D], F32, ts)
        fw.dma(wup[:], I["rw_w_up"][j].rearrange("z r f -> (z r) f"), writes=[wup_r], sem_res=wup_r)
        fw.dma(aup[:], I["rw_a_up"][j].rearrange("z r f -> (z r) f"), writes=[aup_r], sem_res=aup_r)
        bones, bones_r = fw.sb("bones", [128, 128], F32, ts)
        fw.op("vector", lambda e: e.memset(bones[:], 0.0), writes=[bones_r])
        fw.op("vector", lambda e: e.memset(bones[0:64, 0:64], 1.0), writes=[bones_r])
        fw.op("vector", lambda e: e.memset(bones[64:128, 64:128], 1.0), writes=[bones_r])
        eps12, eps12_r = fw.sb("eps12", [128, 1], F32, ts)
        fw.op("vector", lambda e: e.memset(eps12[:], 1e-12), writes=[eps12_r])
        cmask, cmask_r = fw.sb("cmask", [128, 512], F32, ts)
        fw.op("vector", lambda e: e.memset(cmask[:], 1.0), writes=[cmask_r])
        fw.op("vector", lambda e: e.memset(cmask[:].rearrange("p (c t) -> p c t", t=64)[:, :, 0:1], 0.0), writes=[cmask_r])

        cnt = [0]

        def T(name, dt=F32, n=2, w=512):
            return [fw.sb("%s_%d" % (name, k), [128, w], dt, ts) for k in range(n)]
        rTb, kTb, vTb = T("f_r"), T("f_k"), T("f_v")
        lwb = [T("f_lw0"), T("f_lw1")]
        azb = [T("f_az0"), T("f_az1")]
        kdb = [T("f_kd0"), T("f_kd1")]
        bzb = [T("f_b0"), T("f_b1")]
        kkrb, sqb_, kkb, tb1, tb3 = T("f_kkr"), T("f_sq"), T("f_kk"), T("f_t1"), T("f_t3", n=4)
        rnb = sqb_
        tb2 = sqb_
        cwb, e1b, e2b, e3b, e4b = T("f_cw", n=4), T("f_e1", n=4), T("f_e2", n=4), T("f_e3", n=4), T("f_e4", n=2) * 2
        fmo = [T("f_o%d" % k, dt=BF16, n=4) for k in range(4)]
        hato = [T("f_h%d" % k, n=4) for k in range(2)]
        trs = T("f_trs", dt=BF16, n=4)
        bonb = T("f_bon")
        NEG = -math.exp(-0.5)
        nb = 0
        ntr = [0]

        def transpose_store(fx, k, cnt, src_t, src_r, dst_ap_fn, ntile):
            tp, tpr = self.pb[k * 4 + 2 + cnt[0] % 2]
            o_t, o_r = trs[k * 2 + cnt[0] % 2]
            cnt[0] += 1
            for tl in range(ntile):
                fx.op("tensor", lambda e, tl=tl, tp=tp: e.transpose(tp[:, tl * 128:(tl + 1) * 128], src_t[:, tl * 128:(tl + 1) * 128], self.ident[:]),
                      reads=[src_r, self.ident_r], writes=[tpr], signal=(tl == ntile - 1))
            fx.op("scalar", lambda e, tp=tp, o_t=o_t, ntile=ntile: e.copy(out=o_t[:, 0:ntile * 128], in_=tp[:, 0:ntile * 128]), reads=[tpr], writes=[o_r])
            for tl in range(ntile):
                fx.dma(dst_ap_fn(tl), o_t[:, tl * 128:(tl + 1) * 128], reads=[o_r], sem_res=o_r)

        def emit_block(fx, hp, ci, k):
            cnt = [0]
            t0, N = self.chunk(ci)
            tile0 = t0 // 128
            ntile = N // 128
            nch = N // 64
            (rT, rT_r), (kT, kT_r), (vT, vT_r) = rTb[k], kTb[k], vTb[k]
            fx.dma(rT[:, 0:N], self.sTd[hp, :, t0:t0 + N], writes=[rT_r], sem_res=rT_r)
            fx.dma(kT[:, 0:N], self.sTd[8 + hp, :, t0:t0 + N], writes=[kT_r], sem_res=kT_r)
            fx.dma(vT[:, 0:N], self.sTd[16 + hp, :, t0:t0 + N], writes=[vT_r], sem_res=vT_r)
            transpose_store(fx, k, cnt, vT, vT_r, lambda tl, tile0=tile0, hp=hp: self.Vd[tile0 + tl, :, hp, :], ntile)
            for z in range(2):
                rows = slice(z * 64, (z + 1) * 64)
                pw, pwr = self.pb[k * 4]
                pa, par = self.pb[k * 4 + 1]
                lw, lw_r = lwb[z][k]
                az, az_r = azb[z][k]
                fx.op("tensor", lambda e, pw=pw, rows=rows, hp=hp, t0=t0, N=N: e.matmul(pw[:, 0:N], lhsT=wup[rows, hp * 128:(hp + 1) * 128], rhs=twd[rows, t0:t0 + N], start=True, stop=True), reads=[wup_r, twd_r], writes=[pwr])
                fx.op("tensor", lambda e, pa=pa, rows=rows, hp=hp, t0=t0, N=N: e.matmul(pa[:, 0:N], lhsT=aup[rows, hp * 128:(hp + 1) * 128], rhs=adT[rows, t0:t0 + N], start=True, stop=True), reads=[aup_r, adT_r], writes=[par])
                fx.op("scalar", lambda e, pw=pw, lw=lw, z=z, hp=hp, N=N: e.activation(out=lw[:, 0:N], in_=pw[:, 0:N], func=AF.Sigmoid, bias=w0_p[:, z * 8 + hp:z * 8 + hp + 1], scale=1.0), reads=[pwr, w0_pr], writes=[lw_r])
                fx.op("vector", lambda e, lw=lw, N=N: e.tensor_scalar(out=lw[:, 0:N], in0=lw[:, 0:N], scalar1=NEG, scalar2=None, op0=ALU.mult), reads=[lw_r], writes=[lw_r])
                fx.op("scalar", lambda e, pa=pa, az=az, z=z, hp=hp, N=N: e.activation(out=az[:, 0:N], in_=pa[:, 0:N], func=AF.Sigmoid, bias=a0_p[:, z * 8 + hp:z * 8 + hp + 1], scale=1.0), reads=[par, a0_pr], writes=[az_r])
            (kkr, kkr_r), (sq, sq_r), (rn, rn_r), (kk, kk_r) = kkrb[k], sqb_[k], rnb[k], kkb[k]
            fx.op("vector", lambda e, kkr=kkr, kT=kT, hp=hp, N=N: e.tensor_scalar(out=kkr[:, 0:N], in0=kT[:, 0:N], scalar1=kk_p[:, hp:hp + 1], scalar2=None, op0=ALU.mult), reads=[kT_r, kk_pr], writes=[kkr_r])
            fx.op("gpsimd", lambda e, kkr=kkr, sq=sq, N=N: e.tensor_tensor(out=sq[:, 0:N], in0=kkr[:, 0:N], in1=kkr[:, 0:N], op=ALU.mult), reads=[kkr_r], writes=[sq_r])
            pss, pssr = self.pb[k * 4 + 2 + cnt[0] % 2]
            cnt[0] += 1
            fx.op("tensor", lambda e, pss=pss, sq=sq, N=N: e.matmul(pss[:, 0:N], lhsT=bones[:], rhs=sq[:, 0:N], start=True, stop=True), reads=[bones_r, sq_r], writes=[pssr])
            fx.op("scalar", lambda e, pss=pss, rn=rn, N=N: e.activation(out=rn[:, 0:N], in_=pss[:, 0:N], func=AF.Sqrt, bias=eps12[:], scale=1.0), reads=[pssr, eps12_r], writes=[rn_r])
            fx.op("vector", lambda e, rn=rn, N=N: e.reciprocal(out=rn[:, 0:N], in_=rn[:, 0:N]), reads=[rn_r], writes=[rn_r])
            fx.op("gpsimd", lambda e, kk=kk, kkr=kkr, rn=rn, N=N: e.tensor_tensor(out=kk[:, 0:N], in0=kkr[:, 0:N], in1=rn[:, 0:N], op=ALU.mult), reads=[kkr_r, rn_r], writes=[kk_r])
            for z in range(2):
                az, az_r = azb[z][k]
                kd, kd_r = kdb[z][k]
                bz, bz_r = bzb[z][k]
                fx.op("vector", lambda e, kd=kd, az=az, hp=hp, N=N: e.tensor_scalar(out=kd[:, 0:N], in0=az[:, 0:N], scalar1=ka_p[:, hp:hp + 1], scalar2=oka[:, hp:hp + 1], op0=ALU.mult, op1=ALU.add), reads=[az_r, ka_pr, oka_r], writes=[kd_r])
                fx.op("vector", lambda e, kd=kd, kT=kT, N=N: e.tensor_tensor(out=kd[:, 0:N], in0=kd[:, 0:N], in1=kT[:, 0:N], op=ALU.mult), reads=[kd_r, kT_r], writes=[kd_r])
                fx.op("gpsimd", lambda e, bz=bz, kk=kk, az=az, N=N: e.tensor_tensor(out=bz[:, 0:N], in0=kk[:, 0:N], in1=az[:, 0:N], op=ALU.mult), reads=[kk_r, az_r], writes=[bz_r])
            (x1, x1_r), (x2, x2_r) = tb1[k], tb2[k]
            fx.op("vector", lambda e, x1=x1, N=N, k=k: e.tensor_tensor(out=x1[:, 0:N], in0=kdb[0][k][0][:, 0:N], in1=kdb[1][k][0][:, 0:N], op=ALU.add), reads=[kdb[0][k][1], kdb[1][k][1]], writes=[x1_r])
            fx.op("vector", lambda e, x2=x2, rT=rT, hp=hp, N=N: e.tensor_scalar(out=x2[:, 0:N], in0=rT[:, 0:N], scalar1=rk_p[:, hp:hp + 1], scalar2=0.5, op0=ALU.mult, op1=ALU.mult), reads=[rT_r, rk_pr], writes=[x2_r])
            fx.op("vector", lambda e, x1=x1, x2=x2, N=N: e.tensor_tensor(out=x1[:, 0:N], in0=x1[:, 0:N], in1=x2[:, 0:N], op=ALU.mult), reads=[x1_r, x2_r], writes=[x1_r])
            psb, psbr = self.pb[k * 4 + 2 + cnt[0] % 2]
            cnt[0] += 1
            fx.op("tensor", lambda e, psb=psb, x1=x1, N=N: e.matmul(psb[:, 0:N], lhsT=bones[:], rhs=x1[:, 0:N], start=True, stop=True), reads=[bones_r, x1_r], writes=[psbr])
            bo, bo_r = bonb[k]
            fx.op("vector", lambda e, bo=bo, psb=psb, vT=vT, N=N: e.tensor_tensor(out=bo[:, 0:N], in0=psb[:, 0:N], in1=vT[:, 0:N], op=ALU.mult), reads=[psbr, vT_r], writes=[bo_r])
            fx.dma(self.bonusd[hp, :, t0:t0 + N], bo[:, 0:N], reads=[bo_r], sem_res=bo_r)
            for z in range(2):
                lw, lw_r = lwb[z][k]
                kd, kd_r = kdb[z][k]
                bz, bz_r = bzb[z][k]
                cw, cw_r = cwb[z * 2 + k]
                (e1, e1_r), (e2, e2_r), (e3, e3_r), (e4, e4_r) = e1b[z * 2 + k], e2b[z * 2 + k], e3b[z * 2 + k], e4b[z * 2 + k]
                (y1, y1_r) = tb3[z * 2 + k]
                Dall, Dall_r = R["Dall"][z]
                fx.op("vector", lambda e, cw=cw, lw=lw, N=N: e.tensor_tensor_scan(out=cw[:, 0:N], data0=cmask[:, 0:N], data1=lw[:, 0:N], initial=0.0, op0=ALU.mult, op1=ALU.add), reads=[cmask_r, lw_r], writes=[cw_r])
                cw3 = cw[:, 0:N].rearrange("p (c t) -> p c t", t=64)
                totb = cw3[:, :, 63:64].to_broadcast([128, nch, 64])
                ch0 = t0 // 64
                fx.op("scalar", lambda e, Dall=Dall, cw3=cw3, hp=hp, ch0=ch0, nch=nch: e.activation(out=Dall[:, hp, ch0:ch0 + nch], in_=cw3[:, :, 63], func=AF.Exp), reads=[cw_r], writes=[Dall_r])
                y13 = y1[:, 0:N].rearrange("p (c t) -> p c t", t=64)
                if z == 0:
                    fx.op("gpsimd", lambda e, y13=y13, totb=totb, cw3=cw3: e.tensor_tensor(out=y13, in0=totb, in1=cw3, op=ALU.subtract), reads=[cw_r], writes=[y1_r])
                    cwz, cwz_r = cw, cw_r
                else:
                    fx.op("gpsimd", lambda e, y1=y1, cw=cw, lw=lw, N=N: e.tensor_tensor(out=y1[:, 0:N], in0=cw[:, 0:N], in1=lw[:, 0:N], op=ALU.subtract), reads=[cw_r, lw_r], writes=[y1_r])
                    cwz, cwz_r = e4, e4_r
                    e43 = e4[:, 0:N].rearrange("p (c t) -> p c t", t=64)
                    fx.op("gpsimd", lambda e, e43=e43, totb=totb, y13=y13: e.tensor_tensor(out=e43, in0=totb, in1=y13, op=ALU.subtract), reads=[cw_r, y1_r], writes=[e4_r])
                fx.op("scalar", lambda e, e1=e1, cwz=cwz, N=N: e.activation(out=e1[:, 0:N], in_=cwz[:, 0:N], func=AF.Exp), reads=[cwz_r], writes=[e1_r])
                fx.op("scalar", lambda e, e2=e2, cwz=cwz, N=N: e.activation(out=e2[:, 0:N], in_=cwz[:, 0:N], func=AF.Exp, scale=-1.0), reads=[cwz_r], writes=[e2_r])
                fx.op("vector", lambda e, e3=e3, cwz=cwz, lw=lw, N=N: e.tensor_tensor(out=e3[:, 0:N], in0=cwz[:, 0:N], in1=lw[:, 0:N], op=ALU.subtract), reads=[cwz_r, lw_r], writes=[e3_r])
                fx.op("scalar", lambda e, e3=e3, N=N: e.activation(out=e3[:, 0:N], in_=e3[:, 0:N], func=AF.Exp), reads=[e3_r], writes=[e3_r])
                fx.op("scalar", lambda e, y1=y1, N=N: e.activation(out=y1[:, 0:N], in_=y1[:, 0:N], func=AF.Exp), reads=[y1_r], writes=[y1_r])
                outs = [fmo[q][z * 2 + k] for q in range(4)]
                fx.op("vector", lambda e, o=outs[0][0], kk=kk, e3=e3, N=N: e.scalar_tensor_tensor(out=o[:, 0:N], in0=kk[:, 0:N], scalar=-1.0, in1=e3[:, 0:N], op0=ALU.mult, op1=ALU.mult), reads=[kk_r, e3_r], writes=[outs[0][1]])
                fx.op("gpsimd", lambda e, o=outs[1][0], rT=rT, e1=e1, N=N: e.tensor_tensor(out=o[:, 0:N], in0=rT[:, 0:N], in1=e1[:, 0:N], op=ALU.mult), reads=[rT_r, e1_r], writes=[outs[1][1]])
                fx.op("vector", lambda e, o=outs[2][0], bz=bz, e2=e2, N=N: e.tensor_tensor(out=o[:, 0:N], in0=bz[:, 0:N], in1=e2[:, 0:N], op=ALU.mult), reads=[bz_r, e2_r], writes=[outs[2][1]])
                fx.op("vector", lambda e, o=outs[3][0], kd=kd, e2=e2, N=N: e.tensor_tensor(out=o[:, 0:N], in0=kd[:, 0:N], in1=e2[:, 0:N], op=ALU.mult), reads=[kd_r, e2_r], writes=[outs[3][1]])
                for q in range(4):
                    fx.dma(self.FMd[z][tile0:tile0 + ntile, :, hp, q, :].rearrange("n p t -> p n t"), outs[q][0][:, 0:N].rearrange("p (n t) -> p n t", t=128), reads=[outs[q][1]], sem_res=outs[q][1])
                (h0, h0_r), (h1, h1_r) = hato[0][z * 2 + k], hato[1][z * 2 + k]
                fx.op("vector", lambda e, h0=h0, bz=bz, y1=y1, N=N: e.tensor_tensor(out=h0[:, 0:N], in0=bz[:, 0:N], in1=y1[:, 0:N], op=ALU.mult), reads=[bz_r, y1_r], writes=[h0_r])
                fx.op("gpsimd", lambda e, h1=h1, kd=kd, y1=y1, N=N: e.tensor_tensor(out=h1[:, 0:N], in0=kd[:, 0:N], in1=y1[:, 0:N], op=ALU.mult), reads=[kd_r, y1_r], writes=[h1_r])
                transpose_store(fx, k, cnt, h0, h0_r, lambda tl, z=z, tile0=tile0, hp=hp: self.TMd[z][tile0 + tl, :, 0, hp, :], ntile)
                transpose_store(fx, k, cnt, h1, h1_r, lambda tl, z=z, tile0=tile0, hp=hp: self.TMd[z][tile0 + tl, :, 1, hp, :], ntile)


        blocks = [(hp, ci) for hp in range(8) for ci in range(9)]
        for bi in range(0, len(blocks), 2):
            recs = []
            for q in range(2):
                if bi + q < len(blocks):
                    r_ = Rec()
                    emit_block(r_, blocks[bi + q][0], blocks[bi + q][1], q)
                    recs.append(r_)
            interleave(fw, recs)

    def rwkv_S(self, i, j, R, ts):
        fw = self.fw
        mk = {}
        for z in range(2):
            mNf, mNf_r = fw.sb("mNf%d" % z, [128, 256], F32, ts)
            mLf, mLf_r = fw.sb("mLf%d" % z, [128, 128], F32, ts)
            mN, mN_r = fw.sb("mN%d" % z, [128, 2, 256], BF16, ts)
            mL, mL_r = fw.sb("mL%d" % z, [128, 4, 128], BF16, ts)
            fw.dma(mNf[:], self.Cn["mN%d" % z][:, :], writes=[mNf_r], sem_res=mNf_r)
            fw.dma(mLf[:], self.Cn["mL%d" % z][:, :], writes=[mLf_r], sem_res=mLf_r)
            for q in range(2):
                fw.op("vector", lambda e, mN=mN, mNf=mNf, q=q: e.tensor_copy(out=mN[:, q, :], in_=mNf[:]), reads=[mNf_r], writes=[mN_r])
            for q in range(4):
                fw.op("vector", lambda e, mL=mL, mLf=mLf, q=q: e.tensor_copy(out=mL[:, q, :], in_=mLf[:]), reads=[mLf_r], writes=[mL_r])
            mk[z] = (mN, mN_r, mL, mL_r)
        identb4, identb4_r = fw.sb("ident4", [128, 4, 128], BF16, ts)
        for q in range(4):
            fw.op("vector", lambda e, q=q: e.tensor_copy(out=identb4[:, q, :], in_=self.ident[:]), reads=[self.ident_r], writes=[identb4_r])
        Mst = []
        for z in range(2):
            row = []
            for hf in range(2):
                t, r = fw.sb("M%d_%d" % (z, hf), [128, 8, 64], F32, ts)
                fw.op("vector", lambda e, t=t: e.memset(t[:], 0.0), writes=[r])
                tb_, rb_ = fw.sb("Mb%d_%d" % (z, hf), [128, 8, 64], BF16, ts)
                fw.op("vector", lambda e, tb_=tb_: e.memset(tb_[:], 0.0), writes=[rb_])
                row.append((t, r, tb_, rb_))
            Mst.append(row)
        FMb = [fw.sb("sFM%d" % k, [128, 8, 4, 128], BF16, ts) for k in range(2)]
        TMb = [fw.sb("sTM%d" % k, [128, 2, 8, 128], BF16, ts) for k in range(2)]
        Vb = [fw.sb("sV%d" % k, [128, 8, 128], BF16, ts) for k in range(2)]
        G13, _ = fw.sb("sG13", [128, 16, 256], BF16, ts)
        G24, _ = fw.sb("sG24", [128, 16, 256], BF16, ts)
        G13_r = [Res("sG13_%d" % g) for g in range(8)]
        G24_r = [Res("sG24_%d" % g) for g in range(8)]
        Lb_ = [fw.sb("sL%d" % k, [128, 16, 128], BF16, ts)[0] for k in range(2)]
        Nb_ = [fw.sb("sN%d" % k, [128, 16, 128], BF16, ts)[0] for k in range(2)]
        Pm, _ = fw.sb("sP", [128, 16, 128], BF16, ts)
        L_r = [[Res("sL%d_%d" % (k, g)) for g in range(4)] for k in range(2)]
        N_r = [[Res("sN%d_%d" % (k, g)) for g in range(4)] for k in range(2)]
        P_r = [Res("sP_%d" % g) for g in range(4)]
        Z2s = [fw.sb("sZ2s%d" % k, [128, 512], F32, ts) for k in range(2)]
        Zs = [fw.sb("sZs%d" % k, [128, 512], BF16, ts) for k in range(2)]
        Us = [fw.sb("sUs%d" % k, [128, 512], BF16, ts) for k in range(2)]
        Ys = [fw.sb("sYs%d" % k, [128, 512], F32, ts) for k in range(2)]
        npb = [0]

        def prep_bank():
            b = self.pb[4 + npb[0] % 4]
            npb[0] += 1
            return b

        order = [list(range(NTI)), [1, 0] + list(range(NTI - 1, 1, -1))]
        nstep = 0
        for si in range(NTI):
            for z in getattr(self, "zorder", (0, 1)):
                tile = order[z][si]
                k = nstep % 2
                nstep += 1
                FM, FM_r = FMb[k]
                TM, TM_r = TMb[k]
                V, V_r = Vb[k]
                mN, mN_r, mL, mL_r = mk[z]
                fw.dma(FM[:], self.FMd[z][tile], writes=[FM_r], sem_res=FM_r)
                fw.dma(TM[:], self.TMd[z][tile], writes=[TM_r], sem_res=TM_r)
                fw.dma(V[:], self.Vd[tile], writes=[V_r], sem_res=V_r)
                for g2 in range(8):
                    pB, pBr = prep_bank()
                    pK, pKr = prep_bank()
                    pL, pLr = prep_bank()
                    for hl in range(2):
                        h = g2 * 2 + hl
                        hp, hh = h % 8, h // 8
                        rows = slice(hh * 64, (hh + 1) * 64)
                        ar = FM[rows, hp, 0:2, :].rearrange("p a t -> p (a t)")
                        fw.op("tensor", lambda e, FM=FM, TM=TM, V=V, pB=pB, rows=rows, hp=hp, hl=hl, ar=ar: e.matmul(pB[:, hl * 256:(hl + 1) * 256], lhsT=FM[rows, hp, 2, :], rhs=ar, start=True, stop=True), reads=[FM_r], writes=[pBr], signal=(hl == 1))
                        fw.op("tensor", lambda e, FM=FM, TM=TM, V=V, pK=pK, rows=rows, hp=hp, hl=hl, ar=ar: e.matmul(pK[:, hl * 256:(hl + 1) * 256], lhsT=FM[rows, hp, 3, :], rhs=ar, start=True, stop=True), reads=[FM_r], writes=[pKr], signal=(hl == 1))
                        fw.op("tensor", lambda e, FM=FM, TM=TM, V=V, pL=pL, rows=rows, hp=hp, hl=hl: e.matmul(pL[:, hl * 128:(hl + 1) * 128], lhsT=FM[rows, hp, 0, :], rhs=FM[rows, hp, 2, :], start=True, stop=True), reads=[FM_r], writes=[pLr], signal=(hl == 1))
                    h0 = g2 * 2
                    fw.op("vector", lambda e, FM=FM, TM=TM, V=V, pB=pB, h0=h0, mN=mN: e.tensor_tensor(out=G13[:, h0:h0 + 2, :], in0=pB[:, :].rearrange("p (h c) -> p h c", h=2), in1=mN[:], op=ALU.mult), reads=[pBr, mN_r], writes=[G13_r[g2]])
                    fw.op("vector", lambda e, FM=FM, TM=TM, V=V, pK=pK, h0=h0, mN=mN: e.tensor_tensor(out=G24[:, h0:h0 + 2, :], in0=pK[:, :].rearrange("p (h c) -> p h c", h=2), in1=mN[:], op=ALU.mult), reads=[pKr, mN_r], writes=[G24_r[g2]])
                    fw.op("vector", lambda e, FM=FM, TM=TM, V=V, pL=pL, h0=h0, mL=mL: e.tensor_tensor(out=Lb_[0][:, h0:h0 + 2, :], in0=pL[:, 0:256].rearrange("p (h c) -> p h c", h=2), in1=mL[:, 0:2, :], op=ALU.mult), reads=[pLr, mL_r], writes=[L_r[0][g2 // 2]])
                sstop = getattr(self, "s_stop", 99)
                if sstop < 1:
                    continue
                for g4 in range(4):
                    hs = slice(g4 * 4, g4 * 4 + 4)
                    fw.op("gpsimd", lambda e, FM=FM, TM=TM, V=V, hs=hs: e.tensor_copy(out=Nb_[0][:, hs, :], in_=G13[:, hs, 0:128]), reads=[G13_r[g4 * 2], G13_r[g4 * 2 + 1]], writes=[N_r[0][g4]])
                    fw.op("gpsimd", lambda e, FM=FM, TM=TM, V=V, hs=hs: e.tensor_tensor(out=Pm[:, hs, :], in0=G13[:, hs, 0:128], in1=identb4[:], op=ALU.add), reads=[G13_r[g4 * 2], G13_r[g4 * 2 + 1], identb4_r], writes=[P_r[g4]])
                for lev in range(1, 6):
                    a, b = (lev - 1) % 2, lev % 2
                    for g4 in range(4):
                        hs = slice(g4 * 4, g4 * 4 + 4)
                        pl, plr = prep_bank()
                        for hl in range(4):
                            h = g4 * 4 + hl
                            fw.op("tensor", lambda e, FM=FM, TM=TM, V=V, pl=pl, h=h, hl=hl, a=a: e.matmul(pl[:, hl * 128:(hl + 1) * 128], lhsT=Nb_[a][:, h, :], rhs=Lb_[a][:, h, :], start=True, stop=True), reads=[N_r[a][g4], L_r[a][g4]], writes=[plr], signal=(hl == 3))
                        fw.op("scalar", lambda e, FM=FM, TM=TM, V=V, pl=pl, hs=hs, b=b: e.copy(out=Lb_[b][:, hs, :], in_=pl[:, :].rearrange("p (h c) -> p h c", h=4)), reads=[plr], writes=[L_r[b][g4]])
                        if lev < 5:
                            pn, pnr = prep_bank()
                            for hl in range(4):
                                h = g4 * 4 + hl
                                fw.op("tensor", lambda e, FM=FM, TM=TM, V=V, pn=pn, h=h, hl=hl, a=a: e.matmul(pn[:, hl * 128:(hl + 1) * 128], lhsT=Lb_[a][:, h, :], rhs=Nb_[a][:, h, :], start=True, stop=True), reads=[N_r[a][g4], L_r[a][g4]], writes=[pnr], signal=(hl == 3))
                            fw.op("vector", lambda e, FM=FM, TM=TM, V=V, pn=pn, hs=hs, b=b: e.tensor_copy(out=Nb_[b][:, hs, :], in_=pn[:, :].rearrange("p (h c) -> p h c", h=4)), reads=[pnr], writes=[N_r[b][g4]])
                        pq, pqr = prep_bank()
                        for hl in range(4):
                            h = g4 * 4 + hl
                            fw.op("tensor", lambda e, FM=FM, TM=TM, V=V, pq=pq, h=h, hl=hl, b=b: e.matmul(pq[:, hl * 128:(hl + 1) * 128], lhsT=Lb_[b][:, h, :], rhs=Pm[:, h, :], start=True, stop=True), reads=[L_r[b][g4], P_r[g4]], writes=[pqr], signal=(hl == 3))
                        fw.op("vector", lambda e, FM=FM, TM=TM, V=V, pq=pq, hs=hs: e.tensor_tensor(out=Pm[:, hs, :], in0=pq[:, :].rearrange("p (h c) -> p h c", h=4), in1=Pm[:, hs, :], op=ALU.add), reads=[pqr, P_r[g4]], writes=[P_r[g4]])
                if self.dbg and nstep == 1:
                    self.dG13 = self.nc.dram_tensor("dG13", [128, 16, 256], F32, kind="ExternalOutput").ap()
                    self.dG24 = self.nc.dram_tensor("dG24", [128, 16, 256], F32, kind="ExternalOutput").ap()
                    self.dP = self.nc.dram_tensor("dP", [128, 16, 128], F32, kind="ExternalOutput").ap()
                    self.dFM = self.nc.dram_tensor("dFM", [128, 8, 4, 128], F32, kind="ExternalOutput").ap()
                    self.dTM = self.nc.dram_tensor("dTM", [128, 2, 8, 128], F32, kind="ExternalOutput").ap()
                    self.dV = self.nc.dram_tensor("dV", [128, 8, 128], F32, kind="ExternalOutput").ap()
                    dr = fw.res("dbgS", ts)
                    fw.dma(self.dFM, FM[:], reads=[FM_r], sem_res=dr, ek="gpsimd")
                    fw.dma(self.dTM, TM[:], reads=[TM_r], sem_res=dr, ek="gpsimd")
                    fw.dma(self.dV, V[:], reads=[V_r], sem_res=dr, ek="gpsimd")
                    fw.dma(self.dG13, G13[:], reads=list(G13_r), sem_res=dr, ek="gpsimd")
                    fw.dma(self.dG24, G24[:], reads=list(G24_r), sem_res=dr, ek="gpsimd")
                    fw.dma(self.dP, Pm[:], reads=list(P_r), sem_res=dr, ek="gpsimd")
                if sstop < 2:
                    continue
                cks = (0, 1) if z == 0 else (1, 0)
                for hf in range(2):
                    hr = slice(hf * 64, (hf + 1) * 64)
                    M, M_r, Mb, Mb_r = Mst[z][hf]
                    Dall, Dall_r = R["Dall"][z]
                    tA, tAr = self.pb[0]
                    tB, tBr = self.pb[1]
                    tC, tCr = self.pb[2]
                    tD, tDr = self.pb[3]
                    z2, z2_r = Z2s[hf]
                    zs_, zs_r = Zs[hf]
                    us, us_r = Us[hf]
                    ys, ys_r = Ys[hf]
                    g13r = list(G13_r)
                    g24r = list(G24_r)
                    pr_ = list(P_r)
                    for ck in range(2):
                        rows = slice(ck * 64, (ck + 1) * 64)
                        for hp in range(8):
                            h = hf * 8 + hp
                            fw.op("tensor", lambda e, FM=FM, TM=TM, V=V, rows=rows, h=h, hp=hp, hr=hr, ck=ck: e.matmul(tA[rows, hp * 64:(hp + 1) * 64], lhsT=G24[rows, h, ck * 64:(ck + 1) * 64], rhs=V[rows, hp, hr], start=True, stop=True),
                                  reads=g24r + [V_r], writes=[tAr], signal=(ck == 1 and hp == 7))
                    fw.op("scalar", lambda e, FM=FM, TM=TM, V=V, z2=z2: e.copy(out=z2[:], in_=tA[:, :]), reads=[tAr], writes=[z2_r])
                    for ck in (cks if sstop >= 3 else ()):
                        rows = slice(ck * 64, (ck + 1) * 64)
                        cols = slice(ck * 64, (ck + 1) * 64)
                        for hp in range(8):
                            fw.op("tensor", lambda e, FM=FM, TM=TM, V=V, rows=rows, cols=cols, hp=hp, hr=hr, Mb=Mb: e.matmul(tB[rows, hp * 64:(hp + 1) * 64], lhsT=FM[hr, hp, 0, cols], rhs=Mb[hr, hp, :], start=True, stop=True),
                                  reads=[FM_r, Mb_r], writes=[tBr], signal=(hp == 7))
                        for hp in range(8):
                            fw.op("tensor", lambda e, FM=FM, TM=TM, V=V, rows=rows, cols=cols, hp=hp, hr=hr, Mb=Mb: e.matmul(tC[rows, hp * 64:(hp + 1) * 64], lhsT=FM[hr, hp, 1, cols], rhs=Mb[hr, hp, :], start=True, stop=True),
                                  reads=[FM_r, Mb_r], writes=[tCr], signal=(hp == 7))
                        fw.op("vector", lambda e, FM=FM, TM=TM, V=V, rows=rows, zs_=zs_, z2=z2: e.tensor_tensor(out=zs_[rows, :], in0=tB[rows, :], in1=z2[rows, :], op=ALU.add), reads=[tBr, z2_r], writes=[zs_r])
                        if sstop < 4:
                            continue
                        for hp in range(8):
                            h = hf * 8 + hp
                            fw.op("tensor", lambda e, FM=FM, TM=TM, V=V, rows=rows, cols=cols, hp=hp, h=h, zs_=zs_: e.matmul(tB[rows, hp * 64:(hp + 1) * 64], lhsT=Pm[rows, h, cols], rhs=zs_[rows, hp * 64:(hp + 1) * 64], start=True, stop=True),
                                  reads=pr_ + [zs_r], writes=[tBr], signal=(hp == 7))
                        fw.op("scalar", lambda e, FM=FM, TM=TM, V=V, rows=rows, us=us: e.copy(out=us[rows, :], in_=tB[rows, :]), reads=[tBr], writes=[us_r])
                        if sstop < 5:
                            continue
                        for hp in range(8):
                            h = hf * 8 + hp
                            fw.op("tensor", lambda e, FM=FM, TM=TM, V=V, rows=rows, ck=ck, hp=hp, h=h, us=us: e.matmul(tA[rows, hp * 64:(hp + 1) * 64], lhsT=G13[rows, h, 128 + ck * 64:128 + (ck + 1) * 64], rhs=us[rows, hp * 64:(hp + 1) * 64], start=True, stop=False),
                                  reads=g13r + [us_r], writes=[tAr], signal=False)
                            fw.op("tensor", lambda e, FM=FM, TM=TM, V=V, rows=rows, ck=ck, hp=hp, h=h, hr=hr: e.matmul(tA[rows, hp * 64:(hp + 1) * 64], lhsT=G24[rows, h, 128 + ck * 64:128 + (ck + 1) * 64], rhs=V[rows, hp, hr], start=False, stop=True),
                                  reads=g24r + [V_r], writes=[tAr], signal=(hp == 7))
                        if sstop < 6:
                            continue
                        for hp in range(8):
                            fw.op("tensor", lambda e, FM=FM, TM=TM, V=V, rows=rows, hp=hp, hr=hr, us=us: e.matmul(tD[hr, hp * 64:(hp + 1) * 64], lhsT=TM[rows, 0, hp, hr], rhs=us[rows, hp * 64:(hp + 1) * 64], start=True, stop=False),
                                  reads=[TM_r, us_r], writes=[tDr], signal=False)
                            fw.op("tensor", lambda e, FM=FM, TM=TM, V=V, rows=rows, hp=hp, hr=hr: e.matmul(tD[hr, hp * 64:(hp + 1) * 64], lhsT=TM[rows, 1, hp, hr], rhs=V[rows, hp, hr], start=False, stop=True),
                                  reads=[TM_r, V_r], writes=[tDr], signal=(hp == 7))
                        c = tile * 2 + ck
                        dbc = Dall[hr, :, c:c + 1].to_broadcast([64, 8, 64])
                        fw.op("vector", lambda e, FM=FM, TM=TM, V=V, M=M, dbc=dbc, hr=hr: e.tensor_tensor(out=M[hr, :, :], in0=M[hr, :, :], in1=dbc, op=ALU.mult), reads=[M_r, Dall_r], writes=[M_r])
                        fw.op("vector", lambda e, FM=FM, TM=TM, V=V, M=M, hr=hr: e.tensor_tensor(out=M[hr, :, :], in0=tD[hr, :].rearrange("p (h i) -> p h i", h=8), in1=M[hr, :, :], op=ALU.add), reads=[M_r, tDr], writes=[M_r])
                        fw.op("scalar", lambda e, FM=FM, TM=TM, V=V, M=M, Mb=Mb, hr=hr: e.copy(out=Mb[hr, :, :], in_=M[hr, :, :]), reads=[M_r], writes=[Mb_r])
                    if sstop < 7:
                        continue
                    fw.op("scalar", lambda e, FM=FM, TM=TM, V=V, ys=ys: e.copy(out=ys[:], in_=tC[:, :]), reads=[tCr], writes=[ys_r])
                    fw.op("vector", lambda e, FM=FM, TM=TM, V=V, ys=ys: e.tensor_tensor(out=ys[:], in0=tA[:, :], in1=ys[:], op=ALU.add), reads=[tAr, ys_r], writes=[ys_r])
                    ydst = self.yd[z][tile * 128:(tile + 1) * 128, :].rearrange("t (hp hh i) -> t hp hh i", hh=2, i=64)[:, :, hf, :]
                    fw.dma(ydst, ys[:].rearrange("p (hp i) -> p hp i", i=64), reads=[ys_r], sem_res=ys_r)

    def rwkv_O(self, i, j, R, ts):
        fw = self.fw
        I = self.I
        lnw, lnw_r = fw.sb("o_lnw", [128, D], F32, ts)
        lnb, lnb_r = fw.sb("o_lnb", [128, D], F32, ts)
        fw.dma(lnw[:], I["rw_ln_w"][j, :].partition_broadcast(128), writes=[lnw_r], sem_res=lnw_r)
        fw.dma(lnb[:], I["rw_ln_b"][j, :].partition_broadcast(128), writes=[lnb_r], sem_res=lnb_r)
        epsg, epsg_r = fw.sb("o_eps", [128, 1], F32, ts)
        fw.op("vector", lambda e: e.memset(epsg[:], 64e-5), writes=[epsg_r])
        y0b = [fw.sb("o_y0%d" % k, [128, D], F32, ts) for k in range(2)]
        y1b = [fw.sb("o_y1%d" % k, [128, D], F32, ts) for k in range(2)]
        sqb = [fw.sb("o_sq%d" % k, [128, D], F32, ts) for k in range(2)]
        stb = [fw.sb("o_st%d" % k, [128, 2, 16], F32, ts) for k in range(2)]
        bnb = [fw.sb("o_bn%d" % k, [128, 8, 128], F32, ts) for k in range(2)]
        zb = [fw.sb("o_z%d" % k, [128, 8, 128], BF16, ts) for k in range(2)]
        ob = [fw.sb("o_o%d" % k, [128, 8, 128], F32, ts) for k in range(2)]
        ogb = [fw.sb("o_og%d" % k, [128, 8, 128], BF16, ts) for k in range(2)]
        for tt in range(NTI):
            k = tt % 2
            (y0, y0_r), (y1, y1_r), (sq, sq_r), (st, st_r) = y0b[k], y1b[k], sqb[k], stb[k]
            (bn, bn_r), (zt, zt_r), (o_, o_r), (og_, og_r_) = bnb[k], zb[k], ob[k], ogb[k]
            rs = slice(tt * 128, (tt + 1) * 128)
            fw.dma(y0[:], self.yd[0][rs, :], writes=[y0_r], sem_res=y0_r)
            fw.dma(y1[:], self.yd[1][rs, :], writes=[y1_r], sem_res=y1_r)
            fw.dma(bn[:], self.bonusd[:, :, rs].rearrange("h p t -> p h t"), writes=[bn_r], sem_res=bn_r)
            fw.dma(zt[:], self.zsd[:, :, rs].rearrange("h p t -> p h t"), writes=[zt_r], sem_res=zt_r)
            fw.op("gpsimd", lambda e, y0=y0, y1=y1: e.tensor_tensor(out=y0[:], in0=y0[:], in1=y1[:], op=ALU.add), reads=[y0_r, y1_r], writes=[y0_r])
            y3 = y0[:].rearrange("p (h d) -> p h d", d=64)
            s3 = sq[:].rearrange("p (h d) -> p h d", d=64)
            fw.op("vector", lambda e, st=st, y3=y3: e.tensor_reduce(out=st[:, 0, :], in_=y3, axis=AX.X, op=ALU.add), reads=[y0_r], writes=[st_r])
            fw.op("vector", lambda e, st=st: e.tensor_scalar(out=st[:, 0, :], in0=st[:, 0, :], scalar1=1.0 / 64.0, scalar2=None, op0=ALU.mult), reads=[st_r], writes=[st_r])
            mbc = st[:, 0, :].unsqueeze(2).to_broadcast([128, 16, 64])
            fw.op("vector", lambda e, y3=y3, mbc=mbc: e.tensor_tensor(out=y3, in0=y3, in1=mbc, op=ALU.subtract), reads=[y0_r, st_r], writes=[y0_r])
            fw.op("gpsimd", lambda e, sq=sq, y0=y0: e.tensor_tensor(out=sq[:], in0=y0[:], in1=y0[:], op=ALU.mult), reads=[y0_r], writes=[sq_r])
            fw.op("vector", lambda e, st=st, s3=s3: e.tensor_reduce(out=st[:, 1, :], in_=s3, axis=AX.X, op=ALU.add), reads=[sq_r], writes=[st_r])
            fw.op("scalar", lambda e, st=st: e.activation(out=st[:, 1, :], in_=st[:, 1, :], func=AF.Sqrt, bias=epsg[:], scale=1.0 / 64.0), reads=[st_r, epsg_r], writes=[st_r])
            fw.op("vector", lambda e, st=st: e.reciprocal(out=st[:, 1, :], in_=st[:, 1, :]), reads=[st_r], writes=[st_r])
            rbc = st[:, 1, :].unsqueeze(2).to_broadcast([128, 16, 64])
            fw.op("vector", lambda e, y3=y3, rbc=rbc: e.tensor_tensor(out=y3, in0=y3, in1=rbc, op=ALU.mult), reads=[y0_r, st_r], writes=[y0_r])
            fw.op("gpsimd", lambda e, y0=y0: e.tensor_tensor(out=y0[:], in0=y0[:], in1=lnw[:], op=ALU.mult), reads=[y0_r, lnw_r], writes=[y0_r])
            fw.op("gpsimd", lambda e, y0=y0: e.tensor_tensor(out=y0[:], in0=y0[:], in1=lnb[:], op=ALU.add), reads=[y0_r, lnb_r], writes=[y0_r])
            for half in range(2):
                tp, tpr = self.pb[(tt * 2 + half) % 8]
                for q in range(4):
                    hc = half * 4 + q
                    fw.op("tensor", lambda e, tp=tp, q=q, hc=hc, y0=y0: e.transpose(tp[:, q * 128:(q + 1) * 128], y0[:, hc * 128:(hc + 1) * 128], self.ident[:]), reads=[y0_r, self.ident_r], writes=[tpr], signal=(q == 3))
                fw.op("vector", lambda e, tp=tp, half=half, o_=o_, bn=bn: e.tensor_tensor(out=o_[:, half * 4:half * 4 + 4, :], in0=tp[:, :].rearrange("p (h t) -> p h t", h=4), in1=bn[:, half * 4:half * 4 + 4, :], op=ALU.add), reads=[tpr, bn_r], writes=[o_r])
            fw.op("gpsimd", lambda e, og_=og_, o_=o_, zt=zt: e.tensor_tensor(out=og_[:], in0=o_[:], in1=zt[:], op=ALU.mult), reads=[o_r, zt_r], writes=[og_r_])
            fw.dma(self.og[:, :, rs].rearrange("h p t -> p h t"), og_[:], reads=[og_r_], sem_res=og_r_)


_CACHE = {}


def get_prog(n_layers=DEPTH, dbg=False):
    key = (n_layers, dbg)
    if key not in _CACHE:
        p = Prog(n_layers, dbg)
        p.build()
        _CACHE[key] = p
    return _CACHE[key]


def make_in_maps(inputs, consts, cores):
    maps = []
    for b in cores:
        m = {}
        for k, shp in INPUT_SHAPES.items():
            a = np.asarray(inputs[k])
            if k in ("x", "ctx"):
                a = a[b]
            elif k == "c":
                a = a[b:b + 1]
            a = np.ascontiguousarray(a, dtype=np.float32).reshape(shp)
            m[k] = a
        for k, v in consts.items():
            m["k_" + k] = v
        maps.append(m)
    return maps


def kernel(**inputs):
    p = get_prog()
    cores = list(range(8))
    in_maps = make_in_maps(inputs, p.consts, cores)
    res = run_bass_kernel_spmd(p.nc, in_maps, core_ids=cores)
    out = np.stack([np.asarray(r["out"]) for r in res.results], axis=0)
    return out.astype(np.float32)
```

```python
import math
from contextlib import ExitStack

import numpy as np
import concourse.bass as bass
import concourse.mybir as mybir
from concourse.bass_utils import run_bass_kernel_spmd

F32 = mybir.dt.float32
BF16 = mybir.dt.bfloat16
AF = mybir.ActivationFunctionType
ALU = mybir.AluOpType
AX = mybir.AxisListType

D = 1024
NL = 4096
NCX = 256
NT = NL + NCX
NTI = NT // 128
DEPTH = 4
NORM_EPS = 1e-6
SUBLN_EPS = 1e-5
GRID_W = 64


class Sem:
    def __init__(self, fw, name):
        self.name = name
        self.handle = fw.stack.enter_context(fw.nc.semaphore(name))
        self.issued = 0
        fw.sems.append(self)


class Res:
    __slots__ = ("name", "writers", "readers", "dsem")

    def __init__(self, name):
        self.name = name
        self.writers = []
        self.readers = []
        self.dsem = None


class Eng:
    def __init__(self, fw, key):
        self.key = key
        self.sem = Sem(fw, "e_" + key)
        self.ops = []
        self.waited = {}
        self.pend_r = []
        self.pend_w = []


class FW:
    def __init__(self, nc, stack):
        self.nc = nc
        self.stack = stack
        self.sems = []
        self.eng = {k: Eng(self, k) for k in ("tensor", "vector", "scalar", "gpsimd", "sync")}
        self.n_ins = 0
        self.free_sems = []

    def sb(self, name, shape, dtype, stack=None):
        self.uid = getattr(self, "uid", 0) + 1
        name = "%s_u%d" % (name, self.uid)
        t = (stack or self.stack).enter_context(self.nc.sbuf_tensor(name, list(shape), dtype))
        return t, self.res(name, stack)

    def res(self, name, stack=None):
        r = Res(name)
        if stack is not None:
            stack.callback(self._release, r)
        return r

    def _release(self, r):
        if r.dsem is not None:
            self.free_sems.append(r.dsem)
            r.dsem = None

    def _get_dsem(self, name):
        if self.free_sems:
            sem = self.free_sems.pop(0)
            for e in self.eng.values():
                assert e.waited.get(sem, 0) >= sem.issued, "semaphore reused before a barrier"
            return sem
        return Sem(self, "d%d" % len(self.sems))

    def ps(self, name, shape, dtype, stack=None):
        t = (stack or self.stack).enter_context(self.nc.psum_tensor(name, list(shape), dtype))
        return t, Res(name)

    def _waits_for(self, eng, reads, writes):
        need = {}

        def add(tok):
            sem, val = tok
            if val is None:
                val = sem.issued
            if need.get(sem, 0) < val:
                need[sem] = val
        for r in reads:
            for t in r.writers:
                add(t)
        for w in writes:
            for t in w.writers:
                add(t)
            for t in w.readers:
                add(t)
        out = []
        for sem, val in need.items():
            if eng.waited.get(sem, 0) >= val:
                continue
            eng.waited[sem] = val
            out.append((sem.handle, val))
        return out

    def _check_pending(self, eng, reads, writes):
        for e in self.eng.values():
            if e is eng or (not e.pend_w and not e.pend_r):
                continue
            for r in reads:
                assert r not in e.pend_w, ("unsignaled write pending", r.name, e.key)
            for w in writes:
                assert w not in e.pend_w and w not in e.pend_r, ("unsignaled access pending", w.name, e.key)

    def op(self, ek, fn, reads=(), writes=(), signal=True):
        eng = self.eng[ek]
        self._check_pending(eng, reads, writes)
        waits = self._waits_for(eng, reads, writes)
        self.n_ins += 1
        if signal:
            eng.sem.issued += 1
            tok = (eng.sem, eng.sem.issued)
            semh = eng.sem.handle

            def run(e, fn=fn, waits=waits, semh=semh):
                for (h, v) in waits:
                    e.wait_ge(h, v)
                fn(e).then_inc(semh, 1)
            rs = list(reads) + eng.pend_r
            ws = list(writes) + eng.pend_w
            eng.pend_r = []
            eng.pend_w = []
            for r in rs:
                r.readers.append(tok)
            for w in ws:
                w.writers = [tok]
                w.readers = []
        else:
            def run(e, fn=fn, waits=waits):
                for (h, v) in waits:
                    e.wait_ge(h, v)
                fn(e)
            eng.pend_r.extend(reads)
            eng.pend_w.extend(writes)
        eng.ops.append(run)

    def dma(self, out, in_, reads=(), writes=(), sem_res=None, ek="sync", **kw):
        eng = self.eng[ek]
        self._check_pending(eng, reads, writes)
        waits = self._waits_for(eng, reads, writes)
        if sem_res.dsem is None:
            sem_res.dsem = self._get_dsem(sem_res.name)
        sem = sem_res.dsem
        sem.issued += 16
        tok = (sem, None)
        semh = sem.handle
        self.n_ins += 1

        def run(e, waits=waits, semh=semh, out=out, in_=in_, kw=kw):
            for (h, v) in waits:
                e.wait_ge(h, v)
            e.dma_start(out=out, in_=in_, **kw).then_inc(semh, 16)
        eng.ops.append(run)
        for r in reads:
            r.readers.append(tok)
        for w in writes:
            w.writers = [tok]
            w.readers = []

    def barrier(self):
        for e in self.eng.values():
            assert not e.pend_r and not e.pend_w
        for e in self.eng.values():
            waits = []
            for s in self.sems:
                if s.issued > 0 and e.waited.get(s, 0) < s.issued:
                    e.waited[s] = s.issued
                    waits.append((s.handle, s.issued))

            def run(en, waits=waits):
                for (h, v) in waits:
                    en.wait_ge(h, v)
            e.ops.append(run)

    def finish(self):
        self.barrier()
        with self.nc.Block() as block:
            for key in ("sync", "tensor", "vector", "scalar", "gpsimd"):
                e = self.eng[key]

                def body(h, e=e):
                    for o in e.ops:
                        o(h)
                getattr(block, key)(body)


class Rec:
    def __init__(self):
        self.items = []

    def op(self, *a, **k):
        self.items.append((0, a, k))

    def dma(self, *a, **k):
        self.items.append((1, a, k))


def interleave(fw, recs):
    idx = [0] * len(recs)
    n = [len(r.items) for r in recs]
    while True:
        best, bf = -1, 2.0
        for i in range(len(recs)):
            if idx[i] < n[i]:
                f = idx[i] / n[i]
                if f < bf:
                    best, bf = i, f
        if best < 0:
            break
        kind, a, k = recs[best].items[idx[best]]
        idx[best] += 1
        (fw.dma if kind else fw.op)(*a, **k)


def rope_tables():
    n = np.arange(NL)
    row = (n // GRID_W).astype(np.float32)
    col = (n % GRID_W).astype(np.float32)
    inv = (10000.0 ** (-np.arange(0, 32, 2, dtype=np.float32) / 32.0)).astype(np.float32)
    C = np.ones((128, NT), np.float32)
    S = np.zeros((128, NT), np.float32)
    for f in range(128):
        axis = (f % 64) // 32
        j = f % 16
        pos = row if axis == 0 else col
        ang = (pos * inv[j]).astype(np.float32)
        C[f, NCX:] = np.cos(ang)
        S[f, NCX:] = np.sin(ang)
    return C, S


def host_consts():
    cs = {}
    cs["ident"] = np.eye(128, dtype=np.float32)
    C, S = rope_tables()
    cs["ropeC"] = C
    cs["ropeS"] = S
    import ml_dtypes
    bf = ml_dtypes.bfloat16
    for name, L in (("L", NL), ("X", NCX)):
        l = np.arange(L, dtype=np.int64)
        ang = (2.0 * np.pi / L) * ((l[:, None] * l[None, :]) % L).astype(np.float64)
        sc = 1.0 / math.sqrt(L)
        cs["dftC" + name] = (np.cos(ang) * sc).astype(np.float32).astype(bf)
        cs["dftS" + name] = (np.sin(ang) * sc).astype(np.float32).astype(bf)
    c = np.arange(128, dtype=np.int64)
    ang = (2.0 * np.pi / 128) * ((c[:, None] * c[None, :]) % 128).astype(np.float64)
    sc = 1.0 / math.sqrt(128.0)
    cs["dftCc"] = (np.cos(ang) * sc).astype(np.float32).astype(bf)
    cs["dftnSc"] = (-np.sin(ang) * sc).astype(np.float32).astype(bf)
    a = np.arange(128)
    same = (a[:, None] // 64) == (a[None, :] // 64)
    lt = (a[:, None] < a[None, :]) & same
    le = (a[:, None] <= a[None, :]) & same
    gt = (a[:, None] > a[None, :]) & same
    ge = (a[:, None] >= a[None, :]) & same
    cs["mN0"] = np.concatenate([lt, le], axis=1).astype(np.float32)
    cs["mL0"] = gt.astype(np.float32)
    cs["mN1"] = np.concatenate([gt, ge], axis=1).astype(np.float32)
    cs["mL1"] = lt.astype(np.float32)
    return cs


INPUT_SHAPES = {
    "x": [NL, D], "c": [1, D], "ctx": [NCX, D], "c_ctx": [1, D], "norm_gain": [4, D], "ada_w": [4, D, 3 * D],
    "ada_b": [4, 3 * D], "final_gain": [1, D], "da_w_in": [2, D, 4 * D], "da_lam_q": [2, 128], "da_lam_k": [2, 128],
    "da_subln_gain": [2, 128], "da_w_out": [2, D, D], "fn_w_in": [1, D, 2 * D], "fn_w_group": [1, 8, 128, 128],
    "fn_w_out": [1, D, D], "rw_w_in": [1, D, 4352], "rw_mu": [1, 3328], "rw_w0": [1, 2, D], "rw_w_up": [1, 2, 64, D],
    "rw_a0": [1, 2, D], "rw_a_up": [1, 2, 64, D], "rw_k_k": [1, D], "rw_k_a": [1, D], "rw_r_k": [1, D],
    "rw_ln_w": [1, D], "rw_ln_b": [1, D], "rw_w_out": [1, D, D],
}


class Prog:
    def __init__(self, n_layers=DEPTH, dbg=False):
        self.n_layers = n_layers
        self.dbg = dbg
        self.nc = bass.Bass("TRN2", target_bir_lowering=False)
        self.consts = host_consts()

    def din(self, name, shape, dt=F32):
        return self.nc.dram_tensor(name, list(shape), dt, kind="ExternalInput").ap()

    def build(self):
        nc = self.nc
        self.I = {k: self.din(k, s) for k, s in INPUT_SHAPES.items()}
        self.Cn = {k: self.din("k_" + k, v.shape, F32 if v.dtype == np.float32 else BF16) for k, v in self.consts.items()}
        self.out = nc.dram_tensor("out", [NL, D], F32, kind="ExternalOutput").ap()
        if self.dbg:
            self.dbg_ctx = nc.dram_tensor("dbg_ctx", [NCX, D], F32, kind="ExternalOutput").ap()
        self.xres = nc.dram_tensor("xres", [NT, D], F32, kind="Internal").ap()
        self.og = nc.dram_tensor("og", [8, 128, NT], BF16, kind="Internal").ap()
        self.zsd = nc.dram_tensor("zsd", [8, 128, NT], BF16, kind="Internal").ap()
        self.zsd_r = [[Res("zsd%d_%d" % (h, c)) for c in range(9)] for h in range(8)]
        self.xres_r = [Res("xres%d" % t) for t in range(NTI)]
        self.og_r = [[Res("og%d_%d" % (h, c)) for c in range(9)] for h in range(8)]
        self.out_r = [Res("out%d" % t) for t in range(NTI)]
        with ExitStack() as st:
            self.st = st
            fw = self.fw = FW(nc, st)
            self.setup_persistent()
            for i in range(self.n_layers):
                self.layer(i)
            fw.finish()
        return nc

    @staticmethod
    def chunk(ci):
        if ci == 0:
            return 0, NCX
        return NCX + (ci - 1) * 512, 512

    def xsrc(self, i, tt):
        if i == 0:
            if tt < 2:
                return self.I["ctx"][tt * 128:(tt + 1) * 128, :], []
            return self.I["x"][(tt - 2) * 128:(tt - 1) * 128, :], []
        return self.xres[tt * 128:(tt + 1) * 128, :], [self.xres_r[tt]]

    def setup_persistent(self):
        fw, st = self.fw, self.st
        self.ident, self.ident_r = fw.sb("ident", [128, 128], F32)
        fw.dma(self.ident[:], self.Cn["ident"][:, :], writes=[self.ident_r], sem_res=self.ident_r)
        self.identb, self.identb_r = fw.sb("identb", [128, 128], BF16)
        fw.op("vector", lambda e: e.tensor_copy(out=self.identb[:], in_=self.ident[:]), reads=[self.ident_r], writes=[self.identb_r])
        self.ones_b, self.ones_b_r = fw.sb("ones_b", [128, 128], BF16)
        fw.op("vector", lambda e: e.memset(self.ones_b[:], 1.0), writes=[self.ones_b_r])
        self.mean_b, self.mean_b_r = fw.sb("mean_b", [128, 128], BF16)
        fw.op("vector", lambda e: e.memset(self.mean_b[:], 1.0 / 128.0), writes=[self.mean_b_r])
        self.epsn, self.epsn_r = fw.sb("epsn", [128, 1], F32)
        fw.op("vector", lambda e: e.memset(self.epsn[:], NORM_EPS), writes=[self.epsn_r])
        self.epss, self.epss_r = fw.sb("epss", [128, 1], F32)
        fw.op("vector", lambda e: e.memset(self.epss[:], SUBLN_EPS), writes=[self.epss_r])
        self.gsc, self.gsc_r = fw.sb("gsc", [128, 8, 2], F32)
        self.shf, self.shf_r = fw.sb("shf", [128, 8, 2], F32)
        self.gate, self.gate_r = fw.sb("gate", [128, 2, D], F32)
        self.s_fm, self.s_fm_r = fw.sb("s_fm", [128, 8, 2], F32)
        self.pb = []
        for b in range(8):
            t, r = fw.ps("pb%d" % b, [128, 512], F32)
            self.pb.append((t, r))
        with ExitStack() as ts:
            cc, cc_r = fw.sb("cc", [2, D], F32, ts)
            fw.dma(cc[0:1, :], self.I["c"][:, :], writes=[cc_r], sem_res=cc_r)
            fw.dma(cc[1:2, :], self.I["c_ctx"][:, :], writes=[cc_r], sem_res=cc_r)
            fw.op("scalar", lambda e: e.activation(out=cc[:], in_=cc[:], func=AF.Silu), reads=[cc_r], writes=[cc_r])
            pt, pr = self.pb[0]
            for kc in range(8):
                fw.op("tensor", lambda e, kc=kc: e.transpose(pt[:, kc * 2:kc * 2 + 2], cc[:, kc * 128:(kc + 1) * 128], self.ident[0:2, 0:2]),
                      reads=[cc_r, self.ident_r], writes=[pr], signal=(kc == 7))
            fw.op("vector", lambda e: e.tensor_copy(out=self.s_fm[:].rearrange("p k w -> p (k w)"), in_=pt[:, 0:16]), reads=[pr], writes=[self.s_fm_r])
            fw.barrier()

    def load_fm(self, dst, dst_r, src2d, n, ts):
        fw = self.fw
        tmp, tmp_r = fw.sb("lfm_tmp%d" % fw.n_ins, [n, 128], F32, ts)
        fw.dma(tmp[:], src2d, writes=[tmp_r], sem_res=tmp_r)
        pt, pr = self.pb[1]
        fw.op("tensor", lambda e: e.transpose(pt[:, 0:n], tmp[:], self.ident[0:n, 0:n]), reads=[tmp_r, self.ident_r], writes=[pr])
        fw.op("vector", lambda e: e.tensor_copy(out=dst, in_=pt[:, 0:n]), reads=[pr], writes=[dst_r])

    def phase_mod(self, i):
        fw = self.fw
        W = self.I["ada_w"][i]
        with ExitStack() as ts:
            self.s_rep, self.s_rep_r = fw.sb("s_rep", [128, 8, 2, 128], F32, ts)
            for kc in range(8):
                for w in range(2):
                    fw.op("gpsimd", lambda e, kc=kc, w=w: e.tensor_copy(out=self.s_rep[:, kc, w, :], in_=self.s_fm[:, kc, w:w + 1].to_broadcast([128, 128])),
                          reads=[self.s_fm_r], writes=[self.s_rep_r])
            bfm, bfm_r = fw.sb("bfm", [128, 16], F32, ts)
            gfm, gfm_r = fw.sb("gfm", [128, 8], F32, ts)
            self.load_fm(bfm[:], bfm_r, self.I["ada_b"][i, 0:2048].rearrange("(n p) -> n p", p=128), 16, ts)
            self.load_fm(gfm[:], gfm_r, self.I["norm_gain"][i, :].rearrange("(n p) -> n p", p=128), 8, ts)
            bg, bg_r = fw.sb("bg", [128, D], F32, ts)
            fw.dma(bg[:], self.I["ada_b"][i, 2048:3072].partition_broadcast(128), writes=[bg_r], sem_res=bg_r)
            wt = [fw.sb("adaw%d" % k, [128, 8, 512], F32, ts) for k in range(2)]
            pt, pr = self.pb[2]
            for cc in range(6):
                w_t, w_r = wt[cc % 2]
                for kc in range(8):
                    fw.dma(w_t[:, kc, :], W[kc * 128:(kc + 1) * 128, cc * 512:(cc + 1) * 512], writes=[w_r], sem_res=w_r)
                if cc < 4:
                    for fl in range(4):
                        fc = cc * 4 + fl
                        for kc in range(8):
                            fw.op("tensor", lambda e, kc=kc, fl=fl, fc=fc, w_t=w_t: e.matmul(pt[:, fc * 2:fc * 2 + 2], lhsT=w_t[:, kc, fl * 128:(fl + 1) * 128], rhs=self.s_fm[:, kc, :], start=(kc == 0), stop=(kc == 7)),
                                  reads=[w_r, self.s_fm_r], writes=[pr], signal=(kc == 7 and fl == 3))
                    if cc == 3:
                        ps3 = pt[:, 0:32].rearrange("p (f w) -> p f w", w=2)
                        for w in range(2):
                            fw.op("vector", lambda e, w=w: e.tensor_tensor(out=self.shf[:, :, w], in0=ps3[:, 0:8, w], in1=bfm[:, 0:8], op=ALU.add),
                                  reads=[pr, bfm_r], writes=[self.shf_r])
                            fw.op("vector", lambda e, w=w: e.tensor_tensor(out=self.gsc[:, :, w], in0=ps3[:, 8:16, w], in1=bfm[:, 8:16], op=ALU.add),
                                  reads=[pr, bfm_r], writes=[self.gsc_r])
                            fw.op("vector", lambda e, w=w: e.scalar_tensor_tensor(out=self.gsc[:, :, w], in0=self.gsc[:, :, w], scalar=1.0, in1=gfm[:], op0=ALU.add, op1=ALU.mult),
                                  reads=[self.gsc_r, gfm_r], writes=[self.gsc_r])
                else:
                    cg = cc - 4
                    for w in range(2):
                        gp, gr = self.pb[3 + w]
                        for kc in range(8):
                            fw.op("tensor", lambda e, kc=kc, w=w, gp=gp, w_t=w_t: e.matmul(gp[:, :], lhsT=self.s_rep[:, kc, w, :], rhs=w_t[:, kc, :], start=(kc == 0), stop=(kc == 7)),
                                  reads=[w_r, self.s_rep_r], writes=[gr], signal=(kc == 7))
                        fw.op("vector", lambda e, w=w, gp=gp, cg=cg: e.tensor_tensor(out=self.gate[:, w, cg * 512:(cg + 1) * 512], in0=gp[:, :], in1=bg[:, cg * 512:(cg + 1) * 512], op=ALU.add),
                              reads=[gr, bg_r], writes=[self.gate_r])
            fw.barrier()

    def phase_norm(self, i, hT, hT_r, ts):
        fw = self.fw
        xt = [fw.sb("nx%d" % k, [128, D], F32, ts) for k in range(3)]
        sq, sq_r = fw.sb("nsq", [128, D], BF16, ts)
        ss = [fw.sb("nss%d" % k, [128, 1], F32, ts) for k in range(2)]
        dg = [fw.sb("ndg%d" % k, [128, 128], F32, ts) for k in range(2)]
        for tt in range(NTI):
            x_t, x_r = xt[tt % 3]
            s_t, s_r = ss[tt % 2]
            d_t, d_r = dg[tt % 2]
            w = 1 if tt < 2 else 0
            src, src_r = self.xsrc(i, tt)
            fw.dma(x_t[:], src, reads=src_r, writes=[x_r], sem_res=x_r)
            fw.op("scalar", lambda e, x_t=x_t, s_t=s_t: e.activation(out=sq[:], in_=x_t[:], func=AF.Square, accum_out=s_t[:]), reads=[x_r], writes=[sq_r, s_r])
            fw.op("scalar", lambda e, s_t=s_t: e.activation(out=s_t[:], in_=s_t[:], func=AF.Sqrt, bias=self.epsn[:], scale=1.0 / D), reads=[s_r, self.epsn_r], writes=[s_r])
            fw.op("vector", lambda e, s_t=s_t: e.reciprocal(out=s_t[:], in_=s_t[:]), reads=[s_r], writes=[s_r])
            fw.op("vector", lambda e, s_t=s_t, d_t=d_t: e.tensor_scalar(out=d_t[:], in0=self.ident[:], scalar1=s_t[:, 0:1], scalar2=None, op0=ALU.mult), reads=[s_r, self.ident_r], writes=[d_r])
            for half in range(2):
                pt, pr = self.pb[(tt * 2 + half) % 4]
                for k4 in range(4):
                    kc = half * 4 + k4
                    fw.op("tensor", lambda e, kc=kc, k4=k4, pt=pt, x_t=x_t, d_t=d_t: e.matmul(pt[:, k4 * 128:(k4 + 1) * 128], lhsT=x_t[:, kc * 128:(kc + 1) * 128], rhs=d_t[:], start=True, stop=True),
                          reads=[x_r, d_r], writes=[pr], signal=(k4 == 3))
                for k4 in range(4):
                    kc = half * 4 + k4
                    ek = "scalar" if k4 % 2 == 0 else "vector"
                    if ek == "scalar":
                        fw.op("scalar", lambda e, kc=kc, k4=k4, pt=pt, tt=tt, w=w: e.activation(out=hT[:, kc, tt * 128:(tt + 1) * 128], in_=pt[:, k4 * 128:(k4 + 1) * 128], func=AF.Identity, bias=self.shf[:, kc, w:w + 1], scale=self.gsc[:, kc, w:w + 1]),
                              reads=[pr, self.shf_r, self.gsc_r], writes=[hT_r[tt]])
                    else:
                        fw.op("vector", lambda e, kc=kc, k4=k4, pt=pt, tt=tt, w=w: e.tensor_scalar(out=hT[:, kc, tt * 128:(tt + 1) * 128], in0=pt[:, k4 * 128:(k4 + 1) * 128], scalar1=self.gsc[:, kc, w:w + 1], scalar2=self.shf[:, kc, w:w + 1], op0=ALU.mult, op1=ALU.add),
                              reads=[pr, self.shf_r, self.gsc_r], writes=[hT_r[tt]])

    def phase_out(self, i, w_out, ts):
        fw = self.fw
        last = (i == DEPTH - 1)
        wst = [fw.sb("wo_st%d" % k, [128, D], F32, ts) for k in range(2)]
        wb, wb_r = fw.sb("wo_b", [128, 8, D], BF16, ts)
        for kc in range(8):
            s_t, s_r = wst[kc % 2]
            fw.dma(s_t[:], w_out[kc * 128:(kc + 1) * 128, :], writes=[s_r], sem_res=s_r)
            fw.op("gpsimd", lambda e, kc=kc, s_t=s_t: e.tensor_copy(out=wb[:, kc, :], in_=s_t[:]), reads=[s_r], writes=[wb_r])
        ogt = [fw.sb("po_og%d" % k, [128, 8, 512], BF16, ts) for k in range(2)]
        xt = [fw.sb("po_x%d" % k, [128, D], F32, ts) for k in range(2)]
        yt = [fw.sb("po_y%d" % k, [128, D], F32, ts) for k in range(2)]
        if last:
            fg, fg_r = fw.sb("po_fg", [128, D], F32, ts)
            fw.dma(fg[:], self.I["final_gain"][0, :].partition_broadcast(128), writes=[fg_r], sem_res=fg_r)
            sq, sq_r = fw.sb("po_sq", [128, D], BF16, ts)
            ss = [fw.sb("po_ss%d" % k, [128, 1], F32, ts) for k in range(2)]
        cis = range(1, 9) if last else range(9)
        n = 0
        for ci in cis:
            t0, tn = self.chunk(ci)
            o_t, o_r = ogt[ci % 2]
            for hc in range(8):
                fw.dma(o_t[:, hc, 0:tn], self.og[hc, :, t0:t0 + tn], reads=[self.og_r[hc][ci]], writes=[o_r], sem_res=o_r)
            for tl in range(tn // 128):
                tt = t0 // 128 + tl
                w = 1 if tt < 2 else 0
                x_t, x_r = xt[n % 2]
                y_t, y_r = yt[n % 2]
                src, src_r = self.xsrc(i, tt)
                fw.dma(x_t[:], src, reads=src_r, writes=[x_r], sem_res=x_r)
                for half in range(2):
                    pt, pr = self.pb[(n * 2 + half) % 4]
                    for hc in range(8):
                        fw.op("tensor", lambda e, hc=hc, half=half, pt=pt, o_t=o_t, tl=tl: e.matmul(pt[:, :], lhsT=o_t[:, hc, tl * 128:(tl + 1) * 128], rhs=wb[:, hc, half * 512:(half + 1) * 512], start=(hc == 0), stop=(hc == 7)),
                              reads=[o_r, wb_r], writes=[pr], signal=(hc == 7))
                    fw.op("vector", lambda e, half=half, pt=pt, y_t=y_t, w=w: e.tensor_tensor(out=y_t[:, half * 512:(half + 1) * 512], in0=pt[:, :], in1=self.gate[:, w, half * 512:(half + 1) * 512], op=ALU.mult),
                          reads=[pr, self.gate_r], writes=[y_r])
                fw.op("gpsimd", lambda e, y_t=y_t, x_t=x_t: e.tensor_tensor(out=y_t[:], in0=y_t[:], in1=x_t[:], op=ALU.add), reads=[y_r, x_r], writes=[y_r])
                if not last:
                    fw.dma(self.xres[tt * 128:(tt + 1) * 128, :], y_t[:], reads=[y_r], writes=[self.xres_r[tt]], sem_res=y_r)
                    if self.dbg and i == self.n_layers - 1:
                        if tt < 2:
                            fw.dma(self.dbg_ctx[tt * 128:(tt + 1) * 128, :], y_t[:], reads=[y_r], writes=[self.out_r[tt]], sem_res=y_r)
                        else:
                            fw.dma(self.out[(tt - 2) * 128:(tt - 1) * 128, :], y_t[:], reads=[y_r], writes=[self.out_r[tt]], sem_res=y_r)
                else:
                    s_t, s_r = ss[n % 2]
                    fw.op("scalar", lambda e, y_t=y_t, s_t=s_t: e.activation(out=sq[:], in_=y_t[:], func=AF.Square, accum_out=s_t[:]), reads=[y_r], writes=[sq_r, s_r])
                    fw.op("scalar", lambda e, s_t=s_t: e.activation(out=s_t[:], in_=s_t[:], func=AF.Sqrt, bias=self.epsn[:], scale=1.0 / D), reads=[s_r, self.epsn_r], writes=[s_r])
                    fw.op("vector", lambda e, s_t=s_t: e.reciprocal(out=s_t[:], in_=s_t[:]), reads=[s_r], writes=[s_r])
                    fw.op("vector", lambda e, y_t=y_t, s_t=s_t: e.scalar_tensor_tensor(out=y_t[:], in0=y_t[:], scalar=s_t[:, 0:1], in1=fg[:], op0=ALU.mult, op1=ALU.mult), reads=[y_r, s_r, fg_r], writes=[y_r])
                    fw.dma(self.out[(tt - 2) * 128:(tt - 1) * 128, :], y_t[:], reads=[y_r], writes=[self.out_r[tt]], sem_res=y_r)
                n += 1

    def layer(self, i):
        fw = self.fw
        kind = i % 3
        j = i // 3
        self.phase_mod(i)
        with ExitStack() as ts0:
            pre = None
            if kind == 1:
                pre = self.fnet_prealloc(ts0)
            elif kind == 2:
                pre = self.rwkv_prealloc(ts0)
            with ExitStack() as ts:
                hT, _ = fw.sb("hT", [128, 8, NT], BF16, ts)
                hT_r = [Res("hT%d" % t) for t in range(NTI)]
                with ExitStack() as ts2:
                    self.phase_norm(i, hT, hT_r, ts2)
                    fw.barrier()
                with ExitStack() as ts2:
                    if kind == 0:
                        self.mixer_attn(i, j, hT, hT_r, ts2)
                    elif kind == 1:
                        self.fnet_part1(i, j, hT, hT_r, pre, ts2)
                    else:
                        self.rwkv_part1(i, j, hT, hT_r, pre, ts2)
                    fw.barrier()
            if kind == 1:
                with ExitStack() as ts2:
                    self.fnet_part2(i, j, pre, ts2)
                    fw.barrier()
            elif kind == 2:
                self.rwkv_part2(i, j, pre, ts0)
        with ExitStack() as ts:
            w_out = {0: self.I["da_w_out"], 1: self.I["fn_w_out"], 2: self.I["rw_w_out"]}[kind][j]
            self.phase_out(i, w_out, ts)
            fw.barrier()

    def mixer_attn(self, i, j, hT, hT_r, ts):
        fw = self.fw
        last = (i == DEPTH - 1)
        lambda_init = 0.8 - 0.6 * math.exp(-0.3 * i)
        Win = self.I["da_w_in"][j]
        rC, rC_r = fw.sb("ropeC", [128, NT], F32, ts)
        rS, rS_r = fw.sb("ropeS", [128, NT], F32, ts)
        fw.dma(rC[:], self.Cn["ropeC"][:, :], writes=[rC_r], sem_res=rC_r)
        fw.dma(rS[:], self.Cn["ropeS"][:, :], writes=[rS_r], sem_res=rS_r)
        nlam, nlam_r = fw.sb("nlam", [128, 1], F32, ts)
        lq, lq_r = fw.sb("lq", [128, 128], F32, ts)
        lk, lk_r = fw.sb("lk", [128, 128], F32, ts)
        l2, l2_r = fw.sb("l2", [128, 2], F32, ts)
        fw.dma(lq[:], self.I["da_lam_q"][j, :].partition_broadcast(128), writes=[lq_r], sem_res=lq_r)
        fw.dma(lk[:], self.I["da_lam_k"][j, :].partition_broadcast(128), writes=[lk_r], sem_res=lk_r)
        fw.op("vector", lambda e: e.tensor_tensor(out=lq[:], in0=lq[:], in1=lk[:], op=ALU.mult), reads=[lq_r, lk_r], writes=[lq_r])
        fw.op("vector", lambda e: e.tensor_reduce(out=l2[:], in_=lq[:].rearrange("p (z d) -> p z d", z=2), axis=AX.X, op=ALU.add), reads=[lq_r], writes=[l2_r])
        fw.op("scalar", lambda e: e.activation(out=l2[:], in_=l2[:], func=AF.Exp), reads=[l2_r], writes=[l2_r])
        fw.op("vector", lambda e: e.tensor_tensor(out=nlam[:], in0=l2[:, 1:2], in1=l2[:, 0:1], op=ALU.subtract), reads=[l2_r], writes=[nlam_r])
        fw.op("vector", lambda e: e.tensor_scalar(out=nlam[:], in0=nlam[:], scalar1=-lambda_init, scalar2=None, op0=ALU.add), reads=[nlam_r], writes=[nlam_r])
        sg, sg_r = fw.sb("sg", [128, 1], F32, ts)
        self.load_fm(sg[:], sg_r, self.I["da_subln_gain"][j:j + 1, :], 1, ts)
        fw.op("vector", lambda e: e.tensor_scalar(out=sg[:], in0=sg[:], scalar1=1.0 - lambda_init, scalar2=None, op0=ALU.mult), reads=[sg_r], writes=[sg_r])

        wst = [fw.sb("aw_st%d" % k, [128, 8, 128], F32, ts) for k in range(2)]
        WS = [{nm: fw.sb("%s_%d" % (nm, k), [128, 8, 128], BF16, ts) for nm in ("wq", "wq2", "wk", "wk2", "wv", "wz")} for k in range(2)]
        qT, qT_r = fw.sb("qT", [128, NT], BF16, ts)
        kT, kT_r = None, None
        kTz = [fw.sb("kTz%d" % z, [128, NT], BF16, ts) for z in range(2)]
        for z in range(2):
            fw.op("gpsimd", lambda e, z=z: e.memset(kTz[z][0][:], 0.0), writes=[kTz[z][1]])
        zs, zs_r = fw.sb("zs", [128, NT], BF16, ts)
        vt, vt_r = fw.sb("vt", [128, NTI, 128], BF16, ts)
        t1 = [fw.sb("rp1_%d" % k, [128, 512], F32, ts) for k in range(1)] * 2
        t2 = [fw.sb("rp2_%d" % k, [128, 512], F32, ts) for k in range(1)] * 2
        Eb = [fw.sb("E%d" % k, [128, 512], BF16, ts) for k in range(4)]
        r1, r1_r = fw.sb("ep_r1", [128, 512], F32, ts)
        a1, a1_r = fw.sb("ep_a1", [128, 512], F32, ts)
        r2, r2_r = r1, r1_r
        a2, a2_r = fw.sb("ep_a2", [128, 512], F32, ts)
        sqb, sqb_r = fw.sb("ep_sq", [128, 512], BF16, ts)
        rs, rs_r = r1, r1_r
        ogt = [fw.sb("ep_og%d" % k, [128, 512], BF16, ts) for k in range(2)]

        def rot_cast(dst, dst2, dst_r, dst2_r, s_t, s_r, scale):
            fw.op("gpsimd", lambda e: e.tensor_scalar(out=dst[:], in0=s_t[:], scalar1=scale, scalar2=None, op0=ALU.mult), reads=[s_r], writes=[dst_r])
            sv = s_t[:].rearrange("p k (b h j) -> p k b h j", b=4, h=2)
            dv = dst2[:].rearrange("p k (b h j) -> p k b h j", b=4, h=2)
            for kc in range(0, 8, 4):
                fw.op("gpsimd", lambda e, kc=kc: e.tensor_scalar(out=dv[:, kc:kc + 4, :, 0, :], in0=sv[:, kc:kc + 4, :, 1, :], scalar1=-scale, scalar2=None, op0=ALU.mult), reads=[s_r], writes=[dst2_r])
                fw.op("gpsimd", lambda e, kc=kc: e.tensor_scalar(out=dv[:, kc:kc + 4, :, 1, :], in0=sv[:, kc:kc + 4, :, 0, :], scalar1=scale, scalar2=None, op0=ALU.mult), reads=[s_r], writes=[dst2_r])

        nst = [0]

        def load_weights(hd):
            Wd = WS[hd % 2]
            for which in range(4):
                s_t, s_r = wst[nst[0] % 2]
                nst[0] += 1
                col0 = which * D + hd * 128
                fw.dma(s_t[:], Win[:, col0:col0 + 128].rearrange("(k p) c -> p k c", p=128), writes=[s_r], sem_res=s_r)
                if which == 0:
                    rot_cast(Wd["wq"][0], Wd["wq2"][0], Wd["wq"][1], Wd["wq2"][1], s_t, s_r, 0.125)
                elif which == 1:
                    rot_cast(Wd["wk"][0], Wd["wk2"][0], Wd["wk"][1], Wd["wk2"][1], s_t, s_r, 1.0)
                elif which == 2:
                    fw.op("gpsimd", lambda e, s_t=s_t, d=Wd["wv"][0]: e.tensor_copy(out=d[:], in_=s_t[:]), reads=[s_r], writes=[Wd["wv"][1]])
                else:
                    fw.op("gpsimd", lambda e, s_t=s_t, d=Wd["wz"][0]: e.tensor_copy(out=d[:], in_=s_t[:]), reads=[s_r], writes=[Wd["wz"][1]])

        ncnt = 0
        load_weights(0)
        for hd in range(8):
            Wd = WS[hd % 2]
            (wq, wq_r), (wq2, wq2_r), (wk, wk_r), (wk2, wk2_r), (wv, wv_r), (wz, wz_r) = [Wd[nm] for nm in ("wq", "wq2", "wk", "wk2", "wv", "wz")]
            for ci in range(9):
                t0, tn = self.chunk(ci)
                tts = list(range(t0 // 128, (t0 + tn) // 128))
                hrs = [hT_r[t] for t in tts]
                banks = [self.pb[(ci * 6 + b) % 8] for b in range(6)]
                for b, (wt_, wr_) in enumerate([(wq, wq_r), (wq2, wq2_r), (wk, wk_r), (wk2, wk2_r), (wz, wz_r)]):
                    pt, pr = banks[b]
                    for kc in range(8):
                        fw.op("tensor", lambda e, kc=kc, pt=pt, wt_=wt_, t0=t0, tn=tn: e.matmul(pt[:, 0:tn], lhsT=wt_[:, kc, :], rhs=hT[:, kc, t0:t0 + tn], start=(kc == 0), stop=(kc == 7)),
                              reads=[wr_] + hrs, writes=[pr], signal=(kc == 7))
                pt, pr = banks[5]
                for tl, tt in enumerate(tts):
                    for kc in range(8):
                        fw.op("tensor", lambda e, kc=kc, pt=pt, tl=tl, tt=tt, wv=wv: e.matmul(pt[:, tl * 128:(tl + 1) * 128], lhsT=hT[:, kc, tt * 128:(tt + 1) * 128], rhs=wv[:, kc, :], start=(kc == 0), stop=(kc == 7)),
                              reads=[wv_r, hT_r[tt]], writes=[pr], signal=(kc == 7 and tl == len(tts) - 1))
                for b0, dst, dst_r in ((0, qT, qT_r), (2, kT, kT_r)):
                    p1, p1r = banks[b0]
                    p2, p2r = banks[b0 + 1]
                    ta, ta_r = t1[ncnt % 2]
                    tb, tb_r = t2[ncnt % 2]
                    ncnt += 1
                    fw.op("vector", lambda e, p1=p1, ta=ta, t0=t0, tn=tn: e.tensor_tensor(out=ta[:, 0:tn], in0=p1[:, 0:tn], in1=rC[:, t0:t0 + tn], op=ALU.mult), reads=[p1r, rC_r], writes=[ta_r])
                    fw.op("vector", lambda e, p2=p2, tb=tb, t0=t0, tn=tn: e.tensor_tensor(out=tb[:, 0:tn], in0=p2[:, 0:tn], in1=rS[:, t0:t0 + tn], op=ALU.mult), reads=[p2r, rS_r], writes=[tb_r])
                    if b0 == 0:
                        fw.op("gpsimd", lambda e, ta=ta, tb=tb, dst=dst, t0=t0, tn=tn: e.tensor_tensor(out=dst[:, t0:t0 + tn], in0=ta[:, 0:tn], in1=tb[:, 0:tn], op=ALU.add), reads=[ta_r, tb_r], writes=[dst_r])
                    else:
                        for z in range(2):
                            zr = slice(z * 64, (z + 1) * 64)
                            fw.op("gpsimd", lambda e, ta=ta, tb=tb, z=z, zr=zr, t0=t0, tn=tn: e.tensor_tensor(out=kTz[z][0][zr, t0:t0 + tn], in0=ta[zr, 0:tn], in1=tb[zr, 0:tn], op=ALU.add), reads=[ta_r, tb_r], writes=[kTz[z][1]])
                p4, p4r = banks[4]
                fw.op("scalar", lambda e, p4=p4, t0=t0, tn=tn: e.activation(out=zs[:, t0:t0 + tn], in_=p4[:, 0:tn], func=AF.Silu), reads=[p4r], writes=[zs_r])
                p5, p5r = banks[5]
                fw.op("vector", lambda e, p5=p5, t0=t0, tn=tn: e.tensor_copy(out=vt[:, t0 // 128:(t0 + tn) // 128, :].rearrange("p t e -> p (t e)"), in_=p5[:, 0:tn]), reads=[p5r], writes=[vt_r])
            if hd + 1 < 8:
                load_weights(hd + 1)
            groups = ([] if last else [(0, list(range(2)))]) + [(ci, list(range(NTI))) for ci in range(1, 9)]
            items = []
            for gi, (ci, kts) in enumerate(groups):
                for z in range(2):
                    for idx, kt in enumerate(kts):
                        items.append((gi, ci, z, idx, kt, len(kts)))
            LA = 2
            Ebuf = {}

            def emit_front(n):
                gi, ci, z, idx, kt, nk = items[n]
                q0, qn = self.chunk(ci)
                sp, spr = self.pb[n % 3]
                E_t, E_r = Eb[n % len(Eb)]
                Ebuf[n] = (E_t, E_r)
                fw.op("tensor", lambda e, sp=sp, kt=kt, z=z, q0=q0, qn=qn: e.matmul(sp[:, 0:qn], lhsT=kTz[z][0][:, kt * 128:(kt + 1) * 128], rhs=qT[:, q0:q0 + qn], start=True, stop=True),
                      reads=[kTz[z][1], qT_r], writes=[spr])
                fw.op("scalar", lambda e, sp=sp, E_t=E_t, qn=qn: e.activation(out=E_t[:, 0:qn], in_=sp[:, 0:qn], func=AF.Exp), reads=[spr], writes=[E_r])

            def emit_back(n):
                gi, ci, z, idx, kt, nk = items[n]
                q0, qn = self.chunk(ci)
                E_t, E_r = Ebuf.pop(n)
                po, por = self.pb[3 + z * 2]
                psm, psr = self.pb[4 + z * 2]
                fw.op("tensor", lambda e, po=po, E_t=E_t, kt=kt, qn=qn, idx=idx, nk=nk: e.matmul(po[:, 0:qn], lhsT=vt[:, kt, :], rhs=E_t[:, 0:qn], start=(idx == 0), stop=(idx == nk - 1)),
                      reads=[vt_r, E_r], writes=[por], signal=False)
                fw.op("tensor", lambda e, psm=psm, E_t=E_t, qn=qn, idx=idx, nk=nk: e.matmul(psm[:, 0:qn], lhsT=self.ones_b[:], rhs=E_t[:, 0:qn], start=(idx == 0), stop=(idx == nk - 1)),
                      reads=[self.ones_b_r, E_r], writes=[psr])
                if z == 1 and idx == nk - 1:
                    epilogue(ci)

            def epilogue(ci):
                q0, qn = self.chunk(ci)
                po1, por1 = self.pb[3]
                ps1, psr1 = self.pb[4]
                po2, por2 = self.pb[5]
                ps2, psr2 = self.pb[6]
                fw.op("vector", lambda e, qn=qn: e.reciprocal(out=r1[:, 0:qn], in_=ps1[:, 0:qn]), reads=[psr1], writes=[r1_r])
                fw.op("vector", lambda e, qn=qn: e.tensor_tensor(out=a1[:, 0:qn], in0=po1[:, 0:qn], in1=r1[:, 0:qn], op=ALU.mult), reads=[por1, r1_r], writes=[a1_r])
                fw.op("vector", lambda e, qn=qn: e.reciprocal(out=r2[:, 0:qn], in_=ps2[:, 0:qn]), reads=[psr2], writes=[r2_r])
                fw.op("vector", lambda e, qn=qn: e.tensor_tensor(out=a2[:, 0:qn], in0=po2[:, 0:qn], in1=r2[:, 0:qn], op=ALU.mult), reads=[por2, r2_r], writes=[a2_r])
                fw.op("vector", lambda e, qn=qn: e.scalar_tensor_tensor(out=a1[:, 0:qn], in0=a2[:, 0:qn], scalar=nlam[:, 0:1], in1=a1[:, 0:qn], op0=ALU.mult, op1=ALU.add), reads=[a1_r, a2_r, nlam_r], writes=[a1_r])
                fw.op("vector", lambda e, qn=qn: e.tensor_tensor(out=sqb[:, 0:qn], in0=a1[:, 0:qn], in1=a1[:, 0:qn], op=ALU.mult), reads=[a1_r], writes=[sqb_r])
                mp, mpr = self.pb[7]
                fw.op("tensor", lambda e, qn=qn: e.matmul(mp[:, 0:qn], lhsT=self.mean_b[:], rhs=sqb[:, 0:qn], start=True, stop=True), reads=[self.mean_b_r, sqb_r], writes=[mpr])
                fw.op("scalar", lambda e, qn=qn: e.activation(out=rs[:, 0:qn], in_=mp[:, 0:qn], func=AF.Ln, bias=self.epss[:], scale=1.0), reads=[mpr, self.epss_r], writes=[rs_r])
                fw.op("scalar", lambda e, qn=qn: e.activation(out=rs[:, 0:qn], in_=rs[:, 0:qn], func=AF.Exp, scale=-0.5), reads=[rs_r], writes=[rs_r])
                fw.op("vector", lambda e, qn=qn: e.tensor_tensor(out=a1[:, 0:qn], in0=a1[:, 0:qn], in1=rs[:, 0:qn], op=ALU.mult), reads=[a1_r, rs_r], writes=[a1_r])
                og_t, og_r_ = ogt[ci % 2]
                fw.op("vector", lambda e, og_t=og_t, q0=q0, qn=qn: e.scalar_tensor_tensor(out=og_t[:, 0:qn], in0=a1[:, 0:qn], scalar=sg[:, 0:1], in1=zs[:, q0:q0 + qn], op0=ALU.mult, op1=ALU.mult), reads=[a1_r, sg_r, zs_r], writes=[og_r_])
                fw.dma(self.og[hd, :, q0:q0 + qn], og_t[:, 0:qn], reads=[og_r_], writes=[self.og_r[hd][ci]], sem_res=og_r_)

            for n in range(len(items) + LA):
                if n < len(items):
                    emit_front(n)
                if n - LA >= 0:
                    emit_back(n - LA)

    def load_w_bf16(self, dst, dst_r, w2d, ncols, ts, name):
        fw = self.fw
        wst = [fw.sb("%s_st%d" % (name, k), [128, ncols], F32, ts) for k in range(2)]
        for kc in range(8):
            s_t, s_r = wst[kc % 2]
            fw.dma(s_t[:], w2d[kc * 128:(kc + 1) * 128, :], writes=[s_r], sem_res=s_r)
            fw.op("gpsimd", lambda e, kc=kc, s_t=s_t: e.tensor_copy(out=dst[:, kc, :], in_=s_t[:]), reads=[s_r], writes=[dst_r])

    def gate_proj(self, wz, wz_r, hT, hT_r, ts):
        fw = self.fw
        zt = [fw.sb("gz%d" % k, [128, 512], BF16, ts) for k in range(3)]
        n = 0
        for ci in range(9):
            t0, tn = self.chunk(ci)
            hrs = [hT_r[t] for t in range(t0 // 128, (t0 + tn) // 128)]
            for fc in range(8):
                pt, pr = self.pb[n % 4]
                z_t, z_r = zt[n % 3]
                n += 1
                for kc in range(8):
                    fw.op("tensor", lambda e, kc=kc, pt=pt, fc=fc, t0=t0, tn=tn: e.matmul(pt[:, 0:tn], lhsT=wz[:, kc, fc * 128:(fc + 1) * 128], rhs=hT[:, kc, t0:t0 + tn], start=(kc == 0), stop=(kc == 7)),
                          reads=[wz_r] + hrs, writes=[pr], signal=(kc == 7))
                fw.op("scalar", lambda e, pt=pt, z_t=z_t, tn=tn: e.activation(out=z_t[:, 0:tn], in_=pt[:, 0:tn], func=AF.Silu), reads=[pr], writes=[z_r])
                fw.dma(self.zsd[fc, :, t0:t0 + tn], z_t[:, 0:tn], reads=[z_r], writes=[self.zsd_r[fc][ci]], sem_res=z_r)

    def fnet_prealloc(self, ts):
        fw = self.fw
        Utm, _ = fw.sb("Utm", [128, NTI, D], BF16, ts)
        Utm_r = [Res("Utm%d" % t) for t in range(NTI)]
        return Utm, Utm_r

    def fnet_part1(self, i, j, hT, hT_r, pre, ts):
        fw = self.fw
        Utm, Utm_r = pre
        Win = self.I["fn_w_in"][j]
        wu, wu_r = fw.sb("fwu", [128, 8, D], BF16, ts)
        wz, wz_r = fw.sb("fwz", [128, 8, D], BF16, ts)
        self.load_w_bf16(wu, wu_r, Win[:, 0:D], D, ts, "fwu")
        self.load_w_bf16(wz, wz_r, Win[:, D:2 * D], D, ts, "fwz")
        n = 0
        for tt in range(NTI):
            for half in range(2):
                pt, pr = self.pb[4 + n % 4]
                for kc in range(8):
                    fw.op("tensor", lambda e, kc=kc, pt=pt, tt=tt, half=half: e.matmul(pt[:, :], lhsT=hT[:, kc, tt * 128:(tt + 1) * 128], rhs=wu[:, kc, half * 512:(half + 1) * 512], start=(kc == 0), stop=(kc == 7)),
                          reads=[wu_r, hT_r[tt]], writes=[pr], signal=(kc == 7))
                if n % 2 == 0:
                    fw.op("vector", lambda e, pt=pt, tt=tt, half=half: e.tensor_copy(out=Utm[:, tt, half * 512:(half + 1) * 512], in_=pt[:, :]), reads=[pr], writes=[Utm_r[tt]])
                else:
                    fw.op("scalar", lambda e, pt=pt, tt=tt, half=half: e.copy(out=Utm[:, tt, half * 512:(half + 1) * 512], in_=pt[:, :]), reads=[pr], writes=[Utm_r[tt]])
                n += 1
        self.gate_proj(wz, wz_r, hT, hT_r, ts)

    def fnet_part2(self, i, j, pre, ts):
        fw = self.fw
        Utm, Utm_r = pre
        cc_, cc_r = fw.sb("fCc", [128, 128], BF16, ts)
        sc_, sc_r = fw.sb("fSc", [128, 128], BF16, ts)
        fw.dma(cc_[:], self.Cn["dftCc"][:, :], writes=[cc_r], sem_res=cc_r)
        fw.dma(sc_[:], self.Cn["dftnSc"][:, :], writes=[sc_r], sem_res=sc_r)
        wgs, wgs_r = fw.sb("fwg_st", [128, 8, 128], F32, ts)
        wg, wg_r = fw.sb("fwg", [128, 8, 128], BF16, ts)
        fw.dma(wgs[:], self.I["fn_w_group"][j].rearrange("g c e -> c g e"), writes=[wgs_r], sem_res=wgs_r)
        fw.op("gpsimd", lambda e: e.tensor_copy(out=wg[:], in_=wgs[:]), reads=[wgs_r], writes=[wg_r])
        CB, _ = fw.sb("fCB", [128, 32, 512], BF16, ts)
        SB, _ = fw.sb("fSB", [128, 32, 512], BF16, ts)
        CB_r = [fw.res("fCB%d" % k, ts) for k in range(4)]
        SB_r = [fw.res("fSB%d" % k, ts) for k in range(4)]
        Pb = [fw.sb("fPb%d" % k, [128, 512], BF16, ts) for k in range(2)]
        Qb = [fw.sb("fQb%d" % k, [128, 512], BF16, ts) for k in range(2)]
        Fb = [fw.sb("fFb%d" % k, [128, 512], BF16, ts) for k in range(2)]
        zt = [fw.sb("fzt%d" % k, [128, 512], BF16, ts) for k in range(2)]
        ot = [fw.sb("fot%d" % k, [128, 512], BF16, ts) for k in range(2)]
        last = (i == DEPTH - 1)
        n = 0
        for ci in (range(1, 9) if last else range(9)):
            t0, tn = self.chunk(ci)
            if ci == 0:
                nlt, lt0 = 2, 0
                for k in range(2):
                    fw.dma(CB[:, k, 0:256], self.Cn["dftCX"][k * 128:(k + 1) * 128, :], writes=[CB_r[0]], sem_res=CB_r[0])
                    fw.dma(SB[:, k, 0:256], self.Cn["dftSX"][k * 128:(k + 1) * 128, :], writes=[SB_r[0]], sem_res=SB_r[0])
            else:
                nlt, lt0 = 32, 2
                c0 = (ci - 1) * 512
                for k in range(4):
                    fw.dma(CB[:, k * 8:(k + 1) * 8, :], self.Cn["dftCL"][k * 1024:(k + 1) * 1024, c0:c0 + 512].rearrange("(t p) c -> p t c", p=128), writes=[CB_r[k]], sem_res=CB_r[k])
                    fw.dma(SB[:, k * 8:(k + 1) * 8, :], self.Cn["dftSL"][k * 1024:(k + 1) * 1024, c0:c0 + 512].rearrange("(t p) c -> p t c", p=128), writes=[SB_r[k]], sem_res=SB_r[k])
            for g in range(8):
                pp, ppr = self.pb[(n % 2) * 2]
                qp, qpr = self.pb[(n % 2) * 2 + 1]
                fp, fpr = self.pb[4 + n % 2]
                yp, ypr = self.pb[6 + n % 2]
                P_t, P_r = Pb[n % 2]
                Q_t, Q_r = Qb[n % 2]
                F_t, F_r = Fb[n % 2]
                z_t, z_r = zt[n % 2]
                o_t, o_r = ot[n % 2]
                n += 1
                fw.dma(z_t[:, 0:tn], self.zsd[g, :, t0:t0 + tn], reads=[self.zsd_r[g][ci]], writes=[z_r], sem_res=z_r)
                for (acc, accr, buf, bufr) in ((pp, ppr, CB, CB_r), (qp, qpr, SB, SB_r)):
                    for lt in range(nlt):
                        fw.op("tensor", lambda e, acc=acc, buf=buf, lt=lt, g=g, tn=tn, nlt=nlt, lt0=lt0: e.matmul(acc[:, 0:tn], lhsT=Utm[:, lt0 + lt, g * 128:(g + 1) * 128], rhs=buf[:, lt, 0:tn], start=(lt == 0), stop=(lt == nlt - 1)),
                              reads=[Utm_r[lt0 + lt], bufr[lt // 8]], writes=[accr], signal=(lt == nlt - 1 or lt % 8 == 7))
                fw.op("scalar", lambda e, pp=pp, P_t=P_t, tn=tn: e.copy(out=P_t[:, 0:tn], in_=pp[:, 0:tn]), reads=[ppr], writes=[P_r])
                fw.op("vector", lambda e, qp=qp, Q_t=Q_t, tn=tn: e.tensor_copy(out=Q_t[:, 0:tn], in_=qp[:, 0:tn]), reads=[qpr], writes=[Q_r])
                fw.op("tensor", lambda e, fp=fp, P_t=P_t, tn=tn: e.matmul(fp[:, 0:tn], lhsT=cc_[:], rhs=P_t[:, 0:tn], start=True, stop=False), reads=[cc_r, P_r], writes=[fpr], signal=False)
                fw.op("tensor", lambda e, fp=fp, Q_t=Q_t, tn=tn: e.matmul(fp[:, 0:tn], lhsT=sc_[:], rhs=Q_t[:, 0:tn], start=False, stop=True), reads=[sc_r, Q_r], writes=[fpr])
                fw.op("vector", lambda e, fp=fp, F_t=F_t, tn=tn: e.tensor_copy(out=F_t[:, 0:tn], in_=fp[:, 0:tn]), reads=[fpr], writes=[F_r])
                fw.op("tensor", lambda e, yp=yp, F_t=F_t, g=g, tn=tn: e.matmul(yp[:, 0:tn], lhsT=wg[:, g, :], rhs=F_t[:, 0:tn], start=True, stop=True), reads=[wg_r, F_r], writes=[ypr])
                fw.op("vector", lambda e, yp=yp, z_t=z_t, o_t=o_t, tn=tn: e.tensor_tensor(out=o_t[:, 0:tn], in0=yp[:, 0:tn], in1=z_t[:, 0:tn], op=ALU.mult), reads=[ypr, z_r], writes=[o_r])
                fw.dma(self.og[g, :, t0:t0 + tn], o_t[:, 0:tn], reads=[o_r], writes=[self.og_r[g][ci]], sem_res=o_r)

    def rwkv_prealloc(self, ts):
        fw = self.fw
        nc = self.nc
        R = {}
        R["Dall"] = [fw.sb("Dall%d" % z, [128, 8, 68], F32, ts) for z in range(2)]
        R["stack"] = ts.enter_context(ExitStack())
        R["twd"] = fw.sb("twd", [128, NT], F32, R["stack"])
        R["adT"] = fw.sb("adT", [128, NT], F32, R["stack"])
        if not hasattr(self, "sTd"):
            dk = "ExternalOutput" if self.dbg else "Internal"
            self.sTd = nc.dram_tensor("sTd", [24, 128, NT], F32, kind=dk).ap()
            self.FMd = [nc.dram_tensor("FMd%d" % z, [NTI, 128, 8, 4, 128], BF16, kind="Internal").ap() for z in range(2)]
            self.TMd = [nc.dram_tensor("TMd%d" % z, [NTI, 128, 2, 8, 128], BF16, kind="Internal").ap() for z in range(2)]
            self.Vd = nc.dram_tensor("Vd", [NTI, 128, 8, 128], BF16, kind="Internal").ap()
            self.bonusd = nc.dram_tensor("bonusd", [8, 128, NT], F32, kind=dk).ap()
            self.yd = [nc.dram_tensor("yd%d" % z, [NT, D], F32, kind=dk).ap() for z in range(2)]
        return R

    def rwkv_part1(self, i, j, hT, hT_r, R, ts):
        fw = self.fw
        W = self.I["rw_w_in"][j]
        twd, twd_r = R["twd"]
        adT, adT_r = R["adT"]
        wz, wz_r = fw.sb("rwz", [128, 8, D], BF16, ts)
        self.load_w_bf16(wz, wz_r, W[:, 3328:4352], D, ts, "rwz")
        self.gate_proj(wz, wz_r, hT, hT_r, ts)
        mu_fm, mu_r = fw.sb("mu_fm", [128, 26], F32, ts)
        self.load_fm(mu_fm[:], mu_r, self.I["rw_mu"][j, :].rearrange("(n p) -> n p", p=128), 26, ts)
        ca, ca_r = fw.sb("mix_a", [128, 26], F32, ts)
        cb, cb_r = fw.sb("mix_b", [128, 26], F32, ts)
        fw.op("vector", lambda e: e.tensor_scalar(out=ca[:], in0=mu_fm[:], scalar1=-1.0, scalar2=1.0, op0=ALU.mult, op1=ALU.add), reads=[mu_r], writes=[ca_r])
        fw.op("vector", lambda e: e.tensor_scalar(out=cb[:], in0=mu_fm[:], scalar1=0.5, scalar2=None, op0=ALU.mult), reads=[mu_r], writes=[cb_r])
        NP = NT + 3
        sraw = [fw.sb("sraw%d" % k, [128, NP], F32, ts) for k in range(2)]
        for k in range(2):
            fw.op("gpsimd", lambda e, k=k: e.memset(sraw[k][0][:], 0.0), writes=[sraw[k][1]])
        wst = [fw.sb("rw_st%d" % k, [128, 8, 128], F32, ts) for k in range(2)]
        wbs = [fw.sb("rw_wb%d" % k, [128, 8, 128], BF16, ts) for k in range(2)]
        PW = 512
        tmpb = [fw.sb("mixt%d" % k, [128, PW], F32, ts) for k in range(2)]
        smxb = [fw.sb("mixs%d" % k, [128, PW], F32, ts) for k in range(2)]
        npc = 0
        nev = 0
        for fc in range(26):
            s_t, s_r = wst[fc % 2]
            w_t, w_r = wbs[fc % 2]
            sr_t, sr_r = sraw[fc % 2]
            fw.dma(s_t[:], W[:, fc * 128:(fc + 1) * 128].rearrange("(k p) c -> p k c", p=128), writes=[s_r], sem_res=s_r)
            fw.op("gpsimd", lambda e, s_t=s_t, w_t=w_t: e.tensor_copy(out=w_t[:], in_=s_t[:]), reads=[s_r], writes=[w_r])
            for ci in range(9):
                t0, tn = self.chunk(ci)
                off = t0 + (1 if ci == 0 else 2)
                hrs = [hT_r[t] for t in range(t0 // 128, (t0 + tn) // 128)]
                pt, pr = self.pb[4 + nev % 4]
                for kc in range(8):
                    fw.op("tensor", lambda e, kc=kc, pt=pt, w_t=w_t, t0=t0, tn=tn: e.matmul(pt[:, 0:tn], lhsT=w_t[:, kc, :], rhs=hT[:, kc, t0:t0 + tn], start=(kc == 0), stop=(kc == 7)),
                          reads=[w_r] + hrs, writes=[pr], signal=(kc == 7))
                if nev % 2 == 0:
                    fw.op("scalar", lambda e, pt=pt, sr_t=sr_t, off=off, tn=tn: e.copy(out=sr_t[:, off:off + tn], in_=pt[:, 0:tn]), reads=[pr], writes=[sr_r])
                else:
                    fw.op("vector", lambda e, pt=pt, sr_t=sr_t, off=off, tn=tn: e.tensor_copy(out=sr_t[:, off:off + tn], in_=pt[:, 0:tn]), reads=[pr], writes=[sr_r])
                nev += 1
            for (c0, cn, tok0) in [(1, 256, 0)] + [(258 + q * 512, 512, 256 + q * 512) for q in range(8)]:
                tm_t, tm_r = tmpb[npc % 2]
                sm_t, sm_r = smxb[npc % 2]
                npc += 1
                fw.op("gpsimd", lambda e, tm_t=tm_t, sr_t=sr_t, c0=c0, cn=cn: e.tensor_tensor(out=tm_t[:, 0:cn], in0=sr_t[:, c0 - 1:c0 - 1 + cn], in1=sr_t[:, c0 + 1:c0 + 1 + cn], op=ALU.add), reads=[sr_r], writes=[tm_r])
                fw.op("vector", lambda e, tm_t=tm_t, fc=fc, cn=cn: e.tensor_scalar(out=tm_t[:, 0:cn], in0=tm_t[:, 0:cn], scalar1=cb[:, fc:fc + 1], scalar2=None, op0=ALU.mult), reads=[tm_r, cb_r], writes=[tm_r])
                if fc < 24:
                    fw.op("vector", lambda e, tm_t=tm_t, sm_t=sm_t, sr_t=sr_t, fc=fc, c0=c0, cn=cn: e.scalar_tensor_tensor(out=sm_t[:, 0:cn], in0=sr_t[:, c0:c0 + cn], scalar=ca[:, fc:fc + 1], in1=tm_t[:, 0:cn], op0=ALU.mult, op1=ALU.add),
                          reads=[sr_r, tm_r, ca_r], writes=[sm_r])
                    fw.dma(self.sTd[fc, :, tok0:tok0 + cn], sm_t[:, 0:cn], reads=[sm_r], sem_res=sm_r)
                else:
                    dst, dst_r = (twd, twd_r) if fc == 24 else (adT, adT_r)
                    fw.op("vector", lambda e, tm_t=tm_t, dst=dst, sr_t=sr_t, fc=fc, c0=c0, cn=cn, tok0=tok0: e.scalar_tensor_tensor(out=dst[:, tok0:tok0 + cn], in0=sr_t[:, c0:c0 + cn], scalar=ca[:, fc:fc + 1], in1=tm_t[:, 0:cn], op0=ALU.mult, op1=ALU.add),
                          reads=[sr_r, tm_r, ca_r], writes=[dst_r])
                    if fc == 24:
                        fw.op("scalar", lambda e, tok0=tok0, cn=cn: e.activation(out=twd[:, tok0:tok0 + cn], in_=twd[:, tok0:tok0 + cn], func=AF.Tanh), reads=[twd_r], writes=[twd_r])

    def rwkv_part2(self, i, j, R, ts0):
        fw = self.fw
        stop = getattr(self, "stop_at", 99)
        fw.barrier()
        if stop < 2:
            return
        with ExitStack() as ts:
            self.rwkv_F2(i, j, R, ts)
            fw.barrier()
        R["stack"].close()
        if stop < 3:
            return
        with ExitStack() as ts:
            self.rwkv_S(i, j, R, ts)
            fw.barrier()
        if stop < 4:
            return
        with ExitStack() as ts:
            self.rwkv_O(i, j, R, ts)
            fw.barrier()

    def rwkv_F2(self, i, j, R, ts):
        fw = self.fw
        I = self.I
        twd, twd_r = R["twd"]
        adT, adT_r = R["adT"]
        def fm(name, src, n):
            t, r = fw.sb(name, [128, n], F32, ts)
            self.load_fm(t[:], r, src, n, ts)
            return t, r
        kk_p, kk_pr = fm("p_kk", I["rw_k_k"][j, :].rearrange("(n p) -> n p", p=128), 8)
        ka_p, ka_pr = fm("p_ka", I["rw_k_a"][j, :].rearrange("(n p) -> n p", p=128), 8)
        rk_p, rk_pr = fm("p_rk", I["rw_r_k"][j, :].rearrange("(n p) -> n p", p=128), 8)
        w0_p, w0_pr = fm("p_w0", I["rw_w0"][j].rearrange("z (n p) -> (z n) p", p=128), 16)
        a0_p, a0_pr = fm("p_a0", I["rw_a0"][j].rearrange("z (n p) -> (z n) p", p=128), 16)
        oka, oka_r = fw.sb("p_oka", [128, 8], F32, ts)
        fw.op("vector", lambda e: e.tensor_scalar(out=oka[:], in0=ka_p[:], scalar1=-1.0, scalar2=1.0, op0=ALU.mult, op1=ALU.add), reads=[ka_pr], writes=[oka_r])
        wup, wup_r = fw.sb("wup", [128, D], F32, ts)
        aup, aup_r = fw.sb("aup", [128, D], F32, ts)
        fw.dma(wup[:], I["rw_w_up"][j].rearrange("z r f -> (z r) f"), writes=[wup_r], sem_res=wup_r)
        fw.dma(aup[:], I["rw_a_up"][j].rearrange("z r f -> (z r) f"), writes=[aup_r], sem_res=aup_r)
        bones, bones_r = fw.sb("bones", [128, 128], F32, ts)
        fw.op("vector", lambda e: e.memset(bones[:], 0.0), writes=[bones_r])
        fw.op("vector", lambda e: e.memset(bones[0:64, 0:64], 1.0), writes=[bones_r])
        fw.op("vector", lambda e: e.memset(bones[64:128, 64:128], 1.0), writes=[bones_r])
        eps12, eps12_r = fw.sb("eps12", [128, 1], F32, ts)
        fw.op("vector", lambda e: e.memset(eps12[:], 1e-12), writes=[eps12_r])
        cmask, cmask_r = fw.sb("cmask", [128, 512], F32, ts)
        fw.op("vector", lambda e: e.memset(cmask[:], 1.0), writes=[cmask_r])
        fw.op("vector", lambda e: e.memset(cmask[:].rearrange("p (c t) -> p c t", t=64)[:, :, 0:1], 0.0), writes=[cmask_r])

        cnt = [0]

        def T(name, dt=F32, n=2, w=512):
            return [fw.sb("%s_%d" % (name, k), [128, w], dt, ts) for k in range(n)]
        rTb, kTb, vTb = T("f_r"), T("f_k"), T("f_v")
        lwb = [T("f_lw0"), T("f_lw1")]
        azb = [T("f_az0"), T("f_az1")]
        kdb = [T("f_kd0"), T("f_kd1")]
        bzb = [T("f_b0"), T("f_b1")]
        kkrb, sqb_, kkb, tb1, tb3 = T("f_kkr"), T("f_sq"), T("f_kk"), T("f_t1"), T("f_t3", n=4)
        rnb = sqb_
        tb2 = sqb_
        cwb, e1b, e2b, e3b, e4b = T("f_cw", n=4), T("f_e1", n=4), T("f_e2", n=4), T("f_e3", n=4), T("f_e4", n=2) * 2
        fmo = [T("f_o%d" % k, dt=BF16, n=4) for k in range(4)]
        hato = [T("f_h%d" % k, n=4) for k in range(2)]
        trs = T("f_trs", dt=BF16, n=4)
        bonb = T("f_bon")
        NEG = -math.exp(-0.5)
        nb = 0
        ntr = [0]

        def transpose_store(fx, k, cnt, src_t, src_r, dst_ap_fn, ntile):
            tp, tpr = self.pb[k * 4 + 2 + cnt[0] % 2]
            o_t, o_r = trs[k * 2 + cnt[0] % 2]
            cnt[0] += 1
            for tl in range(ntile):
                fx.op("tensor", lambda e, tl=tl, tp=tp: e.transpose(tp[:, tl * 128:(tl + 1) * 128], src_t[:, tl * 128:(tl + 1) * 128], self.ident[:]),
                      reads=[src_r, self.ident_r], writes=[tpr], signal=(tl == ntile - 1))
            fx.op("scalar", lambda e, tp=tp, o_t=o_t, ntile=ntile: e.copy(out=o_t[:, 0:ntile * 128], in_=tp[:, 0:ntile * 128]), reads=[tpr], writes=[o_r])
            for tl in range(ntile):
                fx.dma(dst_ap_fn(tl), o_t[:, tl * 128:(tl + 1) * 128], reads=[o_r], sem_res=o_r)

        def emit_block(fx, hp, ci, k):
            cnt = [0]
            t0, N = self.chunk(ci)
            tile0 = t0 // 128
            ntile = N // 128
            nch = N // 64
            (rT, rT_r), (kT, kT_r), (vT, vT_r) = rTb[k], kTb[k], vTb[k]
            fx.dma(rT[:, 0:N], self.sTd[hp, :, t0:t0 + N], writes=[rT_r], sem_res=rT_r)
            fx.dma(kT[:, 0:N], self.sTd[8 + hp, :, t0:t0 + N], writes=[kT_r], sem_res=kT_r)
            fx.dma(vT[:, 0:N], self.sTd[16 + hp, :, t0:t0 + N], writes=[vT_r], sem_res=vT_r)
            transpose_store(fx, k, cnt, vT, vT_r, lambda tl, tile0=tile0, hp=hp: self.Vd[tile0 + tl, :, hp, :], ntile)
            for z in range(2):
                rows = slice(z * 64, (z + 1) * 64)
                pw, pwr = self.pb[k * 4]
                pa, par = self.pb[k * 4 + 1]
                lw, lw_r = lwb[z][k]
                az, az_r = azb[z][k]
                fx.op("tensor", lambda e, pw=pw, rows=rows, hp=hp, t0=t0, N=N: e.matmul(pw[:, 0:N], lhsT=wup[rows, hp * 128:(hp + 1) * 128], rhs=twd[rows, t0:t0 + N], start=True, stop=True), reads=[wup_r, twd_r], writes=[pwr])
                fx.op("tensor", lambda e, pa=pa, rows=rows, hp=hp, t0=t0, N=N: e.matmul(pa[:, 0:N], lhsT=aup[rows, hp * 128:(hp + 1) * 128], rhs=adT[rows, t0:t0 + N], start=True, stop=True), reads=[aup_r, adT_r], writes=[par])
                fx.op("scalar", lambda e, pw=pw, lw=lw, z=z, hp=hp, N=N: e.activation(out=lw[:, 0:N], in_=pw[:, 0:N], func=AF.Sigmoid, bias=w0_p[:, z * 8 + hp:z * 8 + hp + 1], scale=1.0), reads=[pwr, w0_pr], writes=[lw_r])
                fx.op("vector", lambda e, lw=lw, N=N: e.tensor_scalar(out=lw[:, 0:N], in0=lw[:, 0:N], scalar1=NEG, scalar2=None, op0=ALU.mult), reads=[lw_r], writes=[lw_r])
                fx.op("scalar", lambda e, pa=pa, az=az, z=z, hp=hp, N=N: e.activation(out=az[:, 0:N], in_=pa[:, 0:N], func=AF.Sigmoid, bias=a0_p[:, z * 8 + hp:z * 8 + hp + 1], scale=1.0), reads=[par, a0_pr], writes=[az_r])
            (kkr, kkr_r), (sq, sq_r), (rn, rn_r), (kk, kk_r) = kkrb[k], sqb_[k], rnb[k], kkb[k]
            fx.op("vector", lambda e, kkr=kkr, kT=kT, hp=hp, N=N: e.tensor_scalar(out=kkr[:, 0:N], in0=kT[:, 0:N], scalar1=kk_p[:, hp:hp + 1], scalar2=None, op0=ALU.mult), reads=[kT_r, kk_pr], writes=[kkr_r])
            fx.op("gpsimd", lambda e, kkr=kkr, sq=sq, N=N: e.tensor_tensor(out=sq[:, 0:N], in0=kkr[:, 0:N], in1=kkr[:, 0:N], op=ALU.mult), reads=[kkr_r], writes=[sq_r])
            pss, pssr = self.pb[k * 4 + 2 + cnt[0] % 2]
            cnt[0] += 1
            fx.op("tensor", lambda e, pss=pss, sq=sq, N=N: e.matmul(pss[:, 0:N], lhsT=bones[:], rhs=sq[:, 0:N], start=True, stop=True), reads=[bones_r, sq_r], writes=[pssr])
            fx.op("scalar", lambda e, pss=pss, rn=rn, N=N: e.activation(out=rn[:, 0:N], in_=pss[:, 0:N], func=AF.Sqrt, bias=eps12[:], scale=1.0), reads=[pssr, eps12_r], writes=[rn_r])
            fx.op("vector", lambda e, rn=rn, N=N: e.reciprocal(out=rn[:, 0:N], in_=rn[:, 0:N]), reads=[rn_r], writes=[rn_r])
            fx.op("gpsimd", lambda e, kk=kk, kkr=kkr, rn=rn, N=N: e.tensor_tensor(out=kk[:, 0:N], in0=kkr[:, 0:N], in1=rn[:, 0:N], op=ALU.mult), reads=[kkr_r, rn_r], writes=[kk_r])
            for z in range(2):
                az, az_r = azb[z][k]
                kd, kd_r = kdb[z][k]
                bz, bz_r = bzb[z][k]
                fx.op("vector", lambda e, kd=kd, az=az, hp=hp, N=N: e.tensor_scalar(out=kd[:, 0:N], in0=az[:, 0:N], scalar1=ka_p[:, hp:hp + 1], scalar2=oka[:, hp:hp + 1], op0=ALU.mult, op1=ALU.add), reads=[az_r, ka_pr, oka_r], writes=[kd_r])
                fx.op("vector", lambda e, kd=kd, kT=kT, N=N: e.tensor_tensor(out=kd[:, 0:N], in0=kd[:, 0:N], in1=kT[:, 0:N], op=ALU.mult), reads=[kd_r, kT_r], writes=[kd_r])
                fx.op("gpsimd", lambda e, bz=bz, kk=kk, az=az, N=N: e.tensor_tensor(out=bz[:, 0:N], in0=kk[:, 0:N], in1=az[:, 0:N], op=ALU.mult), reads=[kk_r, az_r], writes=[bz_r])
            (x1, x1_r), (x2, x2_r) = tb1[k], tb2[k]
            fx.op("vector", lambda e, x1=x1, N=N, k=k: e.tensor_tensor(out=x1[:, 0:N], in0=kdb[0][k][0][:, 0:N], in1=kdb[1][k][0][:, 0:N], op=ALU.add), reads=[kdb[0][k][1], kdb[1][k][1]], writes=[x1_r])
            fx.op("vector", lambda e, x2=x2, rT=rT, hp=hp, N=N: e.tensor_scalar(out=x2[:, 0:N], in0=rT[:, 0:N], scalar1=rk_p[:, hp:hp + 1], scalar2=0.5, op0=ALU.mult, op1=ALU.mult), reads=[rT_r, rk_pr], writes=[x2_r])
            fx.op("vector", lambda e, x1=x1, x2=x2, N=N: e.tensor_tensor(out=x1[:, 0:N], in0=x1[:, 0:N], in1=x2[:, 0:N], op=ALU.mult), reads=[x1_r, x2_r], writes=[x1_r])
            psb, psbr = self.pb[k * 4 + 2 + cnt[0] % 2]
            cnt[0] += 1
            fx.op("tensor", lambda e, psb=psb, x1=x1, N=N: e.matmul(psb[:, 0:N], lhsT=bones[:], rhs=x1[:, 0:N], start=True, stop=True), reads=[bones_r, x1_r], writes=[psbr])
            bo, bo_r = bonb[k]
            fx.op("vector", lambda e, bo=bo, psb=psb, vT=vT, N=N: e.tensor_tensor(out=bo[:, 0:N], in0=psb[:, 0:N], in1=vT[:, 0:N], op=ALU.mult), reads=[psbr, vT_r], writes=[bo_r])
            fx.dma(self.bonusd[hp, :, t0:t0 + N], bo[:, 0:N], reads=[bo_r], sem_res=bo_r)
            for z in range(2):
                lw, lw_r = lwb[z][k]
                kd, kd_r = kdb[z][k]
                bz, bz_r = bzb[z][k]
                cw, cw_r = cwb[z * 2 + k]
                (e1, e1_r), (e2, e2_r), (e3, e3_r), (e4, e4_r) = e1b[z * 2 + k], e2b[z * 2 + k], e3b[z * 2 + k], e4b[z * 2 + k]
                (y1, y1_r) = tb3[z * 2 + k]
                Dall, Dall_r = R["Dall"][z]
                fx.op("vector", lambda e, cw=cw, lw=lw, N=N: e.tensor_tensor_scan(out=cw[:, 0:N], data0=cmask[:, 0:N], data1=lw[:, 0:N], initial=0.0, op0=ALU.mult, op1=ALU.add), reads=[cmask_r, lw_r], writes=[cw_r])
                cw3 = cw[:, 0:N].rearrange("p (c t) -> p c t", t=64)
                totb = cw3[:, :, 63:64].to_broadcast([128, nch, 64])
                ch0 = t0 // 64
                fx.op("scalar", lambda e, Dall=Dall, cw3=cw3, hp=hp, ch0=ch0, nch=nch: e.activation(out=Dall[:, hp, ch0:ch0 + nch], in_=cw3[:, :, 63], func=AF.Exp), reads=[cw_r], writes=[Dall_r])
                y13 = y1[:, 0:N].rearrange("p (c t) -> p c t", t=64)
                if z == 0:
                    fx.op("gpsimd", lambda e, y13=y13, totb=totb, cw3=cw3: e.tensor_tensor(out=y13, in0=totb, in1=cw3, op=ALU.subtract), reads=[cw_r], writes=[y1_r])
                    cwz, cwz_r = cw, cw_r
                else:
                    fx.op("gpsimd", lambda e, y1=y1, cw=cw, lw=lw, N=N: e.tensor_tensor(out=y1[:, 0:N], in0=cw[:, 0:N], in1=lw[:, 0:N], op=ALU.subtract), reads=[cw_r, lw_r], writes=[y1_r])
                    cwz, cwz_r = e4, e4_r
                    e43 = e4[:, 0:N].rearrange("p (c t) -> p c t", t=64)
                    fx.op("gpsimd", lambda e, e43=e43, totb=totb, y13=y13: e.tensor_tensor(out=e43, in0=totb, in1=y13, op=ALU.subtract), reads=[cw_r, y1_r], writes=[e4_r])
                fx.op("scalar", lambda e, e1=e1, cwz=cwz, N=N: e.activation(out=e1[:, 0:N], in_=cwz[:, 0:N], func=AF.Exp), reads=[cwz_r], writes=[e1_r])
                fx.op("scalar", lambda e, e2=e2, cwz=cwz, N=N: e.activation(out=e2[:, 0:N], in_=cwz[:, 0:N], func=AF.Exp, scale=-1.0), reads=[cwz_r], writes=[e2_r])
                fx.op("vector", lambda e, e3=e3, cwz=cwz, lw=lw, N=N: e.tensor_tensor(out=e3[:, 0:N], in0=cwz[:, 0:N], in1=lw[:, 0:N], op=ALU.subtract), reads=[cwz_r, lw_r], writes=[e3_r])
                fx.op("scalar", lambda e, e3=e3, N=N: e.activation(out=e3[:, 0:N], in_=e3[:, 0:N], func=AF.Exp), reads=[e3_r], writes=[e3_r])
                fx.op("scalar", lambda e, y1=y1, N=N: e.activation(out=y1[:, 0:N], in_=y1[:, 0:N], func=AF.Exp), reads=[y1_r], writes=[y1_r])
                outs = [fmo[q][z * 2 + k] for q in range(4)]
                fx.op("vector", lambda e, o=outs[0][0], kk=kk, e3=e3, N=N: e.scalar_tensor_tensor(out=o[:, 0:N], in0=kk[:, 0:N], scalar=-1.0, in1=e3[:, 0:N], op0=ALU.mult, op1=ALU.mult), reads=[kk_r, e3_r], writes=[outs[0][1]])
                fx.op("gpsimd", lambda e, o=outs[1][0], rT=rT, e1=e1, N=N: e.tensor_tensor(out=o[:, 0:N], in0=rT[:, 0:N], in1=e1[:, 0:N], op=ALU.mult), reads=[rT_r, e1_r], writes=[outs[1][1]])
                fx.op("vector", lambda e, o=outs[2][0], bz=bz, e2=e2, N=N: e.tensor_tensor(out=o[:, 0:N], in0=bz[:, 0:N], in1=e2[:, 0:N], op=ALU.mult), reads=[bz_r, e2_r], writes=[outs[2][1]])
                fx.op("vector", lambda e, o=outs[3][0], kd=kd, e2=e2, N=N: e.tensor_tensor(out=o[:, 0:N], in0=kd[:, 0:N], in1=e2[:, 0:N], op=ALU.mult), reads=[kd_r, e2_r], writes=[outs[3][1]])
                for q in range(4):
                    fx.dma(self.FMd[z][tile0:tile0 + ntile, :, hp, q, :].rearrange("n p t -> p n t"), outs[q][0][:, 0:N].rearrange("p (n t) -> p n t", t=128), reads=[outs[q][1]], sem_res=outs[q][1])
                (h0, h0_r), (h1, h1_r) = hato[0][z * 2 + k], hato[1][z * 2 + k]
                fx.op("vector", lambda e, h0=h0, bz=bz, y1=y1, N=N: e.tensor_tensor(out=h0[:, 0:N], in0=bz[:, 0:N], in1=y1[:, 0:N], op=ALU.mult), reads=[bz_r, y1_r], writes=[h0_r])
                fx.op("gpsimd", lambda e, h1=h1, kd=kd, y1=y1, N=N: e.tensor_tensor(out=h1[:, 0:N], in0=kd[:, 0:N], in1=y1[:, 0:N], op=ALU.mult), reads=[kd_r, y1_r], writes=[h1_r])
                transpose_store(fx, k, cnt, h0, h0_r, lambda tl, z=z, tile0=tile0, hp=hp: self.TMd[z][tile0 + tl, :, 0, hp, :], ntile)
                transpose_store(fx, k, cnt, h1, h1_r, lambda tl, z=z, tile0=tile0, hp=hp: self.TMd[z][tile0 + tl, :, 1, hp, :], ntile)


        blocks = [(hp, ci) for hp in range(8) for ci in range(9)]
        for bi in range(0, len(blocks), 2):
            recs = []
            for q in range(2):
                if bi + q < len(blocks):
                    r_ = Rec()
                    emit_block(r_, blocks[bi + q][0], blocks[bi + q][1], q)
                    recs.append(r_)
            interleave(fw, recs)

    def rwkv_S(self, i, j, R, ts):
        fw = self.fw
        mk = {}
        for z in range(2):
            mNf, mNf_r = fw.sb("mNf%d" % z, [128, 256], F32, ts)
            mLf, mLf_r = fw.sb("mLf%d" % z, [128, 128], F32, ts)
            mN, mN_r = fw.sb("mN%d" % z, [128, 2, 256], BF16, ts)
            mL, mL_r = fw.sb("mL%d" % z, [128, 4, 128], BF16, ts)
            fw.dma(mNf[:], self.Cn["mN%d" % z][:, :], writes=[mNf_r], sem_res=mNf_r)
            fw.dma(mLf[:], self.Cn["mL%d" % z][:, :], writes=[mLf_r], sem_res=mLf_r)
            for q in range(2):
                fw.op("vector", lambda e, mN=mN, mNf=mNf, q=q: e.tensor_copy(out=mN[:, q, :], in_=mNf[:]), reads=[mNf_r], writes=[mN_r])
            for q in range(4):
                fw.op("vector", lambda e, mL=mL, mLf=mLf, q=q: e.tensor_copy(out=mL[:, q, :], in_=mLf[:]), reads=[mLf_r], writes=[mL_r])
            mk[z] = (mN, mN_r, mL, mL_r)
        identb4, identb4_r = fw.sb("ident4", [128, 4, 128], BF16, ts)
        for q in range(4):
            fw.op("vector", lambda e, q=q: e.tensor_copy(out=identb4[:, q, :], in_=self.ident[:]), reads=[self.ident_r], writes=[identb4_r])
        Mst = []
        for z in range(2):
            row = []
            for hf in range(2):
                t, r = fw.sb("M%d_%d" % (z, hf), [128, 8, 64], F32, ts)
                fw.op("vector", lambda e, t=t: e.memset(t[:], 0.0), writes=[r])
                tb_, rb_ = fw.sb("Mb%d_%d" % (z, hf), [128, 8, 64], BF16, ts)
                fw.op("vector", lambda e, tb_=tb_: e.memset(tb_[:], 0.0), writes=[rb_])
                row.append((t, r, tb_, rb_))
            Mst.append(row)
        FMb = [fw.sb("sFM%d" % k, [128, 8, 4, 128], BF16, ts) for k in range(2)]
        TMb = [fw.sb("sTM%d" % k, [128, 2, 8, 128], BF16, ts) for k in range(2)]
        Vb = [fw.sb("sV%d" % k, [128, 8, 128], BF16, ts) for k in range(2)]
        G13, _ = fw.sb("sG13", [128, 16, 256], BF16, ts)
        G24, _ = fw.sb("sG24", [128, 16, 256], BF16, ts)
        G13_r = [Res("sG13_%d" % g) for g in range(8)]
        G24_r = [Res("sG24_%d" % g) for g in range(8)]
        Lb_ = [fw.sb("sL%d" % k, [128, 16, 128], BF16, ts)[0] for k in range(2)]
        Nb_ = [fw.sb("sN%d" % k, [128, 16, 128], BF16, ts)[0] for k in range(2)]
        Pm, _ = fw.sb("sP", [128, 16, 128], BF16, ts)
        L_r = [[Res("sL%d_%d" % (k, g)) for g in range(4)] for k in range(2)]
        N_r = [[Res("sN%d_%d" % (k, g)) for g in range(4)] for k in range(2)]
        P_r = [Res("sP_%d" % g) for g in range(4)]
        Z2s = [fw.sb("sZ2s%d" % k, [128, 512], F32, ts) for k in range(2)]
        Zs = [fw.sb("sZs%d" % k, [128, 512], BF16, ts) for k in range(2)]
        Us = [fw.sb("sUs%d" % k, [128, 512], BF16, ts) for k in range(2)]
        Ys = [fw.sb("sYs%d" % k, [128, 512], F32, ts) for k in range(2)]
        npb = [0]

        def prep_bank():
            b = self.pb[npb[0] % 8]
            npb[0] += 1
            return b

        order = [list(range(NTI)), [1, 0] + list(range(NTI - 1, 1, -1))]
        def issue_loads(n):
            z_, tile_ = n % 2, order[n % 2][n // 2]
            FM_, FM_r_ = FMb[n % 2]
            TM_, TM_r_ = TMb[n % 2]
            V_, V_r_ = Vb[n % 2]
            fw.dma(FM_[:], self.FMd[z_][tile_], writes=[FM_r_], sem_res=FM_r_)
            fw.dma(TM_[:], self.TMd[z_][tile_], writes=[TM_r_], sem_res=TM_r_)
            fw.dma(V_[:], self.Vd[tile_], writes=[V_r_], sem_res=V_r_)

        nstep = 0
        for si in range(NTI):
            for z in (0, 1):
                tile = order[z][si]
                k = nstep % 2
                nstep += 1
                FM, FM_r = FMb[k]
                TM, TM_r = TMb[k]
                V, V_r = Vb[k]
                mN, mN_r, mL, mL_r = mk[z]
                if nstep == 1:
                    issue_loads(0)
                if nstep < 2 * NTI:
                    issue_loads(nstep)
                for g2 in range(8):
                    pB, pBr = prep_bank()
                    pK, pKr = prep_bank()
                    pL, pLr = prep_bank()
                    for hl in range(2):
                        h = g2 * 2 + hl
                        hp, hh = h % 8, h // 8
                        rows = slice(hh * 64, (hh + 1) * 64)
                        ar = FM[rows, hp, 0:2, :].rearrange("p a t -> p (a t)")
                        fw.op("tensor", lambda e, FM=FM, TM=TM, V=V, pB=pB, rows=rows, hp=hp, hl=hl, ar=ar: e.matmul(pB[:, hl * 256:(hl + 1) * 256], lhsT=FM[rows, hp, 2, :], rhs=ar, start=True, stop=True), reads=[FM_r], writes=[pBr], signal=(hl == 1))
                        fw.op("tensor", lambda e, FM=FM, TM=TM, V=V, pK=pK, rows=rows, hp=hp, hl=hl, ar=ar: e.matmul(pK[:, hl * 256:(hl + 1) * 256], lhsT=FM[rows, hp, 3, :], rhs=ar, start=True, stop=True), reads=[FM_r], writes=[pKr], signal=(hl == 1))
                        fw.op("tensor", lambda e, FM=FM, TM=TM, V=V, pL=pL, rows=rows, hp=hp, hl=hl: e.matmul(pL[:, hl * 128:(hl + 1) * 128], lhsT=FM[rows, hp, 0, :], rhs=FM[rows, hp, 2, :], start=True, stop=True), reads=[FM_r], writes=[pLr], signal=(hl == 1))
                    h0 = g2 * 2
                    fw.op("vector", lambda e, FM=FM, TM=TM, V=V, pB=pB, h0=h0, mN=mN: e.tensor_tensor(out=G13[:, h0:h0 + 2, :], in0=pB[:, :].rearrange("p (h c) -> p h c", h=2), in1=mN[:], op=ALU.mult), reads=[pBr, mN_r], writes=[G13_r[g2]])
                    fw.op("vector", lambda e, FM=FM, TM=TM, V=V, pK=pK, h0=h0, mN=mN: e.tensor_tensor(out=G24[:, h0:h0 + 2, :], in0=pK[:, :].rearrange("p (h c) -> p h c", h=2), in1=mN[:], op=ALU.mult), reads=[pKr, mN_r], writes=[G24_r[g2]])
                    fw.op("vector", lambda e, FM=FM, TM=TM, V=V, pL=pL, h0=h0, mL=mL: e.tensor_tensor(out=Lb_[0][:, h0:h0 + 2, :], in0=pL[:, 0:256].rearrange("p (h c) -> p h c", h=2), in1=mL[:, 0:2, :], op=ALU.mult), reads=[pLr, mL_r], writes=[L_r[0][g2 // 2]])
                sstop = getattr(self, "s_stop", 99)
                if sstop < 1:
                    continue
                for g4 in range(4):
                    hs = slice(g4 * 4, g4 * 4 + 4)
                    fw.op("gpsimd", lambda e, FM=FM, TM=TM, V=V, hs=hs: e.tensor_copy(out=Nb_[0][:, hs, :], in_=G13[:, hs, 0:128]), reads=[G13_r[g4 * 2], G13_r[g4 * 2 + 1]], writes=[N_r[0][g4]])
                    fw.op("gpsimd", lambda e, FM=FM, TM=TM, V=V, hs=hs: e.tensor_tensor(out=Pm[:, hs, :], in0=G13[:, hs, 0:128], in1=identb4[:], op=ALU.add), reads=[G13_r[g4 * 2], G13_r[g4 * 2 + 1], identb4_r], writes=[P_r[g4]])
                for lev in range(1, 6):
                    a, b = (lev - 1) % 2, lev % 2
                    for g4 in range(4):
                        hs = slice(g4 * 4, g4 * 4 + 4)
                        pl, plr = prep_bank()
                        for hl in range(4):
                            h = g4 * 4 + hl
                            fw.op("tensor", lambda e, FM=FM, TM=TM, V=V, pl=pl, h=h, hl=hl, a=a: e.matmul(pl[:, hl * 128:(hl + 1) * 128], lhsT=Nb_[a][:, h, :], rhs=Lb_[a][:, h, :], start=True, stop=True), reads=[N_r[a][g4], L_r[a][g4]], writes=[plr], signal=(hl == 3))
                        fw.op("scalar", lambda e, FM=FM, TM=TM, V=V, pl=pl, hs=hs, b=b: e.copy(out=Lb_[b][:, hs, :], in_=pl[:, :].rearrange("p (h c) -> p h c", h=4)), reads=[plr], writes=[L_r[b][g4]])
                        if lev < 5:
                            pn, pnr = prep_bank()
                            for hl in range(4):
                                h = g4 * 4 + hl
                                fw.op("tensor", lambda e, FM=FM, TM=TM, V=V, pn=pn, h=h, hl=hl, a=a: e.matmul(pn[:, hl * 128:(hl + 1) * 128], lhsT=Lb_[a][:, h, :], rhs=Nb_[a][:, h, :], start=True, stop=True), reads=[N_r[a][g4], L_r[a][g4]], writes=[pnr], signal=(hl == 3))
                            fw.op("vector", lambda e, FM=FM, TM=TM, V=V, pn=pn, hs=hs, b=b: e.tensor_copy(out=Nb_[b][:, hs, :], in_=pn[:, :].rearrange("p (h c) -> p h c", h=4)), reads=[pnr], writes=[N_r[b][g4]])
                    for g4 in range(4):
                        hs = slice(g4 * 4, g4 * 4 + 4)
                        pq, pqr = prep_bank()
                        for hl in range(4):
                            h = g4 * 4 + hl
                            fw.op("tensor", lambda e, FM=FM, TM=TM, V=V, pq=pq, h=h, hl=hl, b=b: e.matmul(pq[:, hl * 128:(hl + 1) * 128], lhsT=Lb_[b][:, h, :], rhs=Pm[:, h, :], start=True, stop=True), reads=[L_r[b][g4], P_r[g4]], writes=[pqr], signal=(hl == 3))
                        fw.op("vector", lambda e, FM=FM, TM=TM, V=V, pq=pq, hs=hs: e.tensor_tensor(out=Pm[:, hs, :], in0=pq[:, :].rearrange("p (h c) -> p h c", h=4), in1=Pm[:, hs, :], op=ALU.add), reads=[pqr, P_r[g4]], writes=[P_r[g4]])
                if self.dbg and nstep == 1:
                    self.dG13 = self.nc.dram_tensor("dG13", [128, 16, 256], F32, kind="ExternalOutput").ap()
                    self.dG24 = self.nc.dram_tensor("dG24", [128, 16, 256], F32, kind="ExternalOutput").ap()
                    self.dP = self.nc.dram_tensor("dP", [128, 16, 128], F32, kind="ExternalOutput").ap()
                    self.dFM = self.nc.dram_tensor("dFM", [128, 8, 4, 128], F32, kind="ExternalOutput").ap()
                    self.dTM = self.nc.dram_tensor("dTM", [128, 2, 8, 128], F32, kind="ExternalOutput").ap()
                    self.dV = self.nc.dram_tensor("dV", [128, 8, 128], F32, kind="ExternalOutput").ap()
                    dr = fw.res("dbgS", ts)
                    fw.dma(self.dFM, FM[:], reads=[FM_r], sem_res=dr, ek="gpsimd")
                    fw.dma(self.dTM, TM[:], reads=[TM_r], sem_res=dr, ek="gpsimd")
                    fw.dma(self.dV, V[:], reads=[V_r], sem_res=dr, ek="gpsimd")
                    fw.dma(self.dG13, G13[:], reads=list(G13_r), sem_res=dr, ek="gpsimd")
                    fw.dma(self.dG24, G24[:], reads=list(G24_r), sem_res=dr, ek="gpsimd")
                    fw.dma(self.dP, Pm[:], reads=list(P_r), sem_res=dr, ek="gpsimd")
                if sstop < 2:
                    continue
                cks = (0, 1) if z == 0 else (1, 0)
                for hf in range(2):
                    hr = slice(hf * 64, (hf + 1) * 64)
                    M, M_r, Mb, Mb_r = Mst[z][hf]
                    Dall, Dall_r = R["Dall"][z]
                    tA, tAr = self.pb[0]
                    tB, tBr = self.pb[1]
                    tC, tCr = self.pb[2]
                    tD, tDr = self.pb[3]
                    z2, z2_r = Z2s[hf]
                    zs_, zs_r = Zs[hf]
                    us, us_r = Us[hf]
                    ys, ys_r = Ys[hf]
                    g13r = list(G13_r)
                    g24r = list(G24_r)
                    pr_ = list(P_r)
                    for ck in range(2):
                        rows = slice(ck * 64, (ck + 1) * 64)
                        for hp in range(8):
                            h = hf * 8 + hp
                            fw.op("tensor", lambda e, FM=FM, TM=TM, V=V, rows=rows, h=h, hp=hp, hr=hr, ck=ck: e.matmul(tA[rows, hp * 64:(hp + 1) * 64], lhsT=G24[rows, h, ck * 64:(ck + 1) * 64], rhs=V[rows, hp, hr], start=True, stop=True),
                                  reads=g24r + [V_r], writes=[tAr], signal=(ck == 1 and hp == 7))
                    fw.op("scalar", lambda e, FM=FM, TM=TM, V=V, z2=z2: e.copy(out=z2[:], in_=tA[:, :]), reads=[tAr], writes=[z2_r])
                    for ck in (cks if sstop >= 3 else ()):
                        rows = slice(ck * 64, (ck + 1) * 64)
                        cols = slice(ck * 64, (ck + 1) * 64)
                        for hp in range(8):
                            fw.op("tensor", lambda e, FM=FM, TM=TM, V=V, rows=rows, cols=cols, hp=hp, hr=hr, Mb=Mb: e.matmul(tB[rows, hp * 64:(hp + 1) * 64], lhsT=FM[hr, hp, 0, cols], rhs=Mb[hr, hp, :], start=True, stop=True),
                                  reads=[FM_r, Mb_r], writes=[tBr], signal=(hp == 7))
                        for hp in range(8):
                            fw.op("tensor", lambda e, FM=FM, TM=TM, V=V, rows=rows, cols=cols, hp=hp, hr=hr, Mb=Mb: e.matmul(tC[rows, hp * 64:(hp + 1) * 64], lhsT=FM[hr, hp, 1, cols], rhs=Mb[hr, hp, :], start=True, stop=True),
                                  reads=[FM_r, Mb_r], writes=[tCr], signal=(hp == 7))
                        fw.op("vector", lambda e, FM=FM, TM=TM, V=V, rows=rows, zs_=zs_, z2=z2: e.tensor_tensor(out=zs_[rows, :], in0=tB[rows, :], in1=z2[rows, :], op=ALU.add), reads=[tBr, z2_r], writes=[zs_r])
                        if sstop < 4:
                            continue
                        for hp in range(8):
                            h = hf * 8 + hp
                            fw.op("tensor", lambda e, FM=FM, TM=TM, V=V, rows=rows, cols=cols, hp=hp, h=h, zs_=zs_: e.matmul(tB[rows, hp * 64:(hp + 1) * 64], lhsT=Pm[rows, h, cols], rhs=zs_[rows, hp * 64:(hp + 1) * 64], start=True, stop=True),
                                  reads=pr_ + [zs_r], writes=[tBr], signal=(hp == 7))
                        fw.op("scalar", lambda e, FM=FM, TM=TM, V=V, rows=rows, us=us: e.copy(out=us[rows, :], in_=tB[rows, :]), reads=[tBr], writes=[us_r])
                        if sstop < 5:
                            continue
                        for hp in range(8):
                            h = hf * 8 + hp
                            fw.op("tensor", lambda e, FM=FM, TM=TM, V=V, rows=rows, ck=ck, hp=hp, h=h, us=us: e.matmul(tA[rows, hp * 64:(hp + 1) * 64], lhsT=G13[rows, h, 128 + ck * 64:128 + (ck + 1) * 64], rhs=us[rows, hp * 64:(hp + 1) * 64], start=True, stop=False),
                                  reads=g13r + [us_r], writes=[tAr], signal=False)
                            fw.op("tensor", lambda e, FM=FM, TM=TM, V=V, rows=rows, ck=ck, hp=hp, h=h, hr=hr: e.matmul(tA[rows, hp * 64:(hp + 1) * 64], lhsT=G24[rows, h, 128 + ck * 64:128 + (ck + 1) * 64], rhs=V[rows, hp, hr], start=False, stop=True),
                                  reads=g24r + [V_r], writes=[tAr], signal=(hp == 7))
                        if sstop < 6:
                            continue
                        for hp in range(8):
                            fw.op("tensor", lambda e, FM=FM, TM=TM, V=V, rows=rows, hp=hp, hr=hr, us=us: e.matmul(tD[hr, hp * 64:(hp + 1) * 64], lhsT=TM[rows, 0, hp, hr], rhs=us[rows, hp * 64:(hp + 1) * 64], start=True, stop=False),
                                  reads=[TM_r, us_r], writes=[tDr], signal=False)
                            fw.op("tensor", lambda e, FM=FM, TM=TM, V=V, rows=rows, hp=hp, hr=hr: e.matmul(tD[hr, hp * 64:(hp + 1) * 64], lhsT=TM[rows, 1, hp, hr], rhs=V[rows, hp, hr], start=False, stop=True),
                                  reads=[TM_r, V_r], writes=[tDr], signal=(hp == 7))
                        c = tile * 2 + ck
                        dbc = Dall[hr, :, c:c + 1].to_broadcast([64, 8, 64])
                        fw.op("vector", lambda e, FM=FM, TM=TM, V=V, M=M, dbc=dbc, hr=hr: e.tensor_tensor(out=M[hr, :, :], in0=M[hr, :, :], in1=dbc, op=ALU.mult), reads=[M_r, Dall_r], writes=[M_r])
                        fw.op("vector", lambda e, FM=FM, TM=TM, V=V, M=M, hr=hr: e.tensor_tensor(out=M[hr, :, :], in0=tD[hr, :].rearrange("p (h i) -> p h i", h=8), in1=M[hr, :, :], op=ALU.add), reads=[M_r, tDr], writes=[M_r])
                        fw.op("scalar", lambda e, FM=FM, TM=TM, V=V, M=M, Mb=Mb, hr=hr: e.copy(out=Mb[hr, :, :], in_=M[hr, :, :]), reads=[M_r], writes=[Mb_r])
                    if sstop < 7:
                        continue
                    fw.op("scalar", lambda e, FM=FM, TM=TM, V=V, ys=ys: e.copy(out=ys[:], in_=tC[:, :]), reads=[tCr], writes=[ys_r])
                    fw.op("vector", lambda e, FM=FM, TM=TM, V=V, ys=ys: e.tensor_tensor(out=ys[:], in0=tA[:, :], in1=ys[:], op=ALU.add), reads=[tAr, ys_r], writes=[ys_r])
                    ydst = self.yd[z][tile * 128:(tile + 1) * 128, :].rearrange("t (hp hh i) -> t hp hh i", hh=2, i=64)[:, :, hf, :]
                    fw.dma(ydst, ys[:].rearrange("p (hp i) -> p hp i", i=64), reads=[ys_r], sem_res=ys_r)

    def rwkv_O(self, i, j, R, ts):
        fw = self.fw
        I = self.I
        lnw, lnw_r = fw.sb("o_lnw", [128, D], F32, ts)
        lnb, lnb_r = fw.sb("o_lnb", [128, D], F32, ts)
        fw.dma(lnw[:], I["rw_ln_w"][j, :].partition_broadcast(128), writes=[lnw_r], sem_res=lnw_r)
        fw.dma(lnb[:], I["rw_ln_b"][j, :].partition_broadcast(128), writes=[lnb_r], sem_res=lnb_r)
        epsg, epsg_r = fw.sb("o_eps", [128, 1], F32, ts)
        fw.op("vector", lambda e: e.memset(epsg[:], 64e-5), writes=[epsg_r])
        y0b = [fw.sb("o_y0%d" % k, [128, D], F32, ts) for k in range(2)]
        y1b = [fw.sb("o_y1%d" % k, [128, D], F32, ts) for k in range(2)]
        sqb = [fw.sb("o_sq%d" % k, [128, D], F32, ts) for k in range(2)]
        stb = [fw.sb("o_st%d" % k, [128, 2, 16], F32, ts) for k in range(2)]
        bnb = [fw.sb("o_bn%d" % k, [128, 8, 128], F32, ts) for k in range(2)]
        zb = [fw.sb("o_z%d" % k, [128, 8, 128], BF16, ts) for k in range(2)]
        ob = [fw.sb("o_o%d" % k, [128, 8, 128], F32, ts) for k in range(2)]
        ogb = [fw.sb("o_og%d" % k, [128, 8, 128], BF16, ts) for k in range(2)]
        for tt in range(NTI):
            k = tt % 2
            (y0, y0_r), (y1, y1_r), (sq, sq_r), (st, st_r) = y0b[k], y1b[k], sqb[k], stb[k]
            (bn, bn_r), (zt, zt_r), (o_, o_r), (og_, og_r_) = bnb[k], zb[k], ob[k], ogb[k]
            rs = slice(tt * 128, (tt + 1) * 128)
            fw.dma(y0[:], self.yd[0][rs, :], writes=[y0_r], sem_res=y0_r)
            fw.dma(y1[:], self.yd[1][rs, :], writes=[y1_r], sem_res=y1_r)
            fw.dma(bn[:], self.bonusd[:, :, rs].rearrange("h p t -> p h t"), writes=[bn_r], sem_res=bn_r)
            fw.dma(zt[:], self.zsd[:, :, rs].rearrange("h p t -> p h t"), writes=[zt_r], sem_res=zt_r)
            fw.op("gpsimd", lambda e, y0=y0, y1=y1: e.tensor_tensor(out=y0[:], in0=y0[:], in1=y1[:], op=ALU.add), reads=[y0_r, y1_r], writes=[y0_r])
            y3 = y0[:].rearrange("p (h d) -> p h d", d=64)
            s3 = sq[:].rearrange("p (h d) -> p h d", d=64)
            fw.op("vector", lambda e, st=st, y3=y3: e.tensor_reduce(out=st[:, 0, :], in_=y3, axis=AX.X, op=ALU.add), reads=[y0_r], writes=[st_r])
            fw.op("vector", lambda e, st=st: e.tensor_scalar(out=st[:, 0, :], in0=st[:, 0, :], scalar1=1.0 / 64.0, scalar2=None, op0=ALU.mult), reads=[st_r], writes=[st_r])
            mbc = st[:, 0, :].unsqueeze(2).to_broadcast([128, 16, 64])
            fw.op("vector", lambda e, y3=y3, mbc=mbc: e.tensor_tensor(out=y3, in0=y3, in1=mbc, op=ALU.subtract), reads=[y0_r, st_r], writes=[y0_r])
            fw.op("gpsimd", lambda e, sq=sq, y0=y0: e.tensor_tensor(out=sq[:], in0=y0[:], in1=y0[:], op=ALU.mult), reads=[y0_r], writes=[sq_r])
            fw.op("vector", lambda e, st=st, s3=s3: e.tensor_reduce(out=st[:, 1, :], in_=s3, axis=AX.X, op=ALU.add), reads=[sq_r], writes=[st_r])
            fw.op("scalar", lambda e, st=st: e.activation(out=st[:, 1, :], in_=st[:, 1, :], func=AF.Sqrt, bias=epsg[:], scale=1.0 / 64.0), reads=[st_r, epsg_r], writes=[st_r])
            fw.op("vector", lambda e, st=st: e.reciprocal(out=st[:, 1, :], in_=st[:, 1, :]), reads=[st_r], writes=[st_r])
            rbc = st[:, 1, :].unsqueeze(2).to_broadcast([128, 16, 64])
            fw.op("vector", lambda e, y3=y3, rbc=rbc: e.tensor_tensor(out=y3, in0=y3, in1=rbc, op=ALU.mult), reads=[y0_r, st_r], writes=[y0_r])
            fw.op("gpsimd", lambda e, y0=y0: e.tensor_tensor(out=y0[:], in0=y0[:], in1=lnw[:], op=ALU.mult), reads=[y0_r, lnw_r], writes=[y0_r])
            fw.op("gpsimd", lambda e, y0=y0: e.tensor_tensor(out=y0[:], in0=y0[:], in1=lnb[:], op=ALU.add), reads=[y0_r, lnb_r], writes=[y0_r])
            for half in range(2):
                tp, tpr = self.pb[(tt * 2 + half) % 8]
                for q in range(4):
                    hc = half * 4 + q
                    fw.op("tensor", lambda e, tp=tp, q=q, hc=hc, y0=y0: e.transpose(tp[:, q * 128:(q + 1) * 128], y0[:, hc * 128:(hc + 1) * 128], self.ident[:]), reads=[y0_r, self.ident_r], writes=[tpr], signal=(q == 3))
                fw.op("vector", lambda e, tp=tp, half=half, o_=o_, bn=bn: e.tensor_tensor(out=o_[:, half * 4:half * 4 + 4, :], in0=tp[:, :].rearrange("p (h t) -> p h t", h=4), in1=bn[:, half * 4:half * 4 + 4, :], op=ALU.add), reads=[tpr, bn_r], writes=[o_r])
            fw.op("gpsimd", lambda e, og_=og_, o_=o_, zt=zt: e.tensor_tensor(out=og_[:], in0=o_[:], in1=zt[:], op=ALU.mult), reads=[o_r, zt_r], writes=[og_r_])
            fw.dma(self.og[:, :, rs].rearrange("h p t -> p h t"), og_[:], reads=[og_r_], sem_res=og_r_)


_CACHE = {}


def get_prog(n_layers=DEPTH, dbg=False):
    key = (n_layers, dbg)
    if key not in _CACHE:
        p = Prog(n_layers, dbg)
        p.build()
        _CACHE[key] = p
    return _CACHE[key]


def make_in_maps(inputs, consts, cores):
    maps = []
    for b in cores:
        m = {}
        for k, shp in INPUT_SHAPES.items():
            a = np.asarray(inputs[k])
            if k in ("x", "ctx"):
                a = a[b]
            elif k == "c":
                a = a[b:b + 1]
            a = np.ascontiguousarray(a, dtype=np.float32).reshape(shp)
            m[k] = a
        for k, v in consts.items():
            m["k_" + k] = v
        maps.append(m)
    return maps


def kernel(**inputs):
    p = get_prog()
    cores = list(range(8))
    in_maps = make_in_maps(inputs, p.consts, cores)
    res = run_bass_kernel_spmd(p.nc, in_maps, core_ids=cores)
    out = np.stack([np.asarray(r["out"]) for r in res.results], axis=0)
    return out.astype(np.float32)
```

```python
import math
from contextlib import ExitStack

import numpy as np
import concourse.bass as bass
import concourse.mybir as mybir
from concourse.bass_utils import run_bass_kernel_spmd

F32 = mybir.dt.float32
BF16 = mybir.dt.bfloat16
AF = mybir.ActivationFunctionType
ALU = mybir.AluOpType
AX = mybir.AxisListType

D = 1024
NL = 4096
NCX = 256
NT = NL + NCX
NTI = NT // 128
DEPTH = 4
NORM_EPS = 1e-6
SUBLN_EPS = 1e-5
GRID_W = 64


class Sem:
    def __init__(self, fw, name):
        self.name = name
        self.handle = fw.stack.enter_context(fw.nc.semaphore(name))
        self.issued = 0
        fw.sems.append(self)


class Res:
    __slots__ = ("name", "writers", "readers", "dsem")

    def __init__(self, name):
        self.name = name
        self.writers = []
        self.readers = []
        self.dsem = None


class Eng:
    def __init__(self, fw, key):
        self.key = key
        self.sem = Sem(fw, "e_" + key)
        self.ops = []
        self.waited = {}
        self.pend_r = []
        self.pend_w = []


class FW:
    def __init__(self, nc, stack):
        self.nc = nc
        self.stack = stack
        self.sems = []
        self.eng = {k: Eng(self, k) for k in ("tensor", "vector", "scalar", "gpsimd", "sync")}
        self.n_ins = 0
        self.free_sems = []

    def sb(self, name, shape, dtype, stack=None):
        self.uid = getattr(self, "uid", 0) + 1
        name = "%s_u%d" % (name, self.uid)
        t = (stack or self.stack).enter_context(self.nc.sbuf_tensor(name, list(shape), dtype))
        return t, self.res(name, stack)

    def res(self, name, stack=None):
        r = Res(name)
        if stack is not None:
            stack.callback(self._release, r)
        return r

    def _release(self, r):
        if r.dsem is not None:
            self.free_sems.append(r.dsem)
            r.dsem = None

    def _get_dsem(self, name):
        if self.free_sems:
            sem = self.free_sems.pop(0)
            for e in self.eng.values():
                assert e.waited.get(sem, 0) >= sem.issued, "semaphore reused before a barrier"
            return sem
        return Sem(self, "d%d" % len(self.sems))

    def ps(self, name, shape, dtype, stack=None):
        t = (stack or self.stack).enter_context(self.nc.psum_tensor(name, list(shape), dtype))
        return t, Res(name)

    def _waits_for(self, eng, reads, writes):
        need = {}

        def add(tok):
            sem, val = tok
            if val is None:
                val = sem.issued
            if need.get(sem, 0) < val:
                need[sem] = val
        for r in reads:
            for t in r.writers:
                add(t)
        for w in writes:
            for t in w.writers:
                add(t)
            for t in w.readers:
                add(t)
        out = []
        for sem, val in need.items():
            if eng.waited.get(sem, 0) >= val:
                continue
            eng.waited[sem] = val
            out.append((sem.handle, val))
        return out

    def _check_pending(self, eng, reads, writes):
        for e in self.eng.values():
            if e is eng or (not e.pend_w and not e.pend_r):
                continue
            for r in reads:
                assert r not in e.pend_w, ("unsignaled write pending", r.name, e.key)
            for w in writes:
                assert w not in e.pend_w and w not in e.pend_r, ("unsignaled access pending", w.name, e.key)

    def op(self, ek, fn, reads=(), writes=(), signal=True):
        eng = self.eng[ek]
        self._check_pending(eng, reads, writes)
        waits = self._waits_for(eng, reads, writes)
        self.n_ins += 1
        if signal:
            eng.sem.issued += 1
            tok = (eng.sem, eng.sem.issued)
            semh = eng.sem.handle

            def run(e, fn=fn, waits=waits, semh=semh):
                for (h, v) in waits:
                    e.wait_ge(h, v)
                fn(e).then_inc(semh, 1)
            rs = list(reads) + eng.pend_r
            ws = list(writes) + eng.pend_w
            eng.pend_r = []
            eng.pend_w = []
            for r in rs:
                r.readers.append(tok)
            for w in ws:
                w.writers = [tok]
                w.readers = []
        else:
            def run(e, fn=fn, waits=waits):
                for (h, v) in waits:
                    e.wait_ge(h, v)
                fn(e)
            eng.pend_r.extend(reads)
            eng.pend_w.extend(writes)
        eng.ops.append(run)

    def dma(self, out, in_, reads=(), writes=(), sem_res=None, ek="sync", **kw):
        eng = self.eng[ek]
        self._check_pending(eng, reads, writes)
        waits = self._waits_for(eng, reads, writes)
        if sem_res.dsem is None:
            sem_res.dsem = self._get_dsem(sem_res.name)
        sem = sem_res.dsem
        sem.issued += 16
        tok = (sem, None)
        semh = sem.handle
        self.n_ins += 1

        def run(e, waits=waits, semh=semh, out=out, in_=in_, kw=kw):
            for (h, v) in waits:
                e.wait_ge(h, v)
            e.dma_start(out=out, in_=in_, **kw).then_inc(semh, 16)
        eng.ops.append(run)
        for r in reads:
            r.readers.append(tok)
        for w in writes:
            w.writers = [tok]
            w.readers = []

    def barrier(self):
        for e in self.eng.values():
            assert not e.pend_r and not e.pend_w
        for e in self.eng.values():
            waits = []
            for s in self.sems:
                if s.issued > 0 and e.waited.get(s, 0) < s.issued:
                    e.waited[s] = s.issued
                    waits.append((s.handle, s.issued))

            def run(en, waits=waits):
                for (h, v) in waits:
                    en.wait_ge(h, v)
            e.ops.append(run)

    def finish(self):
        self.barrier()
        with self.nc.Block() as block:
            for key in ("sync", "tensor", "vector", "scalar", "gpsimd"):
                e = self.eng[key]

                def body(h, e=e):
                    for o in e.ops:
                        o(h)
                getattr(block, key)(body)


class Rec:
    def __init__(self):
        self.items = []

    def op(self, *a, **k):
        self.items.append((0, a, k))

    def dma(self, *a, **k):
        self.items.append((1, a, k))


def interleave(fw, recs):
    idx = [0] * len(recs)
    n = [len(r.items) for r in recs]
    while True:
        best, bf = -1, 2.0
        for i in range(len(recs)):
            if idx[i] < n[i]:
                f = idx[i] / n[i]
                if f < bf:
                    best, bf = i, f
        if best < 0:
            break
        kind, a, k = recs[best].items[idx[best]]
        idx[best] += 1
        (fw.dma if kind else fw.op)(*a, **k)


def rope_tables():
    n = np.arange(NL)
    row = (n // GRID_W).astype(np.float32)
    col = (n % GRID_W).astype(np.float32)
    inv = (10000.0 ** (-np.arange(0, 32, 2, dtype=np.float32) / 32.0)).astype(np.float32)
    C = np.ones((128, NT), np.float32)
    S = np.zeros((128, NT), np.float32)
    for f in range(128):
        axis = (f % 64) // 32
        j = f % 16
        pos = row if axis == 0 else col
        ang = (pos * inv[j]).astype(np.float32)
        C[f, NCX:] = np.cos(ang)
        S[f, NCX:] = np.sin(ang)
    return C, S


def host_consts():
    cs = {}
    cs["ident"] = np.eye(128, dtype=np.float32)
    C, S = rope_tables()
    cs["ropeC"] = C
    cs["ropeS"] = S
    import ml_dtypes
    bf = ml_dtypes.bfloat16
    for name, L in (("L", NL), ("X", NCX)):
        l = np.arange(L, dtype=np.int64)
        ang = (2.0 * np.pi / L) * ((l[:, None] * l[None, :]) % L).astype(np.float64)
        sc = 1.0 / math.sqrt(L)
        cs["dftC" + name] = (np.cos(ang) * sc).astype(np.float32).astype(bf)
        cs["dftS" + name] = (np.sin(ang) * sc).astype(np.float32).astype(bf)
    c = np.arange(128, dtype=np.int64)
    ang = (2.0 * np.pi / 128) * ((c[:, None] * c[None, :]) % 128).astype(np.float64)
    sc = 1.0 / math.sqrt(128.0)
    cs["dftCc"] = (np.cos(ang) * sc).astype(np.float32).astype(bf)
    cs["dftnSc"] = (-np.sin(ang) * sc).astype(np.float32).astype(bf)
    a = np.arange(128)
    same = (a[:, None] // 64) == (a[None, :] // 64)
    lt = (a[:, None] < a[None, :]) & same
    le = (a[:, None] <= a[None, :]) & same
    gt = (a[:, None] > a[None, :]) & same
    ge = (a[:, None] >= a[None, :]) & same
    cs["mN0"] = np.concatenate([lt, le], axis=1).astype(np.float32)
    cs["mL0"] = gt.astype(np.float32)
    cs["mN1"] = np.concatenate([gt, ge], axis=1).astype(np.float32)
    cs["mL1"] = lt.astype(np.float32)
    return cs


INPUT_SHAPES = {
    "x": [NL, D], "c": [1, D], "ctx": [NCX, D], "c_ctx": [1, D], "norm_gain": [4, D], "ada_w": [4, D, 3 * D],
    "ada_b": [4, 3 * D], "final_gain": [1, D], "da_w_in": [2, D, 4 * D], "da_lam_q": [2, 128], "da_lam_k": [2, 128],
    "da_subln_gain": [2, 128], "da_w_out": [2, D, D], "fn_w_in": [1, D, 2 * D], "fn_w_group": [1, 8, 128, 128],
    "fn_w_out": [1, D, D], "rw_w_in": [1, D, 4352], "rw_mu": [1, 3328], "rw_w0": [1, 2, D], "rw_w_up": [1, 2, 64, D],
    "rw_a0": [1, 2, D], "rw_a_up": [1, 2, 64, D], "rw_k_k": [1, D], "rw_k_a": [1, D], "rw_r_k": [1, D],
    "rw_ln_w": [1, D], "rw_ln_b": [1, D], "rw_w_out": [1, D, D],
}


class Prog:
    def __init__(self, n_layers=DEPTH, dbg=False):
        self.n_layers = n_layers
        self.dbg = dbg
        self.nc = bass.Bass("TRN2", target_bir_lowering=False)
        self.consts = host_consts()

    def din(self, name, shape, dt=F32):
        return self.nc.dram_tensor(name, list(shape), dt, kind="ExternalInput").ap()

    def build(self):
        nc = self.nc
        self.I = {k: self.din(k, s) for k, s in INPUT_SHAPES.items()}
        self.Cn = {k: self.din("k_" + k, v.shape, F32 if v.dtype == np.float32 else BF16) for k, v in self.consts.items()}
        self.out = nc.dram_tensor("out", [NL, D], F32, kind="ExternalOutput").ap()
        if self.dbg:
            self.dbg_ctx = nc.dram_tensor("dbg_ctx", [NCX, D], F32, kind="ExternalOutput").ap()
        self.xres = nc.dram_tensor("xres", [NT, D], F32, kind="Internal").ap()
        self.og = nc.dram_tensor("og", [8, 128, NT], BF16, kind="Internal").ap()
        self.zsd = nc.dram_tensor("zsd", [8, 128, NT], BF16, kind="Internal").ap()
        self.zsd_r = [[Res("zsd%d_%d" % (h, c)) for c in range(9)] for h in range(8)]
        self.xres_r = [Res("xres%d" % t) for t in range(NTI)]
        self.og_r = [[Res("og%d_%d" % (h, c)) for c in range(9)] for h in range(8)]
        self.out_r = [Res("out%d" % t) for t in range(NTI)]
        with ExitStack() as st:
            self.st = st
            fw = self.fw = FW(nc, st)
            self.setup_persistent()
            for i in range(self.n_layers):
                self.layer(i)
            fw.finish()
        return nc

    @staticmethod
    def chunk(ci):
        if ci == 0:
            return 0, NCX
        return NCX + (ci - 1) * 512, 512

    def xsrc(self, i, tt):
        if i == 0:
            if tt < 2:
                return self.I["ctx"][tt * 128:(tt + 1) * 128, :], []
            return self.I["x"][(tt - 2) * 128:(tt - 1) * 128, :], []
        return self.xres[tt * 128:(tt + 1) * 128, :], [self.xres_r[tt]]

    def setup_persistent(self):
        fw, st = self.fw, self.st
        self.ident, self.ident_r = fw.sb("ident", [128, 128], F32)
        fw.dma(self.ident[:], self.Cn["ident"][:, :], writes=[self.ident_r], sem_res=self.ident_r)
        self.identb, self.identb_r = fw.sb("identb", [128, 128], BF16)
        fw.op("vector", lambda e: e.tensor_copy(out=self.identb[:], in_=self.ident[:]), reads=[self.ident_r], writes=[self.identb_r])
        self.ones_b, self.ones_b_r = fw.sb("ones_b", [128, 128], BF16)
        fw.op("vector", lambda e: e.memset(self.ones_b[:], 1.0), writes=[self.ones_b_r])
        self.mean_b, self.mean_b_r = fw.sb("mean_b", [128, 128], BF16)
        fw.op("vector", lambda e: e.memset(self.mean_b[:], 1.0 / 128.0), writes=[self.mean_b_r])
        self.epsn, self.epsn_r = fw.sb("epsn", [128, 1], F32)
        fw.op("vector", lambda e: e.memset(self.epsn[:], NORM_EPS), writes=[self.epsn_r])
        self.epss, self.epss_r = fw.sb("epss", [128, 1], F32)
        fw.op("vector", lambda e: e.memset(self.epss[:], SUBLN_EPS), writes=[self.epss_r])
        self.gsc, self.gsc_r = fw.sb("gsc", [128, 8, 2], F32)
        self.shf, self.shf_r = fw.sb("shf", [128, 8, 2], F32)
        self.gate, self.gate_r = fw.sb("gate", [128, 2, D], F32)
        self.s_fm, self.s_fm_r = fw.sb("s_fm", [128, 8, 2], F32)
        self.pb = []
        for b in range(8):
            t, r = fw.ps("pb%d" % b, [128, 512], F32)
            self.pb.append((t, r))
        with ExitStack() as ts:
            cc, cc_r = fw.sb("cc", [2, D], F32, ts)
            fw.dma(cc[0:1, :], self.I["c"][:, :], writes=[cc_r], sem_res=cc_r)
            fw.dma(cc[1:2, :], self.I["c_ctx"][:, :], writes=[cc_r], sem_res=cc_r)
            fw.op("scalar", lambda e: e.activation(out=cc[:], in_=cc[:], func=AF.Silu), reads=[cc_r], writes=[cc_r])
            pt, pr = self.pb[0]
            for kc in range(8):
                fw.op("tensor", lambda e, kc=kc: e.transpose(pt[:, kc * 2:kc * 2 + 2], cc[:, kc * 128:(kc + 1) * 128], self.ident[0:2, 0:2]),
                      reads=[cc_r, self.ident_r], writes=[pr], signal=(kc == 7))
            fw.op("vector", lambda e: e.tensor_copy(out=self.s_fm[:].rearrange("p k w -> p (k w)"), in_=pt[:, 0:16]), reads=[pr], writes=[self.s_fm_r])
            fw.barrier()

    def load_fm(self, dst, dst_r, src2d, n, ts):
        fw = self.fw
        tmp, tmp_r = fw.sb("lfm_tmp%d" % fw.n_ins, [n, 128], F32, ts)
        fw.dma(tmp[:], src2d, writes=[tmp_r], sem_res=tmp_r)
        pt, pr = self.pb[1]
        fw.op("tensor", lambda e: e.transpose(pt[:, 0:n], tmp[:], self.ident[0:n, 0:n]), reads=[tmp_r, self.ident_r], writes=[pr])
        fw.op("vector", lambda e: e.tensor_copy(out=dst, in_=pt[:, 0:n]), reads=[pr], writes=[dst_r])

    def phase_mod(self, i):
        fw = self.fw
        W = self.I["ada_w"][i]
        with ExitStack() as ts:
            self.s_rep, self.s_rep_r = fw.sb("s_rep", [128, 8, 2, 128], F32, ts)
            for kc in range(8):
                for w in range(2):
                    fw.op("gpsimd", lambda e, kc=kc, w=w: e.tensor_copy(out=self.s_rep[:, kc, w, :], in_=self.s_fm[:, kc, w:w + 1].to_broadcast([128, 128])),
                          reads=[self.s_fm_r], writes=[self.s_rep_r])
            bfm, bfm_r = fw.sb("bfm", [128, 16], F32, ts)
            gfm, gfm_r = fw.sb("gfm", [128, 8], F32, ts)
            self.load_fm(bfm[:], bfm_r, self.I["ada_b"][i, 0:2048].rearrange("(n p) -> n p", p=128), 16, ts)
            self.load_fm(gfm[:], gfm_r, self.I["norm_gain"][i, :].rearrange("(n p) -> n p", p=128), 8, ts)
            bg, bg_r = fw.sb("bg", [128, D], F32, ts)
            fw.dma(bg[:], self.I["ada_b"][i, 2048:3072].partition_broadcast(128), writes=[bg_r], sem_res=bg_r)
            wt = [fw.sb("adaw%d" % k, [128, 8, 512], F32, ts) for k in range(2)]
            pt, pr = self.pb[2]
            for cc in range(6):
                w_t, w_r = wt[cc % 2]
                for kc in range(8):
                    fw.dma(w_t[:, kc, :], W[kc * 128:(kc + 1) * 128, cc * 512:(cc + 1) * 512], writes=[w_r], sem_res=w_r)
                if cc < 4:
                    for fl in range(4):
                        fc = cc * 4 + fl
                        for kc in range(8):
                            fw.op("tensor", lambda e, kc=kc, fl=fl, fc=fc, w_t=w_t: e.matmul(pt[:, fc * 2:fc * 2 + 2], lhsT=w_t[:, kc, fl * 128:(fl + 1) * 128], rhs=self.s_fm[:, kc, :], start=(kc == 0), stop=(kc == 7)),
                                  reads=[w_r, self.s_fm_r], writes=[pr], signal=(kc == 7 and fl == 3))
                    if cc == 3:
                        ps3 = pt[:, 0:32].rearrange("p (f w) -> p f w", w=2)
                        for w in range(2):
                            fw.op("vector", lambda e, w=w: e.tensor_tensor(out=self.shf[:, :, w], in0=ps3[:, 0:8, w], in1=bfm[:, 0:8], op=ALU.add),
                                  reads=[pr, bfm_r], writes=[self.shf_r])
                            fw.op("vector", lambda e, w=w: e.tensor_tensor(out=self.gsc[:, :, w], in0=ps3[:, 8:16, w], in1=bfm[:, 8:16], op=ALU.add),
                                  reads=[pr, bfm_r], writes=[self.gsc_r])
                            fw.op("vector", lambda e, w=w: e.scalar_tensor_tensor(out=self.gsc[:, :, w], in0=self.gsc[:, :, w], scalar=1.0, in1=gfm[:], op0=ALU.add, op1=ALU.mult),
                                  reads=[self.gsc_r, gfm_r], writes=[self.gsc_r])
                else:
                    cg = cc - 4
                    for w in range(2):
                        gp, gr = self.pb[3 + w]
                        for kc in range(8):
                            fw.op("tensor", lambda e, kc=kc, w=w, gp=gp, w_t=w_t: e.matmul(gp[:, :], lhsT=self.s_rep[:, kc, w, :], rhs=w_t[:, kc, :], start=(kc == 0), stop=(kc == 7)),
                                  reads=[w_r, self.s_rep_r], writes=[gr], signal=(kc == 7))
                        fw.op("vector", lambda e, w=w, gp=gp, cg=cg: e.tensor_tensor(out=self.gate[:, w, cg * 512:(cg + 1) * 512], in0=gp[:, :], in1=bg[:, cg * 512:(cg + 1) * 512], op=ALU.add),
                              reads=[gr, bg_r], writes=[self.gate_r])
            fw.barrier()

    def phase_norm(self, i, hT, hT_r, ts):
        fw = self.fw
        xt = [fw.sb("nx%d" % k, [128, D], F32, ts) for k in range(3)]
        sq, sq_r = fw.sb("nsq", [128, D], BF16, ts)
        ss = [fw.sb("nss%d" % k, [128, 1], F32, ts) for k in range(2)]
        dg = [fw.sb("ndg%d" % k, [128, 128], F32, ts) for k in range(2)]
        for tt in range(NTI):
            x_t, x_r = xt[tt % 3]
            s_t, s_r = ss[tt % 2]
            d_t, d_r = dg[tt % 2]
            w = 1 if tt < 2 else 0
            src, src_r = self.xsrc(i, tt)
            fw.dma(x_t[:], src, reads=src_r, writes=[x_r], sem_res=x_r)
            fw.op("scalar", lambda e, x_t=x_t, s_t=s_t: e.activation(out=sq[:], in_=x_t[:], func=AF.Square, accum_out=s_t[:]), reads=[x_r], writes=[sq_r, s_r])
            fw.op("scalar", lambda e, s_t=s_t: e.activation(out=s_t[:], in_=s_t[:], func=AF.Sqrt, bias=self.epsn[:], scale=1.0 / D), reads=[s_r, self.epsn_r], writes=[s_r])
            fw.op("vector", lambda e, s_t=s_t: e.reciprocal(out=s_t[:], in_=s_t[:]), reads=[s_r], writes=[s_r])
            fw.op("vector", lambda e, s_t=s_t, d_t=d_t: e.tensor_scalar(out=d_t[:], in0=self.ident[:], scalar1=s_t[:, 0:1], scalar2=None, op0=ALU.mult), reads=[s_r, self.ident_r], writes=[d_r])
            for half in range(2):
                pt, pr = self.pb[(tt * 2 + half) % 4]
                for k4 in range(4):
                    kc = half * 4 + k4
                    fw.op("tensor", lambda e, kc=kc, k4=k4, pt=pt, x_t=x_t, d_t=d_t: e.matmul(pt[:, k4 * 128:(k4 + 1) * 128], lhsT=x_t[:, kc * 128:(kc + 1) * 128], rhs=d_t[:], start=True, stop=True),
                          reads=[x_r, d_r], writes=[pr], signal=(k4 == 3))
                for k4 in range(4):
                    kc = half * 4 + k4
                    ek = "scalar" if k4 % 2 == 0 else "vector"
                    if ek == "scalar":
                        fw.op("scalar", lambda e, kc=kc, k4=k4, pt=pt, tt=tt, w=w: e.activation(out=hT[:, kc, tt * 128:(tt + 1) * 128], in_=pt[:, k4 * 128:(k4 + 1) * 128], func=AF.Identity, bias=self.shf[:, kc, w:w + 1], scale=self.gsc[:, kc, w:w + 1]),
                              reads=[pr, self.shf_r, self.gsc_r], writes=[hT_r[tt]])
                    else:
                        fw.op("vector", lambda e, kc=kc, k4=k4, pt=pt, tt=tt, w=w: e.tensor_scalar(out=hT[:, kc, tt * 128:(tt + 1) * 128], in0=pt[:, k4 * 128:(k4 + 1) * 128], scalar1=self.gsc[:, kc, w:w + 1], scalar2=self.shf[:, kc, w:w + 1], op0=ALU.mult, op1=ALU.add),
                              reads=[pr, self.shf_r, self.gsc_r], writes=[hT_r[tt]])

    def phase_out(self, i, w_out, ts):
        fw = self.fw
        last = (i == DEPTH - 1)
        wst = [fw.sb("wo_st%d" % k, [128, D], F32, ts) for k in range(2)]
        wb, wb_r = fw.sb("wo_b", [128, 8, D], BF16, ts)
        for kc in range(8):
            s_t, s_r = wst[kc % 2]
            fw.dma(s_t[:], w_out[kc * 128:(kc + 1) * 128, :], writes=[s_r], sem_res=s_r)
            fw.op("gpsimd", lambda e, kc=kc, s_t=s_t: e.tensor_copy(out=wb[:, kc, :], in_=s_t[:]), reads=[s_r], writes=[wb_r])
        ogt = [fw.sb("po_og%d" % k, [128, 8, 512], BF16, ts) for k in range(2)]
        xt = [fw.sb("po_x%d" % k, [128, D], F32, ts) for k in range(2)]
        yt = [fw.sb("po_y%d" % k, [128, D], F32, ts) for k in range(2)]
        if last:
            fg, fg_r = fw.sb("po_fg", [128, D], F32, ts)
            fw.dma(fg[:], self.I["final_gain"][0, :].partition_broadcast(128), writes=[fg_r], sem_res=fg_r)
            sq, sq_r = fw.sb("po_sq", [128, D], BF16, ts)
            ss = [fw.sb("po_ss%d" % k, [128, 1], F32, ts) for k in range(2)]
        cis = range(1, 9) if last else range(9)
        n = 0
        for ci in cis:
            t0, tn = self.chunk(ci)
            o_t, o_r = ogt[ci % 2]
            for hc in range(8):
                fw.dma(o_t[:, hc, 0:tn], self.og[hc, :, t0:t0 + tn], reads=[self.og_r[hc][ci]], writes=[o_r], sem_res=o_r)
            for tl in range(tn // 128):
                tt = t0 // 128 + tl
                w = 1 if tt < 2 else 0
                x_t, x_r = xt[n % 2]
                y_t, y_r = yt[n % 2]
                src, src_r = self.xsrc(i, tt)
                fw.dma(x_t[:], src, reads=src_r, writes=[x_r], sem_res=x_r)
                for half in range(2):
                    pt, pr = self.pb[(n * 2 + half) % 4]
                    for hc in range(8):
                        fw.op("tensor", lambda e, hc=hc, half=half, pt=pt, o_t=o_t, tl=tl: e.matmul(pt[:, :], lhsT=o_t[:, hc, tl * 128:(tl + 1) * 128], rhs=wb[:, hc, half * 512:(half + 1) * 512], start=(hc == 0), stop=(hc == 7)),
                              reads=[o_r, wb_r], writes=[pr], signal=(hc == 7))
                    fw.op("vector", lambda e, half=half, pt=pt, y_t=y_t, w=w: e.tensor_tensor(out=y_t[:, half * 512:(half + 1) * 512], in0=pt[:, :], in1=self.gate[:, w, half * 512:(half + 1) * 512], op=ALU.mult),
                          reads=[pr, self.gate_r], writes=[y_r])
                fw.op("gpsimd", lambda e, y_t=y_t, x_t=x_t: e.tensor_tensor(out=y_t[:], in0=y_t[:], in1=x_t[:], op=ALU.add), reads=[y_r, x_r], writes=[y_r])
                if not last:
                    fw.dma(self.xres[tt * 128:(tt + 1) * 128, :], y_t[:], reads=[y_r], writes=[self.xres_r[tt]], sem_res=y_r)
                    if self.dbg and i == self.n_layers - 1:
                        if tt < 2:
                            fw.dma(self.dbg_ctx[tt * 128:(tt + 1) * 128, :], y_t[:], reads=[y_r], writes=[self.out_r[tt]], sem_res=y_r)
                        else:
                            fw.dma(self.out[(tt - 2) * 128:(tt - 1) * 128, :], y_t[:], reads=[y_r], writes=[self.out_r[tt]], sem_res=y_r)
                else:
                    s_t, s_r = ss[n % 2]
                    fw.op("scalar", lambda e, y_t=y_t, s_t=s_t: e.activation(out=sq[:], in_=y_t[:], func=AF.Square, accum_out=s_t[:]), reads=[y_r], writes=[sq_r, s_r])
                    fw.op("scalar", lambda e, s_t=s_t: e.activation(out=s_t[:], in_=s_t[:], func=AF.Sqrt, bias=self.epsn[:], scale=1.0 / D), reads=[s_r, self.epsn_r], writes=[s_r])
                    fw.op("vector", lambda e, s_t=s_t: e.reciprocal(out=s_t[:], in_=s_t[:]), reads=[s_r], writes=[s_r])
                    fw.op("vector", lambda e, y_t=y_t, s_t=s_t: e.scalar_tensor_tensor(out=y_t[:], in0=y_t[:], scalar=s_t[:, 0:1], in1=fg[:], op0=ALU.mult, op1=ALU.mult), reads=[y_r, s_r, fg_r], writes=[y_r])
                    fw.dma(self.out[(tt - 2) * 128:(tt - 1) * 128, :], y_t[:], reads=[y_r], writes=[self.out_r[tt]], sem_res=y_r)
                n += 1

    def layer(self, i):
        fw = self.fw
        kind = i % 3
        j = i // 3
        self.phase_mod(i)
        with ExitStack() as ts0:
            pre = None
            if kind == 1:
                pre = self.fnet_prealloc(ts0)
            elif kind == 2:
                pre = self.rwkv_prealloc(ts0)
            with ExitStack() as ts:
                hT, _ = fw.sb("hT", [128, 8, NT], BF16, ts)
                hT_r = [Res("hT%d" % t) for t in range(NTI)]
                with ExitStack() as ts2:
                    self.phase_norm(i, hT, hT_r, ts2)
                    fw.barrier()
                with ExitStack() as ts2:
                    if kind == 0:
                        self.mixer_attn(i, j, hT, hT_r, ts2)
                    elif kind == 1:
                        self.fnet_part1(i, j, hT, hT_r, pre, ts2)
                    else:
                        self.rwkv_part1(i, j, hT, hT_r, pre, ts2)
                    fw.barrier()
            if kind == 1:
                with ExitStack() as ts2:
                    self.fnet_part2(i, j, pre, ts2)
                    fw.barrier()
            elif kind == 2:
                self.rwkv_part2(i, j, pre, ts0)
        with ExitStack() as ts:
            w_out = {0: self.I["da_w_out"], 1: self.I["fn_w_out"], 2: self.I["rw_w_out"]}[kind][j]
            self.phase_out(i, w_out, ts)
            fw.barrier()

    def mixer_attn(self, i, j, hT, hT_r, ts):
        fw = self.fw
        last = (i == DEPTH - 1)
        lambda_init = 0.8 - 0.6 * math.exp(-0.3 * i)
        Win = self.I["da_w_in"][j]
        rC, rC_r = fw.sb("ropeC", [128, NT], F32, ts)
        rS, rS_r = fw.sb("ropeS", [128, NT], F32, ts)
        fw.dma(rC[:], self.Cn["ropeC"][:, :], writes=[rC_r], sem_res=rC_r)
        fw.dma(rS[:], self.Cn["ropeS"][:, :], writes=[rS_r], sem_res=rS_r)
        nlam, nlam_r = fw.sb("nlam", [128, 1], F32, ts)
        lq, lq_r = fw.sb("lq", [128, 128], F32, ts)
        lk, lk_r = fw.sb("lk", [128, 128], F32, ts)
        l2, l2_r = fw.sb("l2", [128, 2], F32, ts)
        fw.dma(lq[:], self.I["da_lam_q"][j, :].partition_broadcast(128), writes=[lq_r], sem_res=lq_r)
        fw.dma(lk[:], self.I["da_lam_k"][j, :].partition_broadcast(128), writes=[lk_r], sem_res=lk_r)
        fw.op("vector", lambda e: e.tensor_tensor(out=lq[:], in0=lq[:], in1=lk[:], op=ALU.mult), reads=[lq_r, lk_r], writes=[lq_r])
        fw.op("vector", lambda e: e.tensor_reduce(out=l2[:], in_=lq[:].rearrange("p (z d) -> p z d", z=2), axis=AX.X, op=ALU.add), reads=[lq_r], writes=[l2_r])
        fw.op("scalar", lambda e: e.activation(out=l2[:], in_=l2[:], func=AF.Exp), reads=[l2_r], writes=[l2_r])
        fw.op("vector", lambda e: e.tensor_tensor(out=nlam[:], in0=l2[:, 1:2], in1=l2[:, 0:1], op=ALU.subtract), reads=[l2_r], writes=[nlam_r])
        fw.op("vector", lambda e: e.tensor_scalar(out=nlam[:], in0=nlam[:], scalar1=-lambda_init, scalar2=None, op0=ALU.add), reads=[nlam_r], writes=[nlam_r])
        sg, sg_r = fw.sb("sg", [128, 1], F32, ts)
        self.load_fm(sg[:], sg_r, self.I["da_subln_gain"][j:j + 1, :], 1, ts)
        fw.op("vector", lambda e: e.tensor_scalar(out=sg[:], in0=sg[:], scalar1=1.0 - lambda_init, scalar2=None, op0=ALU.mult), reads=[sg_r], writes=[sg_r])

        wst = [fw.sb("aw_st%d" % k, [128, 8, 128], F32, ts) for k in range(2)]
        WS = [{nm: fw.sb("%s_%d" % (nm, k), [128, 8, 128], BF16, ts) for nm in ("wq", "wq2", "wk", "wk2", "wv", "wz")} for k in range(2)]
        qT, qT_r = fw.sb("qT", [128, NT], BF16, ts)
        kT, kT_r = None, None
        kTz = [fw.sb("kTz%d" % z, [128, NT], BF16, ts) for z in range(2)]
        for z in range(2):
            fw.op("gpsimd", lambda e, z=z: e.memset(kTz[z][0][:], 0.0), writes=[kTz[z][1]])
        zs, zs_r = fw.sb("zs", [128, NT], BF16, ts)
        vt, vt_r = fw.sb("vt", [128, NTI, 128], BF16, ts)
        t1 = [fw.sb("rp1_%d" % k, [128, 512], F32, ts) for k in range(1)] * 2
        t2 = [fw.sb("rp2_%d" % k, [128, 512], F32, ts) for k in range(1)] * 2
        Eb = [fw.sb("E%d" % k, [128, 512], BF16, ts) for k in range(4)]
        r1, r1_r = fw.sb("ep_r1", [128, 512], F32, ts)
        a1, a1_r = fw.sb("ep_a1", [128, 512], F32, ts)
        r2, r2_r = r1, r1_r
        a2, a2_r = fw.sb("ep_a2", [128, 512], F32, ts)
        sqb, sqb_r = fw.sb("ep_sq", [128, 512], BF16, ts)
        rs, rs_r = r1, r1_r
        ogt = [fw.sb("ep_og%d" % k, [128, 512], BF16, ts) for k in range(2)]

        def rot_cast(dst, dst2, dst_r, dst2_r, s_t, s_r, scale):
            fw.op("gpsimd", lambda e: e.tensor_scalar(out=dst[:], in0=s_t[:], scalar1=scale, scalar2=None, op0=ALU.mult), reads=[s_r], writes=[dst_r])
            sv = s_t[:].rearrange("p k (b h j) -> p k b h j", b=4, h=2)
            dv = dst2[:].rearrange("p k (b h j) -> p k b h j", b=4, h=2)
            for kc in range(0, 8, 4):
                fw.op("gpsimd", lambda e, kc=kc: e.tensor_scalar(out=dv[:, kc:kc + 4, :, 0, :], in0=sv[:, kc:kc + 4, :, 1, :], scalar1=-scale, scalar2=None, op0=ALU.mult), reads=[s_r], writes=[dst2_r])
                fw.op("gpsimd", lambda e, kc=kc: e.tensor_scalar(out=dv[:, kc:kc + 4, :, 1, :], in0=sv[:, kc:kc + 4, :, 0, :], scalar1=scale, scalar2=None, op0=ALU.mult), reads=[s_r], writes=[dst2_r])

        nst = [0]

        def load_weights(hd):
            Wd = WS[hd % 2]
            for which in range(4):
                s_t, s_r = wst[nst[0] % 2]
                nst[0] += 1
                col0 = which * D + hd * 128
                fw.dma(s_t[:], Win[:, col0:col0 + 128].rearrange("(k p) c -> p k c", p=128), writes=[s_r], sem_res=s_r)
                if which == 0:
                    rot_cast(Wd["wq"][0], Wd["wq2"][0], Wd["wq"][1], Wd["wq2"][1], s_t, s_r, 0.125)
                elif which == 1:
                    rot_cast(Wd["wk"][0], Wd["wk2"][0], Wd["wk"][1], Wd["wk2"][1], s_t, s_r, 1.0)
                elif which == 2:
                    fw.op("gpsimd", lambda e, s_t=s_t, d=Wd["wv"][0]: e.tensor_copy(out=d[:], in_=s_t[:]), reads=[s_r], writes=[Wd["wv"][1]])
                else:
                    fw.op("gpsimd", lambda e, s_t=s_t, d=Wd["wz"][0]: e.tensor_copy(out=d[:], in_=s_t[:]), reads=[s_r], writes=[Wd["wz"][1]])

        ncnt = 0
        load_weights(0)
        for hd in range(8):
            Wd = WS[hd % 2]
            (wq, wq_r), (wq2, wq2_r), (wk, wk_r), (wk2, wk2_r), (wv, wv_r), (wz, wz_r) = [Wd[nm] for nm in ("wq", "wq2", "wk", "wk2", "wv", "wz")]
            for ci in range(9):
                t0, tn = self.chunk(ci)
                tts = list(range(t0 // 128, (t0 + tn) // 128))
                hrs = [hT_r[t] for t in tts]
                banks = [self.pb[(ci * 6 + b) % 8] for b in range(6)]
                for b, (wt_, wr_) in enumerate([(wq, wq_r), (wq2, wq2_r), (wk, wk_r), (wk2, wk2_r), (wz, wz_r)]):
                    pt, pr = banks[b]
                    for kc in range(8):
                        fw.op("tensor", lambda e, kc=kc, pt=pt, wt_=wt_, t0=t0, tn=tn: e.matmul(pt[:, 0:tn], lhsT=wt_[:, kc, :], rhs=hT[:, kc, t0:t0 + tn], start=(kc == 0), stop=(kc == 7)),
                              reads=[wr_] + hrs, writes=[pr], signal=(kc == 7))
                pt, pr = banks[5]
                for tl, tt in enumerate(tts):
                    for kc in range(8):
                        fw.op("tensor", lambda e, kc=kc, pt=pt, tl=tl, tt=tt, wv=wv: e.matmul(pt[:, tl * 128:(tl + 1) * 128], lhsT=hT[:, kc, tt * 128:(tt + 1) * 128], rhs=wv[:, kc, :], start=(kc == 0), stop=(kc == 7)),
                              reads=[wv_r, hT_r[tt]], writes=[pr], signal=(kc == 7 and tl == len(tts) - 1))
                for b0, dst, dst_r in ((0, qT, qT_r), (2, kT, kT_r)):
                    p1, p1r = banks[b0]
                    p2, p2r = banks[b0 + 1]
                    ta, ta_r = t1[ncnt % 2]
                    tb, tb_r = t2[ncnt % 2]
                    ncnt += 1
                    fw.op("vector", lambda e, p1=p1, ta=ta, t0=t0, tn=tn: e.tensor_tensor(out=ta[:, 0:tn], in0=p1[:, 0:tn], in1=rC[:, t0:t0 + tn], op=ALU.mult), reads=[p1r, rC_r], writes=[ta_r])
                    fw.op("vector", lambda e, p2=p2, tb=tb, t0=t0, tn=tn: e.tensor_tensor(out=tb[:, 0:tn], in0=p2[:, 0:tn], in1=rS[:, t0:t0 + tn], op=ALU.mult), reads=[p2r, rS_r], writes=[tb_r])
                    if b0 == 0:
                        fw.op("gpsimd", lambda e, ta=ta, tb=tb, dst=dst, t0=t0, tn=tn: e.tensor_tensor(out=dst[:, t0:t0 + tn], in0=ta[:, 0:tn], in1=tb[:, 0:tn], op=ALU.add), reads=[ta_r, tb_r], writes=[dst_r])
                    else:
                        for z in range(2):
                            zr = slice(z * 64, (z + 1) * 64)
                            fw.op("gpsimd", lambda e, ta=ta, tb=tb, z=z, zr=zr, t0=t0, tn=tn: e.tensor_tensor(out=kTz[z][0][zr, t0:t0 + tn], in0=ta[zr, 0:tn], in1=tb[zr, 0:tn], op=ALU.add), reads=[ta_r, tb_r], writes=[kTz[z][1]])
                p4, p4r = banks[4]
                fw.op("scalar", lambda e, p4=p4, t0=t0, tn=tn: e.activation(out=zs[:, t0:t0 + tn], in_=p4[:, 0:tn], func=AF.Silu), reads=[p4r], writes=[zs_r])
                p5, p5r = banks[5]
                fw.op("vector", lambda e, p5=p5, t0=t0, tn=tn: e.tensor_copy(out=vt[:, t0 // 128:(t0 + tn) // 128, :].rearrange("p t e -> p (t e)"), in_=p5[:, 0:tn]), reads=[p5r], writes=[vt_r])
            if hd + 1 < 8:
                load_weights(hd + 1)
            groups = ([] if last else [(0, list(range(2)))]) + [(ci, list(range(NTI))) for ci in range(1, 9)]
            items = []
            for gi, (ci, kts) in enumerate(groups):
                for z in range(2):
                    for idx, kt in enumerate(kts):
                        items.append((gi, ci, z, idx, kt, len(kts)))
            LA = 2
            Ebuf = {}

            def emit_front(n):
                gi, ci, z, idx, kt, nk = items[n]
                q0, qn = self.chunk(ci)
                sp, spr = self.pb[n % 3]
                E_t, E_r = Eb[n % len(Eb)]
                Ebuf[n] = (E_t, E_r)
                fw.op("tensor", lambda e, sp=sp, kt=kt, z=z, q0=q0, qn=qn: e.matmul(sp[:, 0:qn], lhsT=kTz[z][0][:, kt * 128:(kt + 1) * 128], rhs=qT[:, q0:q0 + qn], start=True, stop=True),
                      reads=[kTz[z][1], qT_r], writes=[spr])
                fw.op("scalar", lambda e, sp=sp, E_t=E_t, qn=qn: e.activation(out=E_t[:, 0:qn], in_=sp[:, 0:qn], func=AF.Exp), reads=[spr], writes=[E_r])

            def emit_back(n):
                gi, ci, z, idx, kt, nk = items[n]
                q0, qn = self.chunk(ci)
                E_t, E_r = Ebuf.pop(n)
                po, por = self.pb[3 + z * 2]
                psm, psr = self.pb[4 + z * 2]
                fw.op("tensor", lambda e, po=po, E_t=E_t, kt=kt, qn=qn, idx=idx, nk=nk: e.matmul(po[:, 0:qn], lhsT=vt[:, kt, :], rhs=E_t[:, 0:qn], start=(idx == 0), stop=(idx == nk - 1)),
                      reads=[vt_r, E_r], writes=[por], signal=False)
                fw.op("tensor", lambda e, psm=psm, E_t=E_t, qn=qn, idx=idx, nk=nk: e.matmul(psm[:, 0:qn], lhsT=self.ones_b[:], rhs=E_t[:, 0:qn], start=(idx == 0), stop=(idx == nk - 1)),
                      reads=[self.ones_b_r, E_r], writes=[psr])
                if z == 1 and idx == nk - 1:
                    epilogue(ci)

            def epilogue(ci):
                q0, qn = self.chunk(ci)
                po1, por1 = self.pb[3]
                ps1, psr1 = self.pb[4]
                po2, por2 = self.pb[5]
                ps2, psr2 = self.pb[6]
                fw.op("vector", lambda e, qn=qn: e.reciprocal(out=r1[:, 0:qn], in_=ps1[:, 0:qn]), reads=[psr1], writes=[r1_r])
                fw.op("vector", lambda e, qn=qn: e.tensor_tensor(out=a1[:, 0:qn], in0=po1[:, 0:qn], in1=r1[:, 0:qn], op=ALU.mult), reads=[por1, r1_r], writes=[a1_r])
                fw.op("vector", lambda e, qn=qn: e.reciprocal(out=r2[:, 0:qn], in_=ps2[:, 0:qn]), reads=[psr2], writes=[r2_r])
                fw.op("vector", lambda e, qn=qn: e.tensor_tensor(out=a2[:, 0:qn], in0=po2[:, 0:qn], in1=r2[:, 0:qn], op=ALU.mult), reads=[por2, r2_r], writes=[a2_r])
                fw.op("vector", lambda e, qn=qn: e.scalar_tensor_tensor(out=a1[:, 0:qn], in0=a2[:, 0:qn], scalar=nlam[:, 0:1], in1=a1[:, 0:qn], op0=ALU.mult, op1=ALU.add), reads=[a1_r, a2_r, nlam_r], writes=[a1_r])
                fw.op("vector", lambda e, qn=qn: e.tensor_tensor(out=sqb[:, 0:qn], in0=a1[:, 0:qn], in1=a1[:, 0:qn], op=ALU.mult), reads=[a1_r], writes=[sqb_r])
                mp, mpr = self.pb[7]
                fw.op("tensor", lambda e, qn=qn: e.matmul(mp[:, 0:qn], lhsT=self.mean_b[:], rhs=sqb[:, 0:qn], start=True, stop=True), reads=[self.mean_b_r, sqb_r], writes=[mpr])
                fw.op("scalar", lambda e, qn=qn: e.activation(out=rs[:, 0:qn], in_=mp[:, 0:qn], func=AF.Ln, bias=self.epss[:], scale=1.0), reads=[mpr, self.epss_r], writes=[rs_r])
                fw.op("scalar", lambda e, qn=qn: e.activation(out=rs[:, 0:qn], in_=rs[:, 0:qn], func=AF.Exp, scale=-0.5), reads=[rs_r], writes=[rs_r])
                fw.op("vector", lambda e, qn=qn: e.tensor_tensor(out=a1[:, 0:qn], in0=a1[:, 0:qn], in1=rs[:, 0:qn], op=ALU.mult), reads=[a1_r, rs_r], writes=[a1_r])
                og_t, og_r_ = ogt[ci % 2]
                fw.op("vector", lambda e, og_t=og_t, q0=q0, qn=qn: e.scalar_tensor_tensor(out=og_t[:, 0:qn], in0=a1[:, 0:qn], scalar=sg[:, 0:1], in1=zs[:, q0:q0 + qn], op0=ALU.mult, op1=ALU.mult), reads=[a1_r, sg_r, zs_r], writes=[og_r_])
                fw.dma(self.og[hd, :, q0:q0 + qn], og_t[:, 0:qn], reads=[og_r_], writes=[self.og_r[hd][ci]], sem_res=og_r_)

            for n in range(len(items) + LA):
                if n < len(items):
                    emit_front(n)
                if n - LA >= 0:
                    emit_back(n - LA)

    def load_w_bf16(self, dst, dst_r, w2d, ncols, ts, name):
        fw = self.fw
        wst = [fw.sb("%s_st%d" % (name, k), [128, ncols], F32, ts) for k in range(2)]
        for kc in range(8):
            s_t, s_r = wst[kc % 2]
            fw.dma(s_t[:], w2d[kc * 128:(kc + 1) * 128, :], writes=[s_r], sem_res=s_r)
            fw.op("gpsimd", lambda e, kc=kc, s_t=s_t: e.tensor_copy(out=dst[:, kc, :], in_=s_t[:]), reads=[s_r], writes=[dst_r])

    def gate_proj(self, wz, wz_r, hT, hT_r, ts):
        fw = self.fw
        zt = [fw.sb("gz%d" % k, [128, 512], BF16, ts) for k in range(3)]
        n = 0
        for ci in range(9):
            t0, tn = self.chunk(ci)
            hrs = [hT_r[t] for t in range(t0 // 128, (t0 + tn) // 128)]
            for fc in range(8):
                pt, pr = self.pb[n % 4]
                z_t, z_r = zt[n % 3]
                n += 1
                for kc in range(8):
                    fw.op("tensor", lambda e, kc=kc, pt=pt, fc=fc, t0=t0, tn=tn: e.matmul(pt[:, 0:tn], lhsT=wz[:, kc, fc * 128:(fc + 1) * 128], rhs=hT[:, kc, t0:t0 + tn], start=(kc == 0), stop=(kc == 7)),
                          reads=[wz_r] + hrs, writes=[pr], signal=(kc == 7))
                fw.op("scalar", lambda e, pt=pt, z_t=z_t, tn=tn: e.activation(out=z_t[:, 0:tn], in_=pt[:, 0:tn], func=AF.Silu), reads=[pr], writes=[z_r])
                fw.dma(self.zsd[fc, :, t0:t0 + tn], z_t[:, 0:tn], reads=[z_r], writes=[self.zsd_r[fc][ci]], sem_res=z_r)

    def fnet_prealloc(self, ts):
        fw = self.fw
        Utm, _ = fw.sb("Utm", [128, NTI, D], BF16, ts)
        Utm_r = [Res("Utm%d" % t) for t in range(NTI)]
        return Utm, Utm_r

    def fnet_part1(self, i, j, hT, hT_r, pre, ts):
        fw = self.fw
        Utm, Utm_r = pre
        Win = self.I["fn_w_in"][j]
        wu, wu_r = fw.sb("fwu", [128, 8, D], BF16, ts)
        wz, wz_r = fw.sb("fwz", [128, 8, D], BF16, ts)
        self.load_w_bf16(wu, wu_r, Win[:, 0:D], D, ts, "fwu")
        self.load_w_bf16(wz, wz_r, Win[:, D:2 * D], D, ts, "fwz")
        n = 0
        for tt in range(NTI):
            for half in range(2):
                pt, pr = self.pb[4 + n % 4]
                for kc in range(8):
                    fw.op("tensor", lambda e, kc=kc, pt=pt, tt=tt, half=half: e.matmul(pt[:, :], lhsT=hT[:, kc, tt * 128:(tt + 1) * 128], rhs=wu[:, kc, half * 512:(half + 1) * 512], start=(kc == 0), stop=(kc == 7)),
                          reads=[wu_r, hT_r[tt]], writes=[pr], signal=(kc == 7))
                if n % 2 == 0:
                    fw.op("vector", lambda e, pt=pt, tt=tt, half=half: e.tensor_copy(out=Utm[:, tt, half * 512:(half + 1) * 512], in_=pt[:, :]), reads=[pr], writes=[Utm_r[tt]])
                else:
                    fw.op("scalar", lambda e, pt=pt, tt=tt, half=half: e.copy(out=Utm[:, tt, half * 512:(half + 1) * 512], in_=pt[:, :]), reads=[pr], writes=[Utm_r[tt]])
                n += 1
        self.gate_proj(wz, wz_r, hT, hT_r, ts)

    def fnet_part2(self, i, j, pre, ts):
        fw = self.fw
        Utm, Utm_r = pre
        cc_, cc_r = fw.sb("fCc", [128, 128], BF16, ts)
        sc_, sc_r = fw.sb("fSc", [128, 128], BF16, ts)
        fw.dma(cc_[:], self.Cn["dftCc"][:, :], writes=[cc_r], sem_res=cc_r)
        fw.dma(sc_[:], self.Cn["dftnSc"][:, :], writes=[sc_r], sem_res=sc_r)
        wgs, wgs_r = fw.sb("fwg_st", [128, 8, 128], F32, ts)
        wg, wg_r = fw.sb("fwg", [128, 8, 128], BF16, ts)
        fw.dma(wgs[:], self.I["fn_w_group"][j].rearrange("g c e -> c g e"), writes=[wgs_r], sem_res=wgs_r)
        fw.op("gpsimd", lambda e: e.tensor_copy(out=wg[:], in_=wgs[:]), reads=[wgs_r], writes=[wg_r])
        CB, _ = fw.sb("fCB", [128, 32, 512], BF16, ts)
        SB, _ = fw.sb("fSB", [128, 32, 512], BF16, ts)
        CB_r = [fw.res("fCB%d" % k, ts) for k in range(4)]
        SB_r = [fw.res("fSB%d" % k, ts) for k in range(4)]
        Pb = [fw.sb("fPb%d" % k, [128, 512], BF16, ts) for k in range(2)]
        Qb = [fw.sb("fQb%d" % k, [128, 512], BF16, ts) for k in range(2)]
        Fb = [fw.sb("fFb%d" % k, [128, 512], BF16, ts) for k in range(2)]
        zt = [fw.sb("fzt%d" % k, [128, 512], BF16, ts) for k in range(2)]
        ot = [fw.sb("fot%d" % k, [128, 512], BF16, ts) for k in range(2)]
        last = (i == DEPTH - 1)
        items = []

        def bufs(n):
            return (self.pb[(n % 2) * 2], self.pb[(n % 2) * 2 + 1], self.pb[4 + n % 2], self.pb[6 + n % 2],
                    Pb[n % 2], Qb[n % 2], Fb[n % 2], zt[n % 2], ot[n % 2])

        def front(n):
            ci, g, t0, tn, nlt, lt0 = items[n]
            (pp, ppr), (qp, qpr), _, _, (P_t, P_r), (Q_t, Q_r), _, (z_t, z_r), _ = bufs(n)
            fw.dma(z_t[:, 0:tn], self.zsd[g, :, t0:t0 + tn], reads=[self.zsd_r[g][ci]], writes=[z_r], sem_res=z_r)
            for (acc, accr, buf, bufr) in ((pp, ppr, CB, CB_r), (qp, qpr, SB, SB_r)):
                for lt in range(nlt):
                    fw.op("tensor", lambda e, acc=acc, buf=buf, lt=lt: e.matmul(acc[:, 0:tn], lhsT=Utm[:, lt0 + lt, g * 128:(g + 1) * 128], rhs=buf[:, lt, 0:tn], start=(lt == 0), stop=(lt == nlt - 1)),
                          reads=[Utm_r[lt0 + lt], bufr[lt // 8]], writes=[accr], signal=(lt == nlt - 1 or lt % 8 == 7))
            fw.op("scalar", lambda e: e.copy(out=P_t[:, 0:tn], in_=pp[:, 0:tn]), reads=[ppr], writes=[P_r])
            fw.op("vector", lambda e: e.tensor_copy(out=Q_t[:, 0:tn], in_=qp[:, 0:tn]), reads=[qpr], writes=[Q_r])

        def back(n):
            ci, g, t0, tn, nlt, lt0 = items[n]
            _, _, (fp, fpr), (yp, ypr), (P_t, P_r), (Q_t, Q_r), (F_t, F_r), (z_t, z_r), (o_t, o_r) = bufs(n)
            fw.op("tensor", lambda e: e.matmul(fp[:, 0:tn], lhsT=cc_[:], rhs=P_t[:, 0:tn], start=True, stop=False), reads=[cc_r, P_r], writes=[fpr], signal=False)
            fw.op("tensor", lambda e: e.matmul(fp[:, 0:tn], lhsT=sc_[:], rhs=Q_t[:, 0:tn], start=False, stop=True), reads=[sc_r, Q_r], writes=[fpr])
            fw.op("vector", lambda e: e.tensor_copy(out=F_t[:, 0:tn], in_=fp[:, 0:tn]), reads=[fpr], writes=[F_r])
            fw.op("tensor", lambda e: e.matmul(yp[:, 0:tn], lhsT=wg[:, g, :], rhs=F_t[:, 0:tn], start=True, stop=True), reads=[wg_r, F_r], writes=[ypr])
            fw.op("vector", lambda e: e.tensor_tensor(out=o_t[:, 0:tn], in0=yp[:, 0:tn], in1=z_t[:, 0:tn], op=ALU.mult), reads=[ypr, z_r], writes=[o_r])
            fw.dma(self.og[g, :, t0:t0 + tn], o_t[:, 0:tn], reads=[o_r], writes=[self.og_r[g][ci]], sem_res=o_r)

        for ci in (range(1, 9) if last else range(9)):
            t0, tn = self.chunk(ci)
            if ci == 0:
                nlt, lt0 = 2, 0
                for k in range(2):
                    fw.dma(CB[:, k, 0:256], self.Cn["dftCX"][k * 128:(k + 1) * 128, :], writes=[CB_r[0]], sem_res=CB_r[0])
                    fw.dma(SB[:, k, 0:256], self.Cn["dftSX"][k * 128:(k + 1) * 128, :], writes=[SB_r[0]], sem_res=SB_r[0])
            else:
                nlt, lt0 = 32, 2
                c0 = (ci - 1) * 512
                for k in range(4):
                    fw.dma(CB[:, k * 8:(k + 1) * 8, :], self.Cn["dftCL"][k * 1024:(k + 1) * 1024, c0:c0 + 512].rearrange("(t p) c -> p t c", p=128), writes=[CB_r[k]], sem_res=CB_r[k])
                    fw.dma(SB[:, k * 8:(k + 1) * 8, :], self.Cn["dftSL"][k * 1024:(k + 1) * 1024, c0:c0 + 512].rearrange("(t p) c -> p t c", p=128), writes=[SB_r[k]], sem_res=SB_r[k])
            for g in range(8):
                items.append((ci, g, t0, tn, nlt, lt0))
                front(len(items) - 1)
                if len(items) >= 2:
                    back(len(items) - 2)
        back(len(items) - 1)

    def rwkv_prealloc(self, ts):
        fw = self.fw
        nc = self.nc
        R = {}
        R["Dall"] = [fw.sb("Dall%d" % z, [128, 8, 68], F32, ts) for z in range(2)]
        R["stack"] = ts.enter_context(ExitStack())
        R["twd"] = fw.sb("twd", [128, NT], F32, R["stack"])
        R["adT"] = fw.sb("adT", [128, NT], F32, R["stack"])
        if not hasattr(self, "sTd"):
            dk = "ExternalOutput" if self.dbg else "Internal"
            self.sTd = nc.dram_tensor("sTd", [24, 128, NT], F32, kind=dk).ap()
            self.FMd = [nc.dram_tensor("FMd%d" % z, [NTI, 128, 8, 4, 128], BF16, kind="Internal").ap() for z in range(2)]
            self.TMd = [nc.dram_tensor("TMd%d" % z, [NTI, 128, 2, 8, 128], BF16, kind="Internal").ap() for z in range(2)]
            self.Vd = nc.dram_tensor("Vd", [NTI, 128, 8, 128], BF16, kind="Internal").ap()
            self.bonusd = nc.dram_tensor("bonusd", [8, 128, NT], F32, kind=dk).ap()
            self.yd = [nc.dram_tensor("yd%d" % z, [NT, D], F32, kind=dk).ap() for z in range(2)]
        return R

    def rwkv_part1(self, i, j, hT, hT_r, R, ts):
        fw = self.fw
        W = self.I["rw_w_in"][j]
        twd, twd_r = R["twd"]
        adT, adT_r = R["adT"]
        wz, wz_r = fw.sb("rwz", [128, 8, D], BF16, ts)
        self.load_w_bf16(wz, wz_r, W[:, 3328:4352], D, ts, "rwz")
        self.gate_proj(wz, wz_r, hT, hT_r, ts)
        mu_fm, mu_r = fw.sb("mu_fm", [128, 26], F32, ts)
        self.load_fm(mu_fm[:], mu_r, self.I["rw_mu"][j, :].rearrange("(n p) -> n p", p=128), 26, ts)
        ca, ca_r = fw.sb("mix_a", [128, 26], F32, ts)
        cb, cb_r = fw.sb("mix_b", [128, 26], F32, ts)
        fw.op("vector", lambda e: e.tensor_scalar(out=ca[:], in0=mu_fm[:], scalar1=-1.0, scalar2=1.0, op0=ALU.mult, op1=ALU.add), reads=[mu_r], writes=[ca_r])
        fw.op("vector", lambda e: e.tensor_scalar(out=cb[:], in0=mu_fm[:], scalar1=0.5, scalar2=None, op0=ALU.mult), reads=[mu_r], writes=[cb_r])
        NP = NT + 3
        sraw = [fw.sb("sraw%d" % k, [128, NP], F32, ts) for k in range(2)]
        for k in range(2):
            fw.op("gpsimd", lambda e, k=k: e.memset(sraw[k][0][:], 0.0), writes=[sraw[k][1]])
        wst = [fw.sb("rw_st%d" % k, [128, 8, 128], F32, ts) for k in range(2)]
        wbs = [fw.sb("rw_wb%d" % k, [128, 8, 128], BF16, ts) for k in range(2)]
        PW = 512
        tmpb = [fw.sb("mixt%d" % k, [128, PW], F32, ts) for k in range(2)]
        smxb = [fw.sb("mixs%d" % k, [128, PW], F32, ts) for k in range(2)]
        npc = 0
        nev = 0
        def f1_weights(fc):
            s_t, s_r = wst[fc % 2]
            w_t, w_r = wbs[fc % 2]
            fw.dma(s_t[:], W[:, fc * 128:(fc + 1) * 128].rearrange("(k p) c -> p k c", p=128), writes=[s_r], sem_res=s_r)
            fw.op("gpsimd", lambda e: e.tensor_copy(out=w_t[:], in_=s_t[:]), reads=[s_r], writes=[w_r])

        f1_weights(0)
        for fc in range(26):
            w_t, w_r = wbs[fc % 2]
            sr_t, sr_r = sraw[fc % 2]
            if fc + 1 < 26:
                f1_weights(fc + 1)
            for ci in range(9):
                t0, tn = self.chunk(ci)
                off = t0 + (1 if ci == 0 else 2)
                hrs = [hT_r[t] for t in range(t0 // 128, (t0 + tn) // 128)]
                pt, pr = self.pb[4 + nev % 4]
                for kc in range(8):
                    fw.op("tensor", lambda e, kc=kc, pt=pt, w_t=w_t, t0=t0, tn=tn: e.matmul(pt[:, 0:tn], lhsT=w_t[:, kc, :], rhs=hT[:, kc, t0:t0 + tn], start=(kc == 0), stop=(kc == 7)),
                          reads=[w_r] + hrs, writes=[pr], signal=(kc == 7))
                if nev % 2 == 0:
                    fw.op("scalar", lambda e, pt=pt, sr_t=sr_t, off=off, tn=tn: e.copy(out=sr_t[:, off:off + tn], in_=pt[:, 0:tn]), reads=[pr], writes=[sr_r])
                else:
                    fw.op("vector", lambda e, pt=pt, sr_t=sr_t, off=off, tn=tn: e.tensor_copy(out=sr_t[:, off:off + tn], in_=pt[:, 0:tn]), reads=[pr], writes=[sr_r])
                nev += 1
            for (c0, cn, tok0) in [(1, 256, 0)] + [(258 + q * 512, 512, 256 + q * 512) for q in range(8)]:
                tm_t, tm_r = tmpb[npc % 2]
                sm_t, sm_r = smxb[npc % 2]
                npc += 1
                fw.op("gpsimd", lambda e, tm_t=tm_t, sr_t=sr_t, c0=c0, cn=cn: e.tensor_tensor(out=tm_t[:, 0:cn], in0=sr_t[:, c0 - 1:c0 - 1 + cn], in1=sr_t[:, c0 + 1:c0 + 1 + cn], op=ALU.add), reads=[sr_r], writes=[tm_r])
                fw.op("vector", lambda e, tm_t=tm_t, fc=fc, cn=cn: e.tensor_scalar(out=tm_t[:, 0:cn], in0=tm_t[:, 0:cn], scalar1=cb[:, fc:fc + 1], scalar2=None, op0=ALU.mult), reads=[tm_r, cb_r], writes=[tm_r])
                if fc < 24:
                    fw.op("vector", lambda e, tm_t=tm_t, sm_t=sm_t, sr_t=sr_t, fc=fc, c0=c0, cn=cn: e.scalar_tensor_tensor(out=sm_t[:, 0:cn], in0=sr_t[:, c0:c0 + cn], scalar=ca[:, fc:fc + 1], in1=tm_t[:, 0:cn], op0=ALU.mult, op1=ALU.add),
                          reads=[sr_r, tm_r, ca_r], writes=[sm_r])
                    fw.dma(self.sTd[fc, :, tok0:tok0 + cn], sm_t[:, 0:cn], reads=[sm_r], sem_res=sm_r)
                else:
                    dst, dst_r = (twd, twd_r) if fc == 24 else (adT, adT_r)
                    fw.op("vector", lambda e, tm_t=tm_t, dst=dst, sr_t=sr_t, fc=fc, c0=c0, cn=cn, tok0=tok0: e.scalar_tensor_tensor(out=dst[:, tok0:tok0 + cn], in0=sr_t[:, c0:c0 + cn], scalar=ca[:, fc:fc + 1], in1=tm_t[:, 0:cn], op0=ALU.mult, op1=ALU.add),
                          reads=[sr_r, tm_r, ca_r], writes=[dst_r])
                    if fc == 24:
                        fw.op("scalar", lambda e, tok0=tok0, cn=cn: e.activation(out=twd[:, tok0:tok0 + cn], in_=twd[:, tok0:tok0 + cn], func=AF.Tanh), reads=[twd_r], writes=[twd_r])

    def rwkv_part2(self, i, j, R, ts0):
        fw = self.fw
        stop = getattr(self, "stop_at", 99)
        fw.barrier()
        if stop < 2:
            return
        with ExitStack() as ts:
            self.rwkv_F2(i, j, R, ts)
            fw.barrier()
        R["stack"].close()
        if stop < 3:
            return
        with ExitStack() as ts:
            self.rwkv_S(i, j, R, ts)
            fw.barrier()
        if stop < 4:
            return
        with ExitStack() as ts:
            self.rwkv_O(i, j, R, ts)
            fw.barrier()

    def rwkv_F2(self, i, j, R, ts):
        fw = self.fw
        I = self.I
        twd, twd_r = R["twd"]
        adT, adT_r = R["adT"]
        def fm(name, src, n):
            t, r = fw.sb(name, [128, n], F32, ts)
            self.load_fm(t[:], r, src, n, ts)
            return t, r
        kk_p, kk_pr = fm("p_kk", I["rw_k_k"][j, :].rearrange("(n p) -> n p", p=128), 8)
        ka_p, ka_pr = fm("p_ka", I["rw_k_a"][j, :].rearrange("(n p) -> n p", p=128), 8)
        rk_p, rk_pr = fm("p_rk", I["rw_r_k"][j, :].rearrange("(n p) -> n p", p=128), 8)
        w0_p, w0_pr = fm("p_w0", I["rw_w0"][j].rearrange("z (n p) -> (z n) p", p=128), 16)
        a0_p, a0_pr = fm("p_a0", I["rw_a0"][j].rearrange("z (n p) -> (z n) p", p=128), 16)
        oka, oka_r = fw.sb("p_oka", [128, 8], F32, ts)
        fw.op("vector", lambda e: e.tensor_scalar(out=oka[:], in0=ka_p[:], scalar1=-1.0, scalar2=1.0, op0=ALU.mult, op1=ALU.add), reads=[ka_pr], writes=[oka_r])
        wup, wup_r = fw.sb("wup", [128, D], F32, ts)
        aup, aup_r = fw.sb("aup", [128, D], F32, ts)
        fw.dma(wup[:], I["rw_w_up"][j].rearrange("z r f -> (z r) f"), writes=[wup_r], sem_res=wup_r)
        fw.dma(aup[:], I["rw_a_up"][j].rearrange("z r f -> (z r) f"), writes=[aup_r], sem_res=aup_r)
        bones, bones_r = fw.sb("bones", [128, 128], F32, ts)
        fw.op("vector", lambda e: e.memset(bones[:], 0.0), writes=[bones_r])
        fw.op("vector", lambda e: e.memset(bones[0:64, 0:64], 1.0), writes=[bones_r])
        fw.op("vector", lambda e: e.memset(bones[64:128, 64:128], 1.0), writes=[bones_r])
        eps12, eps12_r = fw.sb("eps12", [128, 1], F32, ts)
        fw.op("vector", lambda e: e.memset(eps12[:], 1e-12), writes=[eps12_r])
        cmask, cmask_r = fw.sb("cmask", [128, 512], F32, ts)
        fw.op("vector", lambda e: e.memset(cmask[:], 1.0), writes=[cmask_r])
        fw.op("vector", lambda e: e.memset(cmask[:].rearrange("p (c t) -> p c t", t=64)[:, :, 0:1], 0.0), writes=[cmask_r])

        cnt = [0]

        def T(name, dt=F32, n=2, w=512):
            return [fw.sb("%s_%d" % (name, k), [128, w], dt, ts) for k in range(n)]
        rTb, kTb, vTb = T("f_r"), T("f_k"), T("f_v")
        lwb = [T("f_lw0"), T("f_lw1")]
        azb = [T("f_az0"), T("f_az1")]
        kdb = [T("f_kd0"), T("f_kd1")]
        bzb = [T("f_b0"), T("f_b1")]
        kkrb, sqb_, kkb, tb1, tb3 = T("f_kkr"), T("f_sq"), T("f_kk"), T("f_t1"), T("f_t3", n=4)
        rnb = sqb_
        tb2 = sqb_
        cwb, e1b, e2b, e3b, e4b = T("f_cw", n=4), T("f_e1", n=4), T("f_e2", n=4), T("f_e3", n=4), T("f_e4", n=2) * 2
        fmo = [T("f_o%d" % k, dt=BF16, n=4) for k in range(4)]
        hato = [T("f_h%d" % k, n=4) for k in range(2)]
        trs = T("f_trs", dt=BF16, n=4)
        bonb = T("f_bon")
        NEG = -math.exp(-0.5)
        nb = 0
        ntr = [0]

        def transpose_store(fx, k, cnt, src_t, src_r, dst_ap_fn, ntile):
            tp, tpr = self.pb[k * 4 + 2 + cnt[0] % 2]
            o_t, o_r = trs[k * 2 + cnt[0] % 2]
            cnt[0] += 1
            for tl in range(ntile):
                fx.op("tensor", lambda e, tl=tl, tp=tp: e.transpose(tp[:, tl * 128:(tl + 1) * 128], src_t[:, tl * 128:(tl + 1) * 128], self.ident[:]),
                      reads=[src_r, self.ident_r], writes=[tpr], signal=(tl == ntile - 1))
            fx.op("scalar", lambda e, tp=tp, o_t=o_t, ntile=ntile: e.copy(out=o_t[:, 0:ntile * 128], in_=tp[:, 0:ntile * 128]), reads=[tpr], writes=[o_r])
            for tl in range(ntile):
                fx.dma(dst_ap_fn(tl), o_t[:, tl * 128:(tl + 1) * 128], reads=[o_r], sem_res=o_r)

        def emit_block(fx, hp, ci, k):
            cnt = [0]
            t0, N = self.chunk(ci)
            tile0 = t0 // 128
            ntile = N // 128
            nch = N // 64
            (rT, rT_r), (kT, kT_r), (vT, vT_r) = rTb[k], kTb[k], vTb[k]
            fx.dma(rT[:, 0:N], self.sTd[hp, :, t0:t0 + N], writes=[rT_r], sem_res=rT_r)
            fx.dma(kT[:, 0:N], self.sTd[8 + hp, :, t0:t0 + N], writes=[kT_r], sem_res=kT_r)
            fx.dma(vT[:, 0:N], self.sTd[16 + hp, :, t0:t0 + N], writes=[vT_r], sem_res=vT_r)
            transpose_store(fx, k, cnt, vT, vT_r, lambda tl, tile0=tile0, hp=hp: self.Vd[tile0 + tl, :, hp, :], ntile)
            for z in range(2):
                rows = slice(z * 64, (z + 1) * 64)
                pw, pwr = self.pb[k * 4]
                pa, par = self.pb[k * 4 + 1]
                lw, lw_r = lwb[z][k]
                az, az_r = azb[z][k]
                fx.op("tensor", lambda e, pw=pw, rows=rows, hp=hp, t0=t0, N=N: e.matmul(pw[:, 0:N], lhsT=wup[rows, hp * 128:(hp + 1) * 128], rhs=twd[rows, t0:t0 + N], start=True, stop=True), reads=[wup_r, twd_r], writes=[pwr])
                fx.op("tensor", lambda e, pa=pa, rows=rows, hp=hp, t0=t0, N=N: e.matmul(pa[:, 0:N], lhsT=aup[rows, hp * 128:(hp + 1) * 128], rhs=adT[rows, t0:t0 + N], start=True, stop=True), reads=[aup_r, adT_r], writes=[par])
                fx.op("scalar", lambda e, pw=pw, lw=lw, z=z, hp=hp, N=N: e.activation(out=lw[:, 0:N], in_=pw[:, 0:N], func=AF.Sigmoid, bias=w0_p[:, z * 8 + hp:z * 8 + hp + 1], scale=1.0), reads=[pwr, w0_pr], writes=[lw_r])
                fx.op("vector", lambda e, lw=lw, N=N: e.tensor_scalar(out=lw[:, 0:N], in0=lw[:, 0:N], scalar1=NEG, scalar2=None, op0=ALU.mult), reads=[lw_r], writes=[lw_r])
                fx.op("scalar", lambda e, pa=pa, az=az, z=z, hp=hp, N=N: e.activation(out=az[:, 0:N], in_=pa[:, 0:N], func=AF.Sigmoid, bias=a0_p[:, z * 8 + hp:z * 8 + hp + 1], scale=1.0), reads=[par, a0_pr], writes=[az_r])
            (kkr, kkr_r), (sq, sq_r), (rn, rn_r), (kk, kk_r) = kkrb[k], sqb_[k], rnb[k], kkb[k]
            fx.op("vector", lambda e, kkr=kkr, kT=kT, hp=hp, N=N: e.tensor_scalar(out=kkr[:, 0:N], in0=kT[:, 0:N], scalar1=kk_p[:, hp:hp + 1], scalar2=None, op0=ALU.mult), reads=[kT_r, kk_pr], writes=[kkr_r])
            fx.op("gpsimd", lambda e, kkr=kkr, sq=sq, N=N: e.tensor_tensor(out=sq[:, 0:N], in0=kkr[:, 0:N], in1=kkr[:, 0:N], op=ALU.mult), reads=[kkr_r], writes=[sq_r])
            pss, pssr = self.pb[k * 4 + 2 + cnt[0] % 2]
            cnt[0] += 1
            fx.op("tensor", lambda e, pss=pss, sq=sq, N=N: e.matmul(pss[:, 0:N], lhsT=bones[:], rhs=sq[:, 0:N], start=True, stop=True), reads=[bones_r, sq_r], writes=[pssr])
            fx.op("scalar", lambda e, pss=pss, rn=rn, N=N: e.activation(out=rn[:, 0:N], in_=pss[:, 0:N], func=AF.Sqrt, bias=eps12[:], scale=1.0), reads=[pssr, eps12_r], writes=[rn_r])
            fx.op("vector", lambda e, rn=rn, N=N: e.reciprocal(out=rn[:, 0:N], in_=rn[:, 0:N]), reads=[rn_r], writes=[rn_r])
            fx.op("gpsimd", lambda e, kk=kk, kkr=kkr, rn=rn, N=N: e.tensor_tensor(out=kk[:, 0:N], in0=kkr[:, 0:N], in1=rn[:, 0:N], op=ALU.mult), reads=[kkr_r, rn_r], writes=[kk_r])
            for z in range(2):
                az, az_r = azb[z][k]
                kd, kd_r = kdb[z][k]
                bz, bz_r = bzb[z][k]
                fx.op("vector", lambda e, kd=kd, az=az, hp=hp, N=N: e.tensor_scalar(out=kd[:, 0:N], in0=az[:, 0:N], scalar1=ka_p[:, hp:hp + 1], scalar2=oka[:, hp:hp + 1], op0=ALU.mult, op1=ALU.add), reads=[az_r, ka_pr, oka_r], writes=[kd_r])
                fx.op("vector", lambda e, kd=kd, kT=kT, N=N: e.tensor_tensor(out=kd[:, 0:N], in0=kd[:, 0:N], in1=kT[:, 0:N], op=ALU.mult), reads=[kd_r, kT_r], writes=[kd_r])
                fx.op("gpsimd", lambda e, bz=bz, kk=kk, az=az, N=N: e.tensor_tensor(out=bz[:, 0:N], in0=kk[:, 0:N], in1=az[:, 0:N], op=ALU.mult), reads=[kk_r, az_r], writes=[bz_r])
            (x1, x1_r), (x2, x2_r) = tb1[k], tb2[k]
            fx.op("vector", lambda e, x1=x1, N=N, k=k: e.tensor_tensor(out=x1[:, 0:N], in0=kdb[0][k][0][:, 0:N], in1=kdb[1][k][0][:, 0:N], op=ALU.add), reads=[kdb[0][k][1], kdb[1][k][1]], writes=[x1_r])
            fx.op("vector", lambda e, x2=x2, rT=rT, hp=hp, N=N: e.tensor_scalar(out=x2[:, 0:N], in0=rT[:, 0:N], scalar1=rk_p[:, hp:hp + 1], scalar2=0.5, op0=ALU.mult, op1=ALU.mult), reads=[rT_r, rk_pr], writes=[x2_r])
            fx.op("vector", lambda e, x1=x1, x2=x2, N=N: e.tensor_tensor(out=x1[:, 0:N], in0=x1[:, 0:N], in1=x2[:, 0:N], op=ALU.mult), reads=[x1_r, x2_r], writes=[x1_r])
            psb, psbr = self.pb[k * 4 + 2 + cnt[0] % 2]
            cnt[0] += 1
            fx.op("tensor", lambda e, psb=psb, x1=x1, N=N: e.matmul(psb[:, 0:N], lhsT=bones[:], rhs=x1[:, 0:N], start=True, stop=True), reads=[bones_r, x1_r], writes=[psbr])
            bo, bo_r = bonb[k]
            fx.op("vector", lambda e, bo=bo, psb=psb, vT=vT, N=N: e.tensor_tensor(out=bo[:, 0:N], in0=psb[:, 0:N], in1=vT[:, 0:N], op=ALU.mult), reads=[psbr, vT_r], writes=[bo_r])
            fx.dma(self.bonusd[hp, :, t0:t0 + N], bo[:, 0:N], reads=[bo_r], sem_res=bo_r)
            for z in range(2):
                lw, lw_r = lwb[z][k]
                kd, kd_r = kdb[z][k]
                bz, bz_r = bzb[z][k]
                cw, cw_r = cwb[z * 2 + k]
                (e1, e1_r), (e2, e2_r), (e3, e3_r), (e4, e4_r) = e1b[z * 2 + k], e2b[z * 2 + k], e3b[z * 2 + k], e4b[z * 2 + k]
                (y1, y1_r) = tb3[z * 2 + k]
                Dall, Dall_r = R["Dall"][z]
                fx.op("vector", lambda e, cw=cw, lw=lw, N=N: e.tensor_tensor_scan(out=cw[:, 0:N], data0=cmask[:, 0:N], data1=lw[:, 0:N], initial=0.0, op0=ALU.mult, op1=ALU.add), reads=[cmask_r, lw_r], writes=[cw_r])
                cw3 = cw[:, 0:N].rearrange("p (c t) -> p c t", t=64)
                totb = cw3[:, :, 63:64].to_broadcast([128, nch, 64])
                ch0 = t0 // 64
                fx.op("scalar", lambda e, Dall=Dall, cw3=cw3, hp=hp, ch0=ch0, nch=nch: e.activation(out=Dall[:, hp, ch0:ch0 + nch], in_=cw3[:, :, 63], func=AF.Exp), reads=[cw_r], writes=[Dall_r])
                y13 = y1[:, 0:N].rearrange("p (c t) -> p c t", t=64)
                if z == 0:
                    fx.op("gpsimd", lambda e, y13=y13, totb=totb, cw3=cw3: e.tensor_tensor(out=y13, in0=totb, in1=cw3, op=ALU.subtract), reads=[cw_r], writes=[y1_r])
                    cwz, cwz_r = cw, cw_r
                else:
                    fx.op("gpsimd", lambda e, y1=y1, cw=cw, lw=lw, N=N: e.tensor_tensor(out=y1[:, 0:N], in0=cw[:, 0:N], in1=lw[:, 0:N], op=ALU.subtract), reads=[cw_r, lw_r], writes=[y1_r])
                    cwz, cwz_r = e4, e4_r
                    e43 = e4[:, 0:N].rearrange("p (c t) -> p c t", t=64)
                    fx.op("gpsimd", lambda e, e43=e43, totb=totb, y13=y13: e.tensor_tensor(out=e43, in0=totb, in1=y13, op=ALU.subtract), reads=[cw_r, y1_r], writes=[e4_r])
                fx.op("scalar", lambda e, e1=e1, cwz=cwz, N=N: e.activation(out=e1[:, 0:N], in_=cwz[:, 0:N], func=AF.Exp), reads=[cwz_r], writes=[e1_r])
                fx.op("scalar", lambda e, e2=e2, cwz=cwz, N=N: e.activation(out=e2[:, 0:N], in_=cwz[:, 0:N], func=AF.Exp, scale=-1.0), reads=[cwz_r], writes=[e2_r])
                fx.op("vector", lambda e, e3=e3, cwz=cwz, lw=lw, N=N: e.tensor_tensor(out=e3[:, 0:N], in0=cwz[:, 0:N], in1=lw[:, 0:N], op=ALU.subtract), reads=[cwz_r, lw_r], writes=[e3_r])
                fx.op("scalar", lambda e, e3=e3, N=N: e.activation(out=e3[:, 0:N], in_=e3[:, 0:N], func=AF.Exp), reads=[e3_r], writes=[e3_r])
                fx.op("scalar", lambda e, y1=y1, N=N: e.activation(out=y1[:, 0:N], in_=y1[:, 0:N], func=AF.Exp), reads=[y1_r], writes=[y1_r])
                outs = [fmo[q][z * 2 + k] for q in range(4)]
                fx.op("vector", lambda e, o=outs[0][0], kk=kk, e3=e3, N=N: e.scalar_tensor_tensor(out=o[:, 0:N], in0=kk[:, 0:N], scalar=-1.0, in1=e3[:, 0:N], op0=ALU.mult, op1=ALU.mult), reads=[kk_r, e3_r], writes=[outs[0][1]])
                fx.op("gpsimd", lambda e, o=outs[1][0], rT=rT, e1=e1, N=N: e.tensor_tensor(out=o[:, 0:N], in0=rT[:, 0:N], in1=e1[:, 0:N], op=ALU.mult), reads=[rT_r, e1_r], writes=[outs[1][1]])
                fx.op("vector", lambda e, o=outs[2][0], bz=bz, e2=e2, N=N: e.tensor_tensor(out=o[:, 0:N], in0=bz[:, 0:N], in1=e2[:, 0:N], op=ALU.mult), reads=[bz_r, e2_r], writes=[outs[2][1]])
                fx.op("vector", lambda e, o=outs[3][0], kd=kd, e2=e2, N=N: e.tensor_tensor(out=o[:, 0:N], in0=kd[:, 0:N], in1=e2[:, 0:N], op=ALU.mult), reads=[kd_r, e2_r], writes=[outs[3][1]])
                for q in range(4):
                    fx.dma(self.FMd[z][tile0:tile0 + ntile, :, hp, q, :].rearrange("n p t -> p n t"), outs[q][0][:, 0:N].rearrange("p (n t) -> p n t", t=128), reads=[outs[q][1]], sem_res=outs[q][1])
                (h0, h0_r), (h1, h1_r) = hato[0][z * 2 + k], hato[1][z * 2 + k]
                fx.op("vector", lambda e, h0=h0, bz=bz, y1=y1, N=N: e.tensor_tensor(out=h0[:, 0:N], in0=bz[:, 0:N], in1=y1[:, 0:N], op=ALU.mult), reads=[bz_r, y1_r], writes=[h0_r])
                fx.op("gpsimd", lambda e, h1=h1, kd=kd, y1=y1, N=N: e.tensor_tensor(out=h1[:, 0:N], in0=kd[:, 0:N], in1=y1[:, 0:N], op=ALU.mult), reads=[kd_r, y1_r], writes=[h1_r])
                transpose_store(fx, k, cnt, h0, h0_r, lambda tl, z=z, tile0=tile0, hp=hp: self.TMd[z][tile0 + tl, :, 0, hp, :], ntile)
                transpose_store(fx, k, cnt, h1, h1_r, lambda tl, z=z, tile0=tile0, hp=hp: self.TMd[z][tile0 + tl, :, 1, hp, :], ntile)


        blocks = [(hp, ci) for hp in range(8) for ci in range(9)]
        for bi in range(0, len(blocks), 2):
            recs = []
            for q in range(2):
                if bi + q < len(blocks):
                    r_ = Rec()
                    emit_block(r_, blocks[bi + q][0], blocks[bi + q][1], q)
                    recs.append(r_)
            interleave(fw, recs)

    def rwkv_S(self, i, j, R, ts):
        fw = self.fw
        mk = {}
        for z in range(2):
            mNf, mNf_r = fw.sb("mNf%d" % z, [128, 256], F32, ts)
            mLf, mLf_r = fw.sb("mLf%d" % z, [128, 128], F32, ts)
            mN, mN_r = fw.sb("mN%d" % z, [128, 2, 256], BF16, ts)
            mL, mL_r = fw.sb("mL%d" % z, [128, 4, 128], BF16, ts)
            fw.dma(mNf[:], self.Cn["mN%d" % z][:, :], writes=[mNf_r], sem_res=mNf_r)
            fw.dma(mLf[:], self.Cn["mL%d" % z][:, :], writes=[mLf_r], sem_res=mLf_r)
            for q in range(2):
                fw.op("vector", lambda e, mN=mN, mNf=mNf, q=q: e.tensor_copy(out=mN[:, q, :], in_=mNf[:]), reads=[mNf_r], writes=[mN_r])
            for q in range(4):
                fw.op("vector", lambda e, mL=mL, mLf=mLf, q=q: e.tensor_copy(out=mL[:, q, :], in_=mLf[:]), reads=[mLf_r], writes=[mL_r])
            mk[z] = (mN, mN_r, mL, mL_r)
        identb4, identb4_r = fw.sb("ident4", [128, 4, 128], BF16, ts)
        for q in range(4):
            fw.op("vector", lambda e, q=q: e.tensor_copy(out=identb4[:, q, :], in_=self.ident[:]), reads=[self.ident_r], writes=[identb4_r])
        Mst = []
        for z in range(2):
            row = []
            for hf in range(2):
                t, r = fw.sb("M%d_%d" % (z, hf), [128, 8, 64], F32, ts)
                fw.op("vector", lambda e, t=t: e.memset(t[:], 0.0), writes=[r])
                tb_, rb_ = fw.sb("Mb%d_%d" % (z, hf), [128, 8, 64], BF16, ts)
                fw.op("vector", lambda e, tb_=tb_: e.memset(tb_[:], 0.0), writes=[rb_])
                row.append((t, r, tb_, rb_))
            Mst.append(row)
        FMb = [fw.sb("sFM%d" % k, [128, 8, 4, 128], BF16, ts) for k in range(2)]
        TMb = [fw.sb("sTM%d" % k, [128, 2, 8, 128], BF16, ts) for k in range(2)]
        Vb = [fw.sb("sV%d" % k, [128, 8, 128], BF16, ts) for k in range(2)]
        G13, _ = fw.sb("sG13", [128, 16, 256], BF16, ts)
        G24, _ = fw.sb("sG24", [128, 16, 256], BF16, ts)
        G13_r = [Res("sG13_%d" % g) for g in range(8)]
        G24_r = [Res("sG24_%d" % g) for g in range(8)]
        Lb_ = [fw.sb("sL%d" % k, [128, 16, 128], BF16, ts)[0] for k in range(2)]
        Nb_ = [fw.sb("sN%d" % k, [128, 16, 128], BF16, ts)[0] for k in range(2)]
        Pm, _ = fw.sb("sP", [128, 16, 128], BF16, ts)
        L_r = [[Res("sL%d_%d" % (k, g)) for g in range(4)] for k in range(2)]
        N_r = [[Res("sN%d_%d" % (k, g)) for g in range(4)] for k in range(2)]
        P_r = [Res("sP_%d" % g) for g in range(4)]
        Z2s = [fw.sb("sZ2s%d" % k, [128, 512], F32, ts) for k in range(2)]
        Zs = [fw.sb("sZs%d" % k, [128, 512], BF16, ts) for k in range(2)]
        Us = [fw.sb("sUs%d" % k, [128, 512], BF16, ts) for k in range(2)]
        Ys = [fw.sb("sYs%d" % k, [128, 512], F32, ts) for k in range(2)]
        npb = [0]

        def prep_bank():
            b = self.pb[npb[0] % 8]
            npb[0] += 1
            return b

        order = [list(range(NTI)), [1, 0] + list(range(NTI - 1, 1, -1))]
        def issue_loads(n):
            z_, tile_ = n % 2, order[n % 2][n // 2]
            FM_, FM_r_ = FMb[n % 2]
            TM_, TM_r_ = TMb[n % 2]
            V_, V_r_ = Vb[n % 2]
            fw.dma(FM_[:], self.FMd[z_][tile_], writes=[FM_r_], sem_res=FM_r_)
            fw.dma(TM_[:], self.TMd[z_][tile_], writes=[TM_r_], sem_res=TM_r_)
            fw.dma(V_[:], self.Vd[tile_], writes=[V_r_], sem_res=V_r_)

        nstep = 0
        for si in range(NTI):
            for z in (0, 1):
                tile = order[z][si]
                k = nstep % 2
                nstep += 1
                FM, FM_r = FMb[k]
                TM, TM_r = TMb[k]
                V, V_r = Vb[k]
                mN, mN_r, mL, mL_r = mk[z]
                if nstep == 1:
                    issue_loads(0)
                if nstep < 2 * NTI:
                    issue_loads(nstep)
                for g2 in range(8):
                    pB, pBr = prep_bank()
                    pK, pKr = prep_bank()
                    pL, pLr = prep_bank()
                    for hl in range(2):
                        h = g2 * 2 + hl
                        hp, hh = h % 8, h // 8
                        rows = slice(hh * 64, (hh + 1) * 64)
                        ar = FM[rows, hp, 0:2, :].rearrange("p a t -> p (a t)")
                        fw.op("tensor", lambda e, FM=FM, TM=TM, V=V, pB=pB, rows=rows, hp=hp, hl=hl, ar=ar: e.matmul(pB[:, hl * 256:(hl + 1) * 256], lhsT=FM[rows, hp, 2, :], rhs=ar, start=True, stop=True), reads=[FM_r], writes=[pBr], signal=(hl == 1))
                        fw.op("tensor", lambda e, FM=FM, TM=TM, V=V, pK=pK, rows=rows, hp=hp, hl=hl, ar=ar: e.matmul(pK[:, hl * 256:(hl + 1) * 256], lhsT=FM[rows, hp, 3, :], rhs=ar, start=True, stop=True), reads=[FM_r], writes=[pKr], signal=(hl == 1))
                        fw.op("tensor", lambda e, FM=FM, TM=TM, V=V, pL=pL, rows=rows, hp=hp, hl=hl: e.matmul(pL[:, hl * 128:(hl + 1) * 128], lhsT=FM[rows, hp, 0, :], rhs=FM[rows, hp, 2, :], start=True, stop=True), reads=[FM_r], writes=[pLr], signal=(hl == 1))
                    h0 = g2 * 2
                    fw.op("vector", lambda e, FM=FM, TM=TM, V=V, pB=pB, h0=h0, mN=mN: e.tensor_tensor(out=G13[:, h0:h0 + 2, :], in0=pB[:, :].rearrange("p (h c) -> p h c", h=2), in1=mN[:], op=ALU.mult), reads=[pBr, mN_r], writes=[G13_r[g2]])
                    fw.op("vector", lambda e, FM=FM, TM=TM, V=V, pK=pK, h0=h0, mN=mN: e.tensor_tensor(out=G24[:, h0:h0 + 2, :], in0=pK[:, :].rearrange("p (h c) -> p h c", h=2), in1=mN[:], op=ALU.mult), reads=[pKr, mN_r], writes=[G24_r[g2]])
                    fw.op("vector", lambda e, FM=FM, TM=TM, V=V, pL=pL, h0=h0, mL=mL: e.tensor_tensor(out=Lb_[0][:, h0:h0 + 2, :], in0=pL[:, 0:256].rearrange("p (h c) -> p h c", h=2), in1=mL[:, 0:2, :], op=ALU.mult), reads=[pLr, mL_r], writes=[L_r[0][g2 // 2]])
                sstop = getattr(self, "s_stop", 99)
                if sstop < 1:
                    continue
                for g4 in range(4):
                    hs = slice(g4 * 4, g4 * 4 + 4)
                    fw.op("gpsimd", lambda e, FM=FM, TM=TM, V=V, hs=hs: e.tensor_copy(out=Nb_[0][:, hs, :], in_=G13[:, hs, 0:128]), reads=[G13_r[g4 * 2], G13_r[g4 * 2 + 1]], writes=[N_r[0][g4]])
                    fw.op("gpsimd", lambda e, FM=FM, TM=TM, V=V, hs=hs: e.tensor_tensor(out=Pm[:, hs, :], in0=G13[:, hs, 0:128], in1=identb4[:], op=ALU.add), reads=[G13_r[g4 * 2], G13_r[g4 * 2 + 1], identb4_r], writes=[P_r[g4]])
                for lev in range(1, 6):
                    a, b = (lev - 1) % 2, lev % 2
                    for g4 in range(4):
                        hs = slice(g4 * 4, g4 * 4 + 4)
                        pl, plr = prep_bank()
                        for hl in range(4):
                            h = g4 * 4 + hl
                            fw.op("tensor", lambda e, FM=FM, TM=TM, V=V, pl=pl, h=h, hl=hl, a=a: e.matmul(pl[:, hl * 128:(hl + 1) * 128], lhsT=Nb_[a][:, h, :], rhs=Lb_[a][:, h, :], start=True, stop=True), reads=[N_r[a][g4], L_r[a][g4]], writes=[plr], signal=(hl == 3))
                        fw.op("scalar", lambda e, FM=FM, TM=TM, V=V, pl=pl, hs=hs, b=b: e.copy(out=Lb_[b][:, hs, :], in_=pl[:, :].rearrange("p (h c) -> p h c", h=4)), reads=[plr], writes=[L_r[b][g4]])
                        if lev < 5:
                            pn, pnr = prep_bank()
                            for hl in range(4):
                                h = g4 * 4 + hl
                                fw.op("tensor", lambda e, FM=FM, TM=TM, V=V, pn=pn, h=h, hl=hl, a=a: e.matmul(pn[:, hl * 128:(hl + 1) * 128], lhsT=Lb_[a][:, h, :], rhs=Nb_[a][:, h, :], start=True, stop=True), reads=[N_r[a][g4], L_r[a][g4]], writes=[pnr], signal=(hl == 3))
                            fw.op("vector", lambda e, FM=FM, TM=TM, V=V, pn=pn, hs=hs, b=b: e.tensor_copy(out=Nb_[b][:, hs, :], in_=pn[:, :].rearrange("p (h c) -> p h c", h=4)), reads=[pnr], writes=[N_r[b][g4]])
                    for g4 in range(4):
                        hs = slice(g4 * 4, g4 * 4 + 4)
                        pq, pqr = prep_bank()
                        for hl in range(4):
                            h = g4 * 4 + hl
                            fw.op("tensor", lambda e, FM=FM, TM=TM, V=V, pq=pq, h=h, hl=hl, b=b: e.matmul(pq[:, hl * 128:(hl + 1) * 128], lhsT=Lb_[b][:, h, :], rhs=Pm[:, h, :], start=True, stop=True), reads=[L_r[b][g4], P_r[g4]], writes=[pqr], signal=(hl == 3))
                        fw.op("vector", lambda e, FM=FM, TM=TM, V=V, pq=pq, hs=hs: e.tensor_tensor(out=Pm[:, hs, :], in0=pq[:, :].rearrange("p (h c) -> p h c", h=4), in1=Pm[:, hs, :], op=ALU.add), reads=[pqr, P_r[g4]], writes=[P_r[g4]])
                if self.dbg and nstep == 1:
                    self.dG13 = self.nc.dram_tensor("dG13", [128, 16, 256], F32, kind="ExternalOutput").ap()
                    self.dG24 = self.nc.dram_tensor("dG24", [128, 16, 256], F32, kind="ExternalOutput").ap()
                    self.dP = self.nc.dram_tensor("dP", [128, 16, 128], F32, kind="ExternalOutput").ap()
                    self.dFM = self.nc.dram_tensor("dFM", [128, 8, 4, 128], F32, kind="ExternalOutput").ap()
                    self.dTM = self.nc.dram_tensor("dTM", [128, 2, 8, 128], F32, kind="ExternalOutput").ap()
                    self.dV = self.nc.dram_tensor("dV", [128, 8, 128], F32, kind="ExternalOutput").ap()
                    dr = fw.res("dbgS", ts)
                    fw.dma(self.dFM, FM[:], reads=[FM_r], sem_res=dr, ek="gpsimd")
                    fw.dma(self.dTM, TM[:], reads=[TM_r], sem_res=dr, ek="gpsimd")
                    fw.dma(self.dV, V[:], reads=[V_r], sem_res=dr, ek="gpsimd")
                    fw.dma(self.dG13, G13[:], reads=list(G13_r), sem_res=dr, ek="gpsimd")
                    fw.dma(self.dG24, G24[:], reads=list(G24_r), sem_res=dr, ek="gpsimd")
                    fw.dma(self.dP, Pm[:], reads=list(P_r), sem_res=dr, ek="gpsimd")
                if sstop < 2:
                    continue
                cks = (0, 1) if z == 0 else (1, 0)
                for hf in range(2):
                    hr = slice(hf * 64, (hf + 1) * 64)
                    M, M_r, Mb, Mb_r = Mst[z][hf]
                    Dall, Dall_r = R["Dall"][z]
                    tA, tAr = self.pb[0]
                    tB, tBr = self.pb[1]
                    tC, tCr = self.pb[2]
                    tD, tDr = self.pb[3]
                    z2, z2_r = Z2s[hf]
                    zs_, zs_r = Zs[hf]
                    us, us_r = Us[hf]
                    ys, ys_r = Ys[hf]
                    g13r = list(G13_r)
                    g24r = list(G24_r)
                    pr_ = list(P_r)
                    for ck in range(2):
                        rows = slice(ck * 64, (ck + 1) * 64)
                        for hp in range(8):
                            h = hf * 8 + hp
                            fw.op("tensor", lambda e, FM=FM, TM=TM, V=V, rows=rows, h=h, hp=hp, hr=hr, ck=ck: e.matmul(tA[rows, hp * 64:(hp + 1) * 64], lhsT=G24[rows, h, ck * 64:(ck + 1) * 64], rhs=V[rows, hp, hr], start=True, stop=True),
                                  reads=g24r + [V_r], writes=[tAr], signal=(ck == 1 and hp == 7))
                    fw.op("scalar", lambda e, FM=FM, TM=TM, V=V, z2=z2: e.copy(out=z2[:], in_=tA[:, :]), reads=[tAr], writes=[z2_r])
                    for ck in (cks if sstop >= 3 else ()):
                        rows = slice(ck * 64, (ck + 1) * 64)
                        cols = slice(ck * 64, (ck + 1) * 64)
                        for hp in range(8):
                            fw.op("tensor", lambda e, FM=FM, TM=TM, V=V, rows=rows, cols=cols, hp=hp, hr=hr, Mb=Mb: e.matmul(tB[rows, hp * 64:(hp + 1) * 64], lhsT=FM[hr, hp, 0, cols], rhs=Mb[hr, hp, :], start=True, stop=True),
                                  reads=[FM_r, Mb_r], writes=[tBr], signal=(hp == 7))
                        for hp in range(8):
                            fw.op("tensor", lambda e, FM=FM, TM=TM, V=V, rows=rows, cols=cols, hp=hp, hr=hr, Mb=Mb: e.matmul(tC[rows, hp * 64:(hp + 1) * 64], lhsT=FM[hr, hp, 1, cols], rhs=Mb[hr, hp, :], start=True, stop=True),
                                  reads=[FM_r, Mb_r], writes=[tCr], signal=(hp == 7))
                        fw.op("vector", lambda e, FM=FM, TM=TM, V=V, rows=rows, zs_=zs_, z2=z2: e.tensor_tensor(out=zs_[rows, :], in0=tB[rows, :], in1=z2[rows, :], op=ALU.add), reads=[tBr, z2_r], writes=[zs_r])
                        if sstop < 4:
                            continue
                        for hp in range(8):
                            h = hf * 8 + hp
                            fw.op("tensor", lambda e, FM=FM, TM=TM, V=V, rows=rows, cols=cols, hp=hp, h=h, zs_=zs_: e.matmul(tB[rows, hp * 64:(hp + 1) * 64], lhsT=Pm[rows, h, cols], rhs=zs_[rows, hp * 64:(hp + 1) * 64], start=True, stop=True),
                                  reads=pr_ + [zs_r], writes=[tBr], signal=(hp == 7))
                        fw.op("scalar", lambda e, FM=FM, TM=TM, V=V, rows=rows, us=us: e.copy(out=us[rows, :], in_=tB[rows, :]), reads=[tBr], writes=[us_r])
                        if sstop < 5:
                            continue
                        for hp in range(8):
                            h = hf * 8 + hp
                            fw.op("tensor", lambda e, FM=FM, TM=TM, V=V, rows=rows, ck=ck, hp=hp, h=h, us=us: e.matmul(tA[rows, hp * 64:(hp + 1) * 64], lhsT=G13[rows, h, 128 + ck * 64:128 + (ck + 1) * 64], rhs=us[rows, hp * 64:(hp + 1) * 64], start=True, stop=False),
                                  reads=g13r + [us_r], writes=[tAr], signal=False)
                            fw.op("tensor", lambda e, FM=FM, TM=TM, V=V, rows=rows, ck=ck, hp=hp, h=h, hr=hr: e.matmul(tA[rows, hp * 64:(hp + 1) * 64], lhsT=G24[rows, h, 128 + ck * 64:128 + (ck + 1) * 64], rhs=V[rows, hp, hr], start=False, stop=True),
                                  reads=g24r + [V_r], writes=[tAr], signal=(hp == 7))
                        if sstop < 6:
                            continue
                        for hp in range(8):
                            fw.op("tensor", lambda e, FM=FM, TM=TM, V=V, rows=rows, hp=hp, hr=hr, us=us: e.matmul(tD[hr, hp * 64:(hp + 1) * 64], lhsT=TM[rows, 0, hp, hr], rhs=us[rows, hp * 64:(hp + 1) * 64], start=True, stop=False),
                                  reads=[TM_r, us_r], writes=[tDr], signal=False)
                            fw.op("tensor", lambda e, FM=FM, TM=TM, V=V, rows=rows, hp=hp, hr=hr: e.matmul(tD[hr, hp * 64:(hp + 1) * 64], lhsT=TM[rows, 1, hp, hr], rhs=V[rows, hp, hr], start=False, stop=True),
                                  reads=[TM_r, V_r], writes=[tDr], signal=(hp == 7))
                        c = tile * 2 + ck
                        dbc = Dall[hr, :, c:c + 1].to_broadcast([64, 8, 64])
                        fw.op("vector", lambda e, FM=FM, TM=TM, V=V, M=M, dbc=dbc, hr=hr: e.tensor_tensor(out=M[hr, :, :], in0=M[hr, :, :], in1=dbc, op=ALU.mult), reads=[M_r, Dall_r], writes=[M_r])
                        fw.op("vector", lambda e, FM=FM, TM=TM, V=V, M=M, hr=hr: e.tensor_tensor(out=M[hr, :, :], in0=tD[hr, :].rearrange("p (h i) -> p h i", h=8), in1=M[hr, :, :], op=ALU.add), reads=[M_r, tDr], writes=[M_r])
                        fw.op("scalar", lambda e, FM=FM, TM=TM, V=V, M=M, Mb=Mb, hr=hr: e.copy(out=Mb[hr, :, :], in_=M[hr, :, :]), reads=[M_r], writes=[Mb_r])
                    if sstop < 7:
                        continue
                    fw.op("scalar", lambda e, FM=FM, TM=TM, V=V, ys=ys: e.copy(out=ys[:], in_=tC[:, :]), reads=[tCr], writes=[ys_r])
                    fw.op("vector", lambda e, FM=FM, TM=TM, V=V, ys=ys: e.tensor_tensor(out=ys[:], in0=tA[:, :], in1=ys[:], op=ALU.add), reads=[tAr, ys_r], writes=[ys_r])
                    ydst = self.yd[z][tile * 128:(tile + 1) * 128, :].rearrange("t (hp hh i) -> t hp hh i", hh=2, i=64)[:, :, hf, :]
                    fw.dma(ydst, ys[:].rearrange("p (hp i) -> p hp i", i=64), reads=[ys_r], sem_res=ys_r)

    def rwkv_O(self, i, j, R, ts):
        fw = self.fw
        I = self.I
        lnw, lnw_r = fw.sb("o_lnw", [128, D], F32, ts)
        lnb, lnb_r = fw.sb("o_lnb", [128, D], F32, ts)
        fw.dma(lnw[:], I["rw_ln_w"][j, :].partition_broadcast(128), writes=[lnw_r], sem_res=lnw_r)
        fw.dma(lnb[:], I["rw_ln_b"][j, :].partition_broadcast(128), writes=[lnb_r], sem_res=lnb_r)
        epsg, epsg_r = fw.sb("o_eps", [128, 1], F32, ts)
        fw.op("vector", lambda e: e.memset(epsg[:], 64e-5), writes=[epsg_r])
        y0b = [fw.sb("o_y0%d" % k, [128, D], F32, ts) for k in range(2)]
        y1b = [fw.sb("o_y1%d" % k, [128, D], F32, ts) for k in range(2)]
        sqb = [fw.sb("o_sq%d" % k, [128, D], F32, ts) for k in range(2)]
        stb = [fw.sb("o_st%d" % k, [128, 2, 16], F32, ts) for k in range(2)]
        bnb = [fw.sb("o_bn%d" % k, [128, 8, 128], F32, ts) for k in range(2)]
        zb = [fw.sb("o_z%d" % k, [128, 8, 128], BF16, ts) for k in range(2)]
        ob = [fw.sb("o_o%d" % k, [128, 8, 128], F32, ts) for k in range(2)]
        ogb = [fw.sb("o_og%d" % k, [128, 8, 128], BF16, ts) for k in range(2)]
        for tt in range(NTI):
            k = tt % 2
            (y0, y0_r), (y1, y1_r), (sq, sq_r), (st, st_r) = y0b[k], y1b[k], sqb[k], stb[k]
            (bn, bn_r), (zt, zt_r), (o_, o_r), (og_, og_r_) = bnb[k], zb[k], ob[k], ogb[k]
            rs = slice(tt * 128, (tt + 1) * 128)
            fw.dma(y0[:], self.yd[0][rs, :], writes=[y0_r], sem_res=y0_r)
            fw.dma(y1[:], self.yd[1][rs, :], writes=[y1_r], sem_res=y1_r)
            fw.dma(bn[:], self.bonusd[:, :, rs].rearrange("h p t -> p h t"), writes=[bn_r], sem_res=bn_r)
            fw.dma(zt[:], self.zsd[:, :, rs].rearrange("h p t -> p h t"), writes=[zt_r], sem_res=zt_r)
            fw.op("gpsimd", lambda e, y0=y0, y1=y1: e.tensor_tensor(out=y0[:], in0=y0[:], in1=y1[:], op=ALU.add), reads=[y0_r, y1_r], writes=[y0_r])
            y3 = y0[:].rearrange("p (h d) -> p h d", d=64)
            s3 = sq[:].rearrange("p (h d) -> p h d", d=64)
            fw.op("vector", lambda e, st=st, y3=y3: e.tensor_reduce(out=st[:, 0, :], in_=y3, axis=AX.X, op=ALU.add), reads=[y0_r], writes=[st_r])
            fw.op("vector", lambda e, st=st: e.tensor_scalar(out=st[:, 0, :], in0=st[:, 0, :], scalar1=1.0 / 64.0, scalar2=None, op0=ALU.mult), reads=[st_r], writes=[st_r])
            mbc = st[:, 0, :].unsqueeze(2).to_broadcast([128, 16, 64])
            fw.op("vector", lambda e, y3=y3, mbc=mbc: e.tensor_tensor(out=y3, in0=y3, in1=mbc, op=ALU.subtract), reads=[y0_r, st_r], writes=[y0_r])
            fw.op("gpsimd", lambda e, sq=sq, y0=y0: e.tensor_tensor(out=sq[:], in0=y0[:], in1=y0[:], op=ALU.mult), reads=[y0_r], writes=[sq_r])
            fw.op("vector", lambda e, st=st, s3=s3: e.tensor_reduce(out=st[:, 1, :], in_=s3, axis=AX.X, op=ALU.add), reads=[sq_r], writes=[st_r])
            fw.op("scalar", lambda e, st=st: e.activation(out=st[:, 1, :], in_=st[:, 1, :], func=AF.Sqrt, bias=epsg[:], scale=1.0 / 64.0), reads=[st_r, epsg_r], writes=[st_r])
            fw.op("vector", lambda e, st=st: e.reciprocal(out=st[:, 1, :], in_=st[:, 1, :]), reads=[st_r], writes=[st_r])
            rbc = st[:, 1, :].unsqueeze(2).to_broadcast([128, 16, 64])
            fw.op("vector", lambda e, y3=y3, rbc=rbc: e.tensor_tensor(out=y3, in0=y3, in1=rbc, op=ALU.mult), reads=[y0_r, st_r], writes=[y0_r])
            fw.op("gpsimd", lambda e, y0=y0: e.tensor_tensor(out=y0[:], in0=y0[:], in1=lnw[:], op=ALU.mult), reads=[y0_r, lnw_r], writes=[y0_r])
            fw.op("gpsimd", lambda e, y0=y0: e.tensor_tensor(out=y0[:], in0=y0[:], in1=lnb[:], op=ALU.add), reads=[y0_r, lnb_r], writes=[y0_r])
            for half in range(2):
                tp, tpr = self.pb[(tt * 2 + half) % 8]
                for q in range(4):
                    hc = half * 4 + q
                    fw.op("tensor", lambda e, tp=tp, q=q, hc=hc, y0=y0: e.transpose(tp[:, q * 128:(q + 1) * 128], y0[:, hc * 128:(hc + 1) * 128], self.ident[:]), reads=[y0_r, self.ident_r], writes=[tpr], signal=(q == 3))
                fw.op("vector", lambda e, tp=tp, half=half, o_=o_, bn=bn: e.tensor_tensor(out=o_[:, half * 4:half * 4 + 4, :], in0=tp[:, :].rearrange("p (h t) -> p h t", h=4), in1=bn[:, half * 4:half * 4 + 4, :], op=ALU.add), reads=[tpr, bn_r], writes=[o_r])
            fw.op("gpsimd", lambda e, og_=og_, o_=o_, zt=zt: e.tensor_tensor(out=og_[:], in0=o_[:], in1=zt[:], op=ALU.mult), reads=[o_r, zt_r], writes=[og_r_])
            fw.dma(self.og[:, :, rs].rearrange("h p t -> p h t"), og_[:], reads=[og_r_], sem_res=og_r_)


_CACHE = {}


def get_prog(n_layers=DEPTH, dbg=False):
    key = (n_layers, dbg)
    if key not in _CACHE:
        p = Prog(n_layers, dbg)
        p.build()
        _CACHE[key] = p
    return _CACHE[key]


def make_in_maps(inputs, consts, cores):
    maps = []
    for b in cores:
        m = {}
        for k, shp in INPUT_SHAPES.items():
            a = np.asarray(inputs[k])
            if k in ("x", "ctx"):
                a = a[b]
            elif k == "c":
                a = a[b:b + 1]
            a = np.ascontiguousarray(a, dtype=np.float32).reshape(shp)
            m[k] = a
        for k, v in consts.items():
            m["k_" + k] = v
        maps.append(m)
    return maps


def kernel(**inputs):
    p = get_prog()
    cores = list(range(8))
    in_maps = make_in_maps(inputs, p.consts, cores)
    res = run_bass_kernel_spmd(p.nc, in_maps, core_ids=cores)
    out = np.stack([np.asarray(r["out"]) for r in res.results], axis=0)
    return out.astype(np.float32)
```

```python
import math
from contextlib import ExitStack

import numpy as np
import concourse.bass as bass
import concourse.mybir as mybir
from concourse.bass_utils import run_bass_kernel_spmd

F32 = mybir.dt.float32
BF16 = mybir.dt.bfloat16
AF = mybir.ActivationFunctionType
ALU = mybir.AluOpType
AX = mybir.AxisListType

D = 1024
NL = 4096
NCX = 256
NT = NL + NCX
NTI = NT // 128
DEPTH = 4
NORM_EPS = 1e-6
SUBLN_EPS = 1e-5
GRID_W = 64


class Sem:
    def __init__(self, fw, name):
        self.name = name
        self.handle = fw.stack.enter_context(fw.nc.semaphore(name))
        self.issued = 0
        fw.sems.append(self)


class Res:
    __slots__ = ("name", "writers", "readers", "dsem")

    def __init__(self, name):
        self.name = name
        self.writers = []
        self.readers = []
        self.dsem = None


class Eng:
    def __init__(self, fw, key):
        self.key = key
        self.sem = Sem(fw, "e_" + key)
        self.ops = []
        self.waited = {}
        self.pend_r = []
        self.pend_w = []


class FW:
    def __init__(self, nc, stack):
        self.nc = nc
        self.stack = stack
        self.sems = []
        self.eng = {k: Eng(self, k) for k in ("tensor", "vector", "scalar", "gpsimd", "sync")}
        self.n_ins = 0
        self.free_sems = []

    def sb(self, name, shape, dtype, stack=None):
        self.uid = getattr(self, "uid", 0) + 1
        name = "%s_u%d" % (name, self.uid)
        t = (stack or self.stack).enter_context(self.nc.sbuf_tensor(name, list(shape), dtype))
        return t, self.res(name, stack)

    def res(self, name, stack=None):
        r = Res(name)
        if stack is not None:
            stack.callback(self._release, r)
        return r

    def _release(self, r):
        if r.dsem is not None:
            self.free_sems.append(r.dsem)
            r.dsem = None

    def _get_dsem(self, name):
        if self.free_sems:
            sem = self.free_sems.pop(0)
            for e in self.eng.values():
                assert e.waited.get(sem, 0) >= sem.issued, "semaphore reused before a barrier"
            return sem
        return Sem(self, "d%d" % len(self.sems))

    def ps(self, name, shape, dtype, stack=None):
        t = (stack or self.stack).enter_context(self.nc.psum_tensor(name, list(shape), dtype))
        return t, Res(name)

    def _waits_for(self, eng, reads, writes):
        need = {}

        def add(tok):
            sem, val = tok
            if val is None:
                val = sem.issued
            if need.get(sem, 0) < val:
                need[sem] = val
        for r in reads:
            for t in r.writers:
                add(t)
        for w in writes:
            for t in w.writers:
                add(t)
            for t in w.readers:
                add(t)
        out = []
        for sem, val in need.items():
            if eng.waited.get(sem, 0) >= val:
                continue
            eng.waited[sem] = val
            out.append((sem.handle, val))
        return out

    def _check_pending(self, eng, reads, writes):
        for e in self.eng.values():
            if e is eng or (not e.pend_w and not e.pend_r):
                continue
            for r in reads:
                assert r not in e.pend_w, ("unsignaled write pending", r.name, e.key)
            for w in writes:
                assert w not in e.pend_w and w not in e.pend_r, ("unsignaled access pending", w.name, e.key)

    def op(self, ek, fn, reads=(), writes=(), signal=True):
        eng = self.eng[ek]
        self._check_pending(eng, reads, writes)
        waits = self._waits_for(eng, reads, writes)
        self.n_ins += 1
        if signal:
            eng.sem.issued += 1
            tok = (eng.sem, eng.sem.issued)
            semh = eng.sem.handle

            def run(e, fn=fn, waits=waits, semh=semh):
                for (h, v) in waits:
                    e.wait_ge(h, v)
                fn(e).then_inc(semh, 1)
            rs = list(reads) + eng.pend_r
            ws = list(writes) + eng.pend_w
            eng.pend_r = []
            eng.pend_w = []
            for r in rs:
                r.readers.append(tok)
            for w in ws:
                w.writers = [tok]
                w.readers = []
        else:
            def run(e, fn=fn, waits=waits):
                for (h, v) in waits:
                    e.wait_ge(h, v)
                fn(e)
            eng.pend_r.extend(reads)
            eng.pend_w.extend(writes)
        eng.ops.append(run)

    def dma(self, out, in_, reads=(), writes=(), sem_res=None, ek="sync", **kw):
        eng = self.eng[ek]
        self._check_pending(eng, reads, writes)
        waits = self._waits_for(eng, reads, writes)
        if sem_res.dsem is None:
            sem_res.dsem = self._get_dsem(sem_res.name)
        sem = sem_res.dsem
        sem.issued += 16
        tok = (sem, None)
        semh = sem.handle
        self.n_ins += 1

        def run(e, waits=waits, semh=semh, out=out, in_=in_, kw=kw):
            for (h, v) in waits:
                e.wait_ge(h, v)
            e.dma_start(out=out, in_=in_, **kw).then_inc(semh, 16)
        eng.ops.append(run)
        for r in reads:
            r.readers.append(tok)
        for w in writes:
            w.writers = [tok]
            w.readers = []

    def barrier(self):
        for e in self.eng.values():
            assert not e.pend_r and not e.pend_w
        for e in self.eng.values():
            waits = []
            for s in self.sems:
                if s.issued > 0 and e.waited.get(s, 0) < s.issued:
                    e.waited[s] = s.issued
                    waits.append((s.handle, s.issued))

            def run(en, waits=waits):
                for (h, v) in waits:
                    en.wait_ge(h, v)
            e.ops.append(run)

    def finish(self):
        self.barrier()
        with self.nc.Block() as block:
            for key in ("sync", "tensor", "vector", "scalar", "gpsimd"):
                e = self.eng[key]

                def body(h, e=e):
                    for o in e.ops:
                        o(h)
                getattr(block, key)(body)


class Rec:
    def __init__(self):
        self.items = []

    def op(self, *a, **k):
        self.items.append((0, a, k))

    def dma(self, *a, **k):
        self.items.append((1, a, k))


def interleave(fw, recs):
    idx = [0] * len(recs)
    n = [len(r.items) for r in recs]
    while True:
        best, bf = -1, 2.0
        for i in range(len(recs)):
            if idx[i] < n[i]:
                f = idx[i] / n[i]
                if f < bf:
                    best, bf = i, f
        if best < 0:
            break
        kind, a, k = recs[best].items[idx[best]]
        idx[best] += 1
        (fw.dma if kind else fw.op)(*a, **k)


def rope_tables():
    n = np.arange(NL)
    row = (n // GRID_W).astype(np.float32)
    col = (n % GRID_W).astype(np.float32)
    inv = (10000.0 ** (-np.arange(0, 32, 2, dtype=np.float32) / 32.0)).astype(np.float32)
    C = np.ones((128, NT), np.float32)
    S = np.zeros((128, NT), np.float32)
    for f in range(128):
        axis = (f % 64) // 32
        j = f % 16
        pos = row if axis == 0 else col
        ang = (pos * inv[j]).astype(np.float32)
        C[f, NCX:] = np.cos(ang)
        S[f, NCX:] = np.sin(ang)
    return C, S


def host_consts():
    cs = {}
    cs["ident"] = np.eye(128, dtype=np.float32)
    C, S = rope_tables()
    cs["ropeC"] = C
    cs["ropeS"] = S
    import ml_dtypes
    bf = ml_dtypes.bfloat16
    for name, L in (("L", NL), ("X", NCX)):
        l = np.arange(L, dtype=np.int64)
        ang = (2.0 * np.pi / L) * ((l[:, None] * l[None, :]) % L).astype(np.float64)
        sc = 1.0 / math.sqrt(L)
        cs["dftC" + name] = (np.cos(ang) * sc).astype(np.float32).astype(bf)
        cs["dftS" + name] = (np.sin(ang) * sc).astype(np.float32).astype(bf)
    c = np.arange(128, dtype=np.int64)
    ang = (2.0 * np.pi / 128) * ((c[:, None] * c[None, :]) % 128).astype(np.float64)
    sc = 1.0 / math.sqrt(128.0)
    cs["dftCc"] = (np.cos(ang) * sc).astype(np.float32).astype(bf)
    cs["dftnSc"] = (-np.sin(ang) * sc).astype(np.float32).astype(bf)
    a = np.arange(128)
    same = (a[:, None] // 64) == (a[None, :] // 64)
    lt = (a[:, None] < a[None, :]) & same
    le = (a[:, None] <= a[None, :]) & same
    gt = (a[:, None] > a[None, :]) & same
    ge = (a[:, None] >= a[None, :]) & same
    cs["mN0"] = np.concatenate([lt, le], axis=1).astype(np.float32)
    cs["mL0"] = gt.astype(np.float32)
    cs["mN1"] = np.concatenate([gt, ge], axis=1).astype(np.float32)
    cs["mL1"] = lt.astype(np.float32)
    return cs


INPUT_SHAPES = {
    "x": [NL, D], "c": [1, D], "ctx": [NCX, D], "c_ctx": [1, D], "norm_gain": [4, D], "ada_w": [4, D, 3 * D],
    "ada_b": [4, 3 * D], "final_gain": [1, D], "da_w_in": [2, D, 4 * D], "da_lam_q": [2, 128], "da_lam_k": [2, 128],
    "da_subln_gain": [2, 128], "da_w_out": [2, D, D], "fn_w_in": [1, D, 2 * D], "fn_w_group": [1, 8, 128, 128],
    "fn_w_out": [1, D, D], "rw_w_in": [1, D, 4352], "rw_mu": [1, 3328], "rw_w0": [1, 2, D], "rw_w_up": [1, 2, 64, D],
    "rw_a0": [1, 2, D], "rw_a_up": [1, 2, 64, D], "rw_k_k": [1, D], "rw_k_a": [1, D], "rw_r_k": [1, D],
    "rw_ln_w": [1, D], "rw_ln_b": [1, D], "rw_w_out": [1, D, D],
}


class Prog:
    def __init__(self, n_layers=DEPTH, dbg=False):
        self.n_layers = n_layers
        self.dbg = dbg
        self.nc = bass.Bass("TRN2", target_bir_lowering=False)
        self.consts = host_consts()

    def din(self, name, shape, dt=F32):
        return self.nc.dram_tensor(name, list(shape), dt, kind="ExternalInput").ap()

    def build(self):
        nc = self.nc
        self.I = {k: self.din(k, s) for k, s in INPUT_SHAPES.items()}
        self.Cn = {k: self.din("k_" + k, v.shape, F32 if v.dtype == np.float32 else BF16) for k, v in self.consts.items()}
        self.out = nc.dram_tensor("out", [NL, D], F32, kind="ExternalOutput").ap()
        if self.dbg:
            self.dbg_ctx = nc.dram_tensor("dbg_ctx", [NCX, D], F32, kind="ExternalOutput").ap()
        self.xres = nc.dram_tensor("xres", [NT, D], F32, kind="Internal").ap()
        self.og = nc.dram_tensor("og", [8, 128, NT], BF16, kind="Internal").ap()
        self.zsd = nc.dram_tensor("zsd", [8, 128, NT], BF16, kind="Internal").ap()
        self.zsd_r = [[Res("zsd%d_%d" % (h, c)) for c in range(9)] for h in range(8)]
        self.xres_r = [Res("xres%d" % t) for t in range(NTI)]
        self.og_r = [[Res("og%d_%d" % (h, c)) for c in range(9)] for h in range(8)]
        self.out_r = [Res("out%d" % t) for t in range(NTI)]
        with ExitStack() as st:
            self.st = st
            fw = self.fw = FW(nc, st)
            self.setup_persistent()
            for i in range(self.n_layers):
                self.layer(i)
            fw.finish()
        return nc

    @staticmethod
    def chunk(ci):
        if ci == 0:
            return 0, NCX
        return NCX + (ci - 1) * 512, 512

    def xsrc(self, i, tt):
        if i == 0:
            if tt < 2:
                return self.I["ctx"][tt * 128:(tt + 1) * 128, :], []
            return self.I["x"][(tt - 2) * 128:(tt - 1) * 128, :], []
        return self.xres[tt * 128:(tt + 1) * 128, :], [self.xres_r[tt]]

    def setup_persistent(self):
        fw, st = self.fw, self.st
        self.ident, self.ident_r = fw.sb("ident", [128, 128], F32)
        fw.dma(self.ident[:], self.Cn["ident"][:, :], writes=[self.ident_r], sem_res=self.ident_r)
        self.identb, self.identb_r = fw.sb("identb", [128, 128], BF16)
        fw.op("vector", lambda e: e.tensor_copy(out=self.identb[:], in_=self.ident[:]), reads=[self.ident_r], writes=[self.identb_r])
        self.ones_b, self.ones_b_r = fw.sb("ones_b", [128, 128], BF16)
        fw.op("vector", lambda e: e.memset(self.ones_b[:], 1.0), writes=[self.ones_b_r])
        self.mean_b, self.mean_b_r = fw.sb("mean_b", [128, 128], BF16)
        fw.op("vector", lambda e: e.memset(self.mean_b[:], 1.0 / 128.0), writes=[self.mean_b_r])
        self.epsn, self.epsn_r = fw.sb("epsn", [128, 1], F32)
        fw.op("vector", lambda e: e.memset(self.epsn[:], NORM_EPS), writes=[self.epsn_r])
        self.epss, self.epss_r = fw.sb("epss", [128, 1], F32)
        fw.op("vector", lambda e: e.memset(self.epss[:], SUBLN_EPS), writes=[self.epss_r])
        self.gsc, self.gsc_r = fw.sb("gsc", [128, 8, 2], F32)
        self.shf, self.shf_r = fw.sb("shf", [128, 8, 2], F32)
        self.gate, self.gate_r = fw.sb("gate", [128, 2, D], F32)
        self.s_fm, self.s_fm_r = fw.sb("s_fm", [128, 8, 2], F32)
        self.pb = []
        for b in range(8):
            t, r = fw.ps("pb%d" % b, [128, 512], F32)
            self.pb.append((t, r))
        with ExitStack() as ts:
            cc, cc_r = fw.sb("cc", [2, D], F32, ts)
            fw.dma(cc[0:1, :], self.I["c"][:, :], writes=[cc_r], sem_res=cc_r)
            fw.dma(cc[1:2, :], self.I["c_ctx"][:, :], writes=[cc_r], sem_res=cc_r)
            fw.op("scalar", lambda e: e.activation(out=cc[:], in_=cc[:], func=AF.Silu), reads=[cc_r], writes=[cc_r])
            pt, pr = self.pb[0]
            for kc in range(8):
                fw.op("tensor", lambda e, kc=kc: e.transpose(pt[:, kc * 2:kc * 2 + 2], cc[:, kc * 128:(kc + 1) * 128], self.ident[0:2, 0:2]),
                      reads=[cc_r, self.ident_r], writes=[pr], signal=(kc == 7))
            fw.op("vector", lambda e: e.tensor_copy(out=self.s_fm[:].rearrange("p k w -> p (k w)"), in_=pt[:, 0:16]), reads=[pr], writes=[self.s_fm_r])
            fw.barrier()

    def load_fm(self, dst, dst_r, src2d, n, ts):
        fw = self.fw
        tmp, tmp_r = fw.sb("lfm_tmp%d" % fw.n_ins, [n, 128], F32, ts)
        fw.dma(tmp[:], src2d, writes=[tmp_r], sem_res=tmp_r)
        pt, pr = self.pb[1]
        fw.op("tensor", lambda e: e.transpose(pt[:, 0:n], tmp[:], self.ident[0:n, 0:n]), reads=[tmp_r, self.ident_r], writes=[pr])
        fw.op("vector", lambda e: e.tensor_copy(out=dst, in_=pt[:, 0:n]), reads=[pr], writes=[dst_r])

    def phase_mod(self, i):
        fw = self.fw
        W = self.I["ada_w"][i]
        with ExitStack() as ts:
            self.s_rep, self.s_rep_r = fw.sb("s_rep", [128, 8, 2, 128], F32, ts)
            for kc in range(8):
                for w in range(2):
                    fw.op("gpsimd", lambda e, kc=kc, w=w: e.tensor_copy(out=self.s_rep[:, kc, w, :], in_=self.s_fm[:, kc, w:w + 1].to_broadcast([128, 128])),
                          reads=[self.s_fm_r], writes=[self.s_rep_r])
            bfm, bfm_r = fw.sb("bfm", [128, 16], F32, ts)
            gfm, gfm_r = fw.sb("gfm", [128, 8], F32, ts)
            self.load_fm(bfm[:], bfm_r, self.I["ada_b"][i, 0:2048].rearrange("(n p) -> n p", p=128), 16, ts)
            self.load_fm(gfm[:], gfm_r, self.I["norm_gain"][i, :].rearrange("(n p) -> n p", p=128), 8, ts)
            bg, bg_r = fw.sb("bg", [128, D], F32, ts)
            fw.dma(bg[:], self.I["ada_b"][i, 2048:3072].partition_broadcast(128), writes=[bg_r], sem_res=bg_r)
            wt = [fw.sb("adaw%d" % k, [128, 8, 512], F32, ts) for k in range(2)]
            pt, pr = self.pb[2]
            for cc in range(6):
                w_t, w_r = wt[cc % 2]
                for kc in range(8):
                    fw.dma(w_t[:, kc, :], W[kc * 128:(kc + 1) * 128, cc * 512:(cc + 1) * 512], writes=[w_r], sem_res=w_r)
                if cc < 4:
                    for fl in range(4):
                        fc = cc * 4 + fl
                        for kc in range(8):
                            fw.op("tensor", lambda e, kc=kc, fl=fl, fc=fc, w_t=w_t: e.matmul(pt[:, fc * 2:fc * 2 + 2], lhsT=w_t[:, kc, fl * 128:(fl + 1) * 128], rhs=self.s_fm[:, kc, :], start=(kc == 0), stop=(kc == 7)),
                                  reads=[w_r, self.s_fm_r], writes=[pr], signal=(kc == 7 and fl == 3))
                    if cc == 3:
                        ps3 = pt[:, 0:32].rearrange("p (f w) -> p f w", w=2)
                        for w in range(2):
                            fw.op("vector", lambda e, w=w: e.tensor_tensor(out=self.shf[:, :, w], in0=ps3[:, 0:8, w], in1=bfm[:, 0:8], op=ALU.add),
                                  reads=[pr, bfm_r], writes=[self.shf_r])
                            fw.op("vector", lambda e, w=w: e.tensor_tensor(out=self.gsc[:, :, w], in0=ps3[:, 8:16, w], in1=bfm[:, 8:16], op=ALU.add),
                                  reads=[pr, bfm_r], writes=[self.gsc_r])
                            fw.op("vector", lambda e, w=w: e.scalar_tensor_tensor(out=self.gsc[:, :, w], in0=self.gsc[:, :, w], scalar=1.0, in1=gfm[:], op0=ALU.add, op1=ALU.mult),
                                  reads=[self.gsc_r, gfm_r], writes=[self.gsc_r])
                else:
                    cg = cc - 4
                    for w in range(2):
                        gp, gr = self.pb[3 + w]
                        for kc in range(8):
                            fw.op("tensor", lambda e, kc=kc, w=w, gp=gp, w_t=w_t: e.matmul(gp[:, :], lhsT=self.s_rep[:, kc, w, :], rhs=w_t[:, kc, :], start=(kc == 0), stop=(kc == 7)),
                                  reads=[w_r, self.s_rep_r], writes=[gr], signal=(kc == 7))
                        fw.op("vector", lambda e, w=w, gp=gp, cg=cg: e.tensor_tensor(out=self.gate[:, w, cg * 512:(cg + 1) * 512], in0=gp[:, :], in1=bg[:, cg * 512:(cg + 1) * 512], op=ALU.add),
                              reads=[gr, bg_r], writes=[self.gate_r])
            fw.barrier()

    def phase_norm(self, i, hT, hT_r, ts):
        fw = self.fw
        xt = [fw.sb("nx%d" % k, [128, D], F32, ts) for k in range(3)]
        sq, sq_r = fw.sb("nsq", [128, D], BF16, ts)
        ss = [fw.sb("nss%d" % k, [128, 1], F32, ts) for k in range(2)]
        dg = [fw.sb("ndg%d" % k, [128, 128], F32, ts) for k in range(2)]
        def tile_body(fx, tt):
            x_t, x_r = xt[tt % 3]
            s_t, s_r = ss[tt % 2]
            d_t, d_r = dg[tt % 2]
            w = 1 if tt < 2 else 0
            src, src_r = self.xsrc(i, tt)
            fx.dma(x_t[:], src, reads=src_r, writes=[x_r], sem_res=x_r)
            fx.op("scalar", lambda e, x_t=x_t, s_t=s_t: e.activation(out=sq[:], in_=x_t[:], func=AF.Square, accum_out=s_t[:]), reads=[x_r], writes=[sq_r, s_r])
            fx.op("scalar", lambda e, s_t=s_t: e.activation(out=s_t[:], in_=s_t[:], func=AF.Sqrt, bias=self.epsn[:], scale=1.0 / D), reads=[s_r, self.epsn_r], writes=[s_r])
            fx.op("vector", lambda e, s_t=s_t: e.reciprocal(out=s_t[:], in_=s_t[:]), reads=[s_r], writes=[s_r])
            fx.op("vector", lambda e, s_t=s_t, d_t=d_t: e.tensor_scalar(out=d_t[:], in0=self.ident[:], scalar1=s_t[:, 0:1], scalar2=None, op0=ALU.mult), reads=[s_r, self.ident_r], writes=[d_r])
            for half in range(2):
                pt, pr = self.pb[(tt * 2 + half) % 4]
                for k4 in range(4):
                    kc = half * 4 + k4
                    fx.op("tensor", lambda e, kc=kc, k4=k4, pt=pt, x_t=x_t, d_t=d_t: e.matmul(pt[:, k4 * 128:(k4 + 1) * 128], lhsT=x_t[:, kc * 128:(kc + 1) * 128], rhs=d_t[:], start=True, stop=True),
                          reads=[x_r, d_r], writes=[pr], signal=(k4 == 3))
                for k4 in range(4):
                    kc = half * 4 + k4
                    ek = "scalar" if k4 % 2 == 0 else "vector"
                    if ek == "scalar":
                        fx.op("scalar", lambda e, kc=kc, k4=k4, pt=pt, tt=tt, w=w: e.activation(out=hT[:, kc, tt * 128:(tt + 1) * 128], in_=pt[:, k4 * 128:(k4 + 1) * 128], func=AF.Identity, bias=self.shf[:, kc, w:w + 1], scale=self.gsc[:, kc, w:w + 1]),
                              reads=[pr, self.shf_r, self.gsc_r], writes=[hT_r[tt]])
                    else:
                        fx.op("vector", lambda e, kc=kc, k4=k4, pt=pt, tt=tt, w=w: e.tensor_scalar(out=hT[:, kc, tt * 128:(tt + 1) * 128], in0=pt[:, k4 * 128:(k4 + 1) * 128], scalar1=self.gsc[:, kc, w:w + 1], scalar2=self.shf[:, kc, w:w + 1], op0=ALU.mult, op1=ALU.add),
                              reads=[pr, self.shf_r, self.gsc_r], writes=[hT_r[tt]])

        for tt in range(0, NTI, 2):
            ra, rb = Rec(), Rec()
            tile_body(ra, tt)
            tile_body(rb, tt + 1)
            interleave(fw, [ra, rb])

    def phase_out(self, i, w_out, ts):
        fw = self.fw
        last = (i == DEPTH - 1)
        wst = [fw.sb("wo_st%d" % k, [128, D], F32, ts) for k in range(2)]
        wb, wb_r = fw.sb("wo_b", [128, 8, D], BF16, ts)
        for kc in range(8):
            s_t, s_r = wst[kc % 2]
            fw.dma(s_t[:], w_out[kc * 128:(kc + 1) * 128, :], writes=[s_r], sem_res=s_r)
            fw.op("gpsimd", lambda e, kc=kc, s_t=s_t: e.tensor_copy(out=wb[:, kc, :], in_=s_t[:]), reads=[s_r], writes=[wb_r])
        ogt = [fw.sb("po_og%d" % k, [128, 8, 512], BF16, ts) for k in range(2)]
        xt = [fw.sb("po_x%d" % k, [128, D], F32, ts) for k in range(2)]
        yt = [fw.sb("po_y%d" % k, [128, D], F32, ts) for k in range(2)]
        if last:
            fg, fg_r = fw.sb("po_fg", [128, D], F32, ts)
            fw.dma(fg[:], self.I["final_gain"][0, :].partition_broadcast(128), writes=[fg_r], sem_res=fg_r)
            sq, sq_r = fw.sb("po_sq", [128, D], BF16, ts)
            ss = [fw.sb("po_ss%d" % k, [128, 1], F32, ts) for k in range(2)]
        cis = range(1, 9) if last else range(9)
        n = 0
        for ci in cis:
            t0, tn = self.chunk(ci)
            o_t, o_r = ogt[ci % 2]
            for hc in range(8):
                fw.dma(o_t[:, hc, 0:tn], self.og[hc, :, t0:t0 + tn], reads=[self.og_r[hc][ci]], writes=[o_r], sem_res=o_r)
            for tl in range(tn // 128):
                tt = t0 // 128 + tl
                w = 1 if tt < 2 else 0
                x_t, x_r = xt[n % 2]
                y_t, y_r = yt[n % 2]
                src, src_r = self.xsrc(i, tt)
                fw.dma(x_t[:], src, reads=src_r, writes=[x_r], sem_res=x_r)
                for half in range(2):
                    pt, pr = self.pb[(n * 2 + half) % 4]
                    for hc in range(8):
                        fw.op("tensor", lambda e, hc=hc, half=half, pt=pt, o_t=o_t, tl=tl: e.matmul(pt[:, :], lhsT=o_t[:, hc, tl * 128:(tl + 1) * 128], rhs=wb[:, hc, half * 512:(half + 1) * 512], start=(hc == 0), stop=(hc == 7)),
                              reads=[o_r, wb_r], writes=[pr], signal=(hc == 7))
                    fw.op("vector", lambda e, half=half, pt=pt, y_t=y_t, w=w: e.tensor_tensor(out=y_t[:, half * 512:(half + 1) * 512], in0=pt[:, :], in1=self.gate[:, w, half * 512:(half + 1) * 512], op=ALU.mult),
                          reads=[pr, self.gate_r], writes=[y_r])
                fw.op("gpsimd", lambda e, y_t=y_t, x_t=x_t: e.tensor_tensor(out=y_t[:], in0=y_t[:], in1=x_t[:], op=ALU.add), reads=[y_r, x_r], writes=[y_r])
                if not last:
                    fw.dma(self.xres[tt * 128:(tt + 1) * 128, :], y_t[:], reads=[y_r], writes=[self.xres_r[tt]], sem_res=y_r)
                    if self.dbg and i == self.n_layers - 1:
                        if tt < 2:
                            fw.dma(self.dbg_ctx[tt * 128:(tt + 1) * 128, :], y_t[:], reads=[y_r], writes=[self.out_r[tt]], sem_res=y_r)
                        else:
                            fw.dma(self.out[(tt - 2) * 128:(tt - 1) * 128, :], y_t[:], reads=[y_r], writes=[self.out_r[tt]], sem_res=y_r)
                else:
                    s_t, s_r = ss[n % 2]
                    fw.op("scalar", lambda e, y_t=y_t, s_t=s_t: e.activation(out=sq[:], in_=y_t[:], func=AF.Square, accum_out=s_t[:]), reads=[y_r], writes=[sq_r, s_r])
                    fw.op("scalar", lambda e, s_t=s_t: e.activation(out=s_t[:], in_=s_t[:], func=AF.Sqrt, bias=self.epsn[:], scale=1.0 / D), reads=[s_r, self.epsn_r], writes=[s_r])
                    fw.op("vector", lambda e, s_t=s_t: e.reciprocal(out=s_t[:], in_=s_t[:]), reads=[s_r], writes=[s_r])
                    fw.op("vector", lambda e, y_t=y_t, s_t=s_t: e.scalar_tensor_tensor(out=y_t[:], in0=y_t[:], scalar=s_t[:, 0:1], in1=fg[:], op0=ALU.mult, op1=ALU.mult), reads=[y_r, s_r, fg_r], writes=[y_r])
                    fw.dma(self.out[(tt - 2) * 128:(tt - 1) * 128, :], y_t[:], reads=[y_r], writes=[self.out_r[tt]], sem_res=y_r)
                n += 1

    def layer(self, i):
        fw = self.fw
        kind = i % 3
        j = i // 3
        self.phase_mod(i)
        with ExitStack() as ts0:
            pre = None
            if kind == 1:
                pre = self.fnet_prealloc(ts0)
            elif kind == 2:
                pre = self.rwkv_prealloc(ts0)
            with ExitStack() as ts:
                hT, _ = fw.sb("hT", [128, 8, NT], BF16, ts)
                hT_r = [Res("hT%d" % t) for t in range(NTI)]
                with ExitStack() as ts2:
                    self.phase_norm(i, hT, hT_r, ts2)
                    fw.barrier()
                with ExitStack() as ts2:
                    if kind == 0:
                        self.mixer_attn(i, j, hT, hT_r, ts2)
                    elif kind == 1:
                        self.fnet_part1(i, j, hT, hT_r, pre, ts2)
                    else:
                        self.rwkv_part1(i, j, hT, hT_r, pre, ts2)
                    fw.barrier()
            if kind == 1:
                with ExitStack() as ts2:
                    self.fnet_part2(i, j, pre, ts2)
                    fw.barrier()
            elif kind == 2:
                self.rwkv_part2(i, j, pre, ts0)
        with ExitStack() as ts:
            w_out = {0: self.I["da_w_out"], 1: self.I["fn_w_out"], 2: self.I["rw_w_out"]}[kind][j]
            self.phase_out(i, w_out, ts)
            fw.barrier()

    def mixer_attn(self, i, j, hT, hT_r, ts):
        fw = self.fw
        last = (i == DEPTH - 1)
        lambda_init = 0.8 - 0.6 * math.exp(-0.3 * i)
        Win = self.I["da_w_in"][j]
        rC, rC_r = fw.sb("ropeC", [128, NT], F32, ts)
        rS, rS_r = fw.sb("ropeS", [128, NT], F32, ts)
        fw.dma(rC[:], self.Cn["ropeC"][:, :], writes=[rC_r], sem_res=rC_r)
        fw.dma(rS[:], self.Cn["ropeS"][:, :], writes=[rS_r], sem_res=rS_r)
        nlam, nlam_r = fw.sb("nlam", [128, 1], F32, ts)
        lq, lq_r = fw.sb("lq", [128, 128], F32, ts)
        lk, lk_r = fw.sb("lk", [128, 128], F32, ts)
        l2, l2_r = fw.sb("l2", [128, 2], F32, ts)
        fw.dma(lq[:], self.I["da_lam_q"][j, :].partition_broadcast(128), writes=[lq_r], sem_res=lq_r)
        fw.dma(lk[:], self.I["da_lam_k"][j, :].partition_broadcast(128), writes=[lk_r], sem_res=lk_r)
        fw.op("vector", lambda e: e.tensor_tensor(out=lq[:], in0=lq[:], in1=lk[:], op=ALU.mult), reads=[lq_r, lk_r], writes=[lq_r])
        fw.op("vector", lambda e: e.tensor_reduce(out=l2[:], in_=lq[:].rearrange("p (z d) -> p z d", z=2), axis=AX.X, op=ALU.add), reads=[lq_r], writes=[l2_r])
        fw.op("scalar", lambda e: e.activation(out=l2[:], in_=l2[:], func=AF.Exp), reads=[l2_r], writes=[l2_r])
        fw.op("vector", lambda e: e.tensor_tensor(out=nlam[:], in0=l2[:, 1:2], in1=l2[:, 0:1], op=ALU.subtract), reads=[l2_r], writes=[nlam_r])
        fw.op("vector", lambda e: e.tensor_scalar(out=nlam[:], in0=nlam[:], scalar1=-lambda_init, scalar2=None, op0=ALU.add), reads=[nlam_r], writes=[nlam_r])
        sg, sg_r = fw.sb("sg", [128, 1], F32, ts)
        self.load_fm(sg[:], sg_r, self.I["da_subln_gain"][j:j + 1, :], 1, ts)
        fw.op("vector", lambda e: e.tensor_scalar(out=sg[:], in0=sg[:], scalar1=1.0 - lambda_init, scalar2=None, op0=ALU.mult), reads=[sg_r], writes=[sg_r])

        wst = [fw.sb("aw_st%d" % k, [128, 8, 128], F32, ts) for k in range(2)]
        WS = [{nm: fw.sb("%s_%d" % (nm, k), [128, 8, 128], BF16, ts) for nm in ("wq", "wq2", "wk", "wk2", "wv", "wz")} for k in range(2)]
        qT, qT_r = fw.sb("qT", [128, NT], BF16, ts)
        kT, kT_r = None, None
        kTz = [fw.sb("kTz%d" % z, [128, NT], BF16, ts) for z in range(2)]
        for z in range(2):
            fw.op("gpsimd", lambda e, z=z: e.memset(kTz[z][0][:], 0.0), writes=[kTz[z][1]])
        zs, zs_r = fw.sb("zs", [128, NT], BF16, ts)
        vt, vt_r = fw.sb("vt", [128, NTI, 128], BF16, ts)
        t1 = [fw.sb("rp1_%d" % k, [128, 512], F32, ts) for k in range(1)] * 2
        t2 = [fw.sb("rp2_%d" % k, [128, 512], F32, ts) for k in range(1)] * 2
        Eb = [fw.sb("E%d" % k, [128, 512], BF16, ts) for k in range(4)]
        r1, r1_r = fw.sb("ep_r1", [128, 512], F32, ts)
        a1, a1_r = fw.sb("ep_a1", [128, 512], F32, ts)
        r2, r2_r = r1, r1_r
        a2, a2_r = fw.sb("ep_a2", [128, 512], F32, ts)
        sqb, sqb_r = fw.sb("ep_sq", [128, 512], BF16, ts)
        rs, rs_r = r1, r1_r
        ogt = [fw.sb("ep_og%d" % k, [128, 512], BF16, ts) for k in range(2)]

        def rot_cast(dst, dst2, dst_r, dst2_r, s_t, s_r, scale):
            fw.op("gpsimd", lambda e: e.tensor_scalar(out=dst[:], in0=s_t[:], scalar1=scale, scalar2=None, op0=ALU.mult), reads=[s_r], writes=[dst_r])
            sv = s_t[:].rearrange("p k (b h j) -> p k b h j", b=4, h=2)
            dv = dst2[:].rearrange("p k (b h j) -> p k b h j", b=4, h=2)
            for kc in range(0, 8, 4):
                fw.op("gpsimd", lambda e, kc=kc: e.tensor_scalar(out=dv[:, kc:kc + 4, :, 0, :], in0=sv[:, kc:kc + 4, :, 1, :], scalar1=-scale, scalar2=None, op0=ALU.mult), reads=[s_r], writes=[dst2_r])
                fw.op("gpsimd", lambda e, kc=kc: e.tensor_scalar(out=dv[:, kc:kc + 4, :, 1, :], in0=sv[:, kc:kc + 4, :, 0, :], scalar1=scale, scalar2=None, op0=ALU.mult), reads=[s_r], writes=[dst2_r])

        nst = [0]

        def load_weights(hd):
            Wd = WS[hd % 2]
            for which in range(4):
                s_t, s_r = wst[nst[0] % 2]
                nst[0] += 1
                col0 = which * D + hd * 128
                fw.dma(s_t[:], Win[:, col0:col0 + 128].rearrange("(k p) c -> p k c", p=128), writes=[s_r], sem_res=s_r)
                if which == 0:
                    rot_cast(Wd["wq"][0], Wd["wq2"][0], Wd["wq"][1], Wd["wq2"][1], s_t, s_r, 0.125)
                elif which == 1:
                    rot_cast(Wd["wk"][0], Wd["wk2"][0], Wd["wk"][1], Wd["wk2"][1], s_t, s_r, 1.0)
                elif which == 2:
                    fw.op("gpsimd", lambda e, s_t=s_t, d=Wd["wv"][0]: e.tensor_copy(out=d[:], in_=s_t[:]), reads=[s_r], writes=[Wd["wv"][1]])
                else:
                    fw.op("gpsimd", lambda e, s_t=s_t, d=Wd["wz"][0]: e.tensor_copy(out=d[:], in_=s_t[:]), reads=[s_r], writes=[Wd["wz"][1]])

        ncnt = 0
        load_weights(0)
        for hd in range(8):
            Wd = WS[hd % 2]
            (wq, wq_r), (wq2, wq2_r), (wk, wk_r), (wk2, wk2_r), (wv, wv_r), (wz, wz_r) = [Wd[nm] for nm in ("wq", "wq2", "wk", "wk2", "wv", "wz")]
            for ci in range(9):
                t0, tn = self.chunk(ci)
                tts = list(range(t0 // 128, (t0 + tn) // 128))
                hrs = [hT_r[t] for t in tts]
                banks = [self.pb[(ci * 6 + b) % 8] for b in range(6)]
                for b, (wt_, wr_) in enumerate([(wq, wq_r), (wq2, wq2_r), (wk, wk_r), (wk2, wk2_r), (wz, wz_r)]):
                    pt, pr = banks[b]
                    for kc in range(8):
                        fw.op("tensor", lambda e, kc=kc, pt=pt, wt_=wt_, t0=t0, tn=tn: e.matmul(pt[:, 0:tn], lhsT=wt_[:, kc, :], rhs=hT[:, kc, t0:t0 + tn], start=(kc == 0), stop=(kc == 7)),
                              reads=[wr_] + hrs, writes=[pr], signal=(kc == 7))
                pt, pr = banks[5]
                for tl, tt in enumerate(tts):
                    for kc in range(8):
                        fw.op("tensor", lambda e, kc=kc, pt=pt, tl=tl, tt=tt, wv=wv: e.matmul(pt[:, tl * 128:(tl + 1) * 128], lhsT=hT[:, kc, tt * 128:(tt + 1) * 128], rhs=wv[:, kc, :], start=(kc == 0), stop=(kc == 7)),
                              reads=[wv_r, hT_r[tt]], writes=[pr], signal=(kc == 7 and tl == len(tts) - 1))
                for b0, dst, dst_r in ((0, qT, qT_r), (2, kT, kT_r)):
                    p1, p1r = banks[b0]
                    p2, p2r = banks[b0 + 1]
                    ta, ta_r = t1[ncnt % 2]
                    tb, tb_r = t2[ncnt % 2]
                    ncnt += 1
                    fw.op("vector", lambda e, p1=p1, ta=ta, t0=t0, tn=tn: e.tensor_tensor(out=ta[:, 0:tn], in0=p1[:, 0:tn], in1=rC[:, t0:t0 + tn], op=ALU.mult), reads=[p1r, rC_r], writes=[ta_r])
                    fw.op("vector", lambda e, p2=p2, tb=tb, t0=t0, tn=tn: e.tensor_tensor(out=tb[:, 0:tn], in0=p2[:, 0:tn], in1=rS[:, t0:t0 + tn], op=ALU.mult), reads=[p2r, rS_r], writes=[tb_r])
                    if b0 == 0:
                        fw.op("gpsimd", lambda e, ta=ta, tb=tb, dst=dst, t0=t0, tn=tn: e.tensor_tensor(out=dst[:, t0:t0 + tn], in0=ta[:, 0:tn], in1=tb[:, 0:tn], op=ALU.add), reads=[ta_r, tb_r], writes=[dst_r])
                    else:
                        for z in range(2):
                            zr = slice(z * 64, (z + 1) * 64)
                            fw.op("gpsimd", lambda e, ta=ta, tb=tb, z=z, zr=zr, t0=t0, tn=tn: e.tensor_tensor(out=kTz[z][0][zr, t0:t0 + tn], in0=ta[zr, 0:tn], in1=tb[zr, 0:tn], op=ALU.add), reads=[ta_r, tb_r], writes=[kTz[z][1]])
                p4, p4r = banks[4]
                fw.op("scalar", lambda e, p4=p4, t0=t0, tn=tn: e.activation(out=zs[:, t0:t0 + tn], in_=p4[:, 0:tn], func=AF.Silu), reads=[p4r], writes=[zs_r])
                p5, p5r = banks[5]
                fw.op("vector", lambda e, p5=p5, t0=t0, tn=tn: e.tensor_copy(out=vt[:, t0 // 128:(t0 + tn) // 128, :].rearrange("p t e -> p (t e)"), in_=p5[:, 0:tn]), reads=[p5r], writes=[vt_r])
            if hd + 1 < 8:
                load_weights(hd + 1)
            groups = ([] if last else [(0, list(range(2)))]) + [(ci, list(range(NTI))) for ci in range(1, 9)]
            items = []
            for gi, (ci, kts) in enumerate(groups):
                for z in range(2):
                    for idx, kt in enumerate(kts):
                        items.append((gi, ci, z, idx, kt, len(kts)))
            LA = 2
            Ebuf = {}

            def emit_front(n):
                gi, ci, z, idx, kt, nk = items[n]
                q0, qn = self.chunk(ci)
                sp, spr = self.pb[n % 3]
                E_t, E_r = Eb[n % len(Eb)]
                Ebuf[n] = (E_t, E_r)
                fw.op("tensor", lambda e, sp=sp, kt=kt, z=z, q0=q0, qn=qn: e.matmul(sp[:, 0:qn], lhsT=kTz[z][0][:, kt * 128:(kt + 1) * 128], rhs=qT[:, q0:q0 + qn], start=True, stop=True),
                      reads=[kTz[z][1], qT_r], writes=[spr])
                fw.op("scalar", lambda e, sp=sp, E_t=E_t, qn=qn: e.activation(out=E_t[:, 0:qn], in_=sp[:, 0:qn], func=AF.Exp), reads=[spr], writes=[E_r])

            def emit_back(n):
                gi, ci, z, idx, kt, nk = items[n]
                q0, qn = self.chunk(ci)
                E_t, E_r = Ebuf.pop(n)
                po, por = self.pb[3 + z * 2]
                psm, psr = self.pb[4 + z * 2]
                fw.op("tensor", lambda e, po=po, E_t=E_t, kt=kt, qn=qn, idx=idx, nk=nk: e.matmul(po[:, 0:qn], lhsT=vt[:, kt, :], rhs=E_t[:, 0:qn], start=(idx == 0), stop=(idx == nk - 1)),
                      reads=[vt_r, E_r], writes=[por], signal=False)
                fw.op("tensor", lambda e, psm=psm, E_t=E_t, qn=qn, idx=idx, nk=nk: e.matmul(psm[:, 0:qn], lhsT=self.ones_b[:], rhs=E_t[:, 0:qn], start=(idx == 0), stop=(idx == nk - 1)),
                      reads=[self.ones_b_r, E_r], writes=[psr])
                if z == 1 and idx == nk - 1:
                    epilogue(ci)

            def epilogue(ci):
                q0, qn = self.chunk(ci)
                po1, por1 = self.pb[3]
                ps1, psr1 = self.pb[4]
                po2, por2 = self.pb[5]
                ps2, psr2 = self.pb[6]
                fw.op("vector", lambda e, qn=qn: e.reciprocal(out=r1[:, 0:qn], in_=ps1[:, 0:qn]), reads=[psr1], writes=[r1_r])
                fw.op("vector", lambda e, qn=qn: e.tensor_tensor(out=a1[:, 0:qn], in0=po1[:, 0:qn], in1=r1[:, 0:qn], op=ALU.mult), reads=[por1, r1_r], writes=[a1_r])
                fw.op("vector", lambda e, qn=qn: e.reciprocal(out=r2[:, 0:qn], in_=ps2[:, 0:qn]), reads=[psr2], writes=[r2_r])
                fw.op("vector", lambda e, qn=qn: e.tensor_tensor(out=a2[:, 0:qn], in0=po2[:, 0:qn], in1=r2[:, 0:qn], op=ALU.mult), reads=[por2, r2_r], writes=[a2_r])
                fw.op("vector", lambda e, qn=qn: e.scalar_tensor_tensor(out=a1[:, 0:qn], in0=a2[:, 0:qn], scalar=nlam[:, 0:1], in1=a1[:, 0:qn], op0=ALU.mult, op1=ALU.add), reads=[a1_r, a2_r, nlam_r], writes=[a1_r])
                fw.op("vector", lambda e, qn=qn: e.tensor_tensor(out=sqb[:, 0:qn], in0=a1[:, 0:qn], in1=a1[:, 0:qn], op=ALU.mult), reads=[a1_r], writes=[sqb_r])
                mp, mpr = self.pb[7]
                fw.op("tensor", lambda e, qn=qn: e.matmul(mp[:, 0:qn], lhsT=self.mean_b[:], rhs=sqb[:, 0:qn], start=True, stop=True), reads=[self.mean_b_r, sqb_r], writes=[mpr])
                fw.op("scalar", lambda e, qn=qn: e.activation(out=rs[:, 0:qn], in_=mp[:, 0:qn], func=AF.Ln, bias=self.epss[:], scale=1.0), reads=[mpr, self.epss_r], writes=[rs_r])
                fw.op("scalar", lambda e, qn=qn: e.activation(out=rs[:, 0:qn], in_=rs[:, 0:qn], func=AF.Exp, scale=-0.5), reads=[rs_r], writes=[rs_r])
                fw.op("vector", lambda e, qn=qn: e.tensor_tensor(out=a1[:, 0:qn], in0=a1[:, 0:qn], in1=rs[:, 0:qn], op=ALU.mult), reads=[a1_r, rs_r], writes=[a1_r])
                og_t, og_r_ = ogt[ci % 2]
                fw.op("vector", lambda e, og_t=og_t, q0=q0, qn=qn: e.scalar_tensor_tensor(out=og_t[:, 0:qn], in0=a1[:, 0:qn], scalar=sg[:, 0:1], in1=zs[:, q0:q0 + qn], op0=ALU.mult, op1=ALU.mult), reads=[a1_r, sg_r, zs_r], writes=[og_r_])
                fw.dma(self.og[hd, :, q0:q0 + qn], og_t[:, 0:qn], reads=[og_r_], writes=[self.og_r[hd][ci]], sem_res=og_r_)

            for n in range(len(items) + LA):
                if n < len(items):
                    emit_front(n)
                if n - LA >= 0:
                    emit_back(n - LA)

    def load_w_bf16(self, dst, dst_r, w2d, ncols, ts, name):
        fw = self.fw
        wst = [fw.sb("%s_st%d" % (name, k), [128, ncols], F32, ts) for k in range(2)]
        for kc in range(8):
            s_t, s_r = wst[kc % 2]
            fw.dma(s_t[:], w2d[kc * 128:(kc + 1) * 128, :], writes=[s_r], sem_res=s_r)
            fw.op("gpsimd", lambda e, kc=kc, s_t=s_t: e.tensor_copy(out=dst[:, kc, :], in_=s_t[:]), reads=[s_r], writes=[dst_r])

    def gate_proj(self, wz, wz_r, hT, hT_r, ts):
        fw = self.fw
        zt = [fw.sb("gz%d" % k, [128, 512], BF16, ts) for k in range(3)]
        n = 0
        for ci in range(9):
            t0, tn = self.chunk(ci)
            hrs = [hT_r[t] for t in range(t0 // 128, (t0 + tn) // 128)]
            for fc in range(8):
                pt, pr = self.pb[n % 4]
                z_t, z_r = zt[n % 3]
                n += 1
                for kc in range(8):
                    fw.op("tensor", lambda e, kc=kc, pt=pt, fc=fc, t0=t0, tn=tn: e.matmul(pt[:, 0:tn], lhsT=wz[:, kc, fc * 128:(fc + 1) * 128], rhs=hT[:, kc, t0:t0 + tn], start=(kc == 0), stop=(kc == 7)),
                          reads=[wz_r] + hrs, writes=[pr], signal=(kc == 7))
                fw.op("scalar", lambda e, pt=pt, z_t=z_t, tn=tn: e.activation(out=z_t[:, 0:tn], in_=pt[:, 0:tn], func=AF.Silu), reads=[pr], writes=[z_r])
                fw.dma(self.zsd[fc, :, t0:t0 + tn], z_t[:, 0:tn], reads=[z_r], writes=[self.zsd_r[fc][ci]], sem_res=z_r)

    def fnet_prealloc(self, ts):
        fw = self.fw
        Utm, _ = fw.sb("Utm", [128, NTI, D], BF16, ts)
        Utm_r = [Res("Utm%d" % t) for t in range(NTI)]
        return Utm, Utm_r

    def fnet_part1(self, i, j, hT, hT_r, pre, ts):
        fw = self.fw
        Utm, Utm_r = pre
        Win = self.I["fn_w_in"][j]
        wu, wu_r = fw.sb("fwu", [128, 8, D], BF16, ts)
        wz, wz_r = fw.sb("fwz", [128, 8, D], BF16, ts)
        self.load_w_bf16(wu, wu_r, Win[:, 0:D], D, ts, "fwu")
        self.load_w_bf16(wz, wz_r, Win[:, D:2 * D], D, ts, "fwz")
        n = 0
        for tt in range(NTI):
            for half in range(2):
                pt, pr = self.pb[4 + n % 4]
                for kc in range(8):
                    fw.op("tensor", lambda e, kc=kc, pt=pt, tt=tt, half=half: e.matmul(pt[:, :], lhsT=hT[:, kc, tt * 128:(tt + 1) * 128], rhs=wu[:, kc, half * 512:(half + 1) * 512], start=(kc == 0), stop=(kc == 7)),
                          reads=[wu_r, hT_r[tt]], writes=[pr], signal=(kc == 7))
                if n % 2 == 0:
                    fw.op("vector", lambda e, pt=pt, tt=tt, half=half: e.tensor_copy(out=Utm[:, tt, half * 512:(half + 1) * 512], in_=pt[:, :]), reads=[pr], writes=[Utm_r[tt]])
                else:
                    fw.op("scalar", lambda e, pt=pt, tt=tt, half=half: e.copy(out=Utm[:, tt, half * 512:(half + 1) * 512], in_=pt[:, :]), reads=[pr], writes=[Utm_r[tt]])
                n += 1
        self.gate_proj(wz, wz_r, hT, hT_r, ts)

    def fnet_part2(self, i, j, pre, ts):
        fw = self.fw
        Utm, Utm_r = pre
        cc_, cc_r = fw.sb("fCc", [128, 128], BF16, ts)
        sc_, sc_r = fw.sb("fSc", [128, 128], BF16, ts)
        fw.dma(cc_[:], self.Cn["dftCc"][:, :], writes=[cc_r], sem_res=cc_r)
        fw.dma(sc_[:], self.Cn["dftnSc"][:, :], writes=[sc_r], sem_res=sc_r)
        wgs, wgs_r = fw.sb("fwg_st", [128, 8, 128], F32, ts)
        wg, wg_r = fw.sb("fwg", [128, 8, 128], BF16, ts)
        fw.dma(wgs[:], self.I["fn_w_group"][j].rearrange("g c e -> c g e"), writes=[wgs_r], sem_res=wgs_r)
        fw.op("gpsimd", lambda e: e.tensor_copy(out=wg[:], in_=wgs[:]), reads=[wgs_r], writes=[wg_r])
        CB, _ = fw.sb("fCB", [128, 32, 512], BF16, ts)
        SB, _ = fw.sb("fSB", [128, 32, 512], BF16, ts)
        CB_r = [fw.res("fCB%d" % k, ts) for k in range(4)]
        SB_r = [fw.res("fSB%d" % k, ts) for k in range(4)]
        Pb = [fw.sb("fPb%d" % k, [128, 512], BF16, ts) for k in range(2)]
        Qb = [fw.sb("fQb%d" % k, [128, 512], BF16, ts) for k in range(2)]
        Fb = [fw.sb("fFb%d" % k, [128, 512], BF16, ts) for k in range(2)]
        zt = [fw.sb("fzt%d" % k, [128, 512], BF16, ts) for k in range(2)]
        ot = [fw.sb("fot%d" % k, [128, 512], BF16, ts) for k in range(2)]
        last = (i == DEPTH - 1)
        items = []

        def bufs(n):
            return (self.pb[(n % 2) * 2], self.pb[(n % 2) * 2 + 1], self.pb[4 + n % 2], self.pb[6 + n % 2],
                    Pb[n % 2], Qb[n % 2], Fb[n % 2], zt[n % 2], ot[n % 2])

        def front(n):
            ci, g, t0, tn, nlt, lt0 = items[n]
            (pp, ppr), (qp, qpr), _, _, (P_t, P_r), (Q_t, Q_r), _, (z_t, z_r), _ = bufs(n)
            fw.dma(z_t[:, 0:tn], self.zsd[g, :, t0:t0 + tn], reads=[self.zsd_r[g][ci]], writes=[z_r], sem_res=z_r)
            for (acc, accr, buf, bufr) in ((pp, ppr, CB, CB_r), (qp, qpr, SB, SB_r)):
                for lt in range(nlt):
                    fw.op("tensor", lambda e, acc=acc, buf=buf, lt=lt: e.matmul(acc[:, 0:tn], lhsT=Utm[:, lt0 + lt, g * 128:(g + 1) * 128], rhs=buf[:, lt, 0:tn], start=(lt == 0), stop=(lt == nlt - 1)),
                          reads=[Utm_r[lt0 + lt], bufr[lt // 8]], writes=[accr], signal=(lt == nlt - 1 or lt % 8 == 7))
            fw.op("scalar", lambda e: e.copy(out=P_t[:, 0:tn], in_=pp[:, 0:tn]), reads=[ppr], writes=[P_r])
            fw.op("vector", lambda e: e.tensor_copy(out=Q_t[:, 0:tn], in_=qp[:, 0:tn]), reads=[qpr], writes=[Q_r])

        def back(n):
            ci, g, t0, tn, nlt, lt0 = items[n]
            _, _, (fp, fpr), (yp, ypr), (P_t, P_r), (Q_t, Q_r), (F_t, F_r), (z_t, z_r), (o_t, o_r) = bufs(n)
            fw.op("tensor", lambda e: e.matmul(fp[:, 0:tn], lhsT=cc_[:], rhs=P_t[:, 0:tn], start=True, stop=False), reads=[cc_r, P_r], writes=[fpr], signal=False)
            fw.op("tensor", lambda e: e.matmul(fp[:, 0:tn], lhsT=sc_[:], rhs=Q_t[:, 0:tn], start=False, stop=True), reads=[sc_r, Q_r], writes=[fpr])
            fw.op("vector", lambda e: e.tensor_copy(out=F_t[:, 0:tn], in_=fp[:, 0:tn]), reads=[fpr], writes=[F_r])
            fw.op("tensor", lambda e: e.matmul(yp[:, 0:tn], lhsT=wg[:, g, :], rhs=F_t[:, 0:tn], start=True, stop=True), reads=[wg_r, F_r], writes=[ypr])
            fw.op("vector", lambda e: e.tensor_tensor(out=o_t[:, 0:tn], in0=yp[:, 0:tn], in1=z_t[:, 0:tn], op=ALU.mult), reads=[ypr, z_r], writes=[o_r])
            fw.dma(self.og[g, :, t0:t0 + tn], o_t[:, 0:tn], reads=[o_r], writes=[self.og_r[g][ci]], sem_res=o_r)

        for ci in (range(1, 9) if last else range(9)):
            t0, tn = self.chunk(ci)
            if ci == 0:
                nlt, lt0 = 2, 0
                for k in range(2):
                    fw.dma(CB[:, k, 0:256], self.Cn["dftCX"][k * 128:(k + 1) * 128, :], writes=[CB_r[0]], sem_res=CB_r[0])
                    fw.dma(SB[:, k, 0:256], self.Cn["dftSX"][k * 128:(k + 1) * 128, :], writes=[SB_r[0]], sem_res=SB_r[0])
            else:
                nlt, lt0 = 32, 2
                c0 = (ci - 1) * 512
                for k in range(4):
                    fw.dma(CB[:, k * 8:(k + 1) * 8, :], self.Cn["dftCL"][k * 1024:(k + 1) * 1024, c0:c0 + 512].rearrange("(t p) c -> p t c", p=128), writes=[CB_r[k]], sem_res=CB_r[k])
                    fw.dma(SB[:, k * 8:(k + 1) * 8, :], self.Cn["dftSL"][k * 1024:(k + 1) * 1024, c0:c0 + 512].rearrange("(t p) c -> p t c", p=128), writes=[SB_r[k]], sem_res=SB_r[k])
            for g in range(8):
                items.append((ci, g, t0, tn, nlt, lt0))
                front(len(items) - 1)
                if len(items) >= 2:
                    back(len(items) - 2)
        back(len(items) - 1)

    def rwkv_prealloc(self, ts):
        fw = self.fw
        nc = self.nc
        R = {}
        R["Dall"] = [fw.sb("Dall%d" % z, [128, 8, 68], F32, ts) for z in range(2)]
        R["stack"] = ts.enter_context(ExitStack())
        R["twd"] = fw.sb("twd", [128, NT], F32, R["stack"])
        R["adT"] = fw.sb("adT", [128, NT], F32, R["stack"])
        if not hasattr(self, "sTd"):
            dk = "ExternalOutput" if self.dbg else "Internal"
            self.sTd = nc.dram_tensor("sTd", [24, 128, NT], F32, kind=dk).ap()
            self.FMd = [nc.dram_tensor("FMd%d" % z, [NTI, 128, 8, 4, 128], BF16, kind="Internal").ap() for z in range(2)]
            self.TMd = [nc.dram_tensor("TMd%d" % z, [NTI, 128, 2, 8, 128], BF16, kind="Internal").ap() for z in range(2)]
            self.Vd = nc.dram_tensor("Vd", [NTI, 128, 8, 128], BF16, kind="Internal").ap()
            self.bonusd = nc.dram_tensor("bonusd", [8, 128, NT], F32, kind=dk).ap()
            self.yd = [nc.dram_tensor("yd%d" % z, [NT, D], F32, kind=dk).ap() for z in range(2)]
        return R

    def rwkv_part1(self, i, j, hT, hT_r, R, ts):
        fw = self.fw
        W = self.I["rw_w_in"][j]
        twd, twd_r = R["twd"]
        adT, adT_r = R["adT"]
        wz, wz_r = fw.sb("rwz", [128, 8, D], BF16, ts)
        self.load_w_bf16(wz, wz_r, W[:, 3328:4352], D, ts, "rwz")
        self.gate_proj(wz, wz_r, hT, hT_r, ts)
        mu_fm, mu_r = fw.sb("mu_fm", [128, 26], F32, ts)
        self.load_fm(mu_fm[:], mu_r, self.I["rw_mu"][j, :].rearrange("(n p) -> n p", p=128), 26, ts)
        ca, ca_r = fw.sb("mix_a", [128, 26], F32, ts)
        cb, cb_r = fw.sb("mix_b", [128, 26], F32, ts)
        fw.op("vector", lambda e: e.tensor_scalar(out=ca[:], in0=mu_fm[:], scalar1=-1.0, scalar2=1.0, op0=ALU.mult, op1=ALU.add), reads=[mu_r], writes=[ca_r])
        fw.op("vector", lambda e: e.tensor_scalar(out=cb[:], in0=mu_fm[:], scalar1=0.5, scalar2=None, op0=ALU.mult), reads=[mu_r], writes=[cb_r])
        NP = NT + 3
        sraw = [fw.sb("sraw%d" % k, [128, NP], F32, ts) for k in range(2)]
        for k in range(2):
            fw.op("gpsimd", lambda e, k=k: e.memset(sraw[k][0][:], 0.0), writes=[sraw[k][1]])
        wst = [fw.sb("rw_st%d" % k, [128, 8, 128], F32, ts) for k in range(2)]
        wbs = [fw.sb("rw_wb%d" % k, [128, 8, 128], BF16, ts) for k in range(2)]
        PW = 512
        tmpb = [fw.sb("mixt%d" % k, [128, PW], F32, ts) for k in range(2)]
        smxb = [fw.sb("mixs%d" % k, [128, PW], F32, ts) for k in range(2)]
        npc = 0
        nev = 0
        def f1_weights(fc):
            s_t, s_r = wst[fc % 2]
            w_t, w_r = wbs[fc % 2]
            fw.dma(s_t[:], W[:, fc * 128:(fc + 1) * 128].rearrange("(k p) c -> p k c", p=128), writes=[s_r], sem_res=s_r)
            fw.op("gpsimd", lambda e: e.tensor_copy(out=w_t[:], in_=s_t[:]), reads=[s_r], writes=[w_r])

        f1_weights(0)
        for fc in range(26):
            w_t, w_r = wbs[fc % 2]
            sr_t, sr_r = sraw[fc % 2]
            if fc + 1 < 26:
                f1_weights(fc + 1)
            for ci in range(9):
                t0, tn = self.chunk(ci)
                off = t0 + (1 if ci == 0 else 2)
                hrs = [hT_r[t] for t in range(t0 // 128, (t0 + tn) // 128)]
                pt, pr = self.pb[4 + nev % 4]
                for kc in range(8):
                    fw.op("tensor", lambda e, kc=kc, pt=pt, w_t=w_t, t0=t0, tn=tn: e.matmul(pt[:, 0:tn], lhsT=w_t[:, kc, :], rhs=hT[:, kc, t0:t0 + tn], start=(kc == 0), stop=(kc == 7)),
                          reads=[w_r] + hrs, writes=[pr], signal=(kc == 7))
                if nev % 2 == 0:
                    fw.op("scalar", lambda e, pt=pt, sr_t=sr_t, off=off, tn=tn: e.copy(out=sr_t[:, off:off + tn], in_=pt[:, 0:tn]), reads=[pr], writes=[sr_r])
                else:
                    fw.op("vector", lambda e, pt=pt, sr_t=sr_t, off=off, tn=tn: e.tensor_copy(out=sr_t[:, off:off + tn], in_=pt[:, 0:tn]), reads=[pr], writes=[sr_r])
                nev += 1
            for (c0, cn, tok0) in [(1, 256, 0)] + [(258 + q * 512, 512, 256 + q * 512) for q in range(8)]:
                tm_t, tm_r = tmpb[npc % 2]
                sm_t, sm_r = smxb[npc % 2]
                npc += 1
                fw.op("gpsimd", lambda e, tm_t=tm_t, sr_t=sr_t, c0=c0, cn=cn: e.tensor_tensor(out=tm_t[:, 0:cn], in0=sr_t[:, c0 - 1:c0 - 1 + cn], in1=sr_t[:, c0 + 1:c0 + 1 + cn], op=ALU.add), reads=[sr_r], writes=[tm_r])
                fw.op("vector", lambda e, tm_t=tm_t, fc=fc, cn=cn: e.tensor_scalar(out=tm_t[:, 0:cn], in0=tm_t[:, 0:cn], scalar1=cb[:, fc:fc + 1], scalar2=None, op0=ALU.mult), reads=[tm_r, cb_r], writes=[tm_r])
                if fc < 24:
                    fw.op("vector", lambda e, tm_t=tm_t, sm_t=sm_t, sr_t=sr_t, fc=fc, c0=c0, cn=cn: e.scalar_tensor_tensor(out=sm_t[:, 0:cn], in0=sr_t[:, c0:c0 + cn], scalar=ca[:, fc:fc + 1], in1=tm_t[:, 0:cn], op0=ALU.mult, op1=ALU.add),
                          reads=[sr_r, tm_r, ca_r], writes=[sm_r])
                    fw.dma(self.sTd[fc, :, tok0:tok0 + cn], sm_t[:, 0:cn], reads=[sm_r], sem_res=sm_r)
                else:
                    dst, dst_r = (twd, twd_r) if fc == 24 else (adT, adT_r)
                    fw.op("vector", lambda e, tm_t=tm_t, dst=dst, sr_t=sr_t, fc=fc, c0=c0, cn=cn, tok0=tok0: e.scalar_tensor_tensor(out=dst[:, tok0:tok0 + cn], in0=sr_t[:, c0:c0 + cn], scalar=ca[:, fc:fc + 1], in1=tm_t[:, 0:cn], op0=ALU.mult, op1=ALU.add),
                          reads=[sr_r, tm_r, ca_r], writes=[dst_r])
                    if fc == 24:
                        fw.op("scalar", lambda e, tok0=tok0, cn=cn: e.activation(out=twd[:, tok0:tok0 + cn], in_=twd[:, tok0:tok0 + cn], func=AF.Tanh), reads=[twd_r], writes=[twd_r])

    def rwkv_part2(self, i, j, R, ts0):
        fw = self.fw
        stop = getattr(self, "stop_at", 99)
        fw.barrier()
        if stop < 2:
            return
        with ExitStack() as ts:
            self.rwkv_F2(i, j, R, ts)
            fw.barrier()
        R["stack"].close()
        if stop < 3:
            return
        with ExitStack() as ts:
            self.rwkv_S(i, j, R, ts)
            fw.barrier()
        if stop < 4:
            return
        with ExitStack() as ts:
            self.rwkv_O(i, j, R, ts)
            fw.barrier()

    def rwkv_F2(self, i, j, R, ts):
        fw = self.fw
        I = self.I
        twd, twd_r = R["twd"]
        adT, adT_r = R["adT"]
        def fm(name, src, n):
            t, r = fw.sb(name, [128, n], F32, ts)
            self.load_fm(t[:], r, src, n, ts)
            return t, r
        kk_p, kk_pr = fm("p_kk", I["rw_k_k"][j, :].rearrange("(n p) -> n p", p=128), 8)
        ka_p, ka_pr = fm("p_ka", I["rw_k_a"][j, :].rearrange("(n p) -> n p", p=128), 8)
        rk_p, rk_pr = fm("p_rk", I["rw_r_k"][j, :].rearrange("(n p) -> n p", p=128), 8)
        w0_p, w0_pr = fm("p_w0", I["rw_w0"][j].rearrange("z (n p) -> (z n) p", p=128), 16)
        a0_p, a0_pr = fm("p_a0", I["rw_a0"][j].rearrange("z (n p) -> (z n) p", p=128), 16)
        oka, oka_r = fw.sb("p_oka", [128, 8], F32, ts)
        fw.op("vector", lambda e: e.tensor_scalar(out=oka[:], in0=ka_p[:], scalar1=-1.0, scalar2=1.0, op0=ALU.mult, op1=ALU.add), reads=[ka_pr], writes=[oka_r])
        wup, wup_r = fw.sb("wup", [128, D], F32, ts)
        aup, aup_r = fw.sb("aup", [128, D], F32, ts)
        fw.dma(wup[:], I["rw_w_up"][j].rearrange("z r f -> (z r) f"), writes=[wup_r], sem_res=wup_r)
        fw.dma(aup[:], I["rw_a_up"][j].rearrange("z r f -> (z r) f"), writes=[aup_r], sem_res=aup_r)
        bones, bones_r = fw.sb("bones", [128, 128], F32, ts)
        fw.op("vector", lambda e: e.memset(bones[:], 0.0), writes=[bones_r])
        fw.op("vector", lambda e: e.memset(bones[0:64, 0:64], 1.0), writes=[bones_r])
        fw.op("vector", lambda e: e.memset(bones[64:128, 64:128], 1.0), writes=[bones_r])
        eps12, eps12_r = fw.sb("eps12", [128, 1], F32, ts)
        fw.op("vector", lambda e: e.memset(eps12[:], 1e-12), writes=[eps12_r])
        cmask, cmask_r = fw.sb("cmask", [128, 512], F32, ts)
        fw.op("vector", lambda e: e.memset(cmask[:], 1.0), writes=[cmask_r])
        fw.op("vector", lambda e: e.memset(cmask[:].rearrange("p (c t) -> p c t", t=64)[:, :, 0:1], 0.0), writes=[cmask_r])

        cnt = [0]

        def T(name, dt=F32, n=2, w=512):
            return [fw.sb("%s_%d" % (name, k), [128, w], dt, ts) for k in range(n)]
        rTb, kTb, vTb = T("f_r"), T("f_k"), T("f_v")
        lwb = [T("f_lw0"), T("f_lw1")]
        azb = [T("f_az0"), T("f_az1")]
        kdb = [T("f_kd0"), T("f_kd1")]
        bzb = [T("f_b0"), T("f_b1")]
        kkrb, sqb_, kkb, tb1, tb3 = T("f_kkr"), T("f_sq"), T("f_kk"), T("f_t1"), T("f_t3", n=4)
        rnb = sqb_
        tb2 = sqb_
        cwb, e1b, e2b, e3b, e4b = T("f_cw", n=4), T("f_e1", n=4), T("f_e2", n=4), T("f_e3", n=4), T("f_e4", n=2) * 2
        fmo = [T("f_o%d" % k, dt=BF16, n=4) for k in range(4)]
        hato = [T("f_h%d" % k, n=4) for k in range(2)]
        trs = T("f_trs", dt=BF16, n=4)
        bonb = T("f_bon")
        NEG = -math.exp(-0.5)
        nb = 0
        ntr = [0]

        def transpose_store(fx, k, cnt, src_t, src_r, dst_ap_fn, ntile):
            tp, tpr = self.pb[k * 4 + 2 + cnt[0] % 2]
            o_t, o_r = trs[k * 2 + cnt[0] % 2]
            cnt[0] += 1
            for tl in range(ntile):
                fx.op("tensor", lambda e, tl=tl, tp=tp: e.transpose(tp[:, tl * 128:(tl + 1) * 128], src_t[:, tl * 128:(tl + 1) * 128], self.ident[:]),
                      reads=[src_r, self.ident_r], writes=[tpr], signal=(tl == ntile - 1))
            fx.op("scalar", lambda e, tp=tp, o_t=o_t, ntile=ntile: e.copy(out=o_t[:, 0:ntile * 128], in_=tp[:, 0:ntile * 128]), reads=[tpr], writes=[o_r])
            for tl in range(ntile):
                fx.dma(dst_ap_fn(tl), o_t[:, tl * 128:(tl + 1) * 128], reads=[o_r], sem_res=o_r)

        def emit_block(fx, hp, ci, k):
            cnt = [0]
            t0, N = self.chunk(ci)
            tile0 = t0 // 128
            ntile = N // 128
            nch = N // 64
            (rT, rT_r), (kT, kT_r), (vT, vT_r) = rTb[k], kTb[k], vTb[k]
            fx.dma(rT[:, 0:N], self.sTd[hp, :, t0:t0 + N], writes=[rT_r], sem_res=rT_r)
            fx.dma(kT[:, 0:N], self.sTd[8 + hp, :, t0:t0 + N], writes=[kT_r], sem_res=kT_r)
            fx.dma(vT[:, 0:N], self.sTd[16 + hp, :, t0:t0 + N], writes=[vT_r], sem_res=vT_r)
            transpose_store(fx, k, cnt, vT, vT_r, lambda tl, tile0=tile0, hp=hp: self.Vd[tile0 + tl, :, hp, :], ntile)
            for z in range(2):
                rows = slice(z * 64, (z + 1) * 64)
                pw, pwr = self.pb[k * 4]
                pa, par = self.pb[k * 4 + 1]
                lw, lw_r = lwb[z][k]
                az, az_r = azb[z][k]
                fx.op("tensor", lambda e, pw=pw, rows=rows, hp=hp, t0=t0, N=N: e.matmul(pw[:, 0:N], lhsT=wup[rows, hp * 128:(hp + 1) * 128], rhs=twd[rows, t0:t0 + N], start=True, stop=True), reads=[wup_r, twd_r], writes=[pwr])
                fx.op("tensor", lambda e, pa=pa, rows=rows, hp=hp, t0=t0, N=N: e.matmul(pa[:, 0:N], lhsT=aup[rows, hp * 128:(hp + 1) * 128], rhs=adT[rows, t0:t0 + N], start=True, stop=True), reads=[aup_r, adT_r], writes=[par])
                fx.op("scalar", lambda e, pw=pw, lw=lw, z=z, hp=hp, N=N: e.activation(out=lw[:, 0:N], in_=pw[:, 0:N], func=AF.Sigmoid, bias=w0_p[:, z * 8 + hp:z * 8 + hp + 1], scale=1.0), reads=[pwr, w0_pr], writes=[lw_r])
                fx.op("vector", lambda e, lw=lw, N=N: e.tensor_scalar(out=lw[:, 0:N], in0=lw[:, 0:N], scalar1=NEG, scalar2=None, op0=ALU.mult), reads=[lw_r], writes=[lw_r])
                fx.op("scalar", lambda e, pa=pa, az=az, z=z, hp=hp, N=N: e.activation(out=az[:, 0:N], in_=pa[:, 0:N], func=AF.Sigmoid, bias=a0_p[:, z * 8 + hp:z * 8 + hp + 1], scale=1.0), reads=[par, a0_pr], writes=[az_r])
            (kkr, kkr_r), (sq, sq_r), (rn, rn_r), (kk, kk_r) = kkrb[k], sqb_[k], rnb[k], kkb[k]
            fx.op("vector", lambda e, kkr=kkr, kT=kT, hp=hp, N=N: e.tensor_scalar(out=kkr[:, 0:N], in0=kT[:, 0:N], scalar1=kk_p[:, hp:hp + 1], scalar2=None, op0=ALU.mult), reads=[kT_r, kk_pr], writes=[kkr_r])
            fx.op("gpsimd", lambda e, kkr=kkr, sq=sq, N=N: e.tensor_tensor(out=sq[:, 0:N], in0=kkr[:, 0:N], in1=kkr[:, 0:N], op=ALU.mult), reads=[kkr_r], writes=[sq_r])
            pss, pssr = self.pb[k * 4 + 2 + cnt[0] % 2]
            cnt[0] += 1
            fx.op("tensor", lambda e, pss=pss, sq=sq, N=N: e.matmul(pss[:, 0:N], lhsT=bones[:], rhs=sq[:, 0:N], start=True, stop=True), reads=[bones_r, sq_r], writes=[pssr])
            fx.op("scalar", lambda e, pss=pss, rn=rn, N=N: e.activation(out=rn[:, 0:N], in_=pss[:, 0:N], func=AF.Sqrt, bias=eps12[:], scale=1.0), reads=[pssr, eps12_r], writes=[rn_r])
            fx.op("vector", lambda e, rn=rn, N=N: e.reciprocal(out=rn[:, 0:N], in_=rn[:, 0:N]), reads=[rn_r], writes=[rn_r])
            fx.op("gpsimd", lambda e, kk=kk, kkr=kkr, rn=rn, N=N: e.tensor_tensor(out=kk[:, 0:N], in0=kkr[:, 0:N], in1=rn[:, 0:N], op=ALU.mult), reads=[kkr_r, rn_r], writes=[kk_r])
            for z in range(2):
                az, az_r = azb[z][k]
                kd, kd_r = kdb[z][k]
                bz, bz_r = bzb[z][k]
                fx.op("vector", lambda e, kd=kd, az=az, hp=hp, N=N: e.tensor_scalar(out=kd[:, 0:N], in0=az[:, 0:N], scalar1=ka_p[:, hp:hp + 1], scalar2=oka[:, hp:hp + 1], op0=ALU.mult, op1=ALU.add), reads=[az_r, ka_pr, oka_r], writes=[kd_r])
                fx.op("vector", lambda e, kd=kd, kT=kT, N=N: e.tensor_tensor(out=kd[:, 0:N], in0=kd[:, 0:N], in1=kT[:, 0:N], op=ALU.mult), reads=[kd_r, kT_r], writes=[kd_r])
                fx.op("gpsimd", lambda e, bz=bz, kk=kk, az=az, N=N: e.tensor_tensor(out=bz[:, 0:N], in0=kk[:, 0:N], in1=az[:, 0:N], op=ALU.mult), reads=[kk_r, az_r], writes=[bz_r])
            (x1, x1_r), (x2, x2_r) = tb1[k], tb2[k]
            fx.op("vector", lambda e, x1=x1, N=N, k=k: e.tensor_tensor(out=x1[:, 0:N], in0=kdb[0][k][0][:, 0:N], in1=kdb[1][k][0][:, 0:N], op=ALU.add), reads=[kdb[0][k][1], kdb[1][k][1]], writes=[x1_r])
            fx.op("vector", lambda e, x2=x2, rT=rT, hp=hp, N=N: e.tensor_scalar(out=x2[:, 0:N], in0=rT[:, 0:N], scalar1=rk_p[:, hp:hp + 1], scalar2=0.5, op0=ALU.mult, op1=ALU.mult), reads=[rT_r, rk_pr], writes=[x2_r])
            fx.op("vector", lambda e, x1=x1, x2=x2, N=N: e.tensor_tensor(out=x1[:, 0:N], in0=x1[:, 0:N], in1=x2[:, 0:N], op=ALU.mult), reads=[x1_r, x2_r], writes=[x1_r])
            psb, psbr = self.pb[k * 4 + 2 + cnt[0] % 2]
            cnt[0] += 1
            fx.op("tensor", lambda e, psb=psb, x1=x1, N=N: e.matmul(psb[:, 0:N], lhsT=bones[:], rhs=x1[:, 0:N], start=True, stop=True), reads=[bones_r, x1_r], writes=[psbr])
            bo, bo_r = bonb[k]
            fx.op("vector", lambda e, bo=bo, psb=psb, vT=vT, N=N: e.tensor_tensor(out=bo[:, 0:N], in0=psb[:, 0:N], in1=vT[:, 0:N], op=ALU.mult), reads=[psbr, vT_r], writes=[bo_r])
            fx.dma(self.bonusd[hp, :, t0:t0 + N], bo[:, 0:N], reads=[bo_r], sem_res=bo_r)
            for z in range(2):
                lw, lw_r = lwb[z][k]
                kd, kd_r = kdb[z][k]
                bz, bz_r = bzb[z][k]
                cw, cw_r = cwb[z * 2 + k]
                (e1, e1_r), (e2, e2_r), (e3, e3_r), (e4, e4_r) = e1b[z * 2 + k], e2b[z * 2 + k], e3b[z * 2 + k], e4b[z * 2 + k]
                (y1, y1_r) = tb3[z * 2 + k]
                Dall, Dall_r = R["Dall"][z]
                fx.op("vector", lambda e, cw=cw, lw=lw, N=N: e.tensor_tensor_scan(out=cw[:, 0:N], data0=cmask[:, 0:N], data1=lw[:, 0:N], initial=0.0, op0=ALU.mult, op1=ALU.add), reads=[cmask_r, lw_r], writes=[cw_r])
                cw3 = cw[:, 0:N].rearrange("p (c t) -> p c t", t=64)
                totb = cw3[:, :, 63:64].to_broadcast([128, nch, 64])
                ch0 = t0 // 64
                fx.op("scalar", lambda e, Dall=Dall, cw3=cw3, hp=hp, ch0=ch0, nch=nch: e.activation(out=Dall[:, hp, ch0:ch0 + nch], in_=cw3[:, :, 63], func=AF.Exp), reads=[cw_r], writes=[Dall_r])
                y13 = y1[:, 0:N].rearrange("p (c t) -> p c t", t=64)
                if z == 0:
                    fx.op("gpsimd", lambda e, y13=y13, totb=totb, cw3=cw3: e.tensor_tensor(out=y13, in0=totb, in1=cw3, op=ALU.subtract), reads=[cw_r], writes=[y1_r])
                    cwz, cwz_r = cw, cw_r
                else:
                    fx.op("gpsimd", lambda e, y1=y1, cw=cw, lw=lw, N=N: e.tensor_tensor(out=y1[:, 0:N], in0=cw[:, 0:N], in1=lw[:, 0:N], op=ALU.subtract), reads=[cw_r, lw_r], writes=[y1_r])
                    cwz, cwz_r = e4, e4_r
                    e43 = e4[:, 0:N].rearrange("p (c t) -> p c t", t=64)
                    fx.op("gpsimd", lambda e, e43=e43, totb=totb, y13=y13: e.tensor_tensor(out=e43, in0=totb, in1=y13, op=ALU.subtract), reads=[cw_r, y1_r], writes=[e4_r])
                fx.op("scalar", lambda e, e1=e1, cwz=cwz, N=N: e.activation(out=e1[:, 0:N], in_=cwz[:, 0:N], func=AF.Exp), reads=[cwz_r], writes=[e1_r])
                fx.op("scalar", lambda e, e2=e2, cwz=cwz, N=N: e.activation(out=e2[:, 0:N], in_=cwz[:, 0:N], func=AF.Exp, scale=-1.0), reads=[cwz_r], writes=[e2_r])
                fx.op("vector", lambda e, e3=e3, cwz=cwz, lw=lw, N=N: e.tensor_tensor(out=e3[:, 0:N], in0=cwz[:, 0:N], in1=lw[:, 0:N], op=ALU.subtract), reads=[cwz_r, lw_r], writes=[e3_r])
                fx.op("scalar", lambda e, e3=e3, N=N: e.activation(out=e3[:, 0:N], in_=e3[:, 0:N], func=AF.Exp), reads=[e3_r], writes=[e3_r])
                fx.op("scalar", lambda e, y1=y1, N=N: e.activation(out=y1[:, 0:N], in_=y1[:, 0:N], func=AF.Exp), reads=[y1_r], writes=[y1_r])
                outs = [fmo[q][z * 2 + k] for q in range(4)]
                fx.op("vector", lambda e, o=outs[0][0], kk=kk, e3=e3, N=N: e.scalar_tensor_tensor(out=o[:, 0:N], in0=kk[:, 0:N], scalar=-1.0, in1=e3[:, 0:N], op0=ALU.mult, op1=ALU.mult), reads=[kk_r, e3_r], writes=[outs[0][1]])
                fx.op("gpsimd", lambda e, o=outs[1][0], rT=rT, e1=e1, N=N: e.tensor_tensor(out=o[:, 0:N], in0=rT[:, 0:N], in1=e1[:, 0:N], op=ALU.mult), reads=[rT_r, e1_r], writes=[outs[1][1]])
                fx.op("vector", lambda e, o=outs[2][0], bz=bz, e2=e2, N=N: e.tensor_tensor(out=o[:, 0:N], in0=bz[:, 0:N], in1=e2[:, 0:N], op=ALU.mult), reads=[bz_r, e2_r], writes=[outs[2][1]])
                fx.op("vector", lambda e, o=outs[3][0], kd=kd, e2=e2, N=N: e.tensor_tensor(out=o[:, 0:N], in0=kd[:, 0:N], in1=e2[:, 0:N], op=ALU.mult), reads=[kd_r, e2_r], writes=[outs[3][1]])
                for q in range(4):
                    fx.dma(self.FMd[z][tile0:tile0 + ntile, :, hp, q, :].rearrange("n p t -> p n t"), outs[q][0][:, 0:N].rearrange("p (n t) -> p n t", t=128), reads=[outs[q][1]], sem_res=outs[q][1])
                (h0, h0_r), (h1, h1_r) = hato[0][z * 2 + k], hato[1][z * 2 + k]
                fx.op("vector", lambda e, h0=h0, bz=bz, y1=y1, N=N: e.tensor_tensor(out=h0[:, 0:N], in0=bz[:, 0:N], in1=y1[:, 0:N], op=ALU.mult), reads=[bz_r, y1_r], writes=[h0_r])
                fx.op("gpsimd", lambda e, h1=h1, kd=kd, y1=y1, N=N: e.tensor_tensor(out=h1[:, 0:N], in0=kd[:, 0:N], in1=y1[:, 0:N], op=ALU.mult), reads=[kd_r, y1_r], writes=[h1_r])
                transpose_store(fx, k, cnt, h0, h0_r, lambda tl, z=z, tile0=tile0, hp=hp: self.TMd[z][tile0 + tl, :, 0, hp, :], ntile)
                transpose_store(fx, k, cnt, h1, h1_r, lambda tl, z=z, tile0=tile0, hp=hp: self.TMd[z][tile0 + tl, :, 1, hp, :], ntile)


        blocks = [(hp, ci) for hp in range(8) for ci in range(9)]
        for bi in range(0, len(blocks), 2):
            recs = []
            for q in range(2):
                if bi + q < len(blocks):
                    r_ = Rec()
                    emit_block(r_, blocks[bi + q][0], blocks[bi + q][1], q)
                    recs.append(r_)
            interleave(fw, recs)

    def rwkv_S(self, i, j, R, ts):
        fw = self.fw
        mk = {}
        for z in range(2):
            mNf, mNf_r = fw.sb("mNf%d" % z, [128, 256], F32, ts)
            mLf, mLf_r = fw.sb("mLf%d" % z, [128, 128], F32, ts)
            mN, mN_r = fw.sb("mN%d" % z, [128, 2, 256], BF16, ts)
            mL, mL_r = fw.sb("mL%d" % z, [128, 4, 128], BF16, ts)
            fw.dma(mNf[:], self.Cn["mN%d" % z][:, :], writes=[mNf_r], sem_res=mNf_r)
            fw.dma(mLf[:], self.Cn["mL%d" % z][:, :], writes=[mLf_r], sem_res=mLf_r)
            for q in range(2):
                fw.op("vector", lambda e, mN=mN, mNf=mNf, q=q: e.tensor_copy(out=mN[:, q, :], in_=mNf[:]), reads=[mNf_r], writes=[mN_r])
            for q in range(4):
                fw.op("vector", lambda e, mL=mL, mLf=mLf, q=q: e.tensor_copy(out=mL[:, q, :], in_=mLf[:]), reads=[mLf_r], writes=[mL_r])
            mk[z] = (mN, mN_r, mL, mL_r)
        identb4, identb4_r = fw.sb("ident4", [128, 4, 128], BF16, ts)
        for q in range(4):
            fw.op("vector", lambda e, q=q: e.tensor_copy(out=identb4[:, q, :], in_=self.ident[:]), reads=[self.ident_r], writes=[identb4_r])
        Mst = []
        for z in range(2):
            row = []
            for hf in range(2):
                t, r = fw.sb("M%d_%d" % (z, hf), [128, 8, 64], F32, ts)
                fw.op("vector", lambda e, t=t: e.memset(t[:], 0.0), writes=[r])
                tb_, rb_ = fw.sb("Mb%d_%d" % (z, hf), [128, 8, 64], BF16, ts)
                fw.op("vector", lambda e, tb_=tb_: e.memset(tb_[:], 0.0), writes=[rb_])
                row.append((t, r, tb_, rb_))
            Mst.append(row)
        FMb = [fw.sb("sFM%d" % k, [128, 8, 4, 128], BF16, ts) for k in range(2)]
        TMb = [fw.sb("sTM%d" % k, [128, 2, 8, 128], BF16, ts) for k in range(2)]
        Vb = [fw.sb("sV%d" % k, [128, 8, 128], BF16, ts) for k in range(2)]
        G13, _ = fw.sb("sG13", [128, 16, 256], BF16, ts)
        G24, _ = fw.sb("sG24", [128, 16, 256], BF16, ts)
        G13_r = [Res("sG13_%d" % g) for g in range(8)]
        G24_r = [Res("sG24_%d" % g) for g in range(8)]
        Lb_ = [fw.sb("sL%d" % k, [128, 16, 128], BF16, ts)[0] for k in range(2)]
        Nb_ = [fw.sb("sN%d" % k, [128, 16, 128], BF16, ts)[0] for k in range(2)]
        Pm, _ = fw.sb("sP", [128, 16, 128], BF16, ts)
        L_r = [[Res("sL%d_%d" % (k, g)) for g in range(4)] for k in range(2)]
        N_r = [[Res("sN%d_%d" % (k, g)) for g in range(4)] for k in range(2)]
        P_r = [Res("sP_%d" % g) for g in range(4)]
        Z2s = [fw.sb("sZ2s%d" % k, [128, 512], F32, ts) for k in range(2)]
        Zs = [fw.sb("sZs%d" % k, [128, 512], BF16, ts) for k in range(2)]
        Us = [fw.sb("sUs%d" % k, [128, 512], BF16, ts) for k in range(2)]
        Ys = [fw.sb("sYs%d" % k, [128, 512], F32, ts) for k in range(2)]
        npb = [0]

        def prep_bank():
            b = self.pb[npb[0] % 8]
            npb[0] += 1
            return b

        order = [list(range(NTI)), [1, 0] + list(range(NTI - 1, 1, -1))]
        def issue_loads(n):
            z_, tile_ = n % 2, order[n % 2][n // 2]
            FM_, FM_r_ = FMb[n % 2]
            TM_, TM_r_ = TMb[n % 2]
            V_, V_r_ = Vb[n % 2]
            fw.dma(FM_[:], self.FMd[z_][tile_], writes=[FM_r_], sem_res=FM_r_)
            fw.dma(TM_[:], self.TMd[z_][tile_], writes=[TM_r_], sem_res=TM_r_)
            fw.dma(V_[:], self.Vd[tile_], writes=[V_r_], sem_res=V_r_)

        nstep = 0
        for si in range(NTI):
            for z in (0, 1):
                tile = order[z][si]
                k = nstep % 2
                nstep += 1
                FM, FM_r = FMb[k]
                TM, TM_r = TMb[k]
                V, V_r = Vb[k]
                mN, mN_r, mL, mL_r = mk[z]
                if nstep == 1:
                    issue_loads(0)
                if nstep < 2 * NTI:
                    issue_loads(nstep)
                for g2 in range(8):
                    pB, pBr = prep_bank()
                    pK, pKr = prep_bank()
                    pL, pLr = prep_bank()
                    for hl in range(2):
                        h = g2 * 2 + hl
                        hp, hh = h % 8, h // 8
                        rows = slice(hh * 64, (hh + 1) * 64)
                        ar = FM[rows, hp, 0:2, :].rearrange("p a t -> p (a t)")
                        fw.op("tensor", lambda e, FM=FM, TM=TM, V=V, pB=pB, rows=rows, hp=hp, hl=hl, ar=ar: e.matmul(pB[:, hl * 256:(hl + 1) * 256], lhsT=FM[rows, hp, 2, :], rhs=ar, start=True, stop=True), reads=[FM_r], writes=[pBr], signal=(hl == 1))
                        fw.op("tensor", lambda e, FM=FM, TM=TM, V=V, pK=pK, rows=rows, hp=hp, hl=hl, ar=ar: e.matmul(pK[:, hl * 256:(hl + 1) * 256], lhsT=FM[rows, hp, 3, :], rhs=ar, start=True, stop=True), reads=[FM_r], writes=[pKr], signal=(hl == 1))
                        fw.op("tensor", lambda e, FM=FM, TM=TM, V=V, pL=pL, rows=rows, hp=hp, hl=hl: e.matmul(pL[:, hl * 128:(hl + 1) * 128], lhsT=FM[rows, hp, 0, :], rhs=FM[rows, hp, 2, :], start=True, stop=True), reads=[FM_r], writes=[pLr], signal=(hl == 1))
                    h0 = g2 * 2
                    fw.op("vector", lambda e, FM=FM, TM=TM, V=V, pB=pB, h0=h0, mN=mN: e.tensor_tensor(out=G13[:, h0:h0 + 2, :], in0=pB[:, :].rearrange("p (h c) -> p h c", h=2), in1=mN[:], op=ALU.mult), reads=[pBr, mN_r], writes=[G13_r[g2]])
                    fw.op("vector", lambda e, FM=FM, TM=TM, V=V, pK=pK, h0=h0, mN=mN: e.tensor_tensor(out=G24[:, h0:h0 + 2, :], in0=pK[:, :].rearrange("p (h c) -> p h c", h=2), in1=mN[:], op=ALU.mult), reads=[pKr, mN_r], writes=[G24_r[g2]])
                    fw.op("vector", lambda e, FM=FM, TM=TM, V=V, pL=pL, h0=h0, mL=mL: e.tensor_tensor(out=Lb_[0][:, h0:h0 + 2, :], in0=pL[:, 0:256].rearrange("p (h c) -> p h c", h=2), in1=mL[:, 0:2, :], op=ALU.mult), reads=[pLr, mL_r], writes=[L_r[0][g2 // 2]])
                sstop = getattr(self, "s_stop", 99)
                if sstop < 1:
                    continue
                for g4 in range(4):
                    hs = slice(g4 * 4, g4 * 4 + 4)
                    fw.op("gpsimd", lambda e, FM=FM, TM=TM, V=V, hs=hs: e.tensor_copy(out=Nb_[0][:, hs, :], in_=G13[:, hs, 0:128]), reads=[G13_r[g4 * 2], G13_r[g4 * 2 + 1]], writes=[N_r[0][g4]])
                    fw.op("gpsimd", lambda e, FM=FM, TM=TM, V=V, hs=hs: e.tensor_tensor(out=Pm[:, hs, :], in0=G13[:, hs, 0:128], in1=identb4[:], op=ALU.add), reads=[G13_r[g4 * 2], G13_r[g4 * 2 + 1], identb4_r], writes=[P_r[g4]])
                for lev in range(1, 6):
                    a, b = (lev - 1) % 2, lev % 2
                    for g4 in range(4):
                        hs = slice(g4 * 4, g4 * 4 + 4)
                        pl, plr = prep_bank()
                        for hl in range(4):
                            h = g4 * 4 + hl
                            fw.op("tensor", lambda e, FM=FM, TM=TM, V=V, pl=pl, h=h, hl=hl, a=a: e.matmul(pl[:, hl * 128:(hl + 1) * 128], lhsT=Nb_[a][:, h, :], rhs=Lb_[a][:, h, :], start=True, stop=True), reads=[N_r[a][g4], L_r[a][g4]], writes=[plr], signal=(hl == 3))
                        fw.op("scalar", lambda e, FM=FM, TM=TM, V=V, pl=pl, hs=hs, b=b: e.copy(out=Lb_[b][:, hs, :], in_=pl[:, :].rearrange("p (h c) -> p h c", h=4)), reads=[plr], writes=[L_r[b][g4]])
                        if lev < 5:
                            pn, pnr = prep_bank()
                            for hl in range(4):
                                h = g4 * 4 + hl
                                fw.op("tensor", lambda e, FM=FM, TM=TM, V=V, pn=pn, h=h, hl=hl, a=a: e.matmul(pn[:, hl * 128:(hl + 1) * 128], lhsT=Lb_[a][:, h, :], rhs=Nb_[a][:, h, :], start=True, stop=True), reads=[N_r[a][g4], L_r[a][g4]], writes=[pnr], signal=(hl == 3))
                            fw.op("scalar", lambda e, FM=FM, TM=TM, V=V, pn=pn, hs=hs, b=b: e.copy(out=Nb_[b][:, hs, :], in_=pn[:, :].rearrange("p (h c) -> p h c", h=4)), reads=[pnr], writes=[N_r[b][g4]])
                    for g4 in range(4):
                        hs = slice(g4 * 4, g4 * 4 + 4)
                        pq, pqr = prep_bank()
                        for hl in range(4):
                            h = g4 * 4 + hl
                            fw.op("tensor", lambda e, FM=FM, TM=TM, V=V, pq=pq, h=h, hl=hl, b=b: e.matmul(pq[:, hl * 128:(hl + 1) * 128], lhsT=Lb_[b][:, h, :], rhs=Pm[:, h, :], start=True, stop=True), reads=[L_r[b][g4], P_r[g4]], writes=[pqr], signal=(hl == 3))
                        fw.op("vector", lambda e, FM=FM, TM=TM, V=V, pq=pq, hs=hs: e.tensor_tensor(out=Pm[:, hs, :], in0=pq[:, :].rearrange("p (h c) -> p h c", h=4), in1=Pm[:, hs, :], op=ALU.add), reads=[pqr, P_r[g4]], writes=[P_r[g4]])
                if self.dbg and nstep == 1:
                    self.dG13 = self.nc.dram_tensor("dG13", [128, 16, 256], F32, kind="ExternalOutput").ap()
                    self.dG24 = self.nc.dram_tensor("dG24", [128, 16, 256], F32, kind="ExternalOutput").ap()
                    self.dP = self.nc.dram_tensor("dP", [128, 16, 128], F32, kind="ExternalOutput").ap()
                    self.dFM = self.nc.dram_tensor("dFM", [128, 8, 4, 128], F32, kind="ExternalOutput").ap()
                    self.dTM = self.nc.dram_tensor("dTM", [128, 2, 8, 128], F32, kind="ExternalOutput").ap()
                    self.dV = self.nc.dram_tensor("dV", [128, 8, 128], F32, kind="ExternalOutput").ap()
                    dr = fw.res("dbgS", ts)
                    fw.dma(self.dFM, FM[:], reads=[FM_r], sem_res=dr, ek="gpsimd")
                    fw.dma(self.dTM, TM[:], reads=[TM_r], sem_res=dr, ek="gpsimd")
                    fw.dma(self.dV, V[:], reads=[V_r], sem_res=dr, ek="gpsimd")
                    fw.dma(self.dG13, G13[:], reads=list(G13_r), sem_res=dr, ek="gpsimd")
                    fw.dma(self.dG24, G24[:], reads=list(G24_r), sem_res=dr, ek="gpsimd")
                    fw.dma(self.dP, Pm[:], reads=list(P_r), sem_res=dr, ek="gpsimd")
                if sstop < 2:
                    continue
                cks = (0, 1) if z == 0 else (1, 0)
                for hf in range(2):
                    hr = slice(hf * 64, (hf + 1) * 64)
                    M, M_r, Mb, Mb_r = Mst[z][hf]
                    Dall, Dall_r = R["Dall"][z]
                    tA, tAr = self.pb[0]
                    tB, tBr = self.pb[1]
                    tC, tCr = self.pb[2]
                    tD, tDr = self.pb[3]
                    z2, z2_r = Z2s[hf]
                    zs_, zs_r = Zs[hf]
                    us, us_r = Us[hf]
                    ys, ys_r = Ys[hf]
                    g13r = list(G13_r)
                    g24r = list(G24_r)
                    pr_ = list(P_r)
                    for ck in range(2):
                        rows = slice(ck * 64, (ck + 1) * 64)
                        for hp in range(8):
                            h = hf * 8 + hp
                            fw.op("tensor", lambda e, FM=FM, TM=TM, V=V, rows=rows, h=h, hp=hp, hr=hr, ck=ck: e.matmul(tA[rows, hp * 64:(hp + 1) * 64], lhsT=G24[rows, h, ck * 64:(ck + 1) * 64], rhs=V[rows, hp, hr], start=True, stop=True),
                                  reads=g24r + [V_r], writes=[tAr], signal=(ck == 1 and hp == 7))
                    fw.op("scalar", lambda e, FM=FM, TM=TM, V=V, z2=z2: e.copy(out=z2[:], in_=tA[:, :]), reads=[tAr], writes=[z2_r])
                    for ck in (cks if sstop >= 3 else ()):
                        rows = slice(ck * 64, (ck + 1) * 64)
                        cols = slice(ck * 64, (ck + 1) * 64)
                        for hp in range(8):
                            fw.op("tensor", lambda e, FM=FM, TM=TM, V=V, rows=rows, cols=cols, hp=hp, hr=hr, Mb=Mb: e.matmul(tB[rows, hp * 64:(hp + 1) * 64], lhsT=FM[hr, hp, 0, cols], rhs=Mb[hr, hp, :], start=True, stop=True),
                                  reads=[FM_r, Mb_r], writes=[tBr], signal=(hp == 7))
                        for hp in range(8):
                            fw.op("tensor", lambda e, FM=FM, TM=TM, V=V, rows=rows, cols=cols, hp=hp, hr=hr, Mb=Mb: e.matmul(tC[rows, hp * 64:(hp + 1) * 64], lhsT=FM[hr, hp, 1, cols], rhs=Mb[hr, hp, :], start=True, stop=True),
                                  reads=[FM_r, Mb_r], writes=[tCr], signal=(hp == 7))
                        fw.op("vector", lambda e, FM=FM, TM=TM, V=V, rows=rows, zs_=zs_, z2=z2: e.tensor_tensor(out=zs_[rows, :], in0=tB[rows, :], in1=z2[rows, :], op=ALU.add), reads=[tBr, z2_r], writes=[zs_r])
                        if sstop < 4:
                            continue
                        for hp in range(8):
                            h = hf * 8 + hp
                            fw.op("tensor", lambda e, FM=FM, TM=TM, V=V, rows=rows, cols=cols, hp=hp, h=h, zs_=zs_: e.matmul(tB[rows, hp * 64:(hp + 1) * 64], lhsT=Pm[rows, h, cols], rhs=zs_[rows, hp * 64:(hp + 1) * 64], start=True, stop=True),
                                  reads=pr_ + [zs_r], writes=[tBr], signal=(hp == 7))
                        fw.op("scalar", lambda e, FM=FM, TM=TM, V=V, rows=rows, us=us: e.copy(out=us[rows, :], in_=tB[rows, :]), reads=[tBr], writes=[us_r])
                        if sstop < 5:
                            continue
                        for hp in range(8):
                            h = hf * 8 + hp
                            fw.op("tensor", lambda e, FM=FM, TM=TM, V=V, rows=rows, ck=ck, hp=hp, h=h, us=us: e.matmul(tA[rows, hp * 64:(hp + 1) * 64], lhsT=G13[rows, h, 128 + ck * 64:128 + (ck + 1) * 64], rhs=us[rows, hp * 64:(hp + 1) * 64], start=True, stop=False),
                                  reads=g13r + [us_r], writes=[tAr], signal=False)
                            fw.op("tensor", lambda e, FM=FM, TM=TM, V=V, rows=rows, ck=ck, hp=hp, h=h, hr=hr: e.matmul(tA[rows, hp * 64:(hp + 1) * 64], lhsT=G24[rows, h, 128 + ck * 64:128 + (ck + 1) * 64], rhs=V[rows, hp, hr], start=False, stop=True),
                                  reads=g24r + [V_r], writes=[tAr], signal=(hp == 7))
                        if sstop < 6:
                            continue
                        for hp in range(8):
                            fw.op("tensor", lambda e, FM=FM, TM=TM, V=V, rows=rows, hp=hp, hr=hr, us=us: e.matmul(tD[hr, hp * 64:(hp + 1) * 64], lhsT=TM[rows, 0, hp, hr], rhs=us[rows, hp * 64:(hp + 1) * 64], start=True, stop=False),
                                  reads=[TM_r, us_r], writes=[tDr], signal=False)
                            fw.op("tensor", lambda e, FM=FM, TM=TM, V=V, rows=rows, hp=hp, hr=hr: e.matmul(tD[hr, hp * 64:(hp + 1) * 64], lhsT=TM[rows, 1, hp, hr], rhs=V[rows, hp, hr], start=False, stop=True),
                                  reads=[TM_r, V_r], writes=[tDr], signal=(hp == 7))
                        c = tile * 2 + ck
                        dbc = Dall[hr, :, c:c + 1].to_broadcast([64, 8, 64])
                        fw.op("vector", lambda e, FM=FM, TM=TM, V=V, M=M, dbc=dbc, hr=hr: e.tensor_tensor(out=M[hr, :, :], in0=M[hr, :, :], in1=dbc, op=ALU.mult), reads=[M_r, Dall_r], writes=[M_r])
                        fw.op("vector", lambda e, FM=FM, TM=TM, V=V, M=M, hr=hr: e.tensor_tensor(out=M[hr, :, :], in0=tD[hr, :].rearrange("p (h i) -> p h i", h=8), in1=M[hr, :, :], op=ALU.add), reads=[M_r, tDr], writes=[M_r])
                        fw.op("scalar", lambda e, FM=FM, TM=TM, V=V, M=M, Mb=Mb, hr=hr: e.copy(out=Mb[hr, :, :], in_=M[hr, :, :]), reads=[M_r], writes=[Mb_r])
                    if sstop < 7:
                        continue
                    fw.op("scalar", lambda e, FM=FM, TM=TM, V=V, ys=ys: e.copy(out=ys[:], in_=tC[:, :]), reads=[tCr], writes=[ys_r])
                    fw.op("vector", lambda e, FM=FM, TM=TM, V=V, ys=ys: e.tensor_tensor(out=ys[:], in0=tA[:, :], in1=ys[:], op=ALU.add), reads=[tAr, ys_r], writes=[ys_r])
                    ydst = self.yd[z][tile * 128:(tile + 1) * 128, :].rearrange("t (hp hh i) -> t hp hh i", hh=2, i=64)[:, :, hf, :]
                    fw.dma(ydst, ys[:].rearrange("p (hp i) -> p hp i", i=64), reads=[ys_r], sem_res=ys_r)

    def rwkv_O(self, i, j, R, ts):
        fw = self.fw
        I = self.I
        lnw, lnw_r = fw.sb("o_lnw", [128, D], F32, ts)
        lnb, lnb_r = fw.sb("o_lnb", [128, D], F32, ts)
        fw.dma(lnw[:], I["rw_ln_w"][j, :].partition_broadcast(128), writes=[lnw_r], sem_res=lnw_r)
        fw.dma(lnb[:], I["rw_ln_b"][j, :].partition_broadcast(128), writes=[lnb_r], sem_res=lnb_r)
        epsg, epsg_r = fw.sb("o_eps", [128, 1], F32, ts)
        fw.op("vector", lambda e: e.memset(epsg[:], 64e-5), writes=[epsg_r])
        y0b = [fw.sb("o_y0%d" % k, [128, D], F32, ts) for k in range(2)]
        y1b = [fw.sb("o_y1%d" % k, [128, D], F32, ts) for k in range(2)]
        sqb = [fw.sb("o_sq%d" % k, [128, D], F32, ts) for k in range(2)]
        stb = [fw.sb("o_st%d" % k, [128, 2, 16], F32, ts) for k in range(2)]
        bnb = [fw.sb("o_bn%d" % k, [128, 8, 128], F32, ts) for k in range(2)]
        zb = [fw.sb("o_z%d" % k, [128, 8, 128], BF16, ts) for k in range(2)]
        ob = [fw.sb("o_o%d" % k, [128, 8, 128], F32, ts) for k in range(2)]
        ogb = [fw.sb("o_og%d" % k, [128, 8, 128], BF16, ts) for k in range(2)]
        def tile_body(fx, tt):
            k = tt % 2
            (y0, y0_r), (y1, y1_r), (sq, sq_r), (st, st_r) = y0b[k], y1b[k], sqb[k], stb[k]
            (bn, bn_r), (zt, zt_r), (o_, o_r), (og_, og_r_) = bnb[k], zb[k], ob[k], ogb[k]
            rs = slice(tt * 128, (tt + 1) * 128)
            fx.dma(y0[:], self.yd[0][rs, :], writes=[y0_r], sem_res=y0_r)
            fx.dma(y1[:], self.yd[1][rs, :], writes=[y1_r], sem_res=y1_r)
            fx.dma(bn[:], self.bonusd[:, :, rs].rearrange("h p t -> p h t"), writes=[bn_r], sem_res=bn_r)
            fx.dma(zt[:], self.zsd[:, :, rs].rearrange("h p t -> p h t"), writes=[zt_r], sem_res=zt_r)
            fx.op("gpsimd", lambda e, y0=y0, y1=y1: e.tensor_tensor(out=y0[:], in0=y0[:], in1=y1[:], op=ALU.add), reads=[y0_r, y1_r], writes=[y0_r])
            y3 = y0[:].rearrange("p (h d) -> p h d", d=64)
            s3 = sq[:].rearrange("p (h d) -> p h d", d=64)
            fx.op("vector", lambda e, st=st, y3=y3: e.tensor_reduce(out=st[:, 0, :], in_=y3, axis=AX.X, op=ALU.add), reads=[y0_r], writes=[st_r])
            fx.op("vector", lambda e, st=st: e.tensor_scalar(out=st[:, 0, :], in0=st[:, 0, :], scalar1=1.0 / 64.0, scalar2=None, op0=ALU.mult), reads=[st_r], writes=[st_r])
            mbc = st[:, 0, :].unsqueeze(2).to_broadcast([128, 16, 64])
            fx.op("vector", lambda e, y3=y3, mbc=mbc: e.tensor_tensor(out=y3, in0=y3, in1=mbc, op=ALU.subtract), reads=[y0_r, st_r], writes=[y0_r])
            fx.op("gpsimd", lambda e, sq=sq, y0=y0: e.tensor_tensor(out=sq[:], in0=y0[:], in1=y0[:], op=ALU.mult), reads=[y0_r], writes=[sq_r])
            fx.op("vector", lambda e, st=st, s3=s3: e.tensor_reduce(out=st[:, 1, :], in_=s3, axis=AX.X, op=ALU.add), reads=[sq_r], writes=[st_r])
            fx.op("scalar", lambda e, st=st: e.activation(out=st[:, 1, :], in_=st[:, 1, :], func=AF.Sqrt, bias=epsg[:], scale=1.0 / 64.0), reads=[st_r, epsg_r], writes=[st_r])
            fx.op("vector", lambda e, st=st: e.reciprocal(out=st[:, 1, :], in_=st[:, 1, :]), reads=[st_r], writes=[st_r])
            rbc = st[:, 1, :].unsqueeze(2).to_broadcast([128, 16, 64])
            fx.op("vector", lambda e, y3=y3, rbc=rbc: e.tensor_tensor(out=y3, in0=y3, in1=rbc, op=ALU.mult), reads=[y0_r, st_r], writes=[y0_r])
            fx.op("gpsimd", lambda e, y0=y0: e.tensor_tensor(out=y0[:], in0=y0[:], in1=lnw[:], op=ALU.mult), reads=[y0_r, lnw_r], writes=[y0_r])
            fx.op("gpsimd", lambda e, y0=y0: e.tensor_tensor(out=y0[:], in0=y0[:], in1=lnb[:], op=ALU.add), reads=[y0_r, lnb_r], writes=[y0_r])
            for half in range(2):
                tp, tpr = self.pb[(tt * 2 + half) % 8]
                for q in range(4):
                    hc = half * 4 + q
                    fx.op("tensor", lambda e, tp=tp, q=q, hc=hc, y0=y0: e.transpose(tp[:, q * 128:(q + 1) * 128], y0[:, hc * 128:(hc + 1) * 128], self.ident[:]), reads=[y0_r, self.ident_r], writes=[tpr], signal=(q == 3))
                fx.op("vector", lambda e, tp=tp, half=half, o_=o_, bn=bn: e.tensor_tensor(out=o_[:, half * 4:half * 4 + 4, :], in0=tp[:, :].rearrange("p (h t) -> p h t", h=4), in1=bn[:, half * 4:half * 4 + 4, :], op=ALU.add), reads=[tpr, bn_r], writes=[o_r])
            fx.op("gpsimd", lambda e, og_=og_, o_=o_, zt=zt: e.tensor_tensor(out=og_[:], in0=o_[:], in1=zt[:], op=ALU.mult), reads=[o_r, zt_r], writes=[og_r_])
            fx.dma(self.og[:, :, rs].rearrange("h p t -> p h t"), og_[:], reads=[og_r_], sem_res=og_r_)

        for tt in range(0, NTI, 2):
            ra, rb = Rec(), Rec()
            tile_body(ra, tt)
            tile_body(rb, tt + 1)
            interleave(fw, [ra, rb])


_CACHE = {}


def get_prog(n_layers=DEPTH, dbg=False):
    key = (n_layers, dbg)
    if key not in _CACHE:
        p = Prog(n_layers, dbg)
        p.build()
        _CACHE[key] = p
    return _CACHE[key]


def make_in_maps(inputs, consts, cores):
    maps = []
    for b in cores:
        m = {}
        for k, shp in INPUT_SHAPES.items():
            a = np.asarray(inputs[k])
            if k in ("x", "ctx"):
                a = a[b]
            elif k == "c":
                a = a[b:b + 1]
            a = np.ascontiguousarray(a, dtype=np.float32).reshape(shp)
            m[k] = a
        for k, v in consts.items():
            m["k_" + k] = v
        maps.append(m)
    return maps


def kernel(**inputs):
    p = get_prog()
    cores = list(range(8))
    in_maps = make_in_maps(inputs, p.consts, cores)
    res = run_bass_kernel_spmd(p.nc, in_maps, core_ids=cores)
    out = np.stack([np.asarray(r["out"]) for r in res.results], axis=0)
    return out.astype(np.float32)
```

```python
import math
from contextlib import ExitStack

import numpy as np
import concourse.bass as bass
import concourse.mybir as mybir
from concourse.bass_utils import run_bass_kernel_spmd

F32 = mybir.dt.float32
BF16 = mybir.dt.bfloat16
AF = mybir.ActivationFunctionType
ALU = mybir.AluOpType
AX = mybir.AxisListType

D = 1024
NL = 4096
NCX = 256
NT = NL + NCX
NTI = NT // 128
DEPTH = 4
NORM_EPS = 1e-6
SUBLN_EPS = 1e-5
GRID_W = 64


class Sem:
    def __init__(self, fw, name):
        self.name = name
        self.handle = fw.stack.enter_context(fw.nc.semaphore(name))
        self.issued = 0
        fw.sems.append(self)


class Res:
    __slots__ = ("name", "writers", "readers", "dsem")

    def __init__(self, name):
        self.name = name
        self.writers = []
        self.readers = []
        self.dsem = None


class Eng:
    def __init__(self, fw, key):
        self.key = key
        self.sem = Sem(fw, "e_" + key)
        self.ops = []
        self.waited = {}
        self.pend_r = []
        self.pend_w = []


class FW:
    def __init__(self, nc, stack):
        self.nc = nc
        self.stack = stack
        self.sems = []
        self.eng = {k: Eng(self, k) for k in ("tensor", "vector", "scalar", "gpsimd", "sync")}
        self.n_ins = 0
        self.free_sems = []

    def sb(self, name, shape, dtype, stack=None):
        self.uid = getattr(self, "uid", 0) + 1
        name = "%s_u%d" % (name, self.uid)
        t = (stack or self.stack).enter_context(self.nc.sbuf_tensor(name, list(shape), dtype))
        return t, self.res(name, stack)

    def res(self, name, stack=None):
        r = Res(name)
        if stack is not None:
            stack.callback(self._release, r)
        return r

    def _release(self, r):
        if r.dsem is not None:
            self.free_sems.append(r.dsem)
            r.dsem = None

    def _get_dsem(self, name):
        if self.free_sems:
            sem = self.free_sems.pop(0)
            for e in self.eng.values():
                assert e.waited.get(sem, 0) >= sem.issued, "semaphore reused before a barrier"
            return sem
        return Sem(self, "d%d" % len(self.sems))

    def ps(self, name, shape, dtype, stack=None):
        t = (stack or self.stack).enter_context(self.nc.psum_tensor(name, list(shape), dtype))
        return t, Res(name)

    def _waits_for(self, eng, reads, writes):
        need = {}

        def add(tok):
            sem, val = tok
            if val is None:
                val = sem.issued
            if need.get(sem, 0) < val:
                need[sem] = val
        for r in reads:
            for t in r.writers:
                add(t)
        for w in writes:
            for t in w.writers:
                add(t)
            for t in w.readers:
                add(t)
        out = []
        for sem, val in need.items():
            if eng.waited.get(sem, 0) >= val:
                continue
            eng.waited[sem] = val
            out.append((sem.handle, val))
        return out

    def _check_pending(self, eng, reads, writes):
        for e in self.eng.values():
            if e is eng or (not e.pend_w and not e.pend_r):
                continue
            for r in reads:
                assert r not in e.pend_w, ("unsignaled write pending", r.name, e.key)
            for w in writes:
                assert w not in e.pend_w and w not in e.pend_r, ("unsignaled access pending", w.name, e.key)

    def op(self, ek, fn, reads=(), writes=(), signal=True):
        eng = self.eng[ek]
        self._check_pending(eng, reads, writes)
        waits = self._waits_for(eng, reads, writes)
        self.n_ins += 1
        if signal:
            eng.sem.issued += 1
            tok = (eng.sem, eng.sem.issued)
            semh = eng.sem.handle

            def run(e, fn=fn, waits=waits, semh=semh):
                for (h, v) in waits:
                    e.wait_ge(h, v)
                fn(e).then_inc(semh, 1)
            rs = list(reads) + eng.pend_r
            ws = list(writes) + eng.pend_w
            eng.pend_r = []
            eng.pend_w = []
            for r in rs:
                r.readers.append(tok)
            for w in ws:
                w.writers = [tok]
                w.readers = []
        else:
            def run(e, fn=fn, waits=waits):
                for (h, v) in waits:
                    e.wait_ge(h, v)
                fn(e)
            eng.pend_r.extend(reads)
            eng.pend_w.extend(writes)
        eng.ops.append(run)

    def dma(self, out, in_, reads=(), writes=(), sem_res=None, ek="sync", **kw):
        eng = self.eng[ek]
        self._check_pending(eng, reads, writes)
        waits = self._waits_for(eng, reads, writes)
        if sem_res.dsem is None:
            sem_res.dsem = self._get_dsem(sem_res.name)
        sem = sem_res.dsem
        sem.issued += 16
        tok = (sem, None)
        semh = sem.handle
        self.n_ins += 1

        def run(e, waits=waits, semh=semh, out=out, in_=in_, kw=kw):
            for (h, v) in waits:
                e.wait_ge(h, v)
            e.dma_start(out=out, in_=in_, **kw).then_inc(semh, 16)
        eng.ops.append(run)
        for r in reads:
            r.readers.append(tok)
        for w in writes:
            w.writers = [tok]
            w.readers = []

    def barrier(self):
        for e in self.eng.values():
            assert not e.pend_r and not e.pend_w
        for e in self.eng.values():
            waits = []
            for s in self.sems:
                if s.issued > 0 and e.waited.get(s, 0) < s.issued:
                    e.waited[s] = s.issued
                    waits.append((s.handle, s.issued))

            def run(en, waits=waits):
                for (h, v) in waits:
                    en.wait_ge(h, v)
            e.ops.append(run)

    def finish(self):
        self.barrier()
        with self.nc.Block() as block:
            for key in ("sync", "tensor", "vector", "scalar", "gpsimd"):
                e = self.eng[key]

                def body(h, e=e):
                    for o in e.ops:
                        o(h)
                getattr(block, key)(body)


class Rec:
    def __init__(self):
        self.items = []

    def op(self, *a, **k):
        self.items.append((0, a, k))

    def dma(self, *a, **k):
        self.items.append((1, a, k))


def interleave(fw, recs):
    idx = [0] * len(recs)
    n = [len(r.items) for r in recs]
    while True:
        best, bf = -1, 2.0
        for i in range(len(recs)):
            if idx[i] < n[i]:
                f = idx[i] / n[i]
                if f < bf:
                    best, bf = i, f
        if best < 0:
            break
        kind, a, k = recs[best].items[idx[best]]
        idx[best] += 1
        (fw.dma if kind else fw.op)(*a, **k)


def rope_tables():
    n = np.arange(NL)
    row = (n // GRID_W).astype(np.float32)
    col = (n % GRID_W).astype(np.float32)
    inv = (10000.0 ** (-np.arange(0, 32, 2, dtype=np.float32) / 32.0)).astype(np.float32)
    C = np.ones((128, NT), np.float32)
    S = np.zeros((128, NT), np.float32)
    for f in range(128):
        axis = (f % 64) // 32
        j = f % 16
        pos = row if axis == 0 else col
        ang = (pos * inv[j]).astype(np.float32)
        C[f, NCX:] = np.cos(ang)
        S[f, NCX:] = np.sin(ang)
    return C, S


def host_consts():
    cs = {}
    cs["ident"] = np.eye(128, dtype=np.float32)
    C, S = rope_tables()
    cs["ropeC"] = C
    cs["ropeS"] = S
    import ml_dtypes
    bf = ml_dtypes.bfloat16
    for name, L in (("L", NL), ("X", NCX)):
        l = np.arange(L, dtype=np.int64)
        ang = (2.0 * np.pi / L) * ((l[:, None] * l[None, :]) % L).astype(np.float64)
        sc = 1.0 / math.sqrt(L)
        cs["dftC" + name] = (np.cos(ang) * sc).astype(np.float32).astype(bf)
        cs["dftS" + name] = (np.sin(ang) * sc).astype(np.float32).astype(bf)
    c = np.arange(128, dtype=np.int64)
    ang = (2.0 * np.pi / 128) * ((c[:, None] * c[None, :]) % 128).astype(np.float64)
    sc = 1.0 / math.sqrt(128.0)
    cs["dftCc"] = (np.cos(ang) * sc).astype(np.float32).astype(bf)
    cs["dftnSc"] = (-np.sin(ang) * sc).astype(np.float32).astype(bf)
    a = np.arange(128)
    same = (a[:, None] // 64) == (a[None, :] // 64)
    lt = (a[:, None] < a[None, :]) & same
    le = (a[:, None] <= a[None, :]) & same
    gt = (a[:, None] > a[None, :]) & same
    ge = (a[:, None] >= a[None, :]) & same
    cs["mN0"] = np.concatenate([lt, le], axis=1).astype(np.float32)
    cs["mL0"] = gt.astype(np.float32)
    cs["mN1"] = np.concatenate([gt, ge], axis=1).astype(np.float32)
    cs["mL1"] = lt.astype(np.float32)
    return cs


INPUT_SHAPES = {
    "x": [NL, D], "c": [1, D], "ctx": [NCX, D], "c_ctx": [1, D], "norm_gain": [4, D], "ada_w": [4, D, 3 * D],
    "ada_b": [4, 3 * D], "final_gain": [1, D], "da_w_in": [2, D, 4 * D], "da_lam_q": [2, 128], "da_lam_k": [2, 128],
    "da_subln_gain": [2, 128], "da_w_out": [2, D, D], "fn_w_in": [1, D, 2 * D], "fn_w_group": [1, 8, 128, 128],
    "fn_w_out": [1, D, D], "rw_w_in": [1, D, 4352], "rw_mu": [1, 3328], "rw_w0": [1, 2, D], "rw_w_up": [1, 2, 64, D],
    "rw_a0": [1, 2, D], "rw_a_up": [1, 2, 64, D], "rw_k_k": [1, D], "rw_k_a": [1, D], "rw_r_k": [1, D],
    "rw_ln_w": [1, D], "rw_ln_b": [1, D], "rw_w_out": [1, D, D],
}


class Prog:
    def __init__(self, n_layers=DEPTH, dbg=False):
        self.n_layers = n_layers
        self.dbg = dbg
        self.nc = bass.Bass("TRN2", target_bir_lowering=False)
        self.consts = host_consts()

    def din(self, name, shape, dt=F32):
        return self.nc.dram_tensor(name, list(shape), dt, kind="ExternalInput").ap()

    def build(self):
        nc = self.nc
        self.I = {k: self.din(k, s) for k, s in INPUT_SHAPES.items()}
        self.Cn = {k: self.din("k_" + k, v.shape, F32 if v.dtype == np.float32 else BF16) for k, v in self.consts.items()}
        self.out = nc.dram_tensor("out", [NL, D], F32, kind="ExternalOutput").ap()
        if self.dbg:
            self.dbg_ctx = nc.dram_tensor("dbg_ctx", [NCX, D], F32, kind="ExternalOutput").ap()
        self.xres = nc.dram_tensor("xres", [NT, D], F32, kind="Internal").ap()
        self.og = nc.dram_tensor("og", [8, 128, NT], BF16, kind="Internal").ap()
        self.zsd = nc.dram_tensor("zsd", [8, 128, NT], BF16, kind="Internal").ap()
        self.zsd_r = [[Res("zsd%d_%d" % (h, c)) for c in range(9)] for h in range(8)]
        self.xres_r = [Res("xres%d" % t) for t in range(NTI)]
        self.og_r = [[Res("og%d_%d" % (h, c)) for c in range(9)] for h in range(8)]
        self.out_r = [Res("out%d" % t) for t in range(NTI)]
        with ExitStack() as st:
            self.st = st
            fw = self.fw = FW(nc, st)
            self.setup_persistent()
            for i in range(self.n_layers):
                self.layer(i)
            fw.finish()
        return nc

    @staticmethod
    def chunk(ci):
        if ci == 0:
            return 0, NCX
        return NCX + (ci - 1) * 512, 512

    def xsrc(self, i, tt):
        if i == 0:
            if tt < 2:
                return self.I["ctx"][tt * 128:(tt + 1) * 128, :], []
            return self.I["x"][(tt - 2) * 128:(tt - 1) * 128, :], []
        return self.xres[tt * 128:(tt + 1) * 128, :], [self.xres_r[tt]]

    def setup_persistent(self):
        fw, st = self.fw, self.st
        self.ident, self.ident_r = fw.sb("ident", [128, 128], F32)
        fw.dma(self.ident[:], self.Cn["ident"][:, :], writes=[self.ident_r], sem_res=self.ident_r)
        self.identb, self.identb_r = fw.sb("identb", [128, 128], BF16)
        fw.op("vector", lambda e: e.tensor_copy(out=self.identb[:], in_=self.ident[:]), reads=[self.ident_r], writes=[self.identb_r])
        self.ones_b, self.ones_b_r = fw.sb("ones_b", [128, 128], BF16)
        fw.op("vector", lambda e: e.memset(self.ones_b[:], 1.0), writes=[self.ones_b_r])
        self.mean_b, self.mean_b_r = fw.sb("mean_b", [128, 128], BF16)
        fw.op("vector", lambda e: e.memset(self.mean_b[:], 1.0 / 128.0), writes=[self.mean_b_r])
        self.epsn, self.epsn_r = fw.sb("epsn", [128, 1], F32)
        fw.op("vector", lambda e: e.memset(self.epsn[:], NORM_EPS), writes=[self.epsn_r])
        self.epss, self.epss_r = fw.sb("epss", [128, 1], F32)
        fw.op("vector", lambda e: e.memset(self.epss[:], SUBLN_EPS), writes=[self.epss_r])
        self.gsc, self.gsc_r = fw.sb("gsc", [128, 8, 2], F32)
        self.shf, self.shf_r = fw.sb("shf", [128, 8, 2], F32)
        self.gate, self.gate_r = fw.sb("gate", [128, 2, D], F32)
        self.s_fm, self.s_fm_r = fw.sb("s_fm", [128, 8, 2], F32)
        self.pb = []
        for b in range(8):
            t, r = fw.ps("pb%d" % b, [128, 512], F32)
            self.pb.append((t, r))
        with ExitStack() as ts:
            cc, cc_r = fw.sb("cc", [2, D], F32, ts)
            fw.dma(cc[0:1, :], self.I["c"][:, :], writes=[cc_r], sem_res=cc_r)
            fw.dma(cc[1:2, :], self.I["c_ctx"][:, :], writes=[cc_r], sem_res=cc_r)
            fw.op("scalar", lambda e: e.activation(out=cc[:], in_=cc[:], func=AF.Silu), reads=[cc_r], writes=[cc_r])
            pt, pr = self.pb[0]
            for kc in range(8):
                fw.op("tensor", lambda e, kc=kc: e.transpose(pt[:, kc * 2:kc * 2 + 2], cc[:, kc * 128:(kc + 1) * 128], self.ident[0:2, 0:2]),
                      reads=[cc_r, self.ident_r], writes=[pr], signal=(kc == 7))
            fw.op("vector", lambda e: e.tensor_copy(out=self.s_fm[:].rearrange("p k w -> p (k w)"), in_=pt[:, 0:16]), reads=[pr], writes=[self.s_fm_r])
            fw.barrier()

    def load_fm(self, dst, dst_r, src2d, n, ts):
        fw = self.fw
        tmp, tmp_r = fw.sb("lfm_tmp%d" % fw.n_ins, [n, 128], F32, ts)
        fw.dma(tmp[:], src2d, writes=[tmp_r], sem_res=tmp_r)
        pt, pr = self.pb[1]
        fw.op("tensor", lambda e: e.transpose(pt[:, 0:n], tmp[:], self.ident[0:n, 0:n]), reads=[tmp_r, self.ident_r], writes=[pr])
        fw.op("vector", lambda e: e.tensor_copy(out=dst, in_=pt[:, 0:n]), reads=[pr], writes=[dst_r])

    def phase_mod(self, i):
        fw = self.fw
        W = self.I["ada_w"][i]
        with ExitStack() as ts:
            self.s_rep, self.s_rep_r = fw.sb("s_rep", [128, 8, 2, 128], F32, ts)
            for kc in range(8):
                for w in range(2):
                    fw.op("gpsimd", lambda e, kc=kc, w=w: e.tensor_copy(out=self.s_rep[:, kc, w, :], in_=self.s_fm[:, kc, w:w + 1].to_broadcast([128, 128])),
                          reads=[self.s_fm_r], writes=[self.s_rep_r])
            bfm, bfm_r = fw.sb("bfm", [128, 16], F32, ts)
            gfm, gfm_r = fw.sb("gfm", [128, 8], F32, ts)
            self.load_fm(bfm[:], bfm_r, self.I["ada_b"][i, 0:2048].rearrange("(n p) -> n p", p=128), 16, ts)
            self.load_fm(gfm[:], gfm_r, self.I["norm_gain"][i, :].rearrange("(n p) -> n p", p=128), 8, ts)
            bg, bg_r = fw.sb("bg", [128, D], F32, ts)
            fw.dma(bg[:], self.I["ada_b"][i, 2048:3072].partition_broadcast(128), writes=[bg_r], sem_res=bg_r)
            wt = [fw.sb("adaw%d" % k, [128, 8, 512], F32, ts) for k in range(2)]
            pt, pr = self.pb[2]
            for cc in range(6):
                w_t, w_r = wt[cc % 2]
                for kc in range(8):
                    fw.dma(w_t[:, kc, :], W[kc * 128:(kc + 1) * 128, cc * 512:(cc + 1) * 512], writes=[w_r], sem_res=w_r)
                if cc < 4:
                    for fl in range(4):
                        fc = cc * 4 + fl
                        for kc in range(8):
                            fw.op("tensor", lambda e, kc=kc, fl=fl, fc=fc, w_t=w_t: e.matmul(pt[:, fc * 2:fc * 2 + 2], lhsT=w_t[:, kc, fl * 128:(fl + 1) * 128], rhs=self.s_fm[:, kc, :], start=(kc == 0), stop=(kc == 7)),
                                  reads=[w_r, self.s_fm_r], writes=[pr], signal=(kc == 7 and fl == 3))
                    if cc == 3:
                        ps3 = pt[:, 0:32].rearrange("p (f w) -> p f w", w=2)
                        for w in range(2):
                            fw.op("vector", lambda e, w=w: e.tensor_tensor(out=self.shf[:, :, w], in0=ps3[:, 0:8, w], in1=bfm[:, 0:8], op=ALU.add),
                                  reads=[pr, bfm_r], writes=[self.shf_r])
                            fw.op("vector", lambda e, w=w: e.tensor_tensor(out=self.gsc[:, :, w], in0=ps3[:, 8:16, w], in1=bfm[:, 8:16], op=ALU.add),
                                  reads=[pr, bfm_r], writes=[self.gsc_r])
                            fw.op("vector", lambda e, w=w: e.scalar_tensor_tensor(out=self.gsc[:, :, w], in0=self.gsc[:, :, w], scalar=1.0, in1=gfm[:], op0=ALU.add, op1=ALU.mult),
                                  reads=[self.gsc_r, gfm_r], writes=[self.gsc_r])
                else:
                    cg = cc - 4
                    for w in range(2):
                        gp, gr = self.pb[3 + w]
                        for kc in range(8):
                            fw.op("tensor", lambda e, kc=kc, w=w, gp=gp, w_t=w_t: e.matmul(gp[:, :], lhsT=self.s_rep[:, kc, w, :], rhs=w_t[:, kc, :], start=(kc == 0), stop=(kc == 7)),
                                  reads=[w_r, self.s_rep_r], writes=[gr], signal=(kc == 7))
                        fw.op("vector", lambda e, w=w, gp=gp, cg=cg: e.tensor_tensor(out=self.gate[:, w, cg * 512:(cg + 1) * 512], in0=gp[:, :], in1=bg[:, cg * 512:(cg + 1) * 512], op=ALU.add),
                              reads=[gr, bg_r], writes=[self.gate_r])
            fw.barrier()

    def phase_norm(self, i, hT, hT_r, ts):
        fw = self.fw
        xt = [fw.sb("nx%d" % k, [128, D], F32, ts) for k in range(3)]
        sq, sq_r = fw.sb("nsq", [128, D], BF16, ts)
        ss = [fw.sb("nss%d" % k, [128, 1], F32, ts) for k in range(2)]
        dg = [fw.sb("ndg%d" % k, [128, 128], F32, ts) for k in range(2)]
        def tile_body(fx, tt):
            x_t, x_r = xt[tt % 3]
            s_t, s_r = ss[tt % 2]
            d_t, d_r = dg[tt % 2]
            w = 1 if tt < 2 else 0
            src, src_r = self.xsrc(i, tt)
            fx.dma(x_t[:], src, reads=src_r, writes=[x_r], sem_res=x_r)
            fx.op("scalar", lambda e, x_t=x_t, s_t=s_t: e.activation(out=sq[:], in_=x_t[:], func=AF.Square, accum_out=s_t[:]), reads=[x_r], writes=[sq_r, s_r])
            fx.op("scalar", lambda e, s_t=s_t: e.activation(out=s_t[:], in_=s_t[:], func=AF.Sqrt, bias=self.epsn[:], scale=1.0 / D), reads=[s_r, self.epsn_r], writes=[s_r])
            fx.op("vector", lambda e, s_t=s_t: e.reciprocal(out=s_t[:], in_=s_t[:]), reads=[s_r], writes=[s_r])
            fx.op("vector", lambda e, s_t=s_t, d_t=d_t: e.tensor_scalar(out=d_t[:], in0=self.ident[:], scalar1=s_t[:, 0:1], scalar2=None, op0=ALU.mult), reads=[s_r, self.ident_r], writes=[d_r])
            for half in range(2):
                pt, pr = self.pb[(tt * 2 + half) % 4]
                for k4 in range(4):
                    kc = half * 4 + k4
                    fx.op("tensor", lambda e, kc=kc, k4=k4, pt=pt, x_t=x_t, d_t=d_t: e.matmul(pt[:, k4 * 128:(k4 + 1) * 128], lhsT=x_t[:, kc * 128:(kc + 1) * 128], rhs=d_t[:], start=True, stop=True),
                          reads=[x_r, d_r], writes=[pr], signal=(k4 == 3))
                for k4 in range(4):
                    kc = half * 4 + k4
                    ek = "scalar" if k4 % 2 == 0 else "vector"
                    if ek == "scalar":
                        fx.op("scalar", lambda e, kc=kc, k4=k4, pt=pt, tt=tt, w=w: e.activation(out=hT[:, kc, tt * 128:(tt + 1) * 128], in_=pt[:, k4 * 128:(k4 + 1) * 128], func=AF.Identity, bias=self.shf[:, kc, w:w + 1], scale=self.gsc[:, kc, w:w + 1]),
                              reads=[pr, self.shf_r, self.gsc_r], writes=[hT_r[tt]])
                    else:
                        fx.op("vector", lambda e, kc=kc, k4=k4, pt=pt, tt=tt, w=w: e.tensor_scalar(out=hT[:, kc, tt * 128:(tt + 1) * 128], in0=pt[:, k4 * 128:(k4 + 1) * 128], scalar1=self.gsc[:, kc, w:w + 1], scalar2=self.shf[:, kc, w:w + 1], op0=ALU.mult, op1=ALU.add),
                              reads=[pr, self.shf_r, self.gsc_r], writes=[hT_r[tt]])

        for tt in range(0, NTI, 2):
            ra, rb = Rec(), Rec()
            tile_body(ra, tt)
            tile_body(rb, tt + 1)
            interleave(fw, [ra, rb])

    def phase_out(self, i, w_out, ts):
        fw = self.fw
        last = (i == DEPTH - 1)
        wst = [fw.sb("wo_st%d" % k, [128, D], F32, ts) for k in range(2)]
        wb, wb_r = fw.sb("wo_b", [128, 8, D], BF16, ts)
        for kc in range(8):
            s_t, s_r = wst[kc % 2]
            fw.dma(s_t[:], w_out[kc * 128:(kc + 1) * 128, :], writes=[s_r], sem_res=s_r)
            fw.op("gpsimd", lambda e, kc=kc, s_t=s_t: e.tensor_copy(out=wb[:, kc, :], in_=s_t[:]), reads=[s_r], writes=[wb_r])
        ogt = [fw.sb("po_og%d" % k, [128, 8, 512], BF16, ts) for k in range(2)]
        xt = [fw.sb("po_x%d" % k, [128, D], F32, ts) for k in range(2)]
        yt = [fw.sb("po_y%d" % k, [128, D], F32, ts) for k in range(2)]
        if last:
            fg, fg_r = fw.sb("po_fg", [128, D], F32, ts)
            fw.dma(fg[:], self.I["final_gain"][0, :].partition_broadcast(128), writes=[fg_r], sem_res=fg_r)
            sq, sq_r = fw.sb("po_sq", [128, D], BF16, ts)
            ss = [fw.sb("po_ss%d" % k, [128, 1], F32, ts) for k in range(2)]
        cis = range(1, 9) if last else range(9)
        n = 0
        for ci in cis:
            t0, tn = self.chunk(ci)
            o_t, o_r = ogt[ci % 2]
            for hc in range(8):
                fw.dma(o_t[:, hc, 0:tn], self.og[hc, :, t0:t0 + tn], reads=[self.og_r[hc][ci]], writes=[o_r], sem_res=o_r)
            def tile_body(fx, tl, n):
                tt = t0 // 128 + tl
                w = 1 if tt < 2 else 0
                x_t, x_r = xt[n % 2]
                y_t, y_r = yt[n % 2]
                src, src_r = self.xsrc(i, tt)
                fx.dma(x_t[:], src, reads=src_r, writes=[x_r], sem_res=x_r)
                for half in range(2):
                    pt, pr = self.pb[(n * 2 + half) % 4]
                    for hc in range(8):
                        fx.op("tensor", lambda e, hc=hc, half=half, pt=pt, o_t=o_t, tl=tl: e.matmul(pt[:, :], lhsT=o_t[:, hc, tl * 128:(tl + 1) * 128], rhs=wb[:, hc, half * 512:(half + 1) * 512], start=(hc == 0), stop=(hc == 7)),
                              reads=[o_r, wb_r], writes=[pr], signal=(hc == 7))
                    fx.op("vector", lambda e, half=half, pt=pt, y_t=y_t, w=w: e.tensor_tensor(out=y_t[:, half * 512:(half + 1) * 512], in0=pt[:, :], in1=self.gate[:, w, half * 512:(half + 1) * 512], op=ALU.mult),
                          reads=[pr, self.gate_r], writes=[y_r])
                fx.op("gpsimd", lambda e, y_t=y_t, x_t=x_t: e.tensor_tensor(out=y_t[:], in0=y_t[:], in1=x_t[:], op=ALU.add), reads=[y_r, x_r], writes=[y_r])
                if not last:
                    fx.dma(self.xres[tt * 128:(tt + 1) * 128, :], y_t[:], reads=[y_r], writes=[self.xres_r[tt]], sem_res=y_r)
                    if self.dbg and i == self.n_layers - 1:
                        if tt < 2:
                            fx.dma(self.dbg_ctx[tt * 128:(tt + 1) * 128, :], y_t[:], reads=[y_r], writes=[self.out_r[tt]], sem_res=y_r)
                        else:
                            fx.dma(self.out[(tt - 2) * 128:(tt - 1) * 128, :], y_t[:], reads=[y_r], writes=[self.out_r[tt]], sem_res=y_r)
                else:
                    s_t, s_r = ss[n % 2]
                    fx.op("scalar", lambda e, y_t=y_t, s_t=s_t: e.activation(out=sq[:], in_=y_t[:], func=AF.Square, accum_out=s_t[:]), reads=[y_r], writes=[sq_r, s_r])
                    fx.op("scalar", lambda e, s_t=s_t: e.activation(out=s_t[:], in_=s_t[:], func=AF.Sqrt, bias=self.epsn[:], scale=1.0 / D), reads=[s_r, self.epsn_r], writes=[s_r])
                    fx.op("vector", lambda e, s_t=s_t: e.reciprocal(out=s_t[:], in_=s_t[:]), reads=[s_r], writes=[s_r])
                    fx.op("vector", lambda e, y_t=y_t, s_t=s_t: e.scalar_tensor_tensor(out=y_t[:], in0=y_t[:], scalar=s_t[:, 0:1], in1=fg[:], op0=ALU.mult, op1=ALU.mult), reads=[y_r, s_r, fg_r], writes=[y_r])
                    fx.dma(self.out[(tt - 2) * 128:(tt - 1) * 128, :], y_t[:], reads=[y_r], writes=[self.out_r[tt]], sem_res=y_r)

            for tl in range(0, tn // 128, 2):
                ra, rb = Rec(), Rec()
                tile_body(ra, tl, n)
                tile_body(rb, tl + 1, n + 1)
                interleave(fw, [ra, rb])
                n += 2

    def layer(self, i):
        fw = self.fw
        kind = i % 3
        j = i // 3
        self.phase_mod(i)
        with ExitStack() as ts0:
            pre = None
            if kind == 1:
                pre = self.fnet_prealloc(ts0)
            elif kind == 2:
                pre = self.rwkv_prealloc(ts0)
            with ExitStack() as ts:
                hT, _ = fw.sb("hT", [128, 8, NT], BF16, ts)
                hT_r = [Res("hT%d" % t) for t in range(NTI)]
                with ExitStack() as ts2:
                    self.phase_norm(i, hT, hT_r, ts2)
                    fw.barrier()
                with ExitStack() as ts2:
                    if kind == 0:
                        self.mixer_attn(i, j, hT, hT_r, ts2)
                    elif kind == 1:
                        self.fnet_part1(i, j, hT, hT_r, pre, ts2)
                    else:
                        self.rwkv_part1(i, j, hT, hT_r, pre, ts2)
                    fw.barrier()
            if kind == 1:
                with ExitStack() as ts2:
                    self.fnet_part2(i, j, pre, ts2)
                    fw.barrier()
            elif kind == 2:
                self.rwkv_part2(i, j, pre, ts0)
        with ExitStack() as ts:
            w_out = {0: self.I["da_w_out"], 1: self.I["fn_w_out"], 2: self.I["rw_w_out"]}[kind][j]
            self.phase_out(i, w_out, ts)
            fw.barrier()

    def mixer_attn(self, i, j, hT, hT_r, ts):
        fw = self.fw
        last = (i == DEPTH - 1)
        lambda_init = 0.8 - 0.6 * math.exp(-0.3 * i)
        Win = self.I["da_w_in"][j]
        rC, rC_r = fw.sb("ropeC", [128, NT], F32, ts)
        rS, rS_r = fw.sb("ropeS", [128, NT], F32, ts)
        fw.dma(rC[:], self.Cn["ropeC"][:, :], writes=[rC_r], sem_res=rC_r)
        fw.dma(rS[:], self.Cn["ropeS"][:, :], writes=[rS_r], sem_res=rS_r)
        nlam, nlam_r = fw.sb("nlam", [128, 1], F32, ts)
        lq, lq_r = fw.sb("lq", [128, 128], F32, ts)
        lk, lk_r = fw.sb("lk", [128, 128], F32, ts)
        l2, l2_r = fw.sb("l2", [128, 2], F32, ts)
        fw.dma(lq[:], self.I["da_lam_q"][j, :].partition_broadcast(128), writes=[lq_r], sem_res=lq_r)
        fw.dma(lk[:], self.I["da_lam_k"][j, :].partition_broadcast(128), writes=[lk_r], sem_res=lk_r)
        fw.op("vector", lambda e: e.tensor_tensor(out=lq[:], in0=lq[:], in1=lk[:], op=ALU.mult), reads=[lq_r, lk_r], writes=[lq_r])
        fw.op("vector", lambda e: e.tensor_reduce(out=l2[:], in_=lq[:].rearrange("p (z d) -> p z d", z=2), axis=AX.X, op=ALU.add), reads=[lq_r], writes=[l2_r])
        fw.op("scalar", lambda e: e.activation(out=l2[:], in_=l2[:], func=AF.Exp), reads=[l2_r], writes=[l2_r])
        fw.op("vector", lambda e: e.tensor_tensor(out=nlam[:], in0=l2[:, 1:2], in1=l2[:, 0:1], op=ALU.subtract), reads=[l2_r], writes=[nlam_r])
        fw.op("vector", lambda e: e.tensor_scalar(out=nlam[:], in0=nlam[:], scalar1=-lambda_init, scalar2=None, op0=ALU.add), reads=[nlam_r], writes=[nlam_r])
        sg, sg_r = fw.sb("sg", [128, 1], F32, ts)
        self.load_fm(sg[:], sg_r, self.I["da_subln_gain"][j:j + 1, :], 1, ts)
        fw.op("vector", lambda e: e.tensor_scalar(out=sg[:], in0=sg[:], scalar1=1.0 - lambda_init, scalar2=None, op0=ALU.mult), reads=[sg_r], writes=[sg_r])

        wst = [fw.sb("aw_st%d" % k, [128, 8, 128], F32, ts) for k in range(2)]
        WS = [{nm: fw.sb("%s_%d" % (nm, k), [128, 8, 128], BF16, ts) for nm in ("wq", "wq2", "wk", "wk2", "wv", "wz")} for k in range(2)]
        qT, qT_r = fw.sb("qT", [128, NT], BF16, ts)
        kT, kT_r = None, None
        kTz = [fw.sb("kTz%d" % z, [128, NT], BF16, ts) for z in range(2)]
        for z in range(2):
            fw.op("gpsimd", lambda e, z=z: e.memset(kTz[z][0][:], 0.0), writes=[kTz[z][1]])
        zs, zs_r = fw.sb("zs", [128, NT], BF16, ts)
        vt, vt_r = fw.sb("vt", [128, NTI, 128], BF16, ts)
        t1 = [fw.sb("rp1_%d" % k, [128, 512], F32, ts) for k in range(1)] * 2
        t2 = [fw.sb("rp2_%d" % k, [128, 512], F32, ts) for k in range(1)] * 2
        Eb = [fw.sb("E%d" % k, [128, 512], BF16, ts) for k in range(4)]
        r1, r1_r = fw.sb("ep_r1", [128, 512], F32, ts)
        a1, a1_r = fw.sb("ep_a1", [128, 512], F32, ts)
        r2, r2_r = r1, r1_r
        a2, a2_r = fw.sb("ep_a2", [128, 512], F32, ts)
        sqb, sqb_r = fw.sb("ep_sq", [128, 512], BF16, ts)
        rs, rs_r = r1, r1_r
        ogt = [fw.sb("ep_og%d" % k, [128, 512], BF16, ts) for k in range(2)]

        def rot_cast(dst, dst2, dst_r, dst2_r, s_t, s_r, scale):
            fw.op("gpsimd", lambda e: e.tensor_scalar(out=dst[:], in0=s_t[:], scalar1=scale, scalar2=None, op0=ALU.mult), reads=[s_r], writes=[dst_r])
            sv = s_t[:].rearrange("p k (b h j) -> p k b h j", b=4, h=2)
            dv = dst2[:].rearrange("p k (b h j) -> p k b h j", b=4, h=2)
            for kc in range(0, 8, 4):
                fw.op("gpsimd", lambda e, kc=kc: e.tensor_scalar(out=dv[:, kc:kc + 4, :, 0, :], in0=sv[:, kc:kc + 4, :, 1, :], scalar1=-scale, scalar2=None, op0=ALU.mult), reads=[s_r], writes=[dst2_r])
                fw.op("gpsimd", lambda e, kc=kc: e.tensor_scalar(out=dv[:, kc:kc + 4, :, 1, :], in0=sv[:, kc:kc + 4, :, 0, :], scalar1=scale, scalar2=None, op0=ALU.mult), reads=[s_r], writes=[dst2_r])

        nst = [0]

        def load_weights(hd):
            Wd = WS[hd % 2]
            for which in range(4):
                s_t, s_r = wst[nst[0] % 2]
                nst[0] += 1
                col0 = which * D + hd * 128
                fw.dma(s_t[:], Win[:, col0:col0 + 128].rearrange("(k p) c -> p k c", p=128), writes=[s_r], sem_res=s_r)
                if which == 0:
                    rot_cast(Wd["wq"][0], Wd["wq2"][0], Wd["wq"][1], Wd["wq2"][1], s_t, s_r, 0.125)
                elif which == 1:
                    rot_cast(Wd["wk"][0], Wd["wk2"][0], Wd["wk"][1], Wd["wk2"][1], s_t, s_r, 1.0)
                elif which == 2:
                    fw.op("gpsimd", lambda e, s_t=s_t, d=Wd["wv"][0]: e.tensor_copy(out=d[:], in_=s_t[:]), reads=[s_r], writes=[Wd["wv"][1]])
                else:
                    fw.op("gpsimd", lambda e, s_t=s_t, d=Wd["wz"][0]: e.tensor_copy(out=d[:], in_=s_t[:]), reads=[s_r], writes=[Wd["wz"][1]])

        ncnt = 0
        load_weights(0)
        for hd in range(8):
            Wd = WS[hd % 2]
            (wq, wq_r), (wq2, wq2_r), (wk, wk_r), (wk2, wk2_r), (wv, wv_r), (wz, wz_r) = [Wd[nm] for nm in ("wq", "wq2", "wk", "wk2", "wv", "wz")]
            for ci in range(9):
                t0, tn = self.chunk(ci)
                tts = list(range(t0 // 128, (t0 + tn) // 128))
                hrs = [hT_r[t] for t in tts]
                banks = [self.pb[(ci * 6 + b) % 8] for b in range(6)]
                for b, (wt_, wr_) in enumerate([(wq, wq_r), (wq2, wq2_r), (wk, wk_r), (wk2, wk2_r), (wz, wz_r)]):
                    pt, pr = banks[b]
                    for kc in range(8):
                        fw.op("tensor", lambda e, kc=kc, pt=pt, wt_=wt_, t0=t0, tn=tn: e.matmul(pt[:, 0:tn], lhsT=wt_[:, kc, :], rhs=hT[:, kc, t0:t0 + tn], start=(kc == 0), stop=(kc == 7)),
                              reads=[wr_] + hrs, writes=[pr], signal=(kc == 7))
                pt, pr = banks[5]
                for tl, tt in enumerate(tts):
                    for kc in range(8):
                        fw.op("tensor", lambda e, kc=kc, pt=pt, tl=tl, tt=tt, wv=wv: e.matmul(pt[:, tl * 128:(tl + 1) * 128], lhsT=hT[:, kc, tt * 128:(tt + 1) * 128], rhs=wv[:, kc, :], start=(kc == 0), stop=(kc == 7)),
                              reads=[wv_r, hT_r[tt]], writes=[pr], signal=(kc == 7 and tl == len(tts) - 1))
                for b0, dst, dst_r in ((0, qT, qT_r), (2, kT, kT_r)):
                    p1, p1r = banks[b0]
                    p2, p2r = banks[b0 + 1]
                    ta, ta_r = t1[ncnt % 2]
                    tb, tb_r = t2[ncnt % 2]
                    ncnt += 1
                    fw.op("vector", lambda e, p1=p1, ta=ta, t0=t0, tn=tn: e.tensor_tensor(out=ta[:, 0:tn], in0=p1[:, 0:tn], in1=rC[:, t0:t0 + tn], op=ALU.mult), reads=[p1r, rC_r], writes=[ta_r])
                    fw.op("vector", lambda e, p2=p2, tb=tb, t0=t0, tn=tn: e.tensor_tensor(out=tb[:, 0:tn], in0=p2[:, 0:tn], in1=rS[:, t0:t0 + tn], op=ALU.mult), reads=[p2r, rS_r], writes=[tb_r])
                    if b0 == 0:
                        fw.op("gpsimd", lambda e, ta=ta, tb=tb, dst=dst, t0=t0, tn=tn: e.tensor_tensor(out=dst[:, t0:t0 + tn], in0=ta[:, 0:tn], in1=tb[:, 0:tn], op=ALU.add), reads=[ta_r, tb_r], writes=[dst_r])
                    else:
                        for z in range(2):
                            zr = slice(z * 64, (z + 1) * 64)
                            fw.op("gpsimd", lambda e, ta=ta, tb=tb, z=z, zr=zr, t0=t0, tn=tn: e.tensor_tensor(out=kTz[z][0][zr, t0:t0 + tn], in0=ta[zr, 0:tn], in1=tb[zr, 0:tn], op=ALU.add), reads=[ta_r, tb_r], writes=[kTz[z][1]])
                p4, p4r = banks[4]
                fw.op("scalar", lambda e, p4=p4, t0=t0, tn=tn: e.activation(out=zs[:, t0:t0 + tn], in_=p4[:, 0:tn], func=AF.Silu), reads=[p4r], writes=[zs_r])
                p5, p5r = banks[5]
                fw.op("vector", lambda e, p5=p5, t0=t0, tn=tn: e.tensor_copy(out=vt[:, t0 // 128:(t0 + tn) // 128, :].rearrange("p t e -> p (t e)"), in_=p5[:, 0:tn]), reads=[p5r], writes=[vt_r])
            if hd + 1 < 8:
                load_weights(hd + 1)
            groups = ([] if last else [(0, list(range(2)))]) + [(ci, list(range(NTI))) for ci in range(1, 9)]
            items = []
            for gi, (ci, kts) in enumerate(groups):
                for z in range(2):
                    for idx, kt in enumerate(kts):
                        items.append((gi, ci, z, idx, kt, len(kts)))
            LA = 2
            Ebuf = {}

            def emit_front(n):
                gi, ci, z, idx, kt, nk = items[n]
                q0, qn = self.chunk(ci)
                sp, spr = self.pb[n % 3]
                E_t, E_r = Eb[n % len(Eb)]
                Ebuf[n] = (E_t, E_r)
                fw.op("tensor", lambda e, sp=sp, kt=kt, z=z, q0=q0, qn=qn: e.matmul(sp[:, 0:qn], lhsT=kTz[z][0][:, kt * 128:(kt + 1) * 128], rhs=qT[:, q0:q0 + qn], start=True, stop=True),
                      reads=[kTz[z][1], qT_r], writes=[spr])
                fw.op("scalar", lambda e, sp=sp, E_t=E_t, qn=qn: e.activation(out=E_t[:, 0:qn], in_=sp[:, 0:qn], func=AF.Exp), reads=[spr], writes=[E_r])

            def emit_back(n):
                gi, ci, z, idx, kt, nk = items[n]
                q0, qn = self.chunk(ci)
                E_t, E_r = Ebuf.pop(n)
                po, por = self.pb[3 + z * 2]
                psm, psr = self.pb[4 + z * 2]
                fw.op("tensor", lambda e, po=po, E_t=E_t, kt=kt, qn=qn, idx=idx, nk=nk: e.matmul(po[:, 0:qn], lhsT=vt[:, kt, :], rhs=E_t[:, 0:qn], start=(idx == 0), stop=(idx == nk - 1)),
                      reads=[vt_r, E_r], writes=[por], signal=False)
                fw.op("tensor", lambda e, psm=psm, E_t=E_t, qn=qn, idx=idx, nk=nk: e.matmul(psm[:, 0:qn], lhsT=self.ones_b[:], rhs=E_t[:, 0:qn], start=(idx == 0), stop=(idx == nk - 1)),
                      reads=[self.ones_b_r, E_r], writes=[psr])
                if z == 1 and idx == nk - 1:
                    epilogue(ci)

            def epilogue(ci):
                q0, qn = self.chunk(ci)
                po1, por1 = self.pb[3]
                ps1, psr1 = self.pb[4]
                po2, por2 = self.pb[5]
                ps2, psr2 = self.pb[6]
                fw.op("vector", lambda e, qn=qn: e.reciprocal(out=r1[:, 0:qn], in_=ps1[:, 0:qn]), reads=[psr1], writes=[r1_r])
                fw.op("vector", lambda e, qn=qn: e.tensor_tensor(out=a1[:, 0:qn], in0=po1[:, 0:qn], in1=r1[:, 0:qn], op=ALU.mult), reads=[por1, r1_r], writes=[a1_r])
                fw.op("vector", lambda e, qn=qn: e.reciprocal(out=r2[:, 0:qn], in_=ps2[:, 0:qn]), reads=[psr2], writes=[r2_r])
                fw.op("vector", lambda e, qn=qn: e.tensor_tensor(out=a2[:, 0:qn], in0=po2[:, 0:qn], in1=r2[:, 0:qn], op=ALU.mult), reads=[por2, r2_r], writes=[a2_r])
                fw.op("vector", lambda e, qn=qn: e.scalar_tensor_tensor(out=a1[:, 0:qn], in0=a2[:, 0:qn], scalar=nlam[:, 0:1], in1=a1[:, 0:qn], op0=ALU.mult, op1=ALU.add), reads=[a1_r, a2_r, nlam_r], writes=[a1_r])
                fw.op("vector", lambda e, qn=qn: e.tensor_tensor(out=sqb[:, 0:qn], in0=a1[:, 0:qn], in1=a1[:, 0:qn], op=ALU.mult), reads=[a1_r], writes=[sqb_r])
                mp, mpr = self.pb[7]
                fw.op("tensor", lambda e, qn=qn: e.matmul(mp[:, 0:qn], lhsT=self.mean_b[:], rhs=sqb[:, 0:qn], start=True, stop=True), reads=[self.mean_b_r, sqb_r], writes=[mpr])
                fw.op("scalar", lambda e, qn=qn: e.activation(out=rs[:, 0:qn], in_=mp[:, 0:qn], func=AF.Ln, bias=self.epss[:], scale=1.0), reads=[mpr, self.epss_r], writes=[rs_r])
                fw.op("scalar", lambda e, qn=qn: e.activation(out=rs[:, 0:qn], in_=rs[:, 0:qn], func=AF.Exp, scale=-0.5), reads=[rs_r], writes=[rs_r])
                fw.op("vector", lambda e, qn=qn: e.tensor_tensor(out=a1[:, 0:qn], in0=a1[:, 0:qn], in1=rs[:, 0:qn], op=ALU.mult), reads=[a1_r, rs_r], writes=[a1_r])
                og_t, og_r_ = ogt[ci % 2]
                fw.op("vector", lambda e, og_t=og_t, q0=q0, qn=qn: e.scalar_tensor_tensor(out=og_t[:, 0:qn], in0=a1[:, 0:qn], scalar=sg[:, 0:1], in1=zs[:, q0:q0 + qn], op0=ALU.mult, op1=ALU.mult), reads=[a1_r, sg_r, zs_r], writes=[og_r_])
                fw.dma(self.og[hd, :, q0:q0 + qn], og_t[:, 0:qn], reads=[og_r_], writes=[self.og_r[hd][ci]], sem_res=og_r_)

            for n in range(len(items) + LA):
                if n < len(items):
                    emit_front(n)
                if n - LA >= 0:
                    emit_back(n - LA)

    def load_w_bf16(self, dst, dst_r, w2d, ncols, ts, name):
        fw = self.fw
        wst = [fw.sb("%s_st%d" % (name, k), [128, ncols], F32, ts) for k in range(2)]
        for kc in range(8):
            s_t, s_r = wst[kc % 2]
            fw.dma(s_t[:], w2d[kc * 128:(kc + 1) * 128, :], writes=[s_r], sem_res=s_r)
            fw.op("gpsimd", lambda e, kc=kc, s_t=s_t: e.tensor_copy(out=dst[:, kc, :], in_=s_t[:]), reads=[s_r], writes=[dst_r])

    def gate_proj(self, wz, wz_r, hT, hT_r, ts):
        fw = self.fw
        zt = [fw.sb("gz%d" % k, [128, 512], BF16, ts) for k in range(3)]
        n = 0
        for ci in range(9):
            t0, tn = self.chunk(ci)
            hrs = [hT_r[t] for t in range(t0 // 128, (t0 + tn) // 128)]
            for fc in range(8):
                pt, pr = self.pb[n % 4]
                z_t, z_r = zt[n % 3]
                n += 1
                for kc in range(8):
                    fw.op("tensor", lambda e, kc=kc, pt=pt, fc=fc, t0=t0, tn=tn: e.matmul(pt[:, 0:tn], lhsT=wz[:, kc, fc * 128:(fc + 1) * 128], rhs=hT[:, kc, t0:t0 + tn], start=(kc == 0), stop=(kc == 7)),
                          reads=[wz_r] + hrs, writes=[pr], signal=(kc == 7))
                fw.op("scalar", lambda e, pt=pt, z_t=z_t, tn=tn: e.activation(out=z_t[:, 0:tn], in_=pt[:, 0:tn], func=AF.Silu), reads=[pr], writes=[z_r])
                fw.dma(self.zsd[fc, :, t0:t0 + tn], z_t[:, 0:tn], reads=[z_r], writes=[self.zsd_r[fc][ci]], sem_res=z_r)

    def fnet_prealloc(self, ts):
        fw = self.fw
        Utm, _ = fw.sb("Utm", [128, NTI, D], BF16, ts)
        Utm_r = [Res("Utm%d" % t) for t in range(NTI)]
        return Utm, Utm_r

    def fnet_part1(self, i, j, hT, hT_r, pre, ts):
        fw = self.fw
        Utm, Utm_r = pre
        Win = self.I["fn_w_in"][j]
        wu, wu_r = fw.sb("fwu", [128, 8, D], BF16, ts)
        wz, wz_r = fw.sb("fwz", [128, 8, D], BF16, ts)
        self.load_w_bf16(wu, wu_r, Win[:, 0:D], D, ts, "fwu")
        self.load_w_bf16(wz, wz_r, Win[:, D:2 * D], D, ts, "fwz")
        n = 0
        for tt in range(NTI):
            for half in range(2):
                pt, pr = self.pb[4 + n % 4]
                for kc in range(8):
                    fw.op("tensor", lambda e, kc=kc, pt=pt, tt=tt, half=half: e.matmul(pt[:, :], lhsT=hT[:, kc, tt * 128:(tt + 1) * 128], rhs=wu[:, kc, half * 512:(half + 1) * 512], start=(kc == 0), stop=(kc == 7)),
                          reads=[wu_r, hT_r[tt]], writes=[pr], signal=(kc == 7))
                if n % 2 == 0:
                    fw.op("vector", lambda e, pt=pt, tt=tt, half=half: e.tensor_copy(out=Utm[:, tt, half * 512:(half + 1) * 512], in_=pt[:, :]), reads=[pr], writes=[Utm_r[tt]])
                else:
                    fw.op("scalar", lambda e, pt=pt, tt=tt, half=half: e.copy(out=Utm[:, tt, half * 512:(half + 1) * 512], in_=pt[:, :]), reads=[pr], writes=[Utm_r[tt]])
                n += 1
        self.gate_proj(wz, wz_r, hT, hT_r, ts)

    def fnet_part2(self, i, j, pre, ts):
        fw = self.fw
        Utm, Utm_r = pre
        cc_, cc_r = fw.sb("fCc", [128, 128], BF16, ts)
        sc_, sc_r = fw.sb("fSc", [128, 128], BF16, ts)
        fw.dma(cc_[:], self.Cn["dftCc"][:, :], writes=[cc_r], sem_res=cc_r)
        fw.dma(sc_[:], self.Cn["dftnSc"][:, :], writes=[sc_r], sem_res=sc_r)
        wgs, wgs_r = fw.sb("fwg_st", [128, 8, 128], F32, ts)
        wg, wg_r = fw.sb("fwg", [128, 8, 128], BF16, ts)
        fw.dma(wgs[:], self.I["fn_w_group"][j].rearrange("g c e -> c g e"), writes=[wgs_r], sem_res=wgs_r)
        fw.op("gpsimd", lambda e: e.tensor_copy(out=wg[:], in_=wgs[:]), reads=[wgs_r], writes=[wg_r])
        CB, _ = fw.sb("fCB", [128, 32, 512], BF16, ts)
        SB, _ = fw.sb("fSB", [128, 32, 512], BF16, ts)
        CB_r = [fw.res("fCB%d" % k, ts) for k in range(4)]
        SB_r = [fw.res("fSB%d" % k, ts) for k in range(4)]
        Pb = [fw.sb("fPb%d" % k, [128, 512], BF16, ts) for k in range(2)]
        Qb = [fw.sb("fQb%d" % k, [128, 512], BF16, ts) for k in range(2)]
        Fb = [fw.sb("fFb%d" % k, [128, 512], BF16, ts) for k in range(2)]
        zt = [fw.sb("fzt%d" % k, [128, 512], BF16, ts) for k in range(2)]
        ot = [fw.sb("fot%d" % k, [128, 512], BF16, ts) for k in range(2)]
        last = (i == DEPTH - 1)
        items = []

        def bufs(n):
            return (self.pb[(n % 2) * 2], self.pb[(n % 2) * 2 + 1], self.pb[4 + n % 2], self.pb[6 + n % 2],
                    Pb[n % 2], Qb[n % 2], Fb[n % 2], zt[n % 2], ot[n % 2])

        def front(n):
            ci, g, t0, tn, nlt, lt0 = items[n]
            (pp, ppr), (qp, qpr), _, _, (P_t, P_r), (Q_t, Q_r), _, (z_t, z_r), _ = bufs(n)
            fw.dma(z_t[:, 0:tn], self.zsd[g, :, t0:t0 + tn], reads=[self.zsd_r[g][ci]], writes=[z_r], sem_res=z_r)
            for (acc, accr, buf, bufr) in ((pp, ppr, CB, CB_r), (qp, qpr, SB, SB_r)):
                for lt in range(nlt):
                    fw.op("tensor", lambda e, acc=acc, buf=buf, lt=lt: e.matmul(acc[:, 0:tn], lhsT=Utm[:, lt0 + lt, g * 128:(g + 1) * 128], rhs=buf[:, lt, 0:tn], start=(lt == 0), stop=(lt == nlt - 1)),
                          reads=[Utm_r[lt0 + lt], bufr[lt // 8]], writes=[accr], signal=(lt == nlt - 1 or lt % 8 == 7))
            fw.op("scalar", lambda e: e.copy(out=P_t[:, 0:tn], in_=pp[:, 0:tn]), reads=[ppr], writes=[P_r])
            fw.op("vector", lambda e: e.tensor_copy(out=Q_t[:, 0:tn], in_=qp[:, 0:tn]), reads=[qpr], writes=[Q_r])

        def back(n):
            ci, g, t0, tn, nlt, lt0 = items[n]
            _, _, (fp, fpr), (yp, ypr), (P_t, P_r), (Q_t, Q_r), (F_t, F_r), (z_t, z_r), (o_t, o_r) = bufs(n)
            fw.op("tensor", lambda e: e.matmul(fp[:, 0:tn], lhsT=cc_[:], rhs=P_t[:, 0:tn], start=True, stop=False), reads=[cc_r, P_r], writes=[fpr], signal=False)
            fw.op("tensor", lambda e: e.matmul(fp[:, 0:tn], lhsT=sc_[:], rhs=Q_t[:, 0:tn], start=False, stop=True), reads=[sc_r, Q_r], writes=[fpr])
            fw.op("vector", lambda e: e.tensor_copy(out=F_t[:, 0:tn], in_=fp[:, 0:tn]), reads=[fpr], writes=[F_r])
            fw.op("tensor", lambda e: e.matmul(yp[:, 0:tn], lhsT=wg[:, g, :], rhs=F_t[:, 0:tn], start=True, stop=True), reads=[wg_r, F_r], writes=[ypr])
            fw.op("vector", lambda e: e.tensor_tensor(out=o_t[:, 0:tn], in0=yp[:, 0:tn], in1=z_t[:, 0:tn], op=ALU.mult), reads=[ypr, z_r], writes=[o_r])
            fw.dma(self.og[g, :, t0:t0 + tn], o_t[:, 0:tn], reads=[o_r], writes=[self.og_r[g][ci]], sem_res=o_r)

        for ci in (range(1, 9) if last else range(9)):
            t0, tn = self.chunk(ci)
            if ci == 0:
                nlt, lt0 = 2, 0
                for k in range(2):
                    fw.dma(CB[:, k, 0:256], self.Cn["dftCX"][k * 128:(k + 1) * 128, :], writes=[CB_r[0]], sem_res=CB_r[0])
                    fw.dma(SB[:, k, 0:256], self.Cn["dftSX"][k * 128:(k + 1) * 128, :], writes=[SB_r[0]], sem_res=SB_r[0])
            else:
                nlt, lt0 = 32, 2
                c0 = (ci - 1) * 512
                for k in range(4):
                    fw.dma(CB[:, k * 8:(k + 1) * 8, :], self.Cn["dftCL"][k * 1024:(k + 1) * 1024, c0:c0 + 512].rearrange("(t p) c -> p t c", p=128), writes=[CB_r[k]], sem_res=CB_r[k])
                    fw.dma(SB[:, k * 8:(k + 1) * 8, :], self.Cn["dftSL"][k * 1024:(k + 1) * 1024, c0:c0 + 512].rearrange("(t p) c -> p t c", p=128), writes=[SB_r[k]], sem_res=SB_r[k])
            for g in range(8):
                items.append((ci, g, t0, tn, nlt, lt0))
                front(len(items) - 1)
                if len(items) >= 2:
                    back(len(items) - 2)
        back(len(items) - 1)

    def rwkv_prealloc(self, ts):
        fw = self.fw
        nc = self.nc
        R = {}
        R["Dall"] = [fw.sb("Dall%d" % z, [128, 8, 68], F32, ts) for z in range(2)]
        R["stack"] = ts.enter_context(ExitStack())
        R["twd"] = fw.sb("twd", [128, NT], F32, R["stack"])
        R["adT"] = fw.sb("adT", [128, NT], F32, R["stack"])
        if not hasattr(self, "sTd"):
            dk = "ExternalOutput" if self.dbg else "Internal"
            self.sTd = nc.dram_tensor("sTd", [24, 128, NT], F32, kind=dk).ap()
            self.FMd = [nc.dram_tensor("FMd%d" % z, [NTI, 128, 8, 4, 128], BF16, kind="Internal").ap() for z in range(2)]
            self.TMd = [nc.dram_tensor("TMd%d" % z, [NTI, 128, 2, 8, 128], BF16, kind="Internal").ap() for z in range(2)]
            self.Vd = nc.dram_tensor("Vd", [NTI, 128, 8, 128], BF16, kind="Internal").ap()
            self.bonusd = nc.dram_tensor("bonusd", [8, 128, NT], F32, kind=dk).ap()
            self.yd = [nc.dram_tensor("yd%d" % z, [NT, D], F32, kind=dk).ap() for z in range(2)]
        return R

    def rwkv_part1(self, i, j, hT, hT_r, R, ts):
        fw = self.fw
        W = self.I["rw_w_in"][j]
        twd, twd_r = R["twd"]
        adT, adT_r = R["adT"]
        wz, wz_r = fw.sb("rwz", [128, 8, D], BF16, ts)
        self.load_w_bf16(wz, wz_r, W[:, 3328:4352], D, ts, "rwz")
        self.gate_proj(wz, wz_r, hT, hT_r, ts)
        mu_fm, mu_r = fw.sb("mu_fm", [128, 26], F32, ts)
        self.load_fm(mu_fm[:], mu_r, self.I["rw_mu"][j, :].rearrange("(n p) -> n p", p=128), 26, ts)
        ca, ca_r = fw.sb("mix_a", [128, 26], F32, ts)
        cb, cb_r = fw.sb("mix_b", [128, 26], F32, ts)
        fw.op("vector", lambda e: e.tensor_scalar(out=ca[:], in0=mu_fm[:], scalar1=-1.0, scalar2=1.0, op0=ALU.mult, op1=ALU.add), reads=[mu_r], writes=[ca_r])
        fw.op("vector", lambda e: e.tensor_scalar(out=cb[:], in0=mu_fm[:], scalar1=0.5, scalar2=None, op0=ALU.mult), reads=[mu_r], writes=[cb_r])
        NP = NT + 3
        sraw = [fw.sb("sraw%d" % k, [128, NP], F32, ts) for k in range(2)]
        for k in range(2):
            fw.op("gpsimd", lambda e, k=k: e.memset(sraw[k][0][:], 0.0), writes=[sraw[k][1]])
        wst = [fw.sb("rw_st%d" % k, [128, 8, 128], F32, ts) for k in range(2)]
        wbs = [fw.sb("rw_wb%d" % k, [128, 8, 128], BF16, ts) for k in range(2)]
        PW = 512
        tmpb = [fw.sb("mixt%d" % k, [128, PW], F32, ts) for k in range(2)]
        smxb = [fw.sb("mixs%d" % k, [128, PW], F32, ts) for k in range(2)]
        npc = 0
        nev = 0
        def f1_weights(fc):
            s_t, s_r = wst[fc % 2]
            w_t, w_r = wbs[fc % 2]
            fw.dma(s_t[:], W[:, fc * 128:(fc + 1) * 128].rearrange("(k p) c -> p k c", p=128), writes=[s_r], sem_res=s_r)
            fw.op("gpsimd", lambda e: e.tensor_copy(out=w_t[:], in_=s_t[:]), reads=[s_r], writes=[w_r])

        f1_weights(0)
        for fc in range(26):
            w_t, w_r = wbs[fc % 2]
            sr_t, sr_r = sraw[fc % 2]
            if fc + 1 < 26:
                f1_weights(fc + 1)
            for ci in range(9):
                t0, tn = self.chunk(ci)
                off = t0 + (1 if ci == 0 else 2)
                hrs = [hT_r[t] for t in range(t0 // 128, (t0 + tn) // 128)]
                pt, pr = self.pb[4 + nev % 4]
                for kc in range(8):
                    fw.op("tensor", lambda e, kc=kc, pt=pt, w_t=w_t, t0=t0, tn=tn: e.matmul(pt[:, 0:tn], lhsT=w_t[:, kc, :], rhs=hT[:, kc, t0:t0 + tn], start=(kc == 0), stop=(kc == 7)),
                          reads=[w_r] + hrs, writes=[pr], signal=(kc == 7))
                if nev % 2 == 0:
                    fw.op("scalar", lambda e, pt=pt, sr_t=sr_t, off=off, tn=tn: e.copy(out=sr_t[:, off:off + tn], in_=pt[:, 0:tn]), reads=[pr], writes=[sr_r])
                else:
                    fw.op("vector", lambda e, pt=pt, sr_t=sr_t, off=off, tn=tn: e.tensor_copy(out=sr_t[:, off:off + tn], in_=pt[:, 0:tn]), reads=[pr], writes=[sr_r])
                nev += 1
            for (c0, cn, tok0) in [(1, 256, 0)] + [(258 + q * 512, 512, 256 + q * 512) for q in range(8)]:
                tm_t, tm_r = tmpb[npc % 2]
                sm_t, sm_r = smxb[npc % 2]
                npc += 1
                fw.op("gpsimd", lambda e, tm_t=tm_t, sr_t=sr_t, c0=c0, cn=cn: e.tensor_tensor(out=tm_t[:, 0:cn], in0=sr_t[:, c0 - 1:c0 - 1 + cn], in1=sr_t[:, c0 + 1:c0 + 1 + cn], op=ALU.add), reads=[sr_r], writes=[tm_r])
                fw.op("vector", lambda e, tm_t=tm_t, fc=fc, cn=cn: e.tensor_scalar(out=tm_t[:, 0:cn], in0=tm_t[:, 0:cn], scalar1=cb[:, fc:fc + 1], scalar2=None, op0=ALU.mult), reads=[tm_r, cb_r], writes=[tm_r])
                if fc < 24:
                    fw.op("vector", lambda e, tm_t=tm_t, sm_t=sm_t, sr_t=sr_t, fc=fc, c0=c0, cn=cn: e.scalar_tensor_tensor(out=sm_t[:, 0:cn], in0=sr_t[:, c0:c0 + cn], scalar=ca[:, fc:fc + 1], in1=tm_t[:, 0:cn], op0=ALU.mult, op1=ALU.add),
                          reads=[sr_r, tm_r, ca_r], writes=[sm_r])
                    fw.dma(self.sTd[fc, :, tok0:tok0 + cn], sm_t[:, 0:cn], reads=[sm_r], sem_res=sm_r)
                else:
                    dst, dst_r = (twd, twd_r) if fc == 24 else (adT, adT_r)
                    fw.op("vector", lambda e, tm_t=tm_t, dst=dst, sr_t=sr_t, fc=fc, c0=c0, cn=cn, tok0=tok0: e.scalar_tensor_tensor(out=dst[:, tok0:tok0 + cn], in0=sr_t[:, c0:c0 + cn], scalar=ca[:, fc:fc + 1], in1=tm_t[:, 0:cn], op0=ALU.mult, op1=ALU.add),
                          reads=[sr_r, tm_r, ca_r], writes=[dst_r])
                    if fc == 24:
                        fw.op("scalar", lambda e, tok0=tok0, cn=cn: e.activation(out=twd[:, tok0:tok0 + cn], in_=twd[:, tok0:tok0 + cn], func=AF.Tanh), reads=[twd_r], writes=[twd_r])

    def rwkv_part2(self, i, j, R, ts0):
        fw = self.fw
        stop = getattr(self, "stop_at", 99)
        fw.barrier()
        if stop < 2:
            return
        with ExitStack() as ts:
            self.rwkv_F2(i, j, R, ts)
            fw.barrier()
        R["stack"].close()
        if stop < 3:
            return
        with ExitStack() as ts:
            self.rwkv_S(i, j, R, ts)
            fw.barrier()
        if stop < 4:
            return
        with ExitStack() as ts:
            self.rwkv_O(i, j, R, ts)
            fw.barrier()

    def rwkv_F2(self, i, j, R, ts):
        fw = self.fw
        I = self.I
        twd, twd_r = R["twd"]
        adT, adT_r = R["adT"]
        def fm(name, src, n):
            t, r = fw.sb(name, [128, n], F32, ts)
            self.load_fm(t[:], r, src, n, ts)
            return t, r
        kk_p, kk_pr = fm("p_kk", I["rw_k_k"][j, :].rearrange("(n p) -> n p", p=128), 8)
        ka_p, ka_pr = fm("p_ka", I["rw_k_a"][j, :].rearrange("(n p) -> n p", p=128), 8)
        rk_p, rk_pr = fm("p_rk", I["rw_r_k"][j, :].rearrange("(n p) -> n p", p=128), 8)
        w0_p, w0_pr = fm("p_w0", I["rw_w0"][j].rearrange("z (n p) -> (z n) p", p=128), 16)
        a0_p, a0_pr = fm("p_a0", I["rw_a0"][j].rearrange("z (n p) -> (z n) p", p=128), 16)
        oka, oka_r = fw.sb("p_oka", [128, 8], F32, ts)
        fw.op("vector", lambda e: e.tensor_scalar(out=oka[:], in0=ka_p[:], scalar1=-1.0, scalar2=1.0, op0=ALU.mult, op1=ALU.add), reads=[ka_pr], writes=[oka_r])
        wup, wup_r = fw.sb("wup", [128, D], F32, ts)
        aup, aup_r = fw.sb("aup", [128, D], F32, ts)
        fw.dma(wup[:], I["rw_w_up"][j].rearrange("z r f -> (z r) f"), writes=[wup_r], sem_res=wup_r)
        fw.dma(aup[:], I["rw_a_up"][j].rearrange("z r f -> (z r) f"), writes=[aup_r], sem_res=aup_r)
        bones, bones_r = fw.sb("bones", [128, 128], F32, ts)
        fw.op("vector", lambda e: e.memset(bones[:], 0.0), writes=[bones_r])
        fw.op("vector", lambda e: e.memset(bones[0:64, 0:64], 1.0), writes=[bones_r])
        fw.op("vector", lambda e: e.memset(bones[64:128, 64:128], 1.0), writes=[bones_r])
        eps12, eps12_r = fw.sb("eps12", [128, 1], F32, ts)
        fw.op("vector", lambda e: e.memset(eps12[:], 1e-12), writes=[eps12_r])
        cmask, cmask_r = fw.sb("cmask", [128, 512], F32, ts)
        fw.op("vector", lambda e: e.memset(cmask[:], 1.0), writes=[cmask_r])
        fw.op("vector", lambda e: e.memset(cmask[:].rearrange("p (c t) -> p c t", t=64)[:, :, 0:1], 0.0), writes=[cmask_r])

        cnt = [0]

        def T(name, dt=F32, n=2, w=512):
            return [fw.sb("%s_%d" % (name, k), [128, w], dt, ts) for k in range(n)]
        rTb, kTb, vTb = T("f_r"), T("f_k"), T("f_v")
        lwb = [T("f_lw0"), T("f_lw1")]
        azb = [T("f_az0"), T("f_az1")]
        kdb = [T("f_kd0"), T("f_kd1")]
        bzb = [T("f_b0"), T("f_b1")]
        kkrb, sqb_, kkb, tb1, tb3 = T("f_kkr"), T("f_sq"), T("f_kk"), T("f_t1"), T("f_t3", n=4)
        rnb = sqb_
        tb2 = sqb_
        cwb, e1b, e2b, e3b, e4b = T("f_cw", n=4), T("f_e1", n=4), T("f_e2", n=4), T("f_e3", n=4), T("f_e4", n=2) * 2
        fmo = [T("f_o%d" % k, dt=BF16, n=4) for k in range(4)]
        hato = [T("f_h%d" % k, n=4) for k in range(2)]
        trs = T("f_trs", dt=BF16, n=4)
        bonb = T("f_bon")
        NEG = -math.exp(-0.5)
        nb = 0
        ntr = [0]

        def transpose_store(fx, k, cnt, src_t, src_r, dst_ap_fn, ntile):
            tp, tpr = self.pb[k * 4 + 2 + cnt[0] % 2]
            o_t, o_r = trs[k * 2 + cnt[0] % 2]
            cnt[0] += 1
            for tl in range(ntile):
                fx.op("tensor", lambda e, tl=tl, tp=tp: e.transpose(tp[:, tl * 128:(tl + 1) * 128], src_t[:, tl * 128:(tl + 1) * 128], self.ident[:]),
                      reads=[src_r, self.ident_r], writes=[tpr], signal=(tl == ntile - 1))
            fx.op("scalar", lambda e, tp=tp, o_t=o_t, ntile=ntile: e.copy(out=o_t[:, 0:ntile * 128], in_=tp[:, 0:ntile * 128]), reads=[tpr], writes=[o_r])
            for tl in range(ntile):
                fx.dma(dst_ap_fn(tl), o_t[:, tl * 128:(tl + 1) * 128], reads=[o_r], sem_res=o_r)

        def emit_block(fx, hp, ci, k):
            cnt = [0]
            t0, N = self.chunk(ci)
            tile0 = t0 // 128
            ntile = N // 128
            nch = N // 64
            (rT, rT_r), (kT, kT_r), (vT, vT_r) = rTb[k], kTb[k], vTb[k]
            fx.dma(rT[:, 0:N], self.sTd[hp, :, t0:t0 + N], writes=[rT_r], sem_res=rT_r)
            fx.dma(kT[:, 0:N], self.sTd[8 + hp, :, t0:t0 + N], writes=[kT_r], sem_res=kT_r)
            fx.dma(vT[:, 0:N], self.sTd[16 + hp, :, t0:t0 + N], writes=[vT_r], sem_res=vT_r)
            transpose_store(fx, k, cnt, vT, vT_r, lambda tl, tile0=tile0, hp=hp: self.Vd[tile0 + tl, :, hp, :], ntile)
            for z in range(2):
                rows = slice(z * 64, (z + 1) * 64)
                pw, pwr = self.pb[k * 4]
                pa, par = self.pb[k * 4 + 1]
                lw, lw_r = lwb[z][k]
                az, az_r = azb[z][k]
                fx.op("tensor", lambda e, pw=pw, rows=rows, hp=hp, t0=t0, N=N: e.matmul(pw[:, 0:N], lhsT=wup[rows, hp * 128:(hp + 1) * 128], rhs=twd[rows, t0:t0 + N], start=True, stop=True), reads=[wup_r, twd_r], writes=[pwr])
                fx.op("tensor", lambda e, pa=pa, rows=rows, hp=hp, t0=t0, N=N: e.matmul(pa[:, 0:N], lhsT=aup[rows, hp * 128:(hp + 1) * 128], rhs=adT[rows, t0:t0 + N], start=True, stop=True), reads=[aup_r, adT_r], writes=[par])
                fx.op("scalar", lambda e, pw=pw, lw=lw, z=z, hp=hp, N=N: e.activation(out=lw[:, 0:N], in_=pw[:, 0:N], func=AF.Sigmoid, bias=w0_p[:, z * 8 + hp:z * 8 + hp + 1], scale=1.0), reads=[pwr, w0_pr], writes=[lw_r])
                fx.op("vector", lambda e, lw=lw, N=N: e.tensor_scalar(out=lw[:, 0:N], in0=lw[:, 0:N], scalar1=NEG, scalar2=None, op0=ALU.mult), reads=[lw_r], writes=[lw_r])
                fx.op("scalar", lambda e, pa=pa, az=az, z=z, hp=hp, N=N: e.activation(out=az[:, 0:N], in_=pa[:, 0:N], func=AF.Sigmoid, bias=a0_p[:, z * 8 + hp:z * 8 + hp + 1], scale=1.0), reads=[par, a0_pr], writes=[az_r])
            (kkr, kkr_r), (sq, sq_r), (rn, rn_r), (kk, kk_r) = kkrb[k], sqb_[k], rnb[k], kkb[k]
            fx.op("vector", lambda e, kkr=kkr, kT=kT, hp=hp, N=N: e.tensor_scalar(out=kkr[:, 0:N], in0=kT[:, 0:N], scalar1=kk_p[:, hp:hp + 1], scalar2=None, op0=ALU.mult), reads=[kT_r, kk_pr], writes=[kkr_r])
            fx.op("gpsimd", lambda e, kkr=kkr, sq=sq, N=N: e.tensor_tensor(out=sq[:, 0:N], in0=kkr[:, 0:N], in1=kkr[:, 0:N], op=ALU.mult), reads=[kkr_r], writes=[sq_r])
            pss, pssr = self.pb[k * 4 + 2 + cnt[0] % 2]
            cnt[0] += 1
            fx.op("tensor", lambda e, pss=pss, sq=sq, N=N: e.matmul(pss[:, 0:N], lhsT=bones[:], rhs=sq[:, 0:N], start=True, stop=True), reads=[bones_r, sq_r], writes=[pssr])
            fx.op("scalar", lambda e, pss=pss, rn=rn, N=N: e.activation(out=rn[:, 0:N], in_=pss[:, 0:N], func=AF.Sqrt, bias=eps12[:], scale=1.0), reads=[pssr, eps12_r], writes=[rn_r])
            fx.op("vector", lambda e, rn=rn, N=N: e.reciprocal(out=rn[:, 0:N], in_=rn[:, 0:N]), reads=[rn_r], writes=[rn_r])
            fx.op("gpsimd", lambda e, kk=kk, kkr=kkr, rn=rn, N=N: e.tensor_tensor(out=kk[:, 0:N], in0=kkr[:, 0:N], in1=rn[:, 0:N], op=ALU.mult), reads=[kkr_r, rn_r], writes=[kk_r])
            for z in range(2):
                az, az_r = azb[z][k]
                kd, kd_r = kdb[z][k]
                bz, bz_r = bzb[z][k]
                fx.op("vector", lambda e, kd=kd, az=az, hp=hp, N=N: e.tensor_scalar(out=kd[:, 0:N], in0=az[:, 0:N], scalar1=ka_p[:, hp:hp + 1], scalar2=oka[:, hp:hp + 1], op0=ALU.mult, op1=ALU.add), reads=[az_r, ka_pr, oka_r], writes=[kd_r])
                fx.op("vector", lambda e, kd=kd, kT=kT, N=N: e.tensor_tensor(out=kd[:, 0:N], in0=kd[:, 0:N], in1=kT[:, 0:N], op=ALU.mult), reads=[kd_r, kT_r], writes=[kd_r])
                fx.op("gpsimd", lambda e, bz=bz, kk=kk, az=az, N=N: e.tensor_tensor(out=bz[:, 0:N], in0=kk[:, 0:N], in1=az[:, 0:N], op=ALU.mult), reads=[kk_r, az_r], writes=[bz_r])
            (x1, x1_r), (x2, x2_r) = tb1[k], tb2[k]
            fx.op("vector", lambda e, x1=x1, N=N, k=k: e.tensor_tensor(out=x1[:, 0:N], in0=kdb[0][k][0][:, 0:N], in1=kdb[1][k][0][:, 0:N], op=ALU.add), reads=[kdb[0][k][1], kdb[1][k][1]], writes=[x1_r])
            fx.op("vector", lambda e, x2=x2, rT=rT, hp=hp, N=N: e.tensor_scalar(out=x2[:, 0:N], in0=rT[:, 0:N], scalar1=rk_p[:, hp:hp + 1], scalar2=0.5, op0=ALU.mult, op1=ALU.mult), reads=[rT_r, rk_pr], writes=[x2_r])
            fx.op("vector", lambda e, x1=x1, x2=x2, N=N: e.tensor_tensor(out=x1[:, 0:N], in0=x1[:, 0:N], in1=x2[:, 0:N], op=ALU.mult), reads=[x1_r, x2_r], writes=[x1_r])
            psb, psbr = self.pb[k * 4 + 2 + cnt[0] % 2]
            cnt[0] += 1
            fx.op("tensor", lambda e, psb=psb, x1=x1, N=N: e.matmul(psb[:, 0:N], lhsT=bones[:], rhs=x1[:, 0:N], start=True, stop=True), reads=[bones_r, x1_r], writes=[psbr])
            bo, bo_r = bonb[k]
            fx.op("vector", lambda e, bo=bo, psb=psb, vT=vT, N=N: e.tensor_tensor(out=bo[:, 0:N], in0=psb[:, 0:N], in1=vT[:, 0:N], op=ALU.mult), reads=[psbr, vT_r], writes=[bo_r])
            fx.dma(self.bonusd[hp, :, t0:t0 + N], bo[:, 0:N], reads=[bo_r], sem_res=bo_r)
            for z in range(2):
                lw, lw_r = lwb[z][k]
                kd, kd_r = kdb[z][k]
                bz, bz_r = bzb[z][k]
                cw, cw_r = cwb[z * 2 + k]
                (e1, e1_r), (e2, e2_r), (e3, e3_r), (e4, e4_r) = e1b[z * 2 + k], e2b[z * 2 + k], e3b[z * 2 + k], e4b[z * 2 + k]
                (y1, y1_r) = tb3[z * 2 + k]
                Dall, Dall_r = R["Dall"][z]
                fx.op("vector", lambda e, cw=cw, lw=lw, N=N: e.tensor_tensor_scan(out=cw[:, 0:N], data0=cmask[:, 0:N], data1=lw[:, 0:N], initial=0.0, op0=ALU.mult, op1=ALU.add), reads=[cmask_r, lw_r], writes=[cw_r])
                cw3 = cw[:, 0:N].rearrange("p (c t) -> p c t", t=64)
                totb = cw3[:, :, 63:64].to_broadcast([128, nch, 64])
                ch0 = t0 // 64
                fx.op("scalar", lambda e, Dall=Dall, cw3=cw3, hp=hp, ch0=ch0, nch=nch: e.activation(out=Dall[:, hp, ch0:ch0 + nch], in_=cw3[:, :, 63], func=AF.Exp), reads=[cw_r], writes=[Dall_r])
                y13 = y1[:, 0:N].rearrange("p (c t) -> p c t", t=64)
                if z == 0:
                    fx.op("gpsimd", lambda e, y13=y13, totb=totb, cw3=cw3: e.tensor_tensor(out=y13, in0=totb, in1=cw3, op=ALU.subtract), reads=[cw_r], writes=[y1_r])
                    cwz, cwz_r = cw, cw_r
                else:
                    fx.op("gpsimd", lambda e, y1=y1, cw=cw, lw=lw, N=N: e.tensor_tensor(out=y1[:, 0:N], in0=cw[:, 0:N], in1=lw[:, 0:N], op=ALU.subtract), reads=[cw_r, lw_r], writes=[y1_r])
                    cwz, cwz_r = e4, e4_r
                    e43 = e4[:, 0:N].rearrange("p (c t) -> p c t", t=64)
                    fx.op("gpsimd", lambda e, e43=e43, totb=totb, y13=y13: e.tensor_tensor(out=e43, in0=totb, in1=y13, op=ALU.subtract), reads=[cw_r, y1_r], writes=[e4_r])
                fx.op("scalar", lambda e, e1=e1, cwz=cwz, N=N: e.activation(out=e1[:, 0:N], in_=cwz[:, 0:N], func=AF.Exp), reads=[cwz_r], writes=[e1_r])
                fx.op("scalar", lambda e, e2=e2, cwz=cwz, N=N: e.activation(out=e2[:, 0:N], in_=cwz[:, 0:N], func=AF.Exp, scale=-1.0), reads=[cwz_r], writes=[e2_r])
                fx.op("vector", lambda e, e3=e3, cwz=cwz, lw=lw, N=N: e.tensor_tensor(out=e3[:, 0:N], in0=cwz[:, 0:N], in1=lw[:, 0:N], op=ALU.subtract), reads=[cwz_r, lw_r], writes=[e3_r])
                fx.op("scalar", lambda e, e3=e3, N=N: e.activation(out=e3[:, 0:N], in_=e3[:, 0:N], func=AF.Exp), reads=[e3_r], writes=[e3_r])
                fx.op("scalar", lambda e, y1=y1, N=N: e.activation(out=y1[:, 0:N], in_=y1[:, 0:N], func=AF.Exp), reads=[y1_r], writes=[y1_r])
                outs = [fmo[q][z * 2 + k] for q in range(4)]
                fx.op("vector", lambda e, o=outs[0][0], kk=kk, e3=e3, N=N: e.scalar_tensor_tensor(out=o[:, 0:N], in0=kk[:, 0:N], scalar=-1.0, in1=e3[:, 0:N], op0=ALU.mult, op1=ALU.mult), reads=[kk_r, e3_r], writes=[outs[0][1]])
                fx.op("gpsimd", lambda e, o=outs[1][0], rT=rT, e1=e1, N=N: e.tensor_tensor(out=o[:, 0:N], in0=rT[:, 0:N], in1=e1[:, 0:N], op=ALU.mult), reads=[rT_r, e1_r], writes=[outs[1][1]])
                fx.op("vector", lambda e, o=outs[2][0], bz=bz, e2=e2, N=N: e.tensor_tensor(out=o[:, 0:N], in0=bz[:, 0:N], in1=e2[:, 0:N], op=ALU.mult), reads=[bz_r, e2_r], writes=[outs[2][1]])
                fx.op("vector", lambda e, o=outs[3][0], kd=kd, e2=e2, N=N: e.tensor_tensor(out=o[:, 0:N], in0=kd[:, 0:N], in1=e2[:, 0:N], op=ALU.mult), reads=[kd_r, e2_r], writes=[outs[3][1]])
                for q in range(4):
                    fx.dma(self.FMd[z][tile0:tile0 + ntile, :, hp, q, :].rearrange("n p t -> p n t"), outs[q][0][:, 0:N].rearrange("p (n t) -> p n t", t=128), reads=[outs[q][1]], sem_res=outs[q][1])
                (h0, h0_r), (h1, h1_r) = hato[0][z * 2 + k], hato[1][z * 2 + k]
                fx.op("vector", lambda e, h0=h0, bz=bz, y1=y1, N=N: e.tensor_tensor(out=h0[:, 0:N], in0=bz[:, 0:N], in1=y1[:, 0:N], op=ALU.mult), reads=[bz_r, y1_r], writes=[h0_r])
                fx.op("gpsimd", lambda e, h1=h1, kd=kd, y1=y1, N=N: e.tensor_tensor(out=h1[:, 0:N], in0=kd[:, 0:N], in1=y1[:, 0:N], op=ALU.mult), reads=[kd_r, y1_r], writes=[h1_r])
                transpose_store(fx, k, cnt, h0, h0_r, lambda tl, z=z, tile0=tile0, hp=hp: self.TMd[z][tile0 + tl, :, 0, hp, :], ntile)
                transpose_store(fx, k, cnt, h1, h1_r, lambda tl, z=z, tile0=tile0, hp=hp: self.TMd[z][tile0 + tl, :, 1, hp, :], ntile)


        blocks = [(hp, ci) for hp in range(8) for ci in range(9)]
        for bi in range(0, len(blocks), 2):
            recs = []
            for q in range(2):
                if bi + q < len(blocks):
                    r_ = Rec()
                    emit_block(r_, blocks[bi + q][0], blocks[bi + q][1], q)
                    recs.append(r_)
            interleave(fw, recs)

    def rwkv_S(self, i, j, R, ts):
        fw = self.fw
        mk = {}
        for z in range(2):
            mNf, mNf_r = fw.sb("mNf%d" % z, [128, 256], F32, ts)
            mLf, mLf_r = fw.sb("mLf%d" % z, [128, 128], F32, ts)
            mN, mN_r = fw.sb("mN%d" % z, [128, 2, 256], BF16, ts)
            mL, mL_r = fw.sb("mL%d" % z, [128, 4, 128], BF16, ts)
            fw.dma(mNf[:], self.Cn["mN%d" % z][:, :], writes=[mNf_r], sem_res=mNf_r)
            fw.dma(mLf[:], self.Cn["mL%d" % z][:, :], writes=[mLf_r], sem_res=mLf_r)
            for q in range(2):
                fw.op("vector", lambda e, mN=mN, mNf=mNf, q=q: e.tensor_copy(out=mN[:, q, :], in_=mNf[:]), reads=[mNf_r], writes=[mN_r])
            for q in range(4):
                fw.op("vector", lambda e, mL=mL, mLf=mLf, q=q: e.tensor_copy(out=mL[:, q, :], in_=mLf[:]), reads=[mLf_r], writes=[mL_r])
            mk[z] = (mN, mN_r, mL, mL_r)
        identb4, identb4_r = fw.sb("ident4", [128, 4, 128], BF16, ts)
        for q in range(4):
            fw.op("vector", lambda e, q=q: e.tensor_copy(out=identb4[:, q, :], in_=self.ident[:]), reads=[self.ident_r], writes=[identb4_r])
        Mst = []
        for z in range(2):
            row = []
            for hf in range(2):
                t, r = fw.sb("M%d_%d" % (z, hf), [128, 8, 64], F32, ts)
                fw.op("vector", lambda e, t=t: e.memset(t[:], 0.0), writes=[r])
                tb_, rb_ = fw.sb("Mb%d_%d" % (z, hf), [128, 8, 64], BF16, ts)
                fw.op("vector", lambda e, tb_=tb_: e.memset(tb_[:], 0.0), writes=[rb_])
                row.append((t, r, tb_, rb_))
            Mst.append(row)
        FMb = [fw.sb("sFM%d" % k, [128, 8, 4, 128], BF16, ts) for k in range(2)]
        TMb = [fw.sb("sTM%d" % k, [128, 2, 8, 128], BF16, ts) for k in range(2)]
        Vb = [fw.sb("sV%d" % k, [128, 8, 128], BF16, ts) for k in range(2)]
        G13, _ = fw.sb("sG13", [128, 16, 256], BF16, ts)
        G24, _ = fw.sb("sG24", [128, 16, 256], BF16, ts)
        G13_r = [Res("sG13_%d" % g) for g in range(8)]
        G24_r = [Res("sG24_%d" % g) for g in range(8)]
        Lb_ = [fw.sb("sL%d" % k, [128, 16, 128], BF16, ts)[0] for k in range(2)]
        Nb_ = [fw.sb("sN%d" % k, [128, 16, 128], BF16, ts)[0] for k in range(2)]
        Pm, _ = fw.sb("sP", [128, 16, 128], BF16, ts)
        L_r = [[Res("sL%d_%d" % (k, g)) for g in range(4)] for k in range(2)]
        N_r = [[Res("sN%d_%d" % (k, g)) for g in range(4)] for k in range(2)]
        P_r = [Res("sP_%d" % g) for g in range(4)]
        Z2s = [fw.sb("sZ2s%d" % k, [128, 512], F32, ts) for k in range(2)]
        Zs = [fw.sb("sZs%d" % k, [128, 512], BF16, ts) for k in range(2)]
        Us = [fw.sb("sUs%d" % k, [128, 512], BF16, ts) for k in range(2)]
        Ys = [fw.sb("sYs%d" % k, [128, 512], F32, ts) for k in range(2)]
        npb = [0]

        def prep_bank():
            b = self.pb[npb[0] % 8]
            npb[0] += 1
            return b

        order = [list(range(NTI)), [1, 0] + list(range(NTI - 1, 1, -1))]
        def issue_loads(n):
            z_, tile_ = n % 2, order[n % 2][n // 2]
            FM_, FM_r_ = FMb[n % 2]
            TM_, TM_r_ = TMb[n % 2]
            V_, V_r_ = Vb[n % 2]
            fw.dma(FM_[:], self.FMd[z_][tile_], writes=[FM_r_], sem_res=FM_r_)
            fw.dma(TM_[:], self.TMd[z_][tile_], writes=[TM_r_], sem_res=TM_r_)
            fw.dma(V_[:], self.Vd[tile_], writes=[V_r_], sem_res=V_r_)

        nstep = 0
        for si in range(NTI):
            for z in (0, 1):
                tile = order[z][si]
                k = nstep % 2
                nstep += 1
                FM, FM_r = FMb[k]
                TM, TM_r = TMb[k]
                V, V_r = Vb[k]
                mN, mN_r, mL, mL_r = mk[z]
                if nstep == 1:
                    issue_loads(0)
                if nstep < 2 * NTI:
                    issue_loads(nstep)
                for g2 in range(8):
                    pB, pBr = prep_bank()
                    pK, pKr = prep_bank()
                    pL, pLr = prep_bank()
                    for hl in range(2):
                        h = g2 * 2 + hl
                        hp, hh = h % 8, h // 8
                        rows = slice(hh * 64, (hh + 1) * 64)
                        ar = FM[rows, hp, 0:2, :].rearrange("p a t -> p (a t)")
                        fw.op("tensor", lambda e, FM=FM, TM=TM, V=V, pB=pB, rows=rows, hp=hp, hl=hl, ar=ar: e.matmul(pB[:, hl * 256:(hl + 1) * 256], lhsT=FM[rows, hp, 2, :], rhs=ar, start=True, stop=True), reads=[FM_r], writes=[pBr], signal=(hl == 1))
                        fw.op("tensor", lambda e, FM=FM, TM=TM, V=V, pK=pK, rows=rows, hp=hp, hl=hl, ar=ar: e.matmul(pK[:, hl * 256:(hl + 1) * 256], lhsT=FM[rows, hp, 3, :], rhs=ar, start=True, stop=True), reads=[FM_r], writes=[pKr], signal=(hl == 1))
                        fw.op("tensor", lambda e, FM=FM, TM=TM, V=V, pL=pL, rows=rows, hp=hp, hl=hl: e.matmul(pL[:, hl * 128:(hl + 1) * 128], lhsT=FM[rows, hp, 0, :], rhs=FM[rows, hp, 2, :], start=True, stop=True), reads=[FM_r], writes=[pLr], signal=(hl == 1))
                    h0 = g2 * 2
                    fw.op("vector", lambda e, FM=FM, TM=TM, V=V, pB=pB, h0=h0, mN=mN: e.tensor_tensor(out=G13[:, h0:h0 + 2, :], in0=pB[:, :].rearrange("p (h c) -> p h c", h=2), in1=mN[:], op=ALU.mult), reads=[pBr, mN_r], writes=[G13_r[g2]])
                    fw.op("vector", lambda e, FM=FM, TM=TM, V=V, pK=pK, h0=h0, mN=mN: e.tensor_tensor(out=G24[:, h0:h0 + 2, :], in0=pK[:, :].rearrange("p (h c) -> p h c", h=2), in1=mN[:], op=ALU.mult), reads=[pKr, mN_r], writes=[G24_r[g2]])
                    fw.op("vector", lambda e, FM=FM, TM=TM, V=V, pL=pL, h0=h0, mL=mL: e.tensor_tensor(out=Lb_[0][:, h0:h0 + 2, :], in0=pL[:, 0:256].rearrange("p (h c) -> p h c", h=2), in1=mL[:, 0:2, :], op=ALU.mult), reads=[pLr, mL_r], writes=[L_r[0][g2 // 2]])
                sstop = getattr(self, "s_stop", 99)
                if sstop < 1:
                    continue
                for g4 in range(4):
                    hs = slice(g4 * 4, g4 * 4 + 4)
                    fw.op("gpsimd", lambda e, FM=FM, TM=TM, V=V, hs=hs: e.tensor_copy(out=Nb_[0][:, hs, :], in_=G13[:, hs, 0:128]), reads=[G13_r[g4 * 2], G13_r[g4 * 2 + 1]], writes=[N_r[0][g4]])
                    fw.op("gpsimd", lambda e, FM=FM, TM=TM, V=V, hs=hs: e.tensor_tensor(out=Pm[:, hs, :], in0=G13[:, hs, 0:128], in1=identb4[:], op=ALU.add), reads=[G13_r[g4 * 2], G13_r[g4 * 2 + 1], identb4_r], writes=[P_r[g4]])
                for lev in range(1, 6):
                    a, b = (lev - 1) % 2, lev % 2
                    for g4 in range(4):
                        hs = slice(g4 * 4, g4 * 4 + 4)
                        pl, plr = prep_bank()
                        for hl in range(4):
                            h = g4 * 4 + hl
                            fw.op("tensor", lambda e, FM=FM, TM=TM, V=V, pl=pl, h=h, hl=hl, a=a: e.matmul(pl[:, hl * 128:(hl + 1) * 128], lhsT=Nb_[a][:, h, :], rhs=Lb_[a][:, h, :], start=True, stop=True), reads=[N_r[a][g4], L_r[a][g4]], writes=[plr], signal=(hl == 3))
                        fw.op("scalar", lambda e, FM=FM, TM=TM, V=V, pl=pl, hs=hs, b=b: e.copy(out=Lb_[b][:, hs, :], in_=pl[:, :].rearrange("p (h c) -> p h c", h=4)), reads=[plr], writes=[L_r[b][g4]])
                        if lev < 5:
                            pn, pnr = prep_bank()
                            for hl in range(4):
                                h = g4 * 4 + hl
                                fw.op("tensor", lambda e, FM=FM, TM=TM, V=V, pn=pn, h=h, hl=hl, a=a: e.matmul(pn[:, hl * 128:(hl + 1) * 128], lhsT=Lb_[a][:, h, :], rhs=Nb_[a][:, h, :], start=True, stop=True), reads=[N_r[a][g4], L_r[a][g4]], writes=[pnr], signal=(hl == 3))
                            fw.op("scalar", lambda e, FM=FM, TM=TM, V=V, pn=pn, hs=hs, b=b: e.copy(out=Nb_[b][:, hs, :], in_=pn[:, :].rearrange("p (h c) -> p h c", h=4)), reads=[pnr], writes=[N_r[b][g4]])
                    for g4 in range(4):
                        hs = slice(g4 * 4, g4 * 4 + 4)
                        pq, pqr = prep_bank()
                        for hl in range(4):
                            h = g4 * 4 + hl
                            fw.op("tensor", lambda e, FM=FM, TM=TM, V=V, pq=pq, h=h, hl=hl, b=b: e.matmul(pq[:, hl * 128:(hl + 1) * 128], lhsT=Lb_[b][:, h, :], rhs=Pm[:, h, :], start=True, stop=True), reads=[L_r[b][g4], P_r[g4]], writes=[pqr], signal=(hl == 3))
                        fw.op("vector", lambda e, FM=FM, TM=TM, V=V, pq=pq, hs=hs: e.tensor_tensor(out=Pm[:, hs, :], in0=pq[:, :].rearrange("p (h c) -> p h c", h=4), in1=Pm[:, hs, :], op=ALU.add), reads=[pqr, P_r[g4]], writes=[P_r[g4]])
                if self.dbg and nstep == 1:
                    self.dG13 = self.nc.dram_tensor("dG13", [128, 16, 256], F32, kind="ExternalOutput").ap()
                    self.dG24 = self.nc.dram_tensor("dG24", [128, 16, 256], F32, kind="ExternalOutput").ap()
                    self.dP = self.nc.dram_tensor("dP", [128, 16, 128], F32, kind="ExternalOutput").ap()
                    self.dFM = self.nc.dram_tensor("dFM", [128, 8, 4, 128], F32, kind="ExternalOutput").ap()
                    self.dTM = self.nc.dram_tensor("dTM", [128, 2, 8, 128], F32, kind="ExternalOutput").ap()
                    self.dV = self.nc.dram_tensor("dV", [128, 8, 128], F32, kind="ExternalOutput").ap()
                    dr = fw.res("dbgS", ts)
                    fw.dma(self.dFM, FM[:], reads=[FM_r], sem_res=dr, ek="gpsimd")
                    fw.dma(self.dTM, TM[:], reads=[TM_r], sem_res=dr, ek="gpsimd")
                    fw.dma(self.dV, V[:], reads=[V_r], sem_res=dr, ek="gpsimd")
                    fw.dma(self.dG13, G13[:], reads=list(G13_r), sem_res=dr, ek="gpsimd")
                    fw.dma(self.dG24, G24[:], reads=list(G24_r), sem_res=dr, ek="gpsimd")
                    fw.dma(self.dP, Pm[:], reads=list(P_r), sem_res=dr, ek="gpsimd")
                if sstop < 2:
                    continue
                cks = (0, 1) if z == 0 else (1, 0)
                for hf in range(2):
                    hr = slice(hf * 64, (hf + 1) * 64)
                    M, M_r, Mb, Mb_r = Mst[z][hf]
                    Dall, Dall_r = R["Dall"][z]
                    tA, tAr = self.pb[0]
                    tB, tBr = self.pb[1]
                    tC, tCr = self.pb[2]
                    tD, tDr = self.pb[3]
                    z2, z2_r = Z2s[hf]
                    zs_, zs_r = Zs[hf]
                    us, us_r = Us[hf]
                    ys, ys_r = Ys[hf]
                    g13r = list(G13_r)
                    g24r = list(G24_r)
                    pr_ = list(P_r)
                    for ck in range(2):
                        rows = slice(ck * 64, (ck + 1) * 64)
                        for hp in range(8):
                            h = hf * 8 + hp
                            fw.op("tensor", lambda e, FM=FM, TM=TM, V=V, rows=rows, h=h, hp=hp, hr=hr, ck=ck: e.matmul(tA[rows, hp * 64:(hp + 1) * 64], lhsT=G24[rows, h, ck * 64:(ck + 1) * 64], rhs=V[rows, hp, hr], start=True, stop=True),
                                  reads=g24r + [V_r], writes=[tAr], signal=(ck == 1 and hp == 7))
                    fw.op("scalar", lambda e, FM=FM, TM=TM, V=V, z2=z2: e.copy(out=z2[:], in_=tA[:, :]), reads=[tAr], writes=[z2_r])
                    for ck in (cks if sstop >= 3 else ()):
                        rows = slice(ck * 64, (ck + 1) * 64)
                        cols = slice(ck * 64, (ck + 1) * 64)
                        for hp in range(8):
                            fw.op("tensor", lambda e, FM=FM, TM=TM, V=V, rows=rows, cols=cols, hp=hp, hr=hr, Mb=Mb: e.matmul(tB[rows, hp * 64:(hp + 1) * 64], lhsT=FM[hr, hp, 0, cols], rhs=Mb[hr, hp, :], start=True, stop=True),
                                  reads=[FM_r, Mb_r], writes=[tBr], signal=(hp == 7))
                        for hp in range(8):
                            fw.op("tensor", lambda e, FM=FM, TM=TM, V=V, rows=rows, cols=cols, hp=hp, hr=hr, Mb=Mb: e.matmul(tC[rows, hp * 64:(hp + 1) * 64], lhsT=FM[hr, hp, 1, cols], rhs=Mb[hr, hp, :], start=True, stop=True),
                                  reads=[FM_r, Mb_r], writes=[tCr], signal=(hp == 7))
                        fw.op("vector", lambda e, FM=FM, TM=TM, V=V, rows=rows, zs_=zs_, z2=z2: e.tensor_tensor(out=zs_[rows, :], in0=tB[rows, :], in1=z2[rows, :], op=ALU.add), reads=[tBr, z2_r], writes=[zs_r])
                        if sstop < 4:
                            continue
                        for hp in range(8):
                            h = hf * 8 + hp
                            fw.op("tensor", lambda e, FM=FM, TM=TM, V=V, rows=rows, cols=cols, hp=hp, h=h, zs_=zs_: e.matmul(tB[rows, hp * 64:(hp + 1) * 64], lhsT=Pm[rows, h, cols], rhs=zs_[rows, hp * 64:(hp + 1) * 64], start=True, stop=True),
                                  reads=pr_ + [zs_r], writes=[tBr], signal=(hp == 7))
                        fw.op("scalar", lambda e, FM=FM, TM=TM, V=V, rows=rows, us=us: e.copy(out=us[rows, :], in_=tB[rows, :]), reads=[tBr], writes=[us_r])
                        if sstop < 5:
                            continue
                        for hp in range(8):
                            h = hf * 8 + hp
                            fw.op("tensor", lambda e, FM=FM, TM=TM, V=V, rows=rows, ck=ck, hp=hp, h=h, us=us: e.matmul(tA[rows, hp * 64:(hp + 1) * 64], lhsT=G13[rows, h, 128 + ck * 64:128 + (ck + 1) * 64], rhs=us[rows, hp * 64:(hp + 1) * 64], start=True, stop=False),
                                  reads=g13r + [us_r], writes=[tAr], signal=False)
                            fw.op("tensor", lambda e, FM=FM, TM=TM, V=V, rows=rows, ck=ck, hp=hp, h=h, hr=hr: e.matmul(tA[rows, hp * 64:(hp + 1) * 64], lhsT=G24[rows, h, 128 + ck * 64:128 + (ck + 1) * 64], rhs=V[rows, hp, hr], start=False, stop=True),
                                  reads=g24r + [V_r], writes=[tAr], signal=(hp == 7))
                        if sstop < 6:
                            continue
                        for hp in range(8):
                            fw.op("tensor", lambda e, FM=FM, TM=TM, V=V, rows=rows, hp=hp, hr=hr, us=us: e.matmul(tD[hr, hp * 64:(hp + 1) * 64], lhsT=TM[rows, 0, hp, hr], rhs=us[rows, hp * 64:(hp + 1) * 64], start=True, stop=False),
                                  reads=[TM_r, us_r], writes=[tDr], signal=False)
                            fw.op("tensor", lambda e, FM=FM, TM=TM, V=V, rows=rows, hp=hp, hr=hr: e.matmul(tD[hr, hp * 64:(hp + 1) * 64], lhsT=TM[rows, 1, hp, hr], rhs=V[rows, hp, hr], start=False, stop=True),
                                  reads=[TM_r, V_r], writes=[tDr], signal=(hp == 7))
                        c = tile * 2 + ck
                        dbc = Dall[hr, :, c:c + 1].to_broadcast([64, 8, 64])
                        fw.op("vector", lambda e, FM=FM, TM=TM, V=V, M=M, dbc=dbc, hr=hr: e.tensor_tensor(out=M[hr, :, :], in0=M[hr, :, :], in1=dbc, op=ALU.mult), reads=[M_r, Dall_r], writes=[M_r])
                        fw.op("vector", lambda e, FM=FM, TM=TM, V=V, M=M, hr=hr: e.tensor_tensor(out=M[hr, :, :], in0=tD[hr, :].rearrange("p (h i) -> p h i", h=8), in1=M[hr, :, :], op=ALU.add), reads=[M_r, tDr], writes=[M_r])
                        fw.op("scalar", lambda e, FM=FM, TM=TM, V=V, M=M, Mb=Mb, hr=hr: e.copy(out=Mb[hr, :, :], in_=M[hr, :, :]), reads=[M_r], writes=[Mb_r])
                    if sstop < 7:
                        continue
                    fw.op("scalar", lambda e, FM=FM, TM=TM, V=V, ys=ys: e.copy(out=ys[:], in_=tC[:, :]), reads=[tCr], writes=[ys_r])
                    fw.op("vector", lambda e, FM=FM, TM=TM, V=V, ys=ys: e.tensor_tensor(out=ys[:], in0=tA[:, :], in1=ys[:], op=ALU.add), reads=[tAr, ys_r], writes=[ys_r])
                    ydst = self.yd[z][tile * 128:(tile + 1) * 128, :].rearrange("t (hp hh i) -> t hp hh i", hh=2, i=64)[:, :, hf, :]
                    fw.dma(ydst, ys[:].rearrange("p (hp i) -> p hp i", i=64), reads=[ys_r], sem_res=ys_r)

    def rwkv_O(self, i, j, R, ts):
        fw = self.fw
        I = self.I
        lnw, lnw_r = fw.sb("o_lnw", [128, D], F32, ts)
        lnb, lnb_r = fw.sb("o_lnb", [128, D], F32, ts)
        fw.dma(lnw[:], I["rw_ln_w"][j, :].partition_broadcast(128), writes=[lnw_r], sem_res=lnw_r)
        fw.dma(lnb[:], I["rw_ln_b"][j, :].partition_broadcast(128), writes=[lnb_r], sem_res=lnb_r)
        epsg, epsg_r = fw.sb("o_eps", [128, 1], F32, ts)
        fw.op("vector", lambda e: e.memset(epsg[:], 64e-5), writes=[epsg_r])
        y0b = [fw.sb("o_y0%d" % k, [128, D], F32, ts) for k in range(2)]
        y1b = [fw.sb("o_y1%d" % k, [128, D], F32, ts) for k in range(2)]
        sqb = [fw.sb("o_sq%d" % k, [128, D], F32, ts) for k in range(2)]
        stb = [fw.sb("o_st%d" % k, [128, 2, 16], F32, ts) for k in range(2)]
        bnb = [fw.sb("o_bn%d" % k, [128, 8, 128], F32, ts) for k in range(2)]
        zb = [fw.sb("o_z%d" % k, [128, 8, 128], BF16, ts) for k in range(2)]
        ob = [fw.sb("o_o%d" % k, [128, 8, 128], F32, ts) for k in range(2)]
        ogb = [fw.sb("o_og%d" % k, [128, 8, 128], BF16, ts) for k in range(2)]
        def tile_body(fx, tt):
            k = tt % 2
            (y0, y0_r), (y1, y1_r), (sq, sq_r), (st, st_r) = y0b[k], y1b[k], sqb[k], stb[k]
            (bn, bn_r), (zt, zt_r), (o_, o_r), (og_, og_r_) = bnb[k], zb[k], ob[k], ogb[k]
            rs = slice(tt * 128, (tt + 1) * 128)
            fx.dma(y0[:], self.yd[0][rs, :], writes=[y0_r], sem_res=y0_r)
            fx.dma(y1[:], self.yd[1][rs, :], writes=[y1_r], sem_res=y1_r)
            fx.dma(bn[:], self.bonusd[:, :, rs].rearrange("h p t -> p h t"), writes=[bn_r], sem_res=bn_r)
            fx.dma(zt[:], self.zsd[:, :, rs].rearrange("h p t -> p h t"), writes=[zt_r], sem_res=zt_r)
            fx.op("gpsimd", lambda e, y0=y0, y1=y1: e.tensor_tensor(out=y0[:], in0=y0[:], in1=y1[:], op=ALU.add), reads=[y0_r, y1_r], writes=[y0_r])
            y3 = y0[:].rearrange("p (h d) -> p h d", d=64)
            s3 = sq[:].rearrange("p (h d) -> p h d", d=64)
            fx.op("vector", lambda e, st=st, y3=y3: e.tensor_reduce(out=st[:, 0, :], in_=y3, axis=AX.X, op=ALU.add), reads=[y0_r], writes=[st_r])
            fx.op("vector", lambda e, st=st: e.tensor_scalar(out=st[:, 0, :], in0=st[:, 0, :], scalar1=1.0 / 64.0, scalar2=None, op0=ALU.mult), reads=[st_r], writes=[st_r])
            mbc = st[:, 0, :].unsqueeze(2).to_broadcast([128, 16, 64])
            fx.op("vector", lambda e, y3=y3, mbc=mbc: e.tensor_tensor(out=y3, in0=y3, in1=mbc, op=ALU.subtract), reads=[y0_r, st_r], writes=[y0_r])
            fx.op("gpsimd", lambda e, sq=sq, y0=y0: e.tensor_tensor(out=sq[:], in0=y0[:], in1=y0[:], op=ALU.mult), reads=[y0_r], writes=[sq_r])
            fx.op("vector", lambda e, st=st, s3=s3: e.tensor_reduce(out=st[:, 1, :], in_=s3, axis=AX.X, op=ALU.add), reads=[sq_r], writes=[st_r])
            fx.op("scalar", lambda e, st=st: e.activation(out=st[:, 1, :], in_=st[:, 1, :], func=AF.Sqrt, bias=epsg[:], scale=1.0 / 64.0), reads=[st_r, epsg_r], writes=[st_r])
            fx.op("vector", lambda e, st=st: e.reciprocal(out=st[:, 1, :], in_=st[:, 1, :]), reads=[st_r], writes=[st_r])
            rbc = st[:, 1, :].unsqueeze(2).to_broadcast([128, 16, 64])
            fx.op("vector", lambda e, y3=y3, rbc=rbc: e.tensor_tensor(out=y3, in0=y3, in1=rbc, op=ALU.mult), reads=[y0_r, st_r], writes=[y0_r])
            fx.op("gpsimd", lambda e, y0=y0: e.tensor_tensor(out=y0[:], in0=y0[:], in1=lnw[:], op=ALU.mult), reads=[y0_r, lnw_r], writes=[y0_r])
            fx.op("gpsimd", lambda e, y0=y0: e.tensor_tensor(out=y0[:], in0=y0[:], in1=lnb[:], op=ALU.add), reads=[y0_r, lnb_r], writes=[y0_r])
            for half in range(2):
                tp, tpr = self.pb[(tt * 2 + half) % 8]
                for q in range(4):
                    hc = half * 4 + q
                    fx.op("tensor", lambda e, tp=tp, q=q, hc=hc, y0=y0: e.transpose(tp[:, q * 128:(q + 1) * 128], y0[:, hc * 128:(hc + 1) * 128], self.ident[:]), reads=[y0_r, self.ident_r], writes=[tpr], signal=(q == 3))
                fx.op("vector", lambda e, tp=tp, half=half, o_=o_, bn=bn: e.tensor_tensor(out=o_[:, half * 4:half * 4 + 4, :], in0=tp[:, :].rearrange("p (h t) -> p h t", h=4), in1=bn[:, half * 4:half * 4 + 4, :], op=ALU.add), reads=[tpr, bn_r], writes=[o_r])
            fx.op("gpsimd", lambda e, og_=og_, o_=o_, zt=zt: e.tensor_tensor(out=og_[:], in0=o_[:], in1=zt[:], op=ALU.mult), reads=[o_r, zt_r], writes=[og_r_])
            fx.dma(self.og[:, :, rs].rearrange("h p t -> p h t"), og_[:], reads=[og_r_], sem_res=og_r_)

        for tt in range(0, NTI, 2):
            ra, rb = Rec(), Rec()
            tile_body(ra, tt)
            tile_body(rb, tt + 1)
            interleave(fw, [ra, rb])


_CACHE = {}


def get_prog(n_layers=DEPTH, dbg=False):
    key = (n_layers, dbg)
    if key not in _CACHE:
        p = Prog(n_layers, dbg)
        p.build()
        _CACHE[key] = p
    return _CACHE[key]


def make_in_maps(inputs, consts, cores):
    maps = []
    for b in cores:
        m = {}
        for k, shp in INPUT_SHAPES.items():
            a = np.asarray(inputs[k])
            if k in ("x", "ctx"):
                a = a[b]
            elif k == "c":
                a = a[b:b + 1]
            a = np.ascontiguousarray(a, dtype=np.float32).reshape(shp)
            m[k] = a
        for k, v in consts.items():
            m["k_" + k] = v
        maps.append(m)
    return maps


def kernel(**inputs):
    p = get_prog()
    cores = list(range(8))
    in_maps = make_in_maps(inputs, p.consts, cores)
    res = run_bass_kernel_spmd(p.nc, in_maps, core_ids=cores)
    out = np.stack([np.asarray(r["out"]) for r in res.results], axis=0)
    return out.astype(np.float32)
```
